# Optimizing a Trainium2 kernel written in Bass

```python
import math
import jax, jax.numpy as jnp
from jax import lax
import numpy as np

D_MODEL = 1024
BATCH = 8
SEQ = 2048
DEPTH = 2
DEC_BATCH = 2
DEC_SEQ = 8192
PAST_LEN = 128

N_MIXERS = 2
N_ATTN_LAYERS = (DEPTH + 1) // 2
N_CONV_LAYERS = DEPTH // 2
HEAD_DIM = 64
N_HEADS = D_MODEL // HEAD_DIM
N_KV_HEADS = 4
GROUP = N_HEADS // N_KV_HEADS
Q_DIM = N_HEADS * HEAD_DIM
KV_DIM = N_KV_HEADS * HEAD_DIM
QKV_DIM = Q_DIM + 2 * KV_DIM
Q_BLOCK = 128
GRID_W = 64
ROPE_AXIS_DIM = HEAD_DIM // 2
ROPE_THETA = 10000.0
CONV_WIDTH = 31
CONV_PAD = CONV_WIDTH // 2
FFN_DIM = ((8 * D_MODEL + 3 * 256 - 1) // (3 * 256)) * 256
RMS_EPS = 1e-6
LN_EPS = 1e-5

kernel_name = "hybrid_axial_gqa_conformer_encoder"


def _rms_norm(x, g):
    xf = x.astype(jnp.float32)
    y = xf * lax.rsqrt(jnp.mean(xf * xf, axis=-1, keepdims=True) + RMS_EPS)
    return (y * g.astype(jnp.float32)).astype(x.dtype)


def _layer_norm(x, g, b):
    xf = x.astype(jnp.float32)
    mu = jnp.mean(xf, axis=-1, keepdims=True)
    xc = xf - mu
    var = jnp.mean(xc * xc, axis=-1, keepdims=True)
    y = xc * lax.rsqrt(var + LN_EPS)
    return (y * g.astype(jnp.float32) + b.astype(jnp.float32)).astype(x.dtype)


def _axial_rope_tables(seq_len):
    rows = seq_len // GRID_W
    row_ids = jnp.repeat(jnp.arange(rows, dtype=jnp.float32), GRID_W)
    col_ids = jnp.tile(jnp.arange(GRID_W, dtype=jnp.float32), rows)
    inv_freq = ROPE_THETA ** (-jnp.arange(0, ROPE_AXIS_DIM, 2, dtype=jnp.float32) / ROPE_AXIS_DIM)
    ang_r = row_ids[:, None] * inv_freq[None, :]
    ang_c = col_ids[:, None] * inv_freq[None, :]
    return jnp.cos(ang_r), jnp.sin(ang_r), jnp.cos(ang_c), jnp.sin(ang_c)


def _rotate_half(x, cos, sin):
    f = x.shape[-1] // 2
    x1, x2 = x[..., :f], x[..., f:]
    c = cos[:, None, :]
    s = sin[:, None, :]
    return jnp.concatenate([x1 * c - x2 * s, x2 * c + x1 * s], axis=-1)


def _apply_axial_rope(x, tables):
    cos_r, sin_r, cos_c, sin_c = tables
    xf = x.astype(jnp.float32)
    xr = _rotate_half(xf[..., :ROPE_AXIS_DIM], cos_r, sin_r)
    xc = _rotate_half(xf[..., ROPE_AXIS_DIM:], cos_c, sin_c)
    return jnp.concatenate([xr, xc], axis=-1).astype(x.dtype)


def _attention(h, w_qkv, q_norm_g, k_norm_g, w_o):
    B, S, _ = h.shape
    qkv = h @ w_qkv
    q = qkv[..., :Q_DIM].reshape(B, S, N_HEADS, HEAD_DIM)
    k = qkv[..., Q_DIM:Q_DIM + KV_DIM].reshape(B, S, N_KV_HEADS, HEAD_DIM)
    v = qkv[..., Q_DIM + KV_DIM:].reshape(B, S, N_KV_HEADS, HEAD_DIM)
    tables = _axial_rope_tables(S)
    scale = 1.0 / math.sqrt(HEAD_DIM)
    q = _apply_axial_rope(_rms_norm(q, q_norm_g), tables) * jnp.asarray(scale, dtype=h.dtype)
    k = _apply_axial_rope(_rms_norm(k, k_norm_g), tables)
    n_blk = S // Q_BLOCK
    qb = q.reshape(B, n_blk, Q_BLOCK, N_KV_HEADS, GROUP, HEAD_DIM).transpose(1, 0, 2, 3, 4, 5)

    def one_block(q_blk):
        s = jnp.einsum('bqkgd,bskd->bkgqs', q_blk, k, preferred_element_type=jnp.float32)
        p = jax.nn.softmax(s, axis=-1).astype(v.dtype)
        return jnp.einsum('bkgqs,bskd->bqkgd', p, v)

    o = lax.map(one_block, qb)
    o = o.transpose(1, 0, 2, 3, 4, 5).reshape(B, S, Q_DIM)
    return o @ w_o


def _conformer_conv(h, w_in, b_in, dw_w, dw_b, ln_g, ln_b, w_out, b_out):
    C = h.shape[-1]
    u = h @ w_in + b_in
    a, gate = u[..., :C], u[..., C:]
    u = a * jax.nn.sigmoid(gate)
    u = lax.conv_general_dilated(
        u, dw_w[:, None, :], window_strides=(1,), padding=[(CONV_PAD, CONV_PAD)],
        dimension_numbers=('NWC', 'WIO', 'NWC'), feature_group_count=C) + dw_b
    u = jax.nn.silu(_layer_norm(u, ln_g, ln_b))
    return u @ w_out + b_out


def _swiglu(h, w_gate, w_up, w_down):
    return (jax.nn.silu(h @ w_gate) * (h @ w_up)) @ w_down


def _trunk(x, attn_norm_g, w_qkv, q_norm_g, k_norm_g, w_o,
           conv_norm_g, conv_w_in, conv_b_in, dw_w, dw_b, conv_ln_g, conv_ln_b, conv_w_out, conv_b_out,
           ffn_norm_g, w_gate, w_up, w_down, final_norm_g):
    for i in range(DEPTH):
        j = i // N_MIXERS
        if i % N_MIXERS == 0:
            x = x + _attention(_rms_norm(x, attn_norm_g[j]), w_qkv[j], q_norm_g[j], k_norm_g[j], w_o[j])
        else:
            x = x + _conformer_conv(_rms_norm(x, conv_norm_g[j]), conv_w_in[j], conv_b_in[j], dw_w[j], dw_b[j],
                                    conv_ln_g[j], conv_ln_b[j], conv_w_out[j], conv_b_out[j])
        x = x + _swiglu(_rms_norm(x, ffn_norm_g[i]), w_gate[i], w_up[i], w_down[i])
    return _rms_norm(x, final_norm_g)


def setup_inputs(seed: int = 0) -> dict:
    key = jax.random.key(seed)
    ks = jax.random.split(key, 24)
    f32 = jnp.float32

    def w(k, shape, fan_in):
        return jax.random.normal(k, shape, f32) * (fan_in ** -0.5)

    def gain(k, shape):
        return 1.0 + 0.02 * jax.random.normal(k, shape, f32)

    def bias(k, shape):
        return 0.02 * jax.random.normal(k, shape, f32)

    D, F, NA, NC = D_MODEL, FFN_DIM, N_ATTN_LAYERS, N_CONV_LAYERS
    return {
        "x_prompt": jax.random.normal(ks[0], (BATCH, SEQ, D), f32),
        "x_sample": jax.random.normal(ks[1], (DEC_BATCH, DEC_SEQ, D), f32),
        "attn_norm_g": gain(ks[2], (NA, D)),
        "w_qkv": w(ks[3], (NA, D, QKV_DIM), D),
        "q_norm_g": gain(ks[4], (NA, HEAD_DIM)),
        "k_norm_g": gain(ks[5], (NA, HEAD_DIM)),
        "w_o": w(ks[6], (NA, Q_DIM, D), Q_DIM),
        "conv_norm_g": gain(ks[7], (NC, D)),
        "conv_w_in": w(ks[8], (NC, D, 2 * D), D),
        "conv_b_in": bias(ks[9], (NC, 2 * D)),
        "dw_w": w(ks[10], (NC, CONV_WIDTH, D), CONV_WIDTH),
        "dw_b": bias(ks[11], (NC, D)),
        "conv_ln_g": gain(ks[12], (NC, D)),
        "conv_ln_b": bias(ks[13], (NC, D)),
        "conv_w_out": w(ks[14], (NC, D, D), D),
        "conv_b_out": bias(ks[15], (NC, D)),
        "ffn_norm_g": gain(ks[16], (DEPTH, D)),
        "w_gate": w(ks[17], (DEPTH, D, F), D),
        "w_up": w(ks[18], (DEPTH, D, F), D),
        "w_down": w(ks[19], (DEPTH, F, D), F),
        "final_norm_g": gain(ks[20], (D,)),
    }


def reference(x_prompt, x_sample, attn_norm_g, w_qkv, q_norm_g, k_norm_g, w_o,
              conv_norm_g, conv_w_in, conv_b_in, dw_w, dw_b, conv_ln_g, conv_ln_b, conv_w_out, conv_b_out,
              ffn_norm_g, w_gate, w_up, w_down, final_norm_g):
    y_prompt = _trunk(x_prompt, attn_norm_g, w_qkv, q_norm_g, k_norm_g, w_o,
                      conv_norm_g, conv_w_in, conv_b_in, dw_w, dw_b, conv_ln_g, conv_ln_b, conv_w_out, conv_b_out,
                      ffn_norm_g, w_gate, w_up, w_down, final_norm_g)
    y_sample = _trunk(x_sample, attn_norm_g, w_qkv, q_norm_g, k_norm_g, w_o,
                      conv_norm_g, conv_w_in, conv_b_in, dw_w, dw_b, conv_ln_g, conv_ln_b, conv_w_out, conv_b_out,
                      ffn_norm_g, w_gate, w_up, w_down, final_norm_g)
    return (y_prompt, y_sample)
```

```python
import math
import numpy as np
import concourse.bass as bass
import concourse.mybir as mybir
from concourse.bass_utils import run_bass_kernel_spmd

F32 = mybir.dt.float32
BF16 = mybir.dt.bfloat16
AF = mybir.ActivationFunctionType
ALU = mybir.AluOpType
AX = mybir.AxisListType

D = 1024
NH = 16
NKV = 4
HD = 64
FF = 2816
NFC = FF // 128
CW = 31
CP = 15
RMS_EPS = 1e-6
LN_EPS = 1e-5
NCORES = 8
import os
ROPE_ENG1 = os.environ.get("ROPE_ENG1", "dve")
DMA_Q2 = os.environ.get("DMA_Q2", "sp")


class Sched:
    def __init__(self, nc):
        self.nc = nc
        self.names = ["pe", "act", "dve", "pool", "sp"]
        self.sem = {n: nc.alloc_semaphore("s_" + n) for n in self.names}
        self.nops = {n: 0 for n in self.names}
        self.seen = {n: {} for n in self.names}
        self.prog = {n: [] for n in self.names}
        self.awaited = {n: set() for n in self.names}
        self.last_write = {}
        self.readers = {}
        self.dma_sems = {}
        self.nins = 0
        self.parent = {}
        self.children = {}

    def set_parent(self, fine, region):
        self.parent[fine] = region
        self.children.setdefault(region, []).append(fine)

    def _rel(self, k):
        out = [k]
        if k in self.parent:
            out.append(self.parent[k])
        out += self.children.get(k, [])
        return out

    def _deps(self, eng, reads, writes):
        deps = {}
        raw_self = [-1]

        def need(t, raw=False):
            src, v = t
            if src == eng:
                if raw_self[0] < v:
                    raw_self[0] = v
                return
            if deps.get(src, -1) < v:
                deps[src] = v

        for k0 in reads:
            for k in self._rel(k0):
                if k in self.last_write:
                    need(self.last_write[k], True)
        for k0 in writes:
            for k in self._rel(k0):
                if k in self.last_write:
                    need(self.last_write[k])
                for t in self.readers.get(k, {}).values():
                    need(t)
        if raw_self[0] >= 0 and eng not in ("pe", "sp"):
            deps[eng] = raw_self[0]
        out = []
        for src, v in deps.items():
            if self.seen[eng].get(src, -1) < v:
                self.seen[eng][src] = v
                out.append((src, v))
                if isinstance(src, str):
                    self.awaited[src].add(v)
        return out

    def _record(self, t, reads, writes):
        for k in reads:
            self.readers.setdefault(k, {})[t[0]] = t
        for k in writes:
            self.last_write[k] = t
            self.readers[k] = {}

    def op(self, eng, method, reads=(), writes=(), signal=True, **kw):
        waits = self._deps(eng, reads, writes)
        idx = self.nops[eng]
        self.nops[eng] = idx + 1
        self.prog[eng].append((waits, method, kw, ("e", idx)))
        self._record((eng, idx), reads, writes)
        self.nins += 1

    def dma(self, eng, out, in_, reads=(), writes=(), sem=None):
        waits = self._deps(eng, reads, writes)
        if sem not in self.dma_sems:
            self.dma_sems[sem] = [self.nc.alloc_semaphore("d_%d" % len(self.dma_sems)), 0]
        ent = self.dma_sems[sem]
        src = ("dma", sem)
        if ent[1] > 0 and self.seen[eng].get(src, -1) < ent[1]:
            self.seen[eng][src] = ent[1]
            waits.append((src, ent[1]))
        ent[1] += 16
        t = (src, ent[1])
        self.prog[eng].append((waits, "dma_start", dict(out=out, in_=in_), ("d", ent[0])))
        self.nops[eng] += 1
        self._record(t, reads, writes)
        self.nins += 1
        return t

    def final_wait(self, eng):
        ws = [(("dma", k), ent[1]) for k, ent in self.dma_sems.items() if ent[1] > 0]
        self.prog[eng].append((ws, None, None, None))

    def emit(self):
        nc = self.nc
        rank = {}
        for e in self.names:
            rank[e] = {idx: i + 1 for i, idx in enumerate(sorted(self.awaited[e]))}
        self.cnt = {e: len(rank[e]) for e in self.names}
        with nc.Block() as block:
            def run(name):
                def body(h):
                    for waits, method, kw, tag in self.prog[name]:
                        for src, v in waits:
                            if isinstance(src, str):
                                h.wait_ge(self.sem[src], rank[src][v])
                            else:
                                h.wait_ge(self.dma_sems[src[1]][0], v)
                        if method is not None:
                            ins = getattr(h, method)(**kw)
                            if tag[0] == "d":
                                ins.then_inc(tag[1], 16)
                            elif tag[1] in rank[name]:
                                ins.then_inc(self.sem[name], 1)
                return body
            block.tensor(run("pe"))
            block.scalar(run("act"))
            block.vector(run("dve"))
            block.gpsimd(run("pool"))
            block.sync(run("sp"))


class StopBuild(Exception):
    pass


class Builder:
    stop_at = None

    def phase(self, name):
        self.phases.append(name)
        if self.stop_at is not None and name == self.stop_at:
            raise StopBuild()

    def __init__(self, Sp, Ss, debug=False):
        self.phases = []
        self.Sp, self.Ss = Sp, Ss
        self.Q = Ss // 4
        self.debug = debug
        self.nc = bass.Bass("TRN2", target_bir_lowering=False)
        self.S = Sched(self.nc)
        self._ring_next = 0
        self._tp_next = 0
        self._ps_rr = 0

    def din(self, name, shape, dt=F32):
        return self.nc.dram_tensor(name, list(shape), dt, kind="ExternalInput").ap()

    def dout(self, name, shape, dt=F32):
        return self.nc.dram_tensor(name, list(shape), dt, kind="ExternalOutput").ap()

    def declare(self):
        Sp, Ss, Q = self.Sp, self.Ss, self.Q
        nc = self.nc
        self.xp = self.din("xp", [Sp, D])
        self.xs = self.din("xs", [Ss, D])
        self.xq = self.din("xq", [Q + 128, D])
        self.w_kv = self.din("w_kv", [8, 128, 512])
        self.w_q = self.din("w_q", [8, 128, 1024])
        self.w_o = self.din("w_o", [8, 128, 1024])
        self.w_g = self.din("w_g", [2, NFC, 128, 1024])
        self.w_u = self.din("w_u", [2, NFC, 128, 1024])
        self.w_d = self.din("w_d", [2, NFC, 128, 1024])
        self.w_ci = self.din("w_ci", [16, 128, 1024])
        self.w_co = self.din("w_co", [8, 128, 1024])
        self.gcols_d = self.din("gcols", [128, 4, 8])
        self.fvec_d = self.din("fvec", [128, 40])
        self.dww_d = self.din("dww", [128, 8, CW])
        self.gq_d = self.din("gq", [1, 64])
        self.gk_d = self.din("gk", [1, 64])
        self.gf_d = self.din("gf", [1, D])
        self.bout_d = self.din("bout", [1, D])
        self.rt_all_d = self.din("rt_all", [128, Ss // 128, 2, 16])
        self.rt_q_d = self.din("rt_q", [128, Q // 128 + 1, 2, 16])
        self.ct_d = self.din("ct", [128, 2, 16])
        self.flags_d = self.din("flags", [128, 2])
        self.yp = self.dout("yp", [Sp, D])
        self.yq = self.dout("yq", [Q, D])
        self.scr = nc.dram_tensor("wscr", [8 + 8 + 8 + 6 * NFC + 16 + 8, 128, 1024], BF16).ap()

    def alloc(self):
        nc = self.nc
        Ss = self.Ss
        A = nc.alloc_sbuf_tensor
        self.nkc_max = Ss // 128
        self.KT = [A("KT%d" % m, [128, Ss], BF16) for m in range(2)]
        self.V = A("Vst", [128, self.nkc_max, 384], BF16)
        self.xsl = [A("x%d" % i, [128, 4, D], F32) for i in range(2)]
        self.h_tm = A("h_tm", [128, 4, D], BF16)
        self.hT = A("hT", [128, 8, 512], BF16)
        self.R = A("R", [128, NFC * 512], BF16)
        self.PT = A("PT", [128, 6, 512], BF16)
        self.U = [A("U%d" % i, [128, 8, 512 + 2 * CP], BF16) for i in range(2)]
        self.UH = A("UH", [128, 8, 128], BF16)
        self.ring = A("ring", [128, 6, 1024], BF16)
        self.stage = A("stage", [128, 1024], F32)
        self.rtg = A("rtg", [128, 2, 4, 2, 16], F32)
        self._rtg_next = 0
        self.ct = A("ct_s", [128, 2, 16], F32)
        self.gf = A("gf_s", [128, D], F32)
        self.gq = A("gq_s", [128, 64], F32)
        self.gk = A("gk_s", [128, 64], F32)
        self.gcols = A("gcols_s", [128, 4, 8], F32)
        self.fvec = A("fvec_s", [128, 40], F32)
        self.dww = A("dww_s", [128, 8, CW], F32)
        self.flags = A("flags_s", [128, 2], F32)
        self.bout_b = A("bout_b", [1, D], BF16)
        self.ones_b = A("ones_b", [1, 128], BF16)
        self.ident = A("ident", [128, 128], BF16)
        self.nhalf = A("nhalf", [128, 16], F32)
        self.st = A("stats", [128, 128], F32)
        self.rec = A("rec", [128, 512], F32)
        self.oraw = A("oraw", [128, 2, 512], F32)
        self.ps = [nc.alloc_psum_tensor("ps%d" % i, [128, 512], F32) for i in range(8)]
        Rf = self.R[:, :]
        self.qT = Rf[:, 0:4096].rearrange("p (c t) -> p c t", c=8)
        self.oT = Rf[:, 4096:8192].rearrange("p (c t) -> p c t", c=8)
        self.hidT = Rf[:, 0:NFC * 512].rearrange("p (c t) -> p c t", c=NFC)
        self.cT = Rf[:, 0:4096].rearrange("p (c t) -> p c t", c=8)
        self.ktm2 = [Rf[:, i * 1024:(i + 1) * 1024].rearrange("p (s c) -> p s c", s=4) for i in range(2)]
        self.qTh = Rf[:, 0:1024].rearrange("p (c t) -> p c t", c=8)
        self.oTh = Rf[:, 4096:5120].rearrange("p (c t) -> p c t", c=8)
        self.Rf = Rf
        hi = Rf[:, 4096:NFC * 512].bitcast(F32)
        self.sq = hi[:, 0:1024]
        self.qg = hi[:, 1024:2048]
        self.rtmp = hi[:, 2048:3584]
        self.diag = [Rf[:, 0:CW * 128].rearrange("p (j c) -> p j c", j=CW),
                     Rf[:, 4096:4096 + CW * 128].rearrange("p (j c) -> p j c", j=CW)]
        self.diag_keys = [["Ra"], ["Rb", "Rc"]]
        self.extra_stage = False
        self.v_stage_on = False
        c0 = self.Sp // 128
        nfree = (self.nkc_max - c0) * 384 // 2
        self.vstage = []
        if nfree >= 1024 and os.environ.get("VSTAGE"):
            vf = self.V[:, c0:, :].rearrange("p a b -> p (a b)").bitcast(F32)
            self.vstage = [(("vst", i), vf[:, i * 1024:(i + 1) * 1024]) for i in range(min(12, nfree // 1024))]
        self._stage_rr = 0
        self._cast_rr = 0
        ptf = self.PT[:, :, :].rearrange("p a b -> p (a b)")
        self.sg = [ptf[:, 0:1024].bitcast(F32), ptf[:, 1024:2048].bitcast(F32)]
        S = self.S
        S.set_parent(("ktm", 0), "Ra")
        S.set_parent(("ktm", 1), "Ra")
        S.set_parent("sq", "Rb")
        S.set_parent("qg", "Rc")
        S.set_parent(("qgh", 0), "Rc")
        S.set_parent(("qgh", 1), "Rc")
        for hf in range(2):
            for i in range(3):
                S.set_parent(("rtmp", hf, i), "Rd")
        print("sbuf bytes remaining", nc.sbuf_bytes_remaining)

    def make_chunks(self):
        self.chunks = {}
        idx = [0]

        def add(name, src, gain=None, scale=None, ncols=1024):
            self.chunks[name] = dict(src=src, gain=gain, scale=scale, ncols=ncols, scr=self.scr[idx[0]], done=False)
            idx[0] += 1

        for kc in range(8):
            add(("kv", kc), self.w_kv[kc], gain=("pp", 0, kc), ncols=512)
        for kc in range(8):
            add(("q", kc), self.w_q[kc], gain=("pp", 0, kc))
        for pr in range(8):
            add(("o", pr), self.w_o[pr])
        for l in range(2):
            for fc in range(NFC):
                add(("g", l, fc), self.w_g[l, fc], gain=("kc", 1 if l == 0 else 3))
                add(("u", l, fc), self.w_u[l, fc], gain=("kc", 1 if l == 0 else 3))
                add(("d", l, fc), self.w_d[l, fc])
        for oc in range(16):
            add(("ci", oc), self.w_ci[oc], gain=("kc", 2))
        for cc in range(8):
            add(("co", cc), self.w_co[cc])

    def fetch(self, name):
        S = self.S
        ch = self.chunks[name]
        slot = self._ring_next
        self._ring_next = (slot + 1) % 6
        n = ch["ncols"]
        dst = self.ring[:, slot, 0:n]
        key = ("ring", slot)
        if not ch["done"]:
            ch["done"] = True
            bufs = [("stage", self.stage[:, :])]
            if self.v_stage_on:
                bufs += self.vstage
            if self.extra_stage:
                bufs += [(("x", 1, j), self.xsl[1][:, j, :]) for j in range(4)]
            self._stage_rr = (self._stage_rr + 1) % len(bufs)
            skey, sbuf = bufs[self._stage_rr]
            self._cast_rr ^= 1
            ce = "pool" if self._cast_rr else "dve"
            S.dma("sp", out=sbuf[:, 0:n], in_=ch["src"], writes=[skey], sem=("stg", self._stage_rr))
            g = ch["gain"]
            if g is None:
                S.op(ce, "tensor_copy", reads=[skey], writes=[key], out=dst, in_=sbuf[:, 0:n])
            elif g[0] == "pp":
                S.op(ce, "tensor_scalar", reads=[skey, "gcols"], writes=[key], out=dst, in0=sbuf[:, 0:n],
                     scalar1=self.gcols[:, g[1], g[2]:g[2] + 1], scalar2=None, op0=ALU.mult)
            else:
                gb = self.gcols[:, g[1], :].unsqueeze(2).broadcast_to([128, 8, 128])
                S.op(ce, "tensor_tensor", reads=[skey, "gcols"], writes=[key],
                     out=dst.rearrange("p (k c) -> p k c", k=8), in0=sbuf.rearrange("p (k c) -> p k c", k=8),
                     in1=gb, op=ALU.mult)
            S.dma(DMA_Q2, out=ch["scr"][:, 0:n], in_=dst, reads=[key], writes=[("scr", name)], sem=("scr", slot))
        else:
            S.dma("sp", out=dst, in_=ch["scr"][:, 0:n], reads=[("scr", name)], writes=[key], sem=("ringld", slot))
        return key, dst

    def load_rt(self, src, j0, n):
        i = self._rtg_next
        self._rtg_next = 1 - i
        key = ("rtg", i)
        self.S.dma("sp", out=self.rtg[:, i, 0:n], in_=src[:, j0:j0 + n], writes=[key], sem=key)
        return self.rtg[:, i], key

    def bank(self):
        b = self._ps_rr
        self._ps_rr = (b + 1) % 8
        return b

    def setup(self):
        S = self.S
        ld = [("ct", self.ct, self.ct_d),
              ("gcols", self.gcols, self.gcols_d), ("fvec", self.fvec, self.fvec_d), ("dww", self.dww, self.dww_d),
              ("flags", self.flags, self.flags_d)]
        for i, (k, sb, dr) in enumerate(ld):
            S.dma("sp", out=sb[:], in_=dr, writes=[k], sem=("setup", i % 4))
        S.dma("sp", out=self.gf[:], in_=self.gf_d.partition_broadcast(128), writes=["gf"], sem=("setup", 0))
        S.dma("sp", out=self.gq[:], in_=self.gq_d.partition_broadcast(128), writes=["gq"], sem=("setup", 1))
        S.dma("sp", out=self.gk[:], in_=self.gk_d.partition_broadcast(128), writes=["gk"], sem=("setup", 2))
        S.dma("sp", out=self.stage[0:1, :], in_=self.bout_d, writes=["stage"], sem="stage")
        S.op("pool", "tensor_copy", reads=["stage"], writes=["bout_b"], out=self.bout_b[:], in_=self.stage[0:1, :])
        S.op("pool", "memset", writes=["ones_b"], ap=self.ones_b[:], constant=1.0)
        S.op("pool", "memset", writes=["nhalf"], ap=self.nhalf[:], constant=-0.5)
        identf = self.stage[:, 0:128]
        S.op("pool", "memset", reads=["stage"], writes=["stage"], ap=identf, constant=0.0)
        S.op("pool", "affine_select", reads=["stage"], writes=["stage"], out=identf, in_=identf,
             pattern=[[-1, 128]], compare_op=ALU.not_equal, fill=1.0, base=0, channel_multiplier=1)
        S.op("pool", "tensor_copy", reads=["stage"], writes=["ident"], out=self.ident[:], in_=identf)
        S.op("pool", "tensor_scalar", reads=["gq"], writes=["gq"], out=self.gq[:], in0=self.gq[:], scalar1=1.0 / math.sqrt(HD),
             scalar2=None, op0=ALU.mult)
        c1 = self.Sp // 128 if self.vstage else self.nkc_max
        S.op("pool", "memset", writes=["V"], ap=self.V[:, 0:c1, 64:128], constant=1.0)
        S.op("pool", "memset", writes=["V"], ap=self.V[:, 0:c1, 256:320], constant=1.0)

    def xk(self, slot, s):
        return ("x", slot, s)

    def norm_to_hT(self, xsl, slot, nsub, evac_eng="dve", scale_all_act=False):
        S = self.S
        for s in range(nsub):
            S.op("act", "activation", reads=[self.xk(slot, s)], writes=[("h_tm", s), ("ss", s)], out=self.h_tm[:, s, :], in_=xsl[:, s, :],
                 func=AF.Square, accum_out=self.st[:, s:s + 1])
        rr = self.st[:, 8:8 + nsub]
        S.op("pool", "tensor_scalar", reads=[("ss", s) for s in range(nsub)], writes=["rr"], out=rr, in0=self.st[:, 0:nsub],
             scalar1=1.0 / D, scalar2=RMS_EPS, op0=ALU.mult, op1=ALU.add)
        S.op("pool", "tensor_tensor", reads=["rr", "nhalf"], writes=["rr"], out=rr, in0=rr, in1=self.nhalf[:, 0:nsub], op=ALU.pow)
        for s in range(nsub):
            if s % 2 == 0 or scale_all_act:
                S.op("act", "activation", reads=[self.xk(slot, s), "rr"], writes=[("h_tm", s)], out=self.h_tm[:, s, :],
                     in_=xsl[:, s, :], func=AF.Copy, scale=self.st[:, 8 + s:9 + s])
            else:
                S.op("dve", "tensor_scalar", reads=[self.xk(slot, s), "rr"], writes=[("h_tm", s)], out=self.h_tm[:, s, :],
                     in0=xsl[:, s, :], scalar1=self.st[:, 8 + s:9 + s], scalar2=None, op0=ALU.mult)
        self.phase("n_scale")
        for s in range(nsub):
            b = self.bank()
            pk = ("ps", b)
            pv = self.ps[b][:, :].bitcast(BF16)
            for c in range(8):
                S.op("pe", "transpose", reads=[("h_tm", s), "ident"], writes=[pk], out=pv[:, c * 128:(c + 1) * 128],
                     in_=self.h_tm[:, s, c * 128:(c + 1) * 128], identity=self.ident[:])
            dst = self.hT[:, :, s * 128:(s + 1) * 128]
            src = pv.rearrange("p (c t) -> p c t", c=8)
            if evac_eng == "dve":
                S.op("dve", "tensor_copy", reads=[pk], writes=["hT"], out=dst, in_=src)
            else:
                S.op("act", "activation", reads=[pk], writes=["hT"], out=dst, in_=src, func=AF.Copy)
        self.phase("n_hT")

    def transposes(self, srcs, srckeys, nchunks, dst, dstkey, evac_eng="dve", act_kw=None, col0=0):
        S = self.S
        nsub = len(srcs)
        T = nsub * 128
        for c in range(nchunks):
            b = self.bank()
            pk = ("ps", b)
            pv = self.ps[b][:, :].bitcast(BF16)
            for s in range(nsub):
                S.op("pe", "transpose", reads=[srckeys[s], "ident"], writes=[pk], signal=(s == nsub - 1),
                     out=pv[:, s * 128:(s + 1) * 128], in_=srcs[s][:, c * 128:(c + 1) * 128], identity=self.ident[:])
            if act_kw is not None:
                kw = act_kw(c)
                S.op("act", "activation", reads=[pk] + kw.pop("reads", []), writes=[dstkey], out=dst[:, c, col0:col0 + T],
                     in_=pv[:, 0:T], **kw)
            elif evac_eng == "dve":
                S.op("dve", "tensor_copy", reads=[pk], writes=[dstkey], out=dst[:, c, col0:col0 + T], in_=pv[:, 0:T])
            else:
                S.op("act", "activation", reads=[pk], writes=[dstkey], out=dst[:, c, col0:col0 + T], in_=pv[:, 0:T],
                     func=AF.Copy)

    def tm_proj(self, actT, actkey, chunk_names, nsub, ncb, evac, bias=False):
        S = self.S
        banks = [[self.bank() for cb in range(ncb)] for s in range(nsub)]
        nch = len(chunk_names)
        if bias:
            for s in range(nsub):
                for cb in range(ncb):
                    S.op("pe", "matmul", reads=["ones_b", "bout_b"], writes=[("ps", banks[s][cb])], signal=False,
                         out=self.ps[banks[s][cb]][:, :], lhsT=self.ones_b[0:1, :], rhs=self.bout_b[0:1, cb * 512:(cb + 1) * 512],
                         start=True, stop=False)
        for c, name in enumerate(chunk_names):
            wkey, w = self.fetch(name)
            for s in range(nsub):
                for cb in range(ncb):
                    last = (s == nsub - 1 and cb == ncb - 1)
                    S.op("pe", "matmul", reads=(actkey if isinstance(actkey, list) else [actkey]) + [wkey], writes=[("ps", banks[s][cb])], signal=last,
                         out=self.ps[banks[s][cb]][:, :], lhsT=actT[:, c, s * 128:(s + 1) * 128],
                         rhs=w[:, cb * 512:(cb + 1) * 512], start=(c == 0 and not bias), stop=(c == nch - 1))
        self.phase("tm_mm")
        for s in range(nsub):
            for cb in range(ncb):
                evac(s, cb, banks[s][cb])
                self.phase("tm_evac1")

    def resid_evac(self, xsl, slot):
        def evac(s, cb, b):
            xkey = self.xk(slot, s)
            self.S.op("dve", "tensor_tensor", reads=[("ps", b), xkey], writes=[xkey], out=xsl[:, s, cb * 512:(cb + 1) * 512],
                      in0=self.ps[b][:, :], in1=xsl[:, s, cb * 512:(cb + 1) * 512], op=ALU.add)
        return evac

    def qk_post(self, pviews, pkeys, G, H, gbc, gkey, rt, rtkey, out, outkeys):
        S = self.S
        n = G * H * 64
        sq = self.sq[:, 0:n]
        qg = self.qg[:, 0:n]
        off = 0
        for pv, pk in zip(pviews, pkeys):
            w = pv.shape[1]
            S.op("act", "activation", reads=[pk], writes=["sq"], out=sq[:, off:off + w], in_=pv, func=AF.Square)
            S.op("dve", "tensor_tensor", reads=[pk, gkey, "sq"], writes=["qg"], out=qg[:, off:off + w].rearrange("p (h d) -> p h d", d=64),
                 in0=pv.rearrange("p (h d) -> p h d", d=64), in1=gbc[:, :].unsqueeze(1).broadcast_to([128, w // 64, 64]), op=ALU.mult)
            off += w
        GH = G * H
        ssq = self.st[:, 16:16 + GH]
        rs = self.st[:, 32:32 + GH]
        S.op("dve", "tensor_reduce", reads=["sq"], writes=["ssq"], out=ssq, in_=sq.rearrange("p (h d) -> p h d", d=64),
             axis=AX.X, op=ALU.add)
        S.op("pool", "tensor_scalar", reads=["ssq"], writes=["rs"], out=rs, in0=ssq, scalar1=1.0 / HD, scalar2=RMS_EPS,
             op0=ALU.mult, op1=ALU.add)
        S.op("pool", "tensor_tensor", reads=["rs", "nhalf"], writes=["rs"], out=rs, in0=rs, in1=self.nhalf[:, 0:GH], op=ALU.pow)
        v6 = qg.rearrange("p (g h r f d) -> p g h r f d", g=G, h=H, r=2, f=2, d=16)
        shp = [128, G, H, 16]
        for half, eng in ((0, "dve"), (1, ROPE_ENG1)):
            Aa = v6[:, :, :, half, 0, :]
            Bb = v6[:, :, :, half, 1, :]
            if half == 0:
                C = rt[:, :, 0, :].unsqueeze(2).broadcast_to(shp)
                Sn = rt[:, :, 1, :].unsqueeze(2).broadcast_to(shp)
                tk = [rtkey]
            else:
                C = self.ct[:, 0, :].unsqueeze(1).unsqueeze(1).broadcast_to(shp)
                Sn = self.ct[:, 1, :].unsqueeze(1).unsqueeze(1).broadcast_to(shp)
                tk = ["ct"]
            t = [self.rtmp[:, (half * 3 + i) * 256:(half * 3 + i) * 256 + GH * 16].rearrange("p (g h d) -> p g h d", g=G, h=H)
                 for i in range(3)]
            k = [("rtmp", half, i) for i in range(3)]
            qk = ("qgh", half)
            S.op(eng, "tensor_tensor", reads=["qg", qk] + tk, writes=[k[0]], out=t[0], in0=Aa, in1=C, op=ALU.mult)
            S.op(eng, "tensor_tensor", reads=["qg", qk] + tk, writes=[k[1]], out=t[1], in0=Aa, in1=Sn, op=ALU.mult)
            S.op(eng, "tensor_tensor", reads=["qg", qk] + tk, writes=[k[2]], out=t[2], in0=Bb, in1=Sn, op=ALU.mult)
            S.op(eng, "tensor_tensor", reads=["qg", k[0], k[2]], writes=[qk], out=Aa, in0=t[0], in1=t[2], op=ALU.subtract)
            S.op(eng, "tensor_tensor", reads=["qg", qk] + tk, writes=[k[0]], out=t[0], in0=Bb, in1=C, op=ALU.mult)
            S.op(eng, "tensor_tensor", reads=["qg", k[0], k[1]], writes=[qk], out=Bb, in0=t[0], in1=t[1], op=ALU.add)
        S.op("dve", "tensor_tensor", reads=["qg", ("qgh", 0), ("qgh", 1), "rs"], writes=list(outkeys) + ["qg"],
             out=out.rearrange("p g (h d) -> p g h d", d=64), in0=qg.rearrange("p (g h d) -> p g h d", g=G, h=H),
             in1=rs.rearrange("p (g h) -> p g h", g=G).unsqueeze(3).broadcast_to([128, G, H, 64]), op=ALU.mult)

    def k_transposes(self, g, nsub):
        S = self.S
        T = nsub * 128
        kt = self.ktm2[g % 2]
        kk = ("ktm", g % 2)
        for m in range(2):
            b = self.bank()
            pk = ("ps", b)
            pv = self.ps[b][:, :].bitcast(BF16)
            for s in range(nsub):
                S.op("pe", "transpose", reads=[kk, "ident"], writes=[pk],
                     out=pv[:, s * 128:(s + 1) * 128], in_=kt[:, s, m * 128:(m + 1) * 128], identity=self.ident[:])
            S.op("dve", "tensor_copy", reads=[pk], writes=["KT"], out=self.KT[m][:, g * 512:g * 512 + T], in_=pv[:, 0:T])

    def stage1(self, xsrc, S_len):
        S = self.S
        ngrp = (S_len + 511) // 512
        pending = None

        def load_x(g):
            nsub_ = min(4, (S_len - g * 512) // 128)
            for s in range(nsub_):
                S.dma("sp", out=self.xsl[g % 2][:, s, :], in_=xsrc[g * 512 + s * 128:g * 512 + (s + 1) * 128, :],
                      writes=[self.xk(g % 2, s)], sem=("xld", g % 2, s))

        load_x(0)
        for g in range(ngrp):
            nsub = min(4, (S_len - g * 512) // 128)
            slot = g % 2
            xsl = self.xsl[slot]
            rtv, rtk = self.load_rt(self.rt_all_d, g * 4, nsub)
            if g + 1 < ngrp:
                load_x(g + 1)
            self.norm_to_hT(xsl, slot, nsub, evac_eng="act", scale_all_act=True)
            kbanks = []
            self.tm_proj(self.hT, "hT", [("kv", kc) for kc in range(8)], nsub, 1, lambda s, cb, b, kb=kbanks: kb.append(b))
            if pending is not None:
                self.k_transposes(*pending)
            for s, b in enumerate(kbanks):
                pv = self.ps[b]
                pk = ("ps", b)
                j = g * 4 + s
                for (dst0, src0) in ((0, 256), (128, 320), (192, 448), (320, 384)):
                    S.op("act", "activation", reads=[pk], writes=["V"], out=self.V[:, j, dst0:dst0 + 64], in_=pv[:, src0:src0 + 64],
                         func=AF.Copy)
            self.qk_post([self.ps[bb][:, 0:256] for bb in kbanks], [("ps", bb) for bb in kbanks], nsub, 4, self.gk, "gk",
                         rtv[:, 0:nsub], rtk, self.ktm2[g % 2][:, 0:nsub, :], [("ktm", g % 2)])
            pending = (g, nsub)
            self.phase("s1")
        self.k_transposes(*pending)

    def ffn(self, l, xsl, slot, nsub):
        S = self.S
        T = nsub * 128
        self.norm_to_hT(xsl, slot, nsub)
        for fc in range(NFC):
            kg, wg = self.fetch(("g", l, fc))
            ku, wu = self.fetch(("u", l, fc))
            bg, bu = self.bank(), self.bank()
            for (kk, w, b) in ((kg, wg, bg), (ku, wu, bu)):
                for kc in range(8):
                    S.op("pe", "matmul", reads=["hT", kk], writes=[("ps", b)], signal=(kc == 7), out=self.ps[b][:, 0:T],
                         lhsT=w[:, kc * 128:(kc + 1) * 128], rhs=self.hT[:, kc, 0:T], start=(kc == 0), stop=(kc == 7))
            sgi = fc % 2
            S.op("act", "activation", reads=[("ps", bg)], writes=[("PT", 2 * sgi), ("PT", 2 * sgi + 1)], out=self.sg[sgi][:, 0:T], in_=self.ps[bg][:, 0:T],
                 func=AF.Silu)
            S.op("dve", "tensor_tensor", reads=[("ps", bu), ("PT", 2 * sgi), ("PT", 2 * sgi + 1)], writes=[self.hkey(fc)], out=self.hidT[:, fc, 0:T],
                 in0=self.ps[bu][:, 0:T], in1=self.sg[sgi][:, 0:T], op=ALU.mult)
        self.tm_proj(self.hidT, ["Ra", "Rb", "Rc", "Rd"], [("d", l, fc) for fc in range(NFC)], nsub, 2, self.resid_evac(xsl, slot))

    def attention(self, nsub, nkc):
        S = self.S
        T = nsub * 128
        sbanks = [(0, 1), (2, 3), (4, 5)]
        oa, ob = 6, 7
        if nsub == 1:
            N = 512
            units = []
            for m in range(2):
                qa = self.Rf[0:64, m * 512:(m + 1) * 512]
                qb = self.Rf[64:128, m * 512:(m + 1) * 512]
                units.append((m, qa, qb, lambda lo, hi, m=m: self.Rf[lo:hi, 4096 + m * 512:4096 + (m + 1) * 512], "Rb"))
        else:
            N = T
            units = []
            for pr in range(8):
                units.append((pr // 4, self.qT[0:64, pr, 0:T], self.qT[64:128, pr, 0:T],
                              lambda lo, hi, pr=pr: self.oT[lo:hi, pr, 0:T], self.okey(pr)))
        items = [(u, c) for u in range(len(units)) for c in range(nkc)]

        def vviews(m, c):
            if m == 0:
                return self.V[:, c, 0:128], self.V[:, c, 64:192]
            return self.V[:, c, 256:384], self.V[:, c, 192:320]

        def s_mm(k):
            u, c = items[k]
            m, qa, qb, _, _ = units[u]
            ba, bb = sbanks[k % 3]
            S.op("pe", "matmul", reads=["KT", "Ra"], writes=[("ps", ba)], out=self.ps[ba][:, 0:N],
                 lhsT=self.KT[m][0:64, c * 128:(c + 1) * 128], rhs=qa, start=True, stop=True, tile_position=(0, 0))
            S.op("pe", "matmul", reads=["KT", "Ra"], writes=[("ps", bb)], out=self.ps[bb][:, 0:N],
                 lhsT=self.KT[m][64:128, c * 128:(c + 1) * 128], rhs=qb, start=True, stop=True, tile_position=(64, 0))

        def finish(u):
            m, _, _, oview, ok = units[u]
            a_o, b_o = (0, 64) if m == 0 else (64, 0)
            S.op("dve", "tensor_copy", reads=[("ps", oa)], writes=[("oraw", 0)], out=self.oraw[:, 0, 0:N], in_=self.ps[oa][:, 0:N])
            S.op("dve", "tensor_copy", reads=[("ps", ob)], writes=[("oraw", 1)], out=self.oraw[:, 1, 0:N], in_=self.ps[ob][:, 0:N])
            for h, o_off in enumerate((a_o, b_o)):
                s_off = 64 - o_off
                rk = ("rec", o_off)
                S.op("dve", "reciprocal", reads=[("oraw", h)], writes=[rk], out=self.rec[o_off:o_off + 64, 0:N],
                     in_=self.oraw[s_off:s_off + 64, h, 0:N])
                S.op("dve", "tensor_tensor", reads=[("oraw", h), rk], writes=[ok], out=oview(o_off, o_off + 64),
                     in0=self.oraw[o_off:o_off + 64, h, 0:N], in1=self.rec[o_off:o_off + 64, 0:N], op=ALU.mult)

        def pv_mm(k):
            u, c = items[k]
            m = units[u][0]
            ba, bb = sbanks[k % 3]
            pa = (k % 3) * 2
            va, vb = vviews(m, c)
            S.op("act", "activation", reads=[("ps", ba)], writes=[("PT", pa)], out=self.PT[:, pa, 0:N], in_=self.ps[ba][:, 0:N],
                 func=AF.Exp)
            S.op("act", "activation", reads=[("ps", bb)], writes=[("PT", pa + 1)], out=self.PT[:, pa + 1, 0:N],
                 in_=self.ps[bb][:, 0:N], func=AF.Exp)
            S.op("pe", "matmul", reads=["V", ("PT", pa)], writes=[("ps", oa)], out=self.ps[oa][:, 0:N],
                 lhsT=va, rhs=self.PT[:, pa, 0:N], start=(c == 0), stop=(c == nkc - 1))
            S.op("pe", "matmul", reads=["V", ("PT", pa + 1)], writes=[("ps", ob)], out=self.ps[ob][:, 0:N],
                 lhsT=vb, rhs=self.PT[:, pa + 1, 0:N], start=(c == 0), stop=(c == nkc - 1))
            if c == nkc - 1:
                finish(u)

        n = len(items)
        for k in range(min(2, n)):
            s_mm(k)
        for k in range(n):
            if k + 2 < n:
                s_mm(k + 2)
            pv_mm(k)

    def okey(self, pr):
        return "Rb" if pr < 4 else "Rc"

    def hkey(self, fc):
        return "Ra" if fc < 8 else ("Rb" if fc < 12 else ("Rc" if fc < 16 else "Rd"))

    def bank_o(self, pr):
        return (6, 7)

    def S_a(self, xsrc_ap, slot, nsub, nkc, rt_tab, rt_key, rt_j0, u_dst, u_key, u_col0):
        S = self.S
        T = nsub * 128
        xsl = self.xsl[slot]
        for s in range(nsub):
            S.dma("sp", out=xsl[:, s, :], in_=xsrc_ap[s * 128:(s + 1) * 128, :], writes=[self.xk(slot, s)], sem=("xld", slot, s))
        rtv, rtk = self.load_rt(rt_tab, rt_j0, nsub)
        self.norm_to_hT(xsl, slot, nsub)

        def evac_q(s, cb, b):
            if cb == 1:
                b0 = self._qbanks[(s, 0)]
                self.qk_post([self.ps[b0][:, :], self.ps[b][:, :]], [("ps", b0), ("ps", b)], 1, 16, self.gq, "gq",
                             rtv[:, s:s + 1], rtk, self.h_tm[:, s:s + 1, :], [("h_tm", s)])
            else:
                self._qbanks[(s, 0)] = b
        self._qbanks = {}
        self.tm_proj(self.hT, "hT", [("q", kc) for kc in range(8)], nsub, 2, evac_q)
        self.transposes([self.h_tm[:, s, :] for s in range(nsub)], [("h_tm", s) for s in range(nsub)], 8,
                        self.qTh if nsub == 1 else self.qT, "Ra")
        self.phase("q")
        self._ps_rr = 0
        self.attention(nsub, nkc)
        self.phase("att")
        self.tm_proj(self.oTh if nsub == 1 else self.oT, ["Rb", "Rc"], [("o", pr) for pr in range(8)], nsub, 2,
                     self.resid_evac(xsl, slot))
        self.phase("wo")
        self.ffn(0, xsl, slot, nsub)
        self.phase("ffn0")
        self.norm_to_hT(xsl, slot, nsub)
        for cc in range(8):
            ka, wa = self.fetch(("ci", cc))
            kg, wg = self.fetch(("ci", 8 + cc))
            ba, bg = self.bank(), self.bank()
            for (kk, w, b) in ((ka, wa, ba), (kg, wg, bg)):
                for kc in range(8):
                    S.op("pe", "matmul", reads=["hT", kk], writes=[("ps", b)], signal=(kc == 7), out=self.ps[b][:, 0:T],
                         lhsT=w[:, kc * 128:(kc + 1) * 128], rhs=self.hT[:, kc, 0:T], start=(kc == 0), stop=(kc == 7))
            sgi = cc % 2
            S.op("act", "activation", reads=[("ps", bg), "fvec"], writes=[("PT", 2 * sgi), ("PT", 2 * sgi + 1)], out=self.sg[sgi][:, 0:T],
                 in_=self.ps[bg][:, 0:T], func=AF.Sigmoid, bias=self.fvec[:, 8 + cc:9 + cc])
            S.op("dve", "scalar_tensor_tensor", reads=[("ps", ba), ("PT", 2 * sgi), ("PT", 2 * sgi + 1), "fvec"], writes=[u_key],
                 out=u_dst[:, cc, u_col0:u_col0 + T], in0=self.ps[ba][:, 0:T], scalar=self.fvec[:, cc:cc + 1],
                 in1=self.sg[sgi][:, 0:T], op0=ALU.add, op1=ALU.mult)

    def S_b(self, slot, uslot, nsub, ydst_ap):
        S = self.S
        T = nsub * 128
        xsl = self.xsl[slot]
        U = self.U[uslot]
        ukey = ("U", uslot)
        for cc in range(8):
            dg = self.diag[cc % 2]
            dk = self.diag_keys[cc % 2]
            S.op("dve", "tensor_tensor", reads=["ident", "dww"], writes=dk, out=dg,
                 in0=self.ident[:, :].unsqueeze(1).broadcast_to([128, CW, 128]),
                 in1=self.dww[:, cc, :].unsqueeze(2).broadcast_to([128, CW, 128]), op=ALU.mult)
            b = self.bank()
            for j in range(CW):
                S.op("pe", "matmul", reads=dk + [ukey], writes=[("ps", b)], out=self.ps[b][:, 0:T],
                     lhsT=dg[:, j, :], rhs=U[:, cc, j:j + T], start=(j == 0), stop=(j == CW - 1))
            S.op("act", "activation", reads=[("ps", b), "fvec"], writes=["hT"], out=self.hT[:, cc, 0:T], in_=self.ps[b][:, 0:T],
                 func=AF.Identity, bias=self.fvec[:, 16 + cc:17 + cc])
        self.phase("conv")
        lnb = []
        for s in range(nsub):
            b = self.bank()
            pk = ("ps", b)
            pv = self.ps[b][:, :].bitcast(BF16)
            for cc in range(8):
                S.op("pe", "transpose", reads=["hT", "ident"], writes=[pk], out=pv[:, cc * 128:(cc + 1) * 128],
                     in_=self.hT[:, cc, s * 128:(s + 1) * 128], identity=self.ident[:])
            S.op("act", "activation", reads=[pk], writes=[("h_tm", s), ("lns", s)], out=self.h_tm[:, s, :], in_=pv, func=AF.Identity,
                 accum_out=self.st[:, 48 + s:49 + s])
            S.op("act", "activation", reads=[pk], writes=[("h_tm", s), ("lns", s)], out=self.h_tm[:, s, :], in_=pv, func=AF.Square,
                 accum_out=self.st[:, 52 + s:53 + s])
            lnb.append((pk, pv))
        lk = [("lns", s) for s in range(nsub)]
        mean = self.st[:, 56:56 + nsub]
        var = self.st[:, 60:60 + nsub]
        msq = self.st[:, 64:64 + nsub]
        S.op("dve", "tensor_scalar", reads=lk, writes=["lnm"], out=mean, in0=self.st[:, 48:48 + nsub], scalar1=1.0 / D,
             scalar2=None, op0=ALU.mult)
        S.op("dve", "tensor_tensor", reads=["lnm"], writes=["lnq"], out=msq, in0=mean, in1=mean, op=ALU.mult)
        S.op("dve", "scalar_tensor_tensor", reads=lk + ["lnq"], writes=["lnv"], out=var, in0=self.st[:, 52:52 + nsub], scalar=1.0 / D,
             in1=msq, op0=ALU.mult, op1=ALU.subtract)
        S.op("pool", "tensor_scalar", reads=["lnv"], writes=["lnv"], out=var, in0=var, scalar1=LN_EPS, scalar2=None, op0=ALU.add)
        S.op("pool", "tensor_tensor", reads=["lnv", "nhalf"], writes=["lnv"], out=var, in0=var, in1=self.nhalf[:, 0:nsub], op=ALU.pow)
        for s in range(nsub):
            pk, pv = lnb[s]
            S.op("dve", "tensor_scalar", reads=[pk, "lnm", "lnv"], writes=[("h_tm", s)], out=self.h_tm[:, s, :], in0=pv,
                 scalar1=self.st[:, 56 + s:57 + s], scalar2=self.st[:, 60 + s:61 + s], op0=ALU.subtract, op1=ALU.mult)
        self.transposes([self.h_tm[:, s, :] for s in range(nsub)], [("h_tm", s) for s in range(nsub)], 8, self.hT, "hT",
                        act_kw=lambda c: dict(func=AF.Silu, scale=self.fvec[:, 24 + c:25 + c], bias=self.fvec[:, 32 + c:33 + c],
                                              reads=["fvec"]))
        self.phase("ln")
        self.tm_proj(self.hT, "hT", [("co", cc) for cc in range(8)], nsub, 2, self.resid_evac(xsl, slot), bias=True)
        self.phase("co")
        self.ffn(1, xsl, slot, nsub)
        self.phase("ffn1")
        for s in range(nsub):
            S.op("act", "activation", reads=[self.xk(slot, s)], writes=[("h_tm", s), ("fs", s)], out=self.h_tm[:, s, :], in_=xsl[:, s, :],
                 func=AF.Square, accum_out=self.st[:, 68 + s:69 + s])
        fr = self.st[:, 72:72 + nsub]
        S.op("pool", "tensor_scalar", reads=[("fs", s) for s in range(nsub)], writes=["fr"], out=fr, in0=self.st[:, 68:68 + nsub],
             scalar1=1.0 / D, scalar2=RMS_EPS, op0=ALU.mult, op1=ALU.add)
        S.op("pool", "tensor_tensor", reads=["fr", "nhalf"], writes=["fr"], out=fr, in0=fr, in1=self.nhalf[:, 0:nsub], op=ALU.pow)
        for s in range(nsub):
            xkey = self.xk(slot, s)
            S.op("dve", "scalar_tensor_tensor", reads=[xkey, "fr", "gf"], writes=[xkey], out=xsl[:, s, :], in0=xsl[:, s, :],
                 scalar=self.st[:, 72 + s:73 + s], in1=self.gf[:], op0=ALU.mult, op1=ALU.mult)
            S.dma("sp", out=ydst_ap[s * 128:(s + 1) * 128, :], in_=xsl[:, s, :], reads=[xkey], sem=("yst", slot, s))
        self.phase("fin")

    def run_pass(self, xkv, S_len, xown, n_own, rt_tab, rt_key, ydst, has_halo):
        S = self.S
        nkc = S_len // 128
        self.stage1(xkv, S_len)
        ntile = n_own // 512
        if has_halo:
            self.S_a(xown[n_own:n_own + 128, :], 0, 1, nkc, rt_tab, rt_key, n_own // 128, self.UH, "UH", 0)
            S.op("pool", "tensor_scalar", reads=["UH", "flags"], writes=[("U", 0)], out=self.U[0][:, :, 0:CP],
                 in0=self.UH[:, :, 64 - CP:64], scalar1=self.flags[:, 0:1], scalar2=None, op0=ALU.mult)
        else:
            S.op("pool", "memset", writes=[("U", 0)], ap=self.U[0][:, :, 0:CP], constant=0.0)
        for i in range(ntile + 1):
            if i < ntile:
                us = i % 2
                self.extra_stage = (i == 0 and not has_halo)
                self.S_a(xown[i * 512:(i + 1) * 512, :], i % 2, 4, nkc, rt_tab, rt_key, i * 4, self.U[us], ("U", us), CP)
                self.extra_stage = False
                if i > 0:
                    S.op("pool", "tensor_copy", reads=[("U", 1 - us)], writes=[("U", us)], out=self.U[us][:, :, 0:CP],
                         in_=self.U[1 - us][:, :, 512:512 + CP])
                    S.op("pool", "tensor_copy", reads=[("U", us)], writes=[("U", 1 - us)], out=self.U[1 - us][:, :, 512 + CP:512 + 2 * CP],
                         in_=self.U[us][:, :, CP:2 * CP])
            if i == ntile:
                us = (ntile - 1) % 2
                if has_halo:
                    S.op("pool", "tensor_scalar", reads=["UH", "flags"], writes=[("U", us)], out=self.U[us][:, :, 512 + CP:512 + 2 * CP],
                         in0=self.UH[:, :, 64:64 + CP], scalar1=self.flags[:, 1:2], scalar2=None, op0=ALU.mult)
                else:
                    S.op("pool", "memset", writes=[("U", us)], ap=self.U[us][:, :, 512 + CP:512 + 2 * CP], constant=0.0)
            if i >= 1:
                j = i - 1
                self.S_b(j % 2, j % 2, 4, ydst[j * 512:(j + 1) * 512, :])

    def build(self):
        self.declare()
        self.alloc()
        self.make_chunks()
        self.setup()
        try:
            self.phase("setup")
            self.v_stage_on = True
            self.run_pass(self.xp, self.Sp, self.xp, self.Sp, self.rt_all_d, "rt", self.yp, False)
            self.v_stage_on = False
            if self.vstage:
                vk = ["V"] + [k for k, _ in self.vstage]
                c0 = self.Sp // 128
                self.S.op("pool", "memset", writes=vk, ap=self.V[:, c0:, 64:128], constant=1.0)
                self.S.op("pool", "memset", writes=vk, ap=self.V[:, c0:, 256:320], constant=1.0)
            self.phase("passP")
            self.run_pass(self.xs, self.Ss, self.xq, self.Q, self.rt_q_d, "rt", self.yq, True)
        except StopBuild:
            print("STOPPED at", self.stop_at)
        self.S.final_wait("pool")
        self.S.final_wait("sp")
        self.S.emit()
        print("instructions", self.S.nins, dict(self.S.nops), "signals", dict(self.S.cnt))
        return self.nc


def _rope_tabs(pos_rows):
    inv = (10000.0 ** (-np.arange(0, 32, 2, dtype=np.float64) / 32.0))
    ang = pos_rows[..., None].astype(np.float64) * inv
    return np.cos(ang).astype(np.float32), np.sin(ang).astype(np.float32)


def host_layout(inp, Sp, Ss):
    Q = Ss // 4
    f = lambda a: np.ascontiguousarray(np.asarray(a, dtype=np.float32))
    wqkv = f(inp["w_qkv"])[0]
    heads = [8 * m + 4 * hh + i for m in range(2) for i in range(4) for hh in range(2)]
    heads_o = []
    for m in range(2):
        for i in range(4):
            a, b = 8 * m + i, 8 * m + 4 + i
            heads_o += [a, b] if m == 0 else [b, a]
    qcols = np.concatenate([np.arange(h * 64, (h + 1) * 64) for h in heads])
    orows = np.concatenate([np.arange(h * 64, (h + 1) * 64) for h in heads_o])
    shared = {}
    shared["w_q"] = f(wqkv[:, :1024][:, qcols].reshape(8, 128, 1024))
    shared["w_kv"] = f(wqkv[:, 1024:].reshape(8, 128, 512))
    shared["w_o"] = f(f(inp["w_o"])[0][orows].reshape(8, 128, 1024))

    def up(w):
        Fo = w.shape[1]
        return f(w.reshape(8, 128, Fo // 128, 128).transpose(2, 1, 0, 3).reshape(Fo // 128, 128, 1024))
    shared["w_g"] = f(np.stack([up(f(inp["w_gate"])[l]) for l in range(2)]))
    shared["w_u"] = f(np.stack([up(f(inp["w_up"])[l]) for l in range(2)]))
    shared["w_d"] = f(f(inp["w_down"]).reshape(2, NFC, 128, 1024))
    shared["w_ci"] = up(f(inp["conv_w_in"])[0])
    shared["w_co"] = f(f(inp["conv_w_out"])[0].reshape(8, 128, 1024))
    col = lambda v: f(v).reshape(-1, 128).T
    gcols = np.stack([col(f(inp["attn_norm_g"])[0]), col(f(inp["ffn_norm_g"])[0]), col(f(inp["conv_norm_g"])[0]),
                      col(f(inp["ffn_norm_g"])[1])], axis=1)
    shared["gcols"] = f(gcols)
    shared["fvec"] = f(np.concatenate([col(f(inp["conv_b_in"])[0]), col(f(inp["dw_b"])[0]), col(f(inp["conv_ln_g"])[0]),
                                       col(f(inp["conv_ln_b"])[0])], axis=1))
    shared["dww"] = f(f(inp["dw_w"])[0].T.reshape(8, 128, CW).transpose(1, 0, 2))
    shared["gq"] = f(inp["q_norm_g"]).reshape(1, 64)
    shared["gk"] = f(inp["k_norm_g"]).reshape(1, 64)
    shared["gf"] = f(inp["final_norm_g"]).reshape(1, D)
    shared["bout"] = f(inp["conv_b_out"]).reshape(1, D)
    p = np.arange(128)
    j = np.arange(Ss // 128)
    rows = 2 * j[None, :] + (p[:, None] >= 64)
    c, s = _rope_tabs(rows)
    shared["rt_all"] = f(np.stack([c, s], axis=2))
    c, s = _rope_tabs(p % 64)
    shared["ct"] = f(np.stack([c, s], axis=1))
    xp = f(inp["x_prompt"])
    xs = f(inp["x_sample"])
    maps = []
    for core in range(NCORES):
        sb, qi = core // 4, core % 4
        qs = qi * Q
        m = dict(shared)
        m["xp"] = xp[core]
        m["xs"] = xs[sb]
        halo = np.zeros((128, D), np.float32)
        if qs > 0:
            halo[0:64] = xs[sb, qs - 64:qs]
        if qs + Q < Ss:
            halo[64:128] = xs[sb, qs + Q:qs + Q + 64]
        m["xq"] = f(np.concatenate([xs[sb, qs:qs + Q], halo], axis=0))
        jq = np.arange(Q // 128)
        rows = np.concatenate([2 * (jq[None, :] + qs // 128) + (p[:, None] >= 64),
                               np.where(p[:, None] >= 64, (qs + Q) // 64, qs // 64 - 1)], axis=1)
        c, s = _rope_tabs(rows)
        m["rt_q"] = f(np.stack([c, s], axis=2))
        fl = np.zeros((128, 2), np.float32)
        fl[:, 0] = 1.0 if qs > 0 else 0.0
        fl[:, 1] = 1.0 if qs + Q < Ss else 0.0
        m["flags"] = fl
        maps.append(m)
    return maps


_CACHE = {}


def kernel(**inputs):
    xp = np.asarray(inputs["x_prompt"])
    xs = np.asarray(inputs["x_sample"])
    Sp, Ss = xp.shape[1], xs.shape[1]
    Q = Ss // 4
    key = (Sp, Ss)
    if key not in _CACHE:
        _CACHE[key] = Builder(Sp, Ss).build()
    nc = _CACHE[key]
    maps = host_layout(inputs, Sp, Ss)
    res = run_bass_kernel_spmd(nc, maps, core_ids=list(range(NCORES)))
    yp = np.stack([np.asarray(res.results[c]["yp"], dtype=np.float32) for c in range(NCORES)], axis=0)
    ys = np.zeros((2, Ss, D), np.float32)
    for c in range(NCORES):
        sb, qi = c // 4, c % 4
        ys[sb, qi * Q:(qi + 1) * Q] = np.asarray(res.results[c]["yq"], dtype=np.float32)
    return yp, ys
```

```python
import math
import numpy as np
import concourse.bass as bass
import concourse.mybir as mybir
from concourse.bass_utils import run_bass_kernel_spmd

F32 = mybir.dt.float32
BF16 = mybir.dt.bfloat16
AF = mybir.ActivationFunctionType
ALU = mybir.AluOpType
AX = mybir.AxisListType

D = 1024
NH = 16
NKV = 4
HD = 64
FF = 2816
NFC = FF // 128
CW = 31
CP = 15
RMS_EPS = 1e-6
LN_EPS = 1e-5
NCORES = 8
import os
ROPE_ENG1 = os.environ.get("ROPE_ENG1", "dve")
DMA_Q2 = os.environ.get("DMA_Q2", "sp")


class Sched:
    def __init__(self, nc):
        self.nc = nc
        self.names = ["pe", "act", "dve", "pool", "sp"]
        self.sem = {n: nc.alloc_semaphore("s_" + n) for n in self.names}
        self.nops = {n: 0 for n in self.names}
        self.seen = {n: {} for n in self.names}
        self.prog = {n: [] for n in self.names}
        self.awaited = {n: set() for n in self.names}
        self.last_write = {}
        self.readers = {}
        self.dma_sems = {}
        self.nins = 0
        self.parent = {}
        self.children = {}

    def set_parent(self, fine, region):
        self.parent[fine] = region
        self.children.setdefault(region, []).append(fine)

    def _rel(self, k):
        out = [k]
        if k in self.parent:
            out.append(self.parent[k])
        out += self.children.get(k, [])
        return out

    def _deps(self, eng, reads, writes):
        deps = {}
        raw_self = [-1]

        def need(t, raw=False):
            src, v = t
            if src == eng:
                if raw_self[0] < v:
                    raw_self[0] = v
                return
            if deps.get(src, -1) < v:
                deps[src] = v

        for k0 in reads:
            for k in self._rel(k0):
                if k in self.last_write:
                    need(self.last_write[k], True)
        for k0 in writes:
            for k in self._rel(k0):
                if k in self.last_write:
                    need(self.last_write[k])
                for t in self.readers.get(k, {}).values():
                    need(t)
        if raw_self[0] >= 0 and eng not in ("pe", "sp"):
            deps[eng] = raw_self[0]
        out = []
        for src, v in deps.items():
            if self.seen[eng].get(src, -1) < v:
                self.seen[eng][src] = v
                out.append((src, v))
                if isinstance(src, str):
                    self.awaited[src].add(v)
        return out

    def _record(self, t, reads, writes):
        for k in reads:
            self.readers.setdefault(k, {})[t[0]] = t
        for k in writes:
            self.last_write[k] = t
            self.readers[k] = {}

    def op(self, eng, method, reads=(), writes=(), signal=True, **kw):
        waits = self._deps(eng, reads, writes)
        idx = self.nops[eng]
        self.nops[eng] = idx + 1
        self.prog[eng].append((waits, method, kw, ("e", idx)))
        self._record((eng, idx), reads, writes)
        self.nins += 1

    def dma(self, eng, out, in_, reads=(), writes=(), sem=None):
        waits = self._deps(eng, reads, writes)
        if sem not in self.dma_sems:
            self.dma_sems[sem] = [self.nc.alloc_semaphore("d_%d" % len(self.dma_sems)), 0]
        ent = self.dma_sems[sem]
        src = ("dma", sem)
        if ent[1] > 0 and self.seen[eng].get(src, -1) < ent[1]:
            self.seen[eng][src] = ent[1]
            waits.append((src, ent[1]))
        ent[1] += 16
        t = (src, ent[1])
        self.prog[eng].append((waits, "dma_start", dict(out=out, in_=in_), ("d", ent[0])))
        self.nops[eng] += 1
        self._record(t, reads, writes)
        self.nins += 1
        return t

    def final_wait(self, eng):
        ws = [(("dma", k), ent[1]) for k, ent in self.dma_sems.items() if ent[1] > 0]
        self.prog[eng].append((ws, None, None, None))

    def emit(self):
        nc = self.nc
        rank = {}
        for e in self.names:
            rank[e] = {idx: i + 1 for i, idx in enumerate(sorted(self.awaited[e]))}
        self.cnt = {e: len(rank[e]) for e in self.names}
        with nc.Block() as block:
            def run(name):
                def body(h):
                    for waits, method, kw, tag in self.prog[name]:
                        for src, v in waits:
                            if isinstance(src, str):
                                h.wait_ge(self.sem[src], rank[src][v])
                            else:
                                h.wait_ge(self.dma_sems[src[1]][0], v)
                        if method is not None:
                            ins = getattr(h, method)(**kw)
                            if tag[0] == "d":
                                ins.then_inc(tag[1], 16)
                            elif tag[1] in rank[name]:
                                ins.then_inc(self.sem[name], 1)
                return body
            block.tensor(run("pe"))
            block.scalar(run("act"))
            block.vector(run("dve"))
            block.gpsimd(run("pool"))
            block.sync(run("sp"))


class StopBuild(Exception):
    pass


class Builder:
    stop_at = None

    def phase(self, name):
        self.phases.append(name)
        if self.stop_at is not None and name == self.stop_at:
            raise StopBuild()

    def __init__(self, Sp, Ss, debug=False):
        self.phases = []
        self.Sp, self.Ss = Sp, Ss
        self.Q = Ss // 4
        self.debug = debug
        self.nc = bass.Bass("TRN2", target_bir_lowering=False)
        self.S = Sched(self.nc)
        self._ring_next = 0
        self._tp_next = 0
        self._ps_rr = 0

    def din(self, name, shape, dt=F32):
        return self.nc.dram_tensor(name, list(shape), dt, kind="ExternalInput").ap()

    def dout(self, name, shape, dt=F32):
        return self.nc.dram_tensor(name, list(shape), dt, kind="ExternalOutput").ap()

    def declare(self):
        Sp, Ss, Q = self.Sp, self.Ss, self.Q
        nc = self.nc
        self.xp = self.din("xp", [Sp, D])
        self.xs = self.din("xs", [Ss, D])
        self.xq = self.din("xq", [Q + 128, D])
        self.w_kv = self.din("w_kv", [8, 128, 512])
        self.w_q = self.din("w_q", [8, 128, 1024])
        self.w_o = self.din("w_o", [8, 128, 1024])
        self.w_g = self.din("w_g", [2, NFC, 128, 1024])
        self.w_u = self.din("w_u", [2, NFC, 128, 1024])
        self.w_d = self.din("w_d", [2, NFC, 128, 1024])
        self.w_ci = self.din("w_ci", [16, 128, 1024])
        self.w_co = self.din("w_co", [8, 128, 1024])
        self.gcols_d = self.din("gcols", [128, 4, 8])
        self.fvec_d = self.din("fvec", [128, 40])
        self.dww_d = self.din("dww", [128, 8, CW])
        self.gq_d = self.din("gq", [1, 64])
        self.gk_d = self.din("gk", [1, 64])
        self.gf_d = self.din("gf", [1, D])
        self.bout_d = self.din("bout", [1, D])
        self.rt_all_d = self.din("rt_all", [128, Ss // 128, 2, 16])
        self.rt_q_d = self.din("rt_q", [128, Q // 128 + 1, 2, 16])
        self.ct_d = self.din("ct", [128, 2, 16])
        self.flags_d = self.din("flags", [128, 2])
        self.yp = self.dout("yp", [Sp, D])
        self.yq = self.dout("yq", [Q, D])
        self.scr = nc.dram_tensor("wscr", [8 + 8 + 8 + 6 * NFC + 16 + 8, 128, 1024], BF16).ap()

    def alloc(self):
        nc = self.nc
        Ss = self.Ss
        A = nc.alloc_sbuf_tensor
        self.nkc_max = Ss // 128
        self.KT = [A("KT%d" % m, [128, Ss], BF16) for m in range(2)]
        self.V = A("Vst", [128, self.nkc_max, 384], BF16)
        self.xsl = [A("x%d" % i, [128, 4, D], F32) for i in range(2)]
        self.h_tm = A("h_tm", [128, 4, D], BF16)
        self.hT = A("hT", [128, 8, 512], BF16)
        self.R = A("R", [128, NFC * 512], BF16)
        self.PT = A("PT", [128, 6, 512], BF16)
        self.U = [A("U%d" % i, [128, 8, 512 + 2 * CP], BF16) for i in range(2)]
        self.UH = A("UH", [128, 8, 128], BF16)
        self.ring = A("ring", [128, 6, 1024], BF16)
        self.stage = A("stage", [128, 1024], F32)
        self.rtg = A("rtg", [128, 2, 4, 2, 16], F32)
        self._rtg_next = 0
        self.ct = A("ct_s", [128, 2, 16], F32)
        self.gf = A("gf_s", [128, D], F32)
        self.gq = A("gq_s", [128, 64], F32)
        self.gk = A("gk_s", [128, 64], F32)
        self.gcols = A("gcols_s", [128, 4, 8], F32)
        self.fvec = A("fvec_s", [128, 40], F32)
        self.dww = A("dww_s", [128, 8, CW], F32)
        self.flags = A("flags_s", [128, 2], F32)
        self.bout_b = A("bout_b", [1, D], BF16)
        self.ones_b = A("ones_b", [1, 128], BF16)
        self.ident = A("ident", [128, 128], BF16)
        self.nhalf = A("nhalf", [128, 16], F32)
        self.st = A("stats", [128, 128], F32)
        self.rec = A("rec", [128, 512], F32)
        self.oraw = A("oraw", [128, 2, 512], F32)
        self.ps = [nc.alloc_psum_tensor("ps%d" % i, [128, 512], F32) for i in range(8)]
        Rf = self.R[:, :]
        self.qT = Rf[:, 0:4096].rearrange("p (c t) -> p c t", c=8)
        self.oT = Rf[:, 4096:8192].rearrange("p (c t) -> p c t", c=8)
        self.hidT = Rf[:, 0:NFC * 512].rearrange("p (c t) -> p c t", c=NFC)
        self.cT = Rf[:, 0:4096].rearrange("p (c t) -> p c t", c=8)
        self.ktm2 = [Rf[:, i * 1024:(i + 1) * 1024].rearrange("p (s c) -> p s c", s=4) for i in range(2)]
        self.qTh = Rf[:, 0:1024].rearrange("p (c t) -> p c t", c=8)
        self.oTh = Rf[:, 4096:5120].rearrange("p (c t) -> p c t", c=8)
        self.Rf = Rf
        hi = Rf[:, 4096:NFC * 512].bitcast(F32)
        self.sq = hi[:, 0:1024]
        self.qg = hi[:, 1024:2048]
        self.rtmp = hi[:, 2048:3584]
        self.diag = [Rf[:, 0:CW * 128].rearrange("p (j c) -> p j c", j=CW),
                     Rf[:, 4096:4096 + CW * 128].rearrange("p (j c) -> p j c", j=CW)]
        self.diag_keys = [["Ra"], ["Rb", "Rc"]]
        self.extra_stage = False
        self.v_stage_on = False
        c0 = self.Sp // 128
        nfree = (self.nkc_max - c0) * 384 // 2
        self.vstage = []
        if nfree >= 1024 and os.environ.get("VSTAGE"):
            vf = self.V[:, c0:, :].rearrange("p a b -> p (a b)").bitcast(F32)
            self.vstage = [(("vst", i), vf[:, i * 1024:(i + 1) * 1024]) for i in range(min(12, nfree // 1024))]
        self._stage_rr = 0
        self._cast_rr = 0
        ptf = self.PT[:, :, :].rearrange("p a b -> p (a b)")
        self.sg = [ptf[:, 0:1024].bitcast(F32), ptf[:, 1024:2048].bitcast(F32)]
        S = self.S
        S.set_parent(("ktm", 0), "Ra")
        S.set_parent(("ktm", 1), "Ra")
        S.set_parent("sq", "Rb")
        S.set_parent("qg", "Rc")
        S.set_parent(("qgh", 0), "Rc")
        S.set_parent(("qgh", 1), "Rc")
        for hf in range(2):
            for i in range(3):
                S.set_parent(("rtmp", hf, i), "Rd")
        print("sbuf bytes remaining", nc.sbuf_bytes_remaining)

    def make_chunks(self):
        self.chunks = {}
        idx = [0]

        def add(name, src, gain=None, scale=None, ncols=1024):
            self.chunks[name] = dict(src=src, gain=gain, scale=scale, ncols=ncols, scr=self.scr[idx[0]], done=False)
            idx[0] += 1

        for kc in range(8):
            add(("kv", kc), self.w_kv[kc], gain=("pp", 0, kc), ncols=512)
        for kc in range(8):
            add(("q", kc), self.w_q[kc], gain=("pp", 0, kc))
        for pr in range(8):
            add(("o", pr), self.w_o[pr])
        for l in range(2):
            for fc in range(NFC):
                add(("g", l, fc), self.w_g[l, fc], gain=("kc", 1 if l == 0 else 3))
                add(("u", l, fc), self.w_u[l, fc], gain=("kc", 1 if l == 0 else 3))
                add(("d", l, fc), self.w_d[l, fc])
        for oc in range(16):
            add(("ci", oc), self.w_ci[oc], gain=("kc", 2))
        for cc in range(8):
            add(("co", cc), self.w_co[cc])

    def fetch(self, name):
        S = self.S
        ch = self.chunks[name]
        slot = self._ring_next
        self._ring_next = (slot + 1) % 6
        n = ch["ncols"]
        dst = self.ring[:, slot, 0:n]
        key = ("ring", slot)
        if not ch["done"]:
            ch["done"] = True
            bufs = [("stage", self.stage[:, :])]
            if self.v_stage_on:
                bufs += self.vstage
            if self.extra_stage:
                bufs += [(("x", 1, j), self.xsl[1][:, j, :]) for j in range(4)]
            self._stage_rr = (self._stage_rr + 1) % len(bufs)
            skey, sbuf = bufs[self._stage_rr]
            self._cast_rr ^= 1
            ce = "pool" if self._cast_rr else "dve"
            S.dma("sp", out=sbuf[:, 0:n], in_=ch["src"], writes=[skey], sem=("stg", self._stage_rr))
            g = ch["gain"]
            if g is None:
                S.op(ce, "tensor_copy", reads=[skey], writes=[key], out=dst, in_=sbuf[:, 0:n])
            elif g[0] == "pp":
                S.op(ce, "tensor_scalar", reads=[skey, "gcols"], writes=[key], out=dst, in0=sbuf[:, 0:n],
                     scalar1=self.gcols[:, g[1], g[2]:g[2] + 1], scalar2=None, op0=ALU.mult)
            else:
                gb = self.gcols[:, g[1], :].unsqueeze(2).broadcast_to([128, 8, 128])
                S.op(ce, "tensor_tensor", reads=[skey, "gcols"], writes=[key],
                     out=dst.rearrange("p (k c) -> p k c", k=8), in0=sbuf.rearrange("p (k c) -> p k c", k=8),
                     in1=gb, op=ALU.mult)
            S.dma(DMA_Q2, out=ch["scr"][:, 0:n], in_=dst, reads=[key], writes=[("scr", name)], sem=("scr", slot))
        else:
            S.dma("sp", out=dst, in_=ch["scr"][:, 0:n], reads=[("scr", name)], writes=[key], sem=("ringld", slot))
        return key, dst

    def load_rt(self, src, j0, n):
        i = self._rtg_next
        self._rtg_next = 1 - i
        key = ("rtg", i)
        self.S.dma("sp", out=self.rtg[:, i, 0:n], in_=src[:, j0:j0 + n], writes=[key], sem=key)
        return self.rtg[:, i], key

    def bank(self):
        b = self._ps_rr
        self._ps_rr = (b + 1) % 8
        return b

    def setup(self):
        S = self.S
        ld = [("ct", self.ct, self.ct_d),
              ("gcols", self.gcols, self.gcols_d), ("fvec", self.fvec, self.fvec_d), ("dww", self.dww, self.dww_d),
              ("flags", self.flags, self.flags_d)]
        for i, (k, sb, dr) in enumerate(ld):
            S.dma("sp", out=sb[:], in_=dr, writes=[k], sem=("setup", i % 4))
        S.dma("sp", out=self.gf[:], in_=self.gf_d.partition_broadcast(128), writes=["gf"], sem=("setup", 0))
        S.dma("sp", out=self.gq[:], in_=self.gq_d.partition_broadcast(128), writes=["gq"], sem=("setup", 1))
        S.dma("sp", out=self.gk[:], in_=self.gk_d.partition_broadcast(128), writes=["gk"], sem=("setup", 2))
        S.dma("sp", out=self.stage[0:1, :], in_=self.bout_d, writes=["stage"], sem="stage")
        S.op("pool", "tensor_copy", reads=["stage"], writes=["bout_b"], out=self.bout_b[:], in_=self.stage[0:1, :])
        S.op("pool", "memset", writes=["ones_b"], ap=self.ones_b[:], constant=1.0)
        S.op("pool", "memset", writes=["nhalf"], ap=self.nhalf[:], constant=-0.5)
        identf = self.stage[:, 0:128]
        S.op("pool", "memset", reads=["stage"], writes=["stage"], ap=identf, constant=0.0)
        S.op("pool", "affine_select", reads=["stage"], writes=["stage"], out=identf, in_=identf,
             pattern=[[-1, 128]], compare_op=ALU.not_equal, fill=1.0, base=0, channel_multiplier=1)
        S.op("pool", "tensor_copy", reads=["stage"], writes=["ident"], out=self.ident[:], in_=identf)
        S.op("pool", "tensor_scalar", reads=["gq"], writes=["gq"], out=self.gq[:], in0=self.gq[:], scalar1=1.0 / math.sqrt(HD),
             scalar2=None, op0=ALU.mult)
        c1 = self.Sp // 128 if self.vstage else self.nkc_max
        S.op("pool", "memset", writes=["V"], ap=self.V[:, 0:c1, 64:128], constant=1.0)
        S.op("pool", "memset", writes=["V"], ap=self.V[:, 0:c1, 256:320], constant=1.0)

    def xk(self, slot, s):
        return ("x", slot, s)

    def norm_to_hT(self, xsl, slot, nsub, evac_eng="dve", scale_all_act=False):
        self.norm_scale(xsl, slot, nsub, scale_all_act)
        self.norm_transposes(nsub, evac_eng)

    def norm_scale(self, xsl, slot, nsub, scale_all_act=False):
        S = self.S
        for s in range(nsub):
            S.op("act", "activation", reads=[self.xk(slot, s)], writes=[("h_tm", s), ("ss", s)], out=self.h_tm[:, s, :], in_=xsl[:, s, :],
                 func=AF.Square, accum_out=self.st[:, s:s + 1])
        rr = self.st[:, 8:8 + nsub]
        S.op("pool", "tensor_scalar", reads=[("ss", s) for s in range(nsub)], writes=["rr"], out=rr, in0=self.st[:, 0:nsub],
             scalar1=1.0 / D, scalar2=RMS_EPS, op0=ALU.mult, op1=ALU.add)
        S.op("pool", "tensor_tensor", reads=["rr", "nhalf"], writes=["rr"], out=rr, in0=rr, in1=self.nhalf[:, 0:nsub], op=ALU.pow)
        for s in range(nsub):
            if s % 2 == 0 or scale_all_act:
                S.op("act", "activation", reads=[self.xk(slot, s), "rr"], writes=[("h_tm", s)], out=self.h_tm[:, s, :],
                     in_=xsl[:, s, :], func=AF.Copy, scale=self.st[:, 8 + s:9 + s])
            else:
                S.op("dve", "tensor_scalar", reads=[self.xk(slot, s), "rr"], writes=[("h_tm", s)], out=self.h_tm[:, s, :],
                     in0=xsl[:, s, :], scalar1=self.st[:, 8 + s:9 + s], scalar2=None, op0=ALU.mult)
        self.phase("n_scale")

    def norm_transposes(self, nsub, evac_eng="dve"):
        S = self.S
        for s in range(nsub):
            b = self.bank()
            pk = ("ps", b)
            pv = self.ps[b][:, :].bitcast(BF16)
            for c in range(8):
                S.op("pe", "transpose", reads=[("h_tm", s), "ident"], writes=[pk], out=pv[:, c * 128:(c + 1) * 128],
                     in_=self.h_tm[:, s, c * 128:(c + 1) * 128], identity=self.ident[:])
            dst = self.hT[:, :, s * 128:(s + 1) * 128]
            src = pv.rearrange("p (c t) -> p c t", c=8)
            if evac_eng == "dve":
                S.op("dve", "tensor_copy", reads=[pk], writes=["hT"], out=dst, in_=src)
            else:
                S.op("act", "activation", reads=[pk], writes=["hT"], out=dst, in_=src, func=AF.Copy)
        self.phase("n_hT")

    def transposes(self, srcs, srckeys, nchunks, dst, dstkey, evac_eng="dve", act_kw=None, col0=0):
        S = self.S
        nsub = len(srcs)
        T = nsub * 128
        for c in range(nchunks):
            b = self.bank()
            pk = ("ps", b)
            pv = self.ps[b][:, :].bitcast(BF16)
            for s in range(nsub):
                S.op("pe", "transpose", reads=[srckeys[s], "ident"], writes=[pk], signal=(s == nsub - 1),
                     out=pv[:, s * 128:(s + 1) * 128], in_=srcs[s][:, c * 128:(c + 1) * 128], identity=self.ident[:])
            if act_kw is not None:
                kw = act_kw(c)
                S.op("act", "activation", reads=[pk] + kw.pop("reads", []), writes=[dstkey], out=dst[:, c, col0:col0 + T],
                     in_=pv[:, 0:T], **kw)
            elif evac_eng == "dve":
                S.op("dve", "tensor_copy", reads=[pk], writes=[dstkey], out=dst[:, c, col0:col0 + T], in_=pv[:, 0:T])
            else:
                S.op("act", "activation", reads=[pk], writes=[dstkey], out=dst[:, c, col0:col0 + T], in_=pv[:, 0:T],
                     func=AF.Copy)

    def tm_proj(self, actT, actkey, chunk_names, nsub, ncb, evac, bias=False):
        S = self.S
        banks = [[self.bank() for cb in range(ncb)] for s in range(nsub)]
        nch = len(chunk_names)
        if bias:
            for s in range(nsub):
                for cb in range(ncb):
                    S.op("pe", "matmul", reads=["ones_b", "bout_b"], writes=[("ps", banks[s][cb])], signal=False,
                         out=self.ps[banks[s][cb]][:, :], lhsT=self.ones_b[0:1, :], rhs=self.bout_b[0:1, cb * 512:(cb + 1) * 512],
                         start=True, stop=False)
        for c, name in enumerate(chunk_names):
            wkey, w = self.fetch(name)
            for s in range(nsub):
                for cb in range(ncb):
                    last = (s == nsub - 1 and cb == ncb - 1)
                    S.op("pe", "matmul", reads=(actkey if isinstance(actkey, list) else [actkey]) + [wkey], writes=[("ps", banks[s][cb])], signal=last,
                         out=self.ps[banks[s][cb]][:, :], lhsT=actT[:, c, s * 128:(s + 1) * 128],
                         rhs=w[:, cb * 512:(cb + 1) * 512], start=(c == 0 and not bias), stop=(c == nch - 1))
        self.phase("tm_mm")
        for s in range(nsub):
            for cb in range(ncb):
                evac(s, cb, banks[s][cb])
                self.phase("tm_evac1")

    def resid_evac(self, xsl, slot):
        def evac(s, cb, b):
            xkey = self.xk(slot, s)
            self.S.op("dve", "tensor_tensor", reads=[("ps", b), xkey], writes=[xkey], out=xsl[:, s, cb * 512:(cb + 1) * 512],
                      in0=self.ps[b][:, :], in1=xsl[:, s, cb * 512:(cb + 1) * 512], op=ALU.add)
        return evac

    def qk_post(self, pviews, pkeys, G, H, gbc, gkey, rt, rtkey, out, outkeys):
        S = self.S
        n = G * H * 64
        sq = self.sq[:, 0:n]
        qg = self.qg[:, 0:n]
        off = 0
        for pv, pk in zip(pviews, pkeys):
            w = pv.shape[1]
            S.op("act", "activation", reads=[pk], writes=["sq"], out=sq[:, off:off + w], in_=pv, func=AF.Square)
            S.op("dve", "tensor_tensor", reads=[pk, gkey, "sq"], writes=["qg"], out=qg[:, off:off + w].rearrange("p (h d) -> p h d", d=64),
                 in0=pv.rearrange("p (h d) -> p h d", d=64), in1=gbc[:, :].unsqueeze(1).broadcast_to([128, w // 64, 64]), op=ALU.mult)
            off += w
        GH = G * H
        ssq = self.st[:, 16:16 + GH]
        rs = self.st[:, 32:32 + GH]
        S.op("dve", "tensor_reduce", reads=["sq"], writes=["ssq"], out=ssq, in_=sq.rearrange("p (h d) -> p h d", d=64),
             axis=AX.X, op=ALU.add)
        S.op("pool", "tensor_scalar", reads=["ssq"], writes=["rs"], out=rs, in0=ssq, scalar1=1.0 / HD, scalar2=RMS_EPS,
             op0=ALU.mult, op1=ALU.add)
        S.op("pool", "tensor_tensor", reads=["rs", "nhalf"], writes=["rs"], out=rs, in0=rs, in1=self.nhalf[:, 0:GH], op=ALU.pow)
        v6 = qg.rearrange("p (g h r f d) -> p g h r f d", g=G, h=H, r=2, f=2, d=16)
        shp = [128, G, H, 16]
        for half, eng in ((0, "dve"), (1, ROPE_ENG1)):
            Aa = v6[:, :, :, half, 0, :]
            Bb = v6[:, :, :, half, 1, :]
            if half == 0:
                C = rt[:, :, 0, :].unsqueeze(2).broadcast_to(shp)
                Sn = rt[:, :, 1, :].unsqueeze(2).broadcast_to(shp)
                tk = [rtkey]
            else:
                C = self.ct[:, 0, :].unsqueeze(1).unsqueeze(1).broadcast_to(shp)
                Sn = self.ct[:, 1, :].unsqueeze(1).unsqueeze(1).broadcast_to(shp)
                tk = ["ct"]
            t = [self.rtmp[:, (half * 3 + i) * 256:(half * 3 + i) * 256 + GH * 16].rearrange("p (g h d) -> p g h d", g=G, h=H)
                 for i in range(3)]
            k = [("rtmp", half, i) for i in range(3)]
            qk = ("qgh", half)
            S.op(eng, "tensor_tensor", reads=["qg", qk] + tk, writes=[k[0]], out=t[0], in0=Aa, in1=C, op=ALU.mult)
            S.op(eng, "tensor_tensor", reads=["qg", qk] + tk, writes=[k[1]], out=t[1], in0=Aa, in1=Sn, op=ALU.mult)
            S.op(eng, "tensor_tensor", reads=["qg", qk] + tk, writes=[k[2]], out=t[2], in0=Bb, in1=Sn, op=ALU.mult)
            S.op(eng, "tensor_tensor", reads=["qg", k[0], k[2]], writes=[qk], out=Aa, in0=t[0], in1=t[2], op=ALU.subtract)
            S.op(eng, "tensor_tensor", reads=["qg", qk] + tk, writes=[k[0]], out=t[0], in0=Bb, in1=C, op=ALU.mult)
            S.op(eng, "tensor_tensor", reads=["qg", k[0], k[1]], writes=[qk], out=Bb, in0=t[0], in1=t[1], op=ALU.add)
        S.op("dve", "tensor_tensor", reads=["qg", ("qgh", 0), ("qgh", 1), "rs"], writes=list(outkeys) + ["qg"],
             out=out.rearrange("p g (h d) -> p g h d", d=64), in0=qg.rearrange("p (g h d) -> p g h d", g=G, h=H),
             in1=rs.rearrange("p (g h) -> p g h", g=G).unsqueeze(3).broadcast_to([128, G, H, 64]), op=ALU.mult)

    def k_transposes(self, g, nsub):
        S = self.S
        T = nsub * 128
        kt = self.ktm2[g % 2]
        kk = ("ktm", g % 2)
        for m in range(2):
            b = self.bank()
            pk = ("ps", b)
            pv = self.ps[b][:, :].bitcast(BF16)
            for s in range(nsub):
                S.op("pe", "transpose", reads=[kk, "ident"], writes=[pk],
                     out=pv[:, s * 128:(s + 1) * 128], in_=kt[:, s, m * 128:(m + 1) * 128], identity=self.ident[:])
            S.op("dve", "tensor_copy", reads=[pk], writes=["KT"], out=self.KT[m][:, g * 512:g * 512 + T], in_=pv[:, 0:T])

    def stage1(self, xsrc, S_len):
        S = self.S
        ngrp = (S_len + 511) // 512
        pending = None

        def load_x(g):
            nsub_ = min(4, (S_len - g * 512) // 128)
            for s in range(nsub_):
                S.dma("sp", out=self.xsl[g % 2][:, s, :], in_=xsrc[g * 512 + s * 128:g * 512 + (s + 1) * 128, :],
                      writes=[self.xk(g % 2, s)], sem=("xld", g % 2, s))

        load_x(0)
        self.norm_scale(self.xsl[0], 0, min(4, S_len // 128), True)
        for g in range(ngrp):
            nsub = min(4, (S_len - g * 512) // 128)
            slot = g % 2
            xsl = self.xsl[slot]
            rtv, rtk = self.load_rt(self.rt_all_d, g * 4, nsub)
            if g + 1 < ngrp:
                load_x(g + 1)
            self.norm_transposes(nsub, "act")
            kbanks = []
            self.tm_proj(self.hT, "hT", [("kv", kc) for kc in range(8)], nsub, 1, lambda s, cb, b, kb=kbanks: kb.append(b))
            if pending is not None:
                self.k_transposes(*pending)
            if g + 1 < ngrp:
                self.norm_scale(self.xsl[(g + 1) % 2], (g + 1) % 2, min(4, (S_len - (g + 1) * 512) // 128), True)
            for s, b in enumerate(kbanks):
                pv = self.ps[b]
                pk = ("ps", b)
                j = g * 4 + s
                for (dst0, src0) in ((0, 256), (128, 320), (192, 448), (320, 384)):
                    S.op("act", "activation", reads=[pk], writes=["V"], out=self.V[:, j, dst0:dst0 + 64], in_=pv[:, src0:src0 + 64],
                         func=AF.Copy)
            self.qk_post([self.ps[bb][:, 0:256] for bb in kbanks], [("ps", bb) for bb in kbanks], nsub, 4, self.gk, "gk",
                         rtv[:, 0:nsub], rtk, self.ktm2[g % 2][:, 0:nsub, :], [("ktm", g % 2)])
            pending = (g, nsub)
            self.phase("s1")
        self.k_transposes(*pending)

    def ffn(self, l, xsl, slot, nsub):
        S = self.S
        T = nsub * 128
        self.norm_to_hT(xsl, slot, nsub)
        for fc in range(NFC):
            kg, wg = self.fetch(("g", l, fc))
            ku, wu = self.fetch(("u", l, fc))
            bg, bu = self.bank(), self.bank()
            for (kk, w, b) in ((kg, wg, bg), (ku, wu, bu)):
                for kc in range(8):
                    S.op("pe", "matmul", reads=["hT", kk], writes=[("ps", b)], signal=(kc == 7), out=self.ps[b][:, 0:T],
                         lhsT=w[:, kc * 128:(kc + 1) * 128], rhs=self.hT[:, kc, 0:T], start=(kc == 0), stop=(kc == 7))
            sgi = fc % 2
            S.op("act", "activation", reads=[("ps", bg)], writes=[("PT", 2 * sgi), ("PT", 2 * sgi + 1)], out=self.sg[sgi][:, 0:T], in_=self.ps[bg][:, 0:T],
                 func=AF.Silu)
            S.op("dve", "tensor_tensor", reads=[("ps", bu), ("PT", 2 * sgi), ("PT", 2 * sgi + 1)], writes=[self.hkey(fc)], out=self.hidT[:, fc, 0:T],
                 in0=self.ps[bu][:, 0:T], in1=self.sg[sgi][:, 0:T], op=ALU.mult)
        self.tm_proj(self.hidT, ["Ra", "Rb", "Rc", "Rd"], [("d", l, fc) for fc in range(NFC)], nsub, 2, self.resid_evac(xsl, slot))

    def attention(self, nsub, nkc):
        S = self.S
        T = nsub * 128
        sbanks = [(0, 1), (2, 3), (4, 5)]
        oa, ob = 6, 7
        if nsub == 1:
            N = 512
            units = []
            for m in range(2):
                qa = self.Rf[0:64, m * 512:(m + 1) * 512]
                qb = self.Rf[64:128, m * 512:(m + 1) * 512]
                units.append((m, qa, qb, lambda lo, hi, m=m: self.Rf[lo:hi, 4096 + m * 512:4096 + (m + 1) * 512], "Rb"))
        else:
            N = T
            units = []
            for pr in range(8):
                units.append((pr // 4, self.qT[0:64, pr, 0:T], self.qT[64:128, pr, 0:T],
                              lambda lo, hi, pr=pr: self.oT[lo:hi, pr, 0:T], self.okey(pr)))
        items = [(u, c) for u in range(len(units)) for c in range(nkc)]

        def vviews(m, c):
            if m == 0:
                return self.V[:, c, 0:128], self.V[:, c, 64:192]
            return self.V[:, c, 256:384], self.V[:, c, 192:320]

        def s_mm(k):
            u, c = items[k]
            m, qa, qb, _, _ = units[u]
            ba, bb = sbanks[k % 3]
            S.op("pe", "matmul", reads=["KT", "Ra"], writes=[("ps", ba)], out=self.ps[ba][:, 0:N],
                 lhsT=self.KT[m][0:64, c * 128:(c + 1) * 128], rhs=qa, start=True, stop=True, tile_position=(0, 0))
            S.op("pe", "matmul", reads=["KT", "Ra"], writes=[("ps", bb)], out=self.ps[bb][:, 0:N],
                 lhsT=self.KT[m][64:128, c * 128:(c + 1) * 128], rhs=qb, start=True, stop=True, tile_position=(64, 0))

        def finish(u):
            m, _, _, oview, ok = units[u]
            a_o, b_o = (0, 64) if m == 0 else (64, 0)
            S.op("dve", "tensor_copy", reads=[("ps", oa)], writes=[("oraw", 0)], out=self.oraw[:, 0, 0:N], in_=self.ps[oa][:, 0:N])
            S.op("dve", "tensor_copy", reads=[("ps", ob)], writes=[("oraw", 1)], out=self.oraw[:, 1, 0:N], in_=self.ps[ob][:, 0:N])
            for h, o_off in enumerate((a_o, b_o)):
                s_off = 64 - o_off
                rk = ("rec", o_off)
                S.op("dve", "reciprocal", reads=[("oraw", h)], writes=[rk], out=self.rec[o_off:o_off + 64, 0:N],
                     in_=self.oraw[s_off:s_off + 64, h, 0:N])
                S.op("dve", "tensor_tensor", reads=[("oraw", h), rk], writes=[ok], out=oview(o_off, o_off + 64),
                     in0=self.oraw[o_off:o_off + 64, h, 0:N], in1=self.rec[o_off:o_off + 64, 0:N], op=ALU.mult)

        def pv_mm(k):
            u, c = items[k]
            m = units[u][0]
            ba, bb = sbanks[k % 3]
            pa = (k % 3) * 2
            va, vb = vviews(m, c)
            S.op("act", "activation", reads=[("ps", ba)], writes=[("PT", pa)], out=self.PT[:, pa, 0:N], in_=self.ps[ba][:, 0:N],
                 func=AF.Exp)
            S.op("act", "activation", reads=[("ps", bb)], writes=[("PT", pa + 1)], out=self.PT[:, pa + 1, 0:N],
                 in_=self.ps[bb][:, 0:N], func=AF.Exp)
            S.op("pe", "matmul", reads=["V", ("PT", pa)], writes=[("ps", oa)], out=self.ps[oa][:, 0:N],
                 lhsT=va, rhs=self.PT[:, pa, 0:N], start=(c == 0), stop=(c == nkc - 1))
            S.op("pe", "matmul", reads=["V", ("PT", pa + 1)], writes=[("ps", ob)], out=self.ps[ob][:, 0:N],
                 lhsT=vb, rhs=self.PT[:, pa + 1, 0:N], start=(c == 0), stop=(c == nkc - 1))
            if c == nkc - 1:
                finish(u)

        n = len(items)
        for k in range(min(2, n)):
            s_mm(k)
        for k in range(n):
            if k + 2 < n:
                s_mm(k + 2)
            pv_mm(k)

    def okey(self, pr):
        return "Rb" if pr < 4 else "Rc"

    def hkey(self, fc):
        return "Ra" if fc < 8 else ("Rb" if fc < 12 else ("Rc" if fc < 16 else "Rd"))

    def bank_o(self, pr):
        return (6, 7)

    def S_a(self, xsrc_ap, slot, nsub, nkc, rt_tab, rt_key, rt_j0, u_dst, u_key, u_col0):
        S = self.S
        T = nsub * 128
        xsl = self.xsl[slot]
        for s in range(nsub):
            S.dma("sp", out=xsl[:, s, :], in_=xsrc_ap[s * 128:(s + 1) * 128, :], writes=[self.xk(slot, s)], sem=("xld", slot, s))
        rtv, rtk = self.load_rt(rt_tab, rt_j0, nsub)
        self.norm_to_hT(xsl, slot, nsub)

        def evac_q(s, cb, b):
            if cb == 1:
                b0 = self._qbanks[(s, 0)]
                self.qk_post([self.ps[b0][:, :], self.ps[b][:, :]], [("ps", b0), ("ps", b)], 1, 16, self.gq, "gq",
                             rtv[:, s:s + 1], rtk, self.h_tm[:, s:s + 1, :], [("h_tm", s)])
            else:
                self._qbanks[(s, 0)] = b
        self._qbanks = {}
        self.tm_proj(self.hT, "hT", [("q", kc) for kc in range(8)], nsub, 2, evac_q)
        self.transposes([self.h_tm[:, s, :] for s in range(nsub)], [("h_tm", s) for s in range(nsub)], 8,
                        self.qTh if nsub == 1 else self.qT, "Ra")
        self.phase("q")
        self._ps_rr = 0
        self.attention(nsub, nkc)
        self.phase("att")
        self.tm_proj(self.oTh if nsub == 1 else self.oT, ["Rb", "Rc"], [("o", pr) for pr in range(8)], nsub, 2,
                     self.resid_evac(xsl, slot))
        self.phase("wo")
        self.ffn(0, xsl, slot, nsub)
        self.phase("ffn0")
        self.norm_to_hT(xsl, slot, nsub)
        for cc in range(8):
            ka, wa = self.fetch(("ci", cc))
            kg, wg = self.fetch(("ci", 8 + cc))
            ba, bg = self.bank(), self.bank()
            for (kk, w, b) in ((ka, wa, ba), (kg, wg, bg)):
                for kc in range(8):
                    S.op("pe", "matmul", reads=["hT", kk], writes=[("ps", b)], signal=(kc == 7), out=self.ps[b][:, 0:T],
                         lhsT=w[:, kc * 128:(kc + 1) * 128], rhs=self.hT[:, kc, 0:T], start=(kc == 0), stop=(kc == 7))
            sgi = cc % 2
            S.op("act", "activation", reads=[("ps", bg), "fvec"], writes=[("PT", 2 * sgi), ("PT", 2 * sgi + 1)], out=self.sg[sgi][:, 0:T],
                 in_=self.ps[bg][:, 0:T], func=AF.Sigmoid, bias=self.fvec[:, 8 + cc:9 + cc])
            S.op("dve", "scalar_tensor_tensor", reads=[("ps", ba), ("PT", 2 * sgi), ("PT", 2 * sgi + 1), "fvec"], writes=[u_key],
                 out=u_dst[:, cc, u_col0:u_col0 + T], in0=self.ps[ba][:, 0:T], scalar=self.fvec[:, cc:cc + 1],
                 in1=self.sg[sgi][:, 0:T], op0=ALU.add, op1=ALU.mult)

    def S_b(self, slot, uslot, nsub, ydst_ap):
        S = self.S
        T = nsub * 128
        xsl = self.xsl[slot]
        U = self.U[uslot]
        ukey = ("U", uslot)
        for cc in range(8):
            dg = self.diag[cc % 2]
            dk = self.diag_keys[cc % 2]
            S.op("dve", "tensor_tensor", reads=["ident", "dww"], writes=dk, out=dg,
                 in0=self.ident[:, :].unsqueeze(1).broadcast_to([128, CW, 128]),
                 in1=self.dww[:, cc, :].unsqueeze(2).broadcast_to([128, CW, 128]), op=ALU.mult)
            b = self.bank()
            for j in range(CW):
                S.op("pe", "matmul", reads=dk + [ukey], writes=[("ps", b)], out=self.ps[b][:, 0:T],
                     lhsT=dg[:, j, :], rhs=U[:, cc, j:j + T], start=(j == 0), stop=(j == CW - 1))
            S.op("act", "activation", reads=[("ps", b), "fvec"], writes=["hT"], out=self.hT[:, cc, 0:T], in_=self.ps[b][:, 0:T],
                 func=AF.Identity, bias=self.fvec[:, 16 + cc:17 + cc])
        self.phase("conv")
        lnb = []
        for s in range(nsub):
            b = self.bank()
            pk = ("ps", b)
            pv = self.ps[b][:, :].bitcast(BF16)
            for cc in range(8):
                S.op("pe", "transpose", reads=["hT", "ident"], writes=[pk], out=pv[:, cc * 128:(cc + 1) * 128],
                     in_=self.hT[:, cc, s * 128:(s + 1) * 128], identity=self.ident[:])
            S.op("act", "activation", reads=[pk], writes=[("h_tm", s), ("lns", s)], out=self.h_tm[:, s, :], in_=pv, func=AF.Identity,
                 accum_out=self.st[:, 48 + s:49 + s])
            S.op("act", "activation", reads=[pk], writes=[("h_tm", s), ("lns", s)], out=self.h_tm[:, s, :], in_=pv, func=AF.Square,
                 accum_out=self.st[:, 52 + s:53 + s])
            lnb.append((pk, pv))
        lk = [("lns", s) for s in range(nsub)]
        mean = self.st[:, 56:56 + nsub]
        var = self.st[:, 60:60 + nsub]
        msq = self.st[:, 64:64 + nsub]
        S.op("dve", "tensor_scalar", reads=lk, writes=["lnm"], out=mean, in0=self.st[:, 48:48 + nsub], scalar1=1.0 / D,
             scalar2=None, op0=ALU.mult)
        S.op("dve", "tensor_tensor", reads=["lnm"], writes=["lnq"], out=msq, in0=mean, in1=mean, op=ALU.mult)
        S.op("dve", "scalar_tensor_tensor", reads=lk + ["lnq"], writes=["lnv"], out=var, in0=self.st[:, 52:52 + nsub], scalar=1.0 / D,
             in1=msq, op0=ALU.mult, op1=ALU.subtract)
        S.op("pool", "tensor_scalar", reads=["lnv"], writes=["lnv"], out=var, in0=var, scalar1=LN_EPS, scalar2=None, op0=ALU.add)
        S.op("pool", "tensor_tensor", reads=["lnv", "nhalf"], writes=["lnv"], out=var, in0=var, in1=self.nhalf[:, 0:nsub], op=ALU.pow)
        for s in range(nsub):
            pk, pv = lnb[s]
            S.op("dve", "tensor_scalar", reads=[pk, "lnm", "lnv"], writes=[("h_tm", s)], out=self.h_tm[:, s, :], in0=pv,
                 scalar1=self.st[:, 56 + s:57 + s], scalar2=self.st[:, 60 + s:61 + s], op0=ALU.subtract, op1=ALU.mult)
        self.transposes([self.h_tm[:, s, :] for s in range(nsub)], [("h_tm", s) for s in range(nsub)], 8, self.hT, "hT",
                        act_kw=lambda c: dict(func=AF.Silu, scale=self.fvec[:, 24 + c:25 + c], bias=self.fvec[:, 32 + c:33 + c],
                                              reads=["fvec"]))
        self.phase("ln")
        self.tm_proj(self.hT, "hT", [("co", cc) for cc in range(8)], nsub, 2, self.resid_evac(xsl, slot), bias=True)
        self.phase("co")
        self.ffn(1, xsl, slot, nsub)
        self.phase("ffn1")
        for s in range(nsub):
            S.op("act", "activation", reads=[self.xk(slot, s)], writes=[("h_tm", s), ("fs", s)], out=self.h_tm[:, s, :], in_=xsl[:, s, :],
                 func=AF.Square, accum_out=self.st[:, 68 + s:69 + s])
        fr = self.st[:, 72:72 + nsub]
        S.op("pool", "tensor_scalar", reads=[("fs", s) for s in range(nsub)], writes=["fr"], out=fr, in0=self.st[:, 68:68 + nsub],
             scalar1=1.0 / D, scalar2=RMS_EPS, op0=ALU.mult, op1=ALU.add)
        S.op("pool", "tensor_tensor", reads=["fr", "nhalf"], writes=["fr"], out=fr, in0=fr, in1=self.nhalf[:, 0:nsub], op=ALU.pow)
        for s in range(nsub):
            xkey = self.xk(slot, s)
            S.op("dve", "scalar_tensor_tensor", reads=[xkey, "fr", "gf"], writes=[xkey], out=xsl[:, s, :], in0=xsl[:, s, :],
                 scalar=self.st[:, 72 + s:73 + s], in1=self.gf[:], op0=ALU.mult, op1=ALU.mult)
            S.dma("sp", out=ydst_ap[s * 128:(s + 1) * 128, :], in_=xsl[:, s, :], reads=[xkey], sem=("yst", slot, s))
        self.phase("fin")

    def run_pass(self, xkv, S_len, xown, n_own, rt_tab, rt_key, ydst, has_halo):
        S = self.S
        nkc = S_len // 128
        self.stage1(xkv, S_len)
        ntile = n_own // 512
        if has_halo:
            self.S_a(xown[n_own:n_own + 128, :], 0, 1, nkc, rt_tab, rt_key, n_own // 128, self.UH, "UH", 0)
            S.op("pool", "tensor_scalar", reads=["UH", "flags"], writes=[("U", 0)], out=self.U[0][:, :, 0:CP],
                 in0=self.UH[:, :, 64 - CP:64], scalar1=self.flags[:, 0:1], scalar2=None, op0=ALU.mult)
        else:
            S.op("pool", "memset", writes=[("U", 0)], ap=self.U[0][:, :, 0:CP], constant=0.0)
        for i in range(ntile + 1):
            if i < ntile:
                us = i % 2
                self.extra_stage = (i == 0 and not has_halo)
                self.S_a(xown[i * 512:(i + 1) * 512, :], i % 2, 4, nkc, rt_tab, rt_key, i * 4, self.U[us], ("U", us), CP)
                self.extra_stage = False
                if i > 0:
                    S.op("pool", "tensor_copy", reads=[("U", 1 - us)], writes=[("U", us)], out=self.U[us][:, :, 0:CP],
                         in_=self.U[1 - us][:, :, 512:512 + CP])
                    S.op("pool", "tensor_copy", reads=[("U", us)], writes=[("U", 1 - us)], out=self.U[1 - us][:, :, 512 + CP:512 + 2 * CP],
                         in_=self.U[us][:, :, CP:2 * CP])
            if i == ntile:
                us = (ntile - 1) % 2
                if has_halo:
                    S.op("pool", "tensor_scalar", reads=["UH", "flags"], writes=[("U", us)], out=self.U[us][:, :, 512 + CP:512 + 2 * CP],
                         in0=self.UH[:, :, 64:64 + CP], scalar1=self.flags[:, 1:2], scalar2=None, op0=ALU.mult)
                else:
                    S.op("pool", "memset", writes=[("U", us)], ap=self.U[us][:, :, 512 + CP:512 + 2 * CP], constant=0.0)
            if i >= 1:
                j = i - 1
                self.S_b(j % 2, j % 2, 4, ydst[j * 512:(j + 1) * 512, :])

    def build(self):
        self.declare()
        self.alloc()
        self.make_chunks()
        self.setup()
        try:
            self.phase("setup")
            self.v_stage_on = True
            self.run_pass(self.xp, self.Sp, self.xp, self.Sp, self.rt_all_d, "rt", self.yp, False)
            self.v_stage_on = False
            if self.vstage:
                vk = ["V"] + [k for k, _ in self.vstage]
                c0 = self.Sp // 128
                self.S.op("pool", "memset", writes=vk, ap=self.V[:, c0:, 64:128], constant=1.0)
                self.S.op("pool", "memset", writes=vk, ap=self.V[:, c0:, 256:320], constant=1.0)
            self.phase("passP")
            self.run_pass(self.xs, self.Ss, self.xq, self.Q, self.rt_q_d, "rt", self.yq, True)
        except StopBuild:
            print("STOPPED at", self.stop_at)
        self.S.final_wait("pool")
        self.S.final_wait("sp")
        self.S.emit()
        print("instructions", self.S.nins, dict(self.S.nops), "signals", dict(self.S.cnt))
        return self.nc


def _rope_tabs(pos_rows):
    inv = (10000.0 ** (-np.arange(0, 32, 2, dtype=np.float64) / 32.0))
    ang = pos_rows[..., None].astype(np.float64) * inv
    return np.cos(ang).astype(np.float32), np.sin(ang).astype(np.float32)


def host_layout(inp, Sp, Ss):
    Q = Ss // 4
    f = lambda a: np.ascontiguousarray(np.asarray(a, dtype=np.float32))
    wqkv = f(inp["w_qkv"])[0]
    heads = [8 * m + 4 * hh + i for m in range(2) for i in range(4) for hh in range(2)]
    heads_o = []
    for m in range(2):
        for i in range(4):
            a, b = 8 * m + i, 8 * m + 4 + i
            heads_o += [a, b] if m == 0 else [b, a]
    qcols = np.concatenate([np.arange(h * 64, (h + 1) * 64) for h in heads])
    orows = np.concatenate([np.arange(h * 64, (h + 1) * 64) for h in heads_o])
    shared = {}
    shared["w_q"] = f(wqkv[:, :1024][:, qcols].reshape(8, 128, 1024))
    shared["w_kv"] = f(wqkv[:, 1024:].reshape(8, 128, 512))
    shared["w_o"] = f(f(inp["w_o"])[0][orows].reshape(8, 128, 1024))

    def up(w):
        Fo = w.shape[1]
        return f(w.reshape(8, 128, Fo // 128, 128).transpose(2, 1, 0, 3).reshape(Fo // 128, 128, 1024))
    shared["w_g"] = f(np.stack([up(f(inp["w_gate"])[l]) for l in range(2)]))
    shared["w_u"] = f(np.stack([up(f(inp["w_up"])[l]) for l in range(2)]))
    shared["w_d"] = f(f(inp["w_down"]).reshape(2, NFC, 128, 1024))
    shared["w_ci"] = up(f(inp["conv_w_in"])[0])
    shared["w_co"] = f(f(inp["conv_w_out"])[0].reshape(8, 128, 1024))
    col = lambda v: f(v).reshape(-1, 128).T
    gcols = np.stack([col(f(inp["attn_norm_g"])[0]), col(f(inp["ffn_norm_g"])[0]), col(f(inp["conv_norm_g"])[0]),
                      col(f(inp["ffn_norm_g"])[1])], axis=1)
    shared["gcols"] = f(gcols)
    shared["fvec"] = f(np.concatenate([col(f(inp["conv_b_in"])[0]), col(f(inp["dw_b"])[0]), col(f(inp["conv_ln_g"])[0]),
                                       col(f(inp["conv_ln_b"])[0])], axis=1))
    shared["dww"] = f(f(inp["dw_w"])[0].T.reshape(8, 128, CW).transpose(1, 0, 2))
    shared["gq"] = f(inp["q_norm_g"]).reshape(1, 64)
    shared["gk"] = f(inp["k_norm_g"]).reshape(1, 64)
    shared["gf"] = f(inp["final_norm_g"]).reshape(1, D)
    shared["bout"] = f(inp["conv_b_out"]).reshape(1, D)
    p = np.arange(128)
    j = np.arange(Ss // 128)
    rows = 2 * j[None, :] + (p[:, None] >= 64)
    c, s = _rope_tabs(rows)
    shared["rt_all"] = f(np.stack([c, s], axis=2))
    c, s = _rope_tabs(p % 64)
    shared["ct"] = f(np.stack([c, s], axis=1))
    xp = f(inp["x_prompt"])
    xs = f(inp["x_sample"])
    maps = []
    for core in range(NCORES):
        sb, qi = core // 4, core % 4
        qs = qi * Q
        m = dict(shared)
        m["xp"] = xp[core]
        m["xs"] = xs[sb]
        halo = np.zeros((128, D), np.float32)
        if qs > 0:
            halo[0:64] = xs[sb, qs - 64:qs]
        if qs + Q < Ss:
            halo[64:128] = xs[sb, qs + Q:qs + Q + 64]
        m["xq"] = f(np.concatenate([xs[sb, qs:qs + Q], halo], axis=0))
        jq = np.arange(Q // 128)
        rows = np.concatenate([2 * (jq[None, :] + qs // 128) + (p[:, None] >= 64),
                               np.where(p[:, None] >= 64, (qs + Q) // 64, qs // 64 - 1)], axis=1)
        c, s = _rope_tabs(rows)
        m["rt_q"] = f(np.stack([c, s], axis=2))
        fl = np.zeros((128, 2), np.float32)
        fl[:, 0] = 1.0 if qs > 0 else 0.0
        fl[:, 1] = 1.0 if qs + Q < Ss else 0.0
        m["flags"] = fl
        maps.append(m)
    return maps


_CACHE = {}


def kernel(**inputs):
    xp = np.asarray(inputs["x_prompt"])
    xs = np.asarray(inputs["x_sample"])
    Sp, Ss = xp.shape[1], xs.shape[1]
    Q = Ss // 4
    key = (Sp, Ss)
    if key not in _CACHE:
        _CACHE[key] = Builder(Sp, Ss).build()
    nc = _CACHE[key]
    maps = host_layout(inputs, Sp, Ss)
    res = run_bass_kernel_spmd(nc, maps, core_ids=list(range(NCORES)))
    yp = np.stack([np.asarray(res.results[c]["yp"], dtype=np.float32) for c in range(NCORES)], axis=0)
    ys = np.zeros((2, Ss, D), np.float32)
    for c in range(NCORES):
        sb, qi = c // 4, c % 4
        ys[sb, qi * Q:(qi + 1) * Q] = np.asarray(res.results[c]["yq"], dtype=np.float32)
    return yp, ys
```

```python
import math
import numpy as np
import concourse.bass as bass
import concourse.mybir as mybir
from concourse.bass_utils import run_bass_kernel_spmd

F32 = mybir.dt.float32
BF16 = mybir.dt.bfloat16
AF = mybir.ActivationFunctionType
ALU = mybir.AluOpType
AX = mybir.AxisListType

D = 1024
NH = 16
NKV = 4
HD = 64
FF = 2816
NFC = FF // 128
CW = 31
CP = 15
RMS_EPS = 1e-6
LN_EPS = 1e-5
NCORES = 8
import os
ROPE_ENG1 = os.environ.get("ROPE_ENG1", "dve")
DMA_Q2 = os.environ.get("DMA_Q2", "sp")


class Sched:
    def __init__(self, nc):
        self.nc = nc
        self.names = ["pe", "act", "dve", "pool", "sp"]
        self.sem = {n: nc.alloc_semaphore("s_" + n) for n in self.names}
        self.nops = {n: 0 for n in self.names}
        self.seen = {n: {} for n in self.names}
        self.prog = {n: [] for n in self.names}
        self.awaited = {n: set() for n in self.names}
        self.last_write = {}
        self.readers = {}
        self.dma_sems = {}
        self.nins = 0
        self.parent = {}
        self.children = {}

    def set_parent(self, fine, region):
        self.parent[fine] = region
        self.children.setdefault(region, []).append(fine)

    def _rel(self, k):
        out = [k]
        if k in self.parent:
            out.append(self.parent[k])
        out += self.children.get(k, [])
        return out

    def _deps(self, eng, reads, writes):
        deps = {}
        raw_self = [-1]

        def need(t, raw=False):
            src, v = t
            if src == eng:
                if raw_self[0] < v:
                    raw_self[0] = v
                return
            if deps.get(src, -1) < v:
                deps[src] = v

        for k0 in reads:
            for k in self._rel(k0):
                if k in self.last_write:
                    need(self.last_write[k], True)
        for k0 in writes:
            for k in self._rel(k0):
                if k in self.last_write:
                    need(self.last_write[k])
                for t in self.readers.get(k, {}).values():
                    need(t)
        if raw_self[0] >= 0 and eng not in ("pe", "sp"):
            deps[eng] = raw_self[0]
        out = []
        for src, v in deps.items():
            if self.seen[eng].get(src, -1) < v:
                self.seen[eng][src] = v
                out.append((src, v))
                if isinstance(src, str):
                    self.awaited[src].add(v)
        return out

    def _record(self, t, reads, writes):
        for k in reads:
            self.readers.setdefault(k, {})[t[0]] = t
        for k in writes:
            self.last_write[k] = t
            self.readers[k] = {}

    def op(self, eng, method, reads=(), writes=(), signal=True, **kw):
        waits = self._deps(eng, reads, writes)
        idx = self.nops[eng]
        self.nops[eng] = idx + 1
        self.prog[eng].append((waits, method, kw, ("e", idx)))
        self._record((eng, idx), reads, writes)
        self.nins += 1

    def dma(self, eng, out, in_, reads=(), writes=(), sem=None):
        waits = self._deps(eng, reads, writes)
        if sem not in self.dma_sems:
            self.dma_sems[sem] = [self.nc.alloc_semaphore("d_%d" % len(self.dma_sems)), 0]
        ent = self.dma_sems[sem]
        src = ("dma", sem)
        if ent[1] > 0 and self.seen[eng].get(src, -1) < ent[1]:
            self.seen[eng][src] = ent[1]
            waits.append((src, ent[1]))
        ent[1] += 16
        t = (src, ent[1])
        self.prog[eng].append((waits, "dma_start", dict(out=out, in_=in_), ("d", ent[0])))
        self.nops[eng] += 1
        self._record(t, reads, writes)
        self.nins += 1
        return t

    def final_wait(self, eng):
        ws = [(("dma", k), ent[1]) for k, ent in self.dma_sems.items() if ent[1] > 0]
        self.prog[eng].append((ws, None, None, None))

    def emit(self):
        nc = self.nc
        rank = {}
        for e in self.names:
            rank[e] = {idx: i + 1 for i, idx in enumerate(sorted(self.awaited[e]))}
        self.cnt = {e: len(rank[e]) for e in self.names}
        with nc.Block() as block:
            def run(name):
                def body(h):
                    for waits, method, kw, tag in self.prog[name]:
                        for src, v in waits:
                            if isinstance(src, str):
                                h.wait_ge(self.sem[src], rank[src][v])
                            else:
                                h.wait_ge(self.dma_sems[src[1]][0], v)
                        if method is not None:
                            ins = getattr(h, method)(**kw)
                            if tag[0] == "d":
                                ins.then_inc(tag[1], 16)
                            elif tag[1] in rank[name]:
                                ins.then_inc(self.sem[name], 1)
                return body
            block.tensor(run("pe"))
            block.scalar(run("act"))
            block.vector(run("dve"))
            block.gpsimd(run("pool"))
            block.sync(run("sp"))


class StopBuild(Exception):
    pass


class Builder:
    stop_at = None

    def phase(self, name):
        self.phases.append(name)
        if self.stop_at is not None and name == self.stop_at:
            raise StopBuild()

    def __init__(self, Sp, Ss, debug=False):
        self.phases = []
        self.Sp, self.Ss = Sp, Ss
        self.Q = Ss // 4
        self.debug = debug
        self.nc = bass.Bass("TRN2", target_bir_lowering=False)
        self.S = Sched(self.nc)
        self._ring_next = 0
        self._tp_next = 0
        self._ps_rr = 0

    def din(self, name, shape, dt=F32):
        return self.nc.dram_tensor(name, list(shape), dt, kind="ExternalInput").ap()

    def dout(self, name, shape, dt=F32):
        return self.nc.dram_tensor(name, list(shape), dt, kind="ExternalOutput").ap()

    def declare(self):
        Sp, Ss, Q = self.Sp, self.Ss, self.Q
        nc = self.nc
        self.xp = self.din("xp", [Sp, D])
        self.xs = self.din("xs", [Ss, D])
        self.xq = self.din("xq", [Q + 128, D])
        self.w_kv = self.din("w_kv", [8, 128, 512])
        self.w_q = self.din("w_q", [8, 128, 1024])
        self.w_o = self.din("w_o", [8, 128, 1024])
        self.w_g = self.din("w_g", [2, NFC, 128, 1024])
        self.w_u = self.din("w_u", [2, NFC, 128, 1024])
        self.w_d = self.din("w_d", [2, NFC, 128, 1024])
        self.w_ci = self.din("w_ci", [16, 128, 1024])
        self.w_co = self.din("w_co", [8, 128, 1024])
        self.gcols_d = self.din("gcols", [128, 4, 8])
        self.fvec_d = self.din("fvec", [128, 40])
        self.dww_d = self.din("dww", [128, 8, CW])
        self.gq_d = self.din("gq", [1, 64])
        self.gk_d = self.din("gk", [1, 64])
        self.gf_d = self.din("gf", [1, D])
        self.bout_d = self.din("bout", [1, D])
        self.rt_all_d = self.din("rt_all", [128, Ss // 128, 2, 16])
        self.rt_q_d = self.din("rt_q", [128, Q // 128 + 1, 2, 16])
        self.ct_d = self.din("ct", [128, 2, 16])
        self.flags_d = self.din("flags", [128, 2])
        self.yp = self.dout("yp", [Sp, D])
        self.yq = self.dout("yq", [Q, D])
        self.scr = nc.dram_tensor("wscr", [8 + 8 + 8 + 6 * NFC + 16 + 8, 128, 1024], BF16).ap()

    def alloc(self):
        nc = self.nc
        Ss = self.Ss
        A = nc.alloc_sbuf_tensor
        self.nkc_max = Ss // 128
        self.KT = [A("KT%d" % m, [128, Ss], BF16) for m in range(2)]
        self.V = A("Vst", [128, self.nkc_max, 384], BF16)
        self.xsl = [A("x%d" % i, [128, 4, D], F32) for i in range(2)]
        self.h_tm = A("h_tm", [128, 4, D], BF16)
        self.hT = A("hT", [128, 8, 512], BF16)
        self.R = A("R", [128, NFC * 512], BF16)
        self.PT = A("PT", [128, 6, 512], BF16)
        self.U = [A("U%d" % i, [128, 8, 512 + 2 * CP], BF16) for i in range(2)]
        self.UH = A("UH", [128, 8, 128], BF16)
        self.ring = A("ring", [128, 6, 1024], BF16)
        self.stage = A("stage", [128, 1024], F32)
        self.rtg = A("rtg", [128, 2, 4, 2, 16], F32)
        self._rtg_next = 0
        self.ct = A("ct_s", [128, 2, 16], F32)
        self.gf = A("gf_s", [128, D], F32)
        self.gq = A("gq_s", [128, 64], F32)
        self.gk = A("gk_s", [128, 64], F32)
        self.gcols = A("gcols_s", [128, 4, 8], F32)
        self.fvec = A("fvec_s", [128, 40], F32)
        self.dww = A("dww_s", [128, 8, CW], F32)
        self.flags = A("flags_s", [128, 2], F32)
        self.bout_b = A("bout_b", [1, D], BF16)
        self.ones_b = A("ones_b", [1, 128], BF16)
        self.ident = A("ident", [128, 128], BF16)
        self.nhalf = A("nhalf", [128, 16], F32)
        self.st = A("stats", [128, 128], F32)
        self.rec = A("rec", [128, 512], F32)
        self.oraw = A("oraw", [128, 2, 512], F32)
        self.ps = [nc.alloc_psum_tensor("ps%d" % i, [128, 512], F32) for i in range(8)]
        Rf = self.R[:, :]
        self.qT = Rf[:, 0:4096].rearrange("p (c t) -> p c t", c=8)
        self.oT = Rf[:, 4096:8192].rearrange("p (c t) -> p c t", c=8)
        self.hidT = Rf[:, 0:NFC * 512].rearrange("p (c t) -> p c t", c=NFC)
        self.cT = Rf[:, 0:4096].rearrange("p (c t) -> p c t", c=8)
        self.ktm2 = [Rf[:, i * 1024:(i + 1) * 1024].rearrange("p (s c) -> p s c", s=4) for i in range(2)]
        self.qTh = Rf[:, 0:1024].rearrange("p (c t) -> p c t", c=8)
        self.oTh = Rf[:, 4096:5120].rearrange("p (c t) -> p c t", c=8)
        self.Rf = Rf
        hi = Rf[:, 4096:NFC * 512].bitcast(F32)
        self.sq = hi[:, 0:1024]
        self.qg = hi[:, 1024:2048]
        self.rtmp = hi[:, 2048:3584]
        self.diag = [Rf[:, 0:CW * 128].rearrange("p (j c) -> p j c", j=CW),
                     Rf[:, 4096:4096 + CW * 128].rearrange("p (j c) -> p j c", j=CW)]
        self.diag_keys = [["Ra"], ["Rb", "Rc"]]
        self.extra_stage = False
        self.v_stage_on = False
        c0 = self.Sp // 128
        nfree = (self.nkc_max - c0) * 384 // 2
        self.vstage = []
        if nfree >= 1024 and os.environ.get("VSTAGE"):
            vf = self.V[:, c0:, :].rearrange("p a b -> p (a b)").bitcast(F32)
            self.vstage = [(("vst", i), vf[:, i * 1024:(i + 1) * 1024]) for i in range(min(12, nfree // 1024))]
        self._stage_rr = 0
        self._cast_rr = 0
        ptf = self.PT[:, :, :].rearrange("p a b -> p (a b)")
        self.sg = [ptf[:, 0:1024].bitcast(F32), ptf[:, 1024:2048].bitcast(F32)]
        S = self.S
        for ss in range(4):
            for cb in range(2):
                S.set_parent(("h_tmh", ss, cb), ("h_tm", ss))
        for c in range(8):
            S.set_parent(("qT", c), "Ra")
            S.set_parent(("oT", c), "Rb" if c < 4 else "Rc")
        S.set_parent(("ktm", 0), "Ra")
        S.set_parent(("ktm", 1), "Ra")
        S.set_parent("sq", "Rb")
        S.set_parent("qg", "Rc")
        S.set_parent(("qgh", 0), "Rc")
        S.set_parent(("qgh", 1), "Rc")
        for hf in range(2):
            for i in range(3):
                S.set_parent(("rtmp", hf, i), "Rd")
        print("sbuf bytes remaining", nc.sbuf_bytes_remaining)

    def make_chunks(self):
        self.chunks = {}
        idx = [0]

        def add(name, src, gain=None, scale=None, ncols=1024):
            self.chunks[name] = dict(src=src, gain=gain, scale=scale, ncols=ncols, scr=self.scr[idx[0]], done=False)
            idx[0] += 1

        for kc in range(8):
            add(("kv", kc), self.w_kv[kc], gain=("pp", 0, kc), ncols=512)
        for kc in range(8):
            add(("q", kc), self.w_q[kc], gain=("pp", 0, kc))
        for pr in range(8):
            add(("o", pr), self.w_o[pr])
        for l in range(2):
            for fc in range(NFC):
                add(("g", l, fc), self.w_g[l, fc], gain=("kc", 1 if l == 0 else 3))
                add(("u", l, fc), self.w_u[l, fc], gain=("kc", 1 if l == 0 else 3))
                add(("d", l, fc), self.w_d[l, fc])
        for oc in range(16):
            add(("ci", oc), self.w_ci[oc], gain=("kc", 2))
        for cc in range(8):
            add(("co", cc), self.w_co[cc])

    def fetch(self, name):
        S = self.S
        ch = self.chunks[name]
        slot = self._ring_next
        self._ring_next = (slot + 1) % 6
        n = ch["ncols"]
        dst = self.ring[:, slot, 0:n]
        key = ("ring", slot)
        if not ch["done"]:
            ch["done"] = True
            bufs = [("stage", self.stage[:, :])]
            if self.v_stage_on:
                bufs += self.vstage
            if self.extra_stage:
                bufs += [(("x", 1, j), self.xsl[1][:, j, :]) for j in range(4)]
            self._stage_rr = (self._stage_rr + 1) % len(bufs)
            skey, sbuf = bufs[self._stage_rr]
            self._cast_rr ^= 1
            ce = "pool" if self._cast_rr else "dve"
            S.dma("sp", out=sbuf[:, 0:n], in_=ch["src"], writes=[skey], sem=("stg", self._stage_rr))
            g = ch["gain"]
            if g is None:
                S.op(ce, "tensor_copy", reads=[skey], writes=[key], out=dst, in_=sbuf[:, 0:n])
            elif g[0] == "pp":
                S.op(ce, "tensor_scalar", reads=[skey, "gcols"], writes=[key], out=dst, in0=sbuf[:, 0:n],
                     scalar1=self.gcols[:, g[1], g[2]:g[2] + 1], scalar2=None, op0=ALU.mult)
            else:
                gb = self.gcols[:, g[1], :].unsqueeze(2).broadcast_to([128, 8, 128])
                S.op(ce, "tensor_tensor", reads=[skey, "gcols"], writes=[key],
                     out=dst.rearrange("p (k c) -> p k c", k=8), in0=sbuf.rearrange("p (k c) -> p k c", k=8),
                     in1=gb, op=ALU.mult)
            S.dma(DMA_Q2, out=ch["scr"][:, 0:n], in_=dst, reads=[key], writes=[("scr", name)], sem=("scr", slot))
        else:
            S.dma("sp", out=dst, in_=ch["scr"][:, 0:n], reads=[("scr", name)], writes=[key], sem=("ringld", slot))
        return key, dst

    def load_rt(self, src, j0, n):
        i = self._rtg_next
        self._rtg_next = 1 - i
        key = ("rtg", i)
        self.S.dma("sp", out=self.rtg[:, i, 0:n], in_=src[:, j0:j0 + n], writes=[key], sem=key)
        return self.rtg[:, i], key

    def bank(self):
        b = self._ps_rr
        self._ps_rr = (b + 1) % 8
        return b

    def setup(self):
        S = self.S
        ld = [("ct", self.ct, self.ct_d),
              ("gcols", self.gcols, self.gcols_d), ("fvec", self.fvec, self.fvec_d), ("dww", self.dww, self.dww_d),
              ("flags", self.flags, self.flags_d)]
        for i, (k, sb, dr) in enumerate(ld):
            S.dma("sp", out=sb[:], in_=dr, writes=[k], sem=("setup", i % 4))
        S.dma("sp", out=self.gf[:], in_=self.gf_d.partition_broadcast(128), writes=["gf"], sem=("setup", 0))
        S.dma("sp", out=self.gq[:], in_=self.gq_d.partition_broadcast(128), writes=["gq"], sem=("setup", 1))
        S.dma("sp", out=self.gk[:], in_=self.gk_d.partition_broadcast(128), writes=["gk"], sem=("setup", 2))
        S.dma("sp", out=self.stage[0:1, :], in_=self.bout_d, writes=["stage"], sem="stage")
        S.op("pool", "tensor_copy", reads=["stage"], writes=["bout_b"], out=self.bout_b[:], in_=self.stage[0:1, :])
        S.op("pool", "memset", writes=["ones_b"], ap=self.ones_b[:], constant=1.0)
        S.op("pool", "memset", writes=["nhalf"], ap=self.nhalf[:], constant=-0.5)
        identf = self.stage[:, 0:128]
        S.op("pool", "memset", reads=["stage"], writes=["stage"], ap=identf, constant=0.0)
        S.op("pool", "affine_select", reads=["stage"], writes=["stage"], out=identf, in_=identf,
             pattern=[[-1, 128]], compare_op=ALU.not_equal, fill=1.0, base=0, channel_multiplier=1)
        S.op("pool", "tensor_copy", reads=["stage"], writes=["ident"], out=self.ident[:], in_=identf)
        S.op("pool", "tensor_scalar", reads=["gq"], writes=["gq"], out=self.gq[:], in0=self.gq[:], scalar1=1.0 / math.sqrt(HD),
             scalar2=None, op0=ALU.mult)
        c1 = self.Sp // 128 if self.vstage else self.nkc_max
        S.op("pool", "memset", writes=["V"], ap=self.V[:, 0:c1, 64:128], constant=1.0)
        S.op("pool", "memset", writes=["V"], ap=self.V[:, 0:c1, 256:320], constant=1.0)

    def xk(self, slot, s):
        return ("x", slot, s)

    def norm_to_hT(self, xsl, slot, nsub, evac_eng="dve", scale_all_act=False):
        self.norm_scale(xsl, slot, nsub, scale_all_act)
        self.norm_transposes(nsub, evac_eng)

    def norm_scale(self, xsl, slot, nsub, scale_all_act=False):
        S = self.S
        for s in range(nsub):
            S.op("act", "activation", reads=[self.xk(slot, s)], writes=[("h_tm", s), ("ss", s)], out=self.h_tm[:, s, :], in_=xsl[:, s, :],
                 func=AF.Square, accum_out=self.st[:, s:s + 1])
        rr = self.st[:, 8:8 + nsub]
        S.op("pool", "tensor_scalar", reads=[("ss", s) for s in range(nsub)], writes=["rr"], out=rr, in0=self.st[:, 0:nsub],
             scalar1=1.0 / D, scalar2=RMS_EPS, op0=ALU.mult, op1=ALU.add)
        S.op("pool", "tensor_tensor", reads=["rr", "nhalf"], writes=["rr"], out=rr, in0=rr, in1=self.nhalf[:, 0:nsub], op=ALU.pow)
        for s in range(nsub):
            if s % 2 == 0 or scale_all_act:
                S.op("act", "activation", reads=[self.xk(slot, s), "rr"], writes=[("h_tm", s)], out=self.h_tm[:, s, :],
                     in_=xsl[:, s, :], func=AF.Copy, scale=self.st[:, 8 + s:9 + s])
            else:
                S.op("dve", "tensor_scalar", reads=[self.xk(slot, s), "rr"], writes=[("h_tm", s)], out=self.h_tm[:, s, :],
                     in0=xsl[:, s, :], scalar1=self.st[:, 8 + s:9 + s], scalar2=None, op0=ALU.mult)
        self.phase("n_scale")

    def norm_transposes(self, nsub, evac_eng="dve"):
        S = self.S
        for s in range(nsub):
            b = self.bank()
            pk = ("ps", b)
            pv = self.ps[b][:, :].bitcast(BF16)
            for c in range(8):
                S.op("pe", "transpose", reads=[("h_tm", s), "ident"], writes=[pk], out=pv[:, c * 128:(c + 1) * 128],
                     in_=self.h_tm[:, s, c * 128:(c + 1) * 128], identity=self.ident[:])
            dst = self.hT[:, :, s * 128:(s + 1) * 128]
            src = pv.rearrange("p (c t) -> p c t", c=8)
            if evac_eng == "dve":
                S.op("dve", "tensor_copy", reads=[pk], writes=["hT"], out=dst, in_=src)
            else:
                S.op("act", "activation", reads=[pk], writes=["hT"], out=dst, in_=src, func=AF.Copy)
        self.phase("n_hT")

    def transposes(self, srcs, srckeys, nchunks, dst, dstkey, evac_eng="dve", act_kw=None, col0=0):
        S = self.S
        nsub = len(srcs)
        T = nsub * 128
        for c in range(nchunks):
            b = self.bank()
            pk = ("ps", b)
            pv = self.ps[b][:, :].bitcast(BF16)
            for s in range(nsub):
                S.op("pe", "transpose", reads=[srckeys[s], "ident"], writes=[pk], signal=(s == nsub - 1),
                     out=pv[:, s * 128:(s + 1) * 128], in_=srcs[s][:, c * 128:(c + 1) * 128], identity=self.ident[:])
            if act_kw is not None:
                kw = act_kw(c)
                S.op("act", "activation", reads=[pk] + kw.pop("reads", []), writes=[dstkey], out=dst[:, c, col0:col0 + T],
                     in_=pv[:, 0:T], **kw)
            elif evac_eng == "dve":
                S.op("dve", "tensor_copy", reads=[pk], writes=[dstkey], out=dst[:, c, col0:col0 + T], in_=pv[:, 0:T])
            else:
                S.op("act", "activation", reads=[pk], writes=[dstkey], out=dst[:, c, col0:col0 + T], in_=pv[:, 0:T],
                     func=AF.Copy)

    def tm_proj(self, actT, actkey, chunk_names, nsub, ncb, evac, bias=False):
        S = self.S
        banks = [[self.bank() for cb in range(ncb)] for s in range(nsub)]
        nch = len(chunk_names)
        if bias:
            for s in range(nsub):
                for cb in range(ncb):
                    S.op("pe", "matmul", reads=["ones_b", "bout_b"], writes=[("ps", banks[s][cb])], signal=False,
                         out=self.ps[banks[s][cb]][:, :], lhsT=self.ones_b[0:1, :], rhs=self.bout_b[0:1, cb * 512:(cb + 1) * 512],
                         start=True, stop=False)
        for c, name in enumerate(chunk_names):
            wkey, w = self.fetch(name)
            for s in range(nsub):
                for cb in range(ncb):
                    last = (s == nsub - 1 and cb == ncb - 1)
                    ak = actkey(c) if callable(actkey) else (actkey if isinstance(actkey, list) else [actkey])
                    S.op("pe", "matmul", reads=ak + [wkey], writes=[("ps", banks[s][cb])], signal=last,
                         out=self.ps[banks[s][cb]][:, :], lhsT=actT[:, c, s * 128:(s + 1) * 128],
                         rhs=w[:, cb * 512:(cb + 1) * 512], start=(c == 0 and not bias), stop=(c == nch - 1))
        self.phase("tm_mm")
        for s in range(nsub):
            for cb in range(ncb):
                evac(s, cb, banks[s][cb])
                self.phase("tm_evac1")

    def resid_evac(self, xsl, slot):
        def evac(s, cb, b):
            xkey = self.xk(slot, s)
            self.S.op("dve", "tensor_tensor", reads=[("ps", b), xkey], writes=[xkey], out=xsl[:, s, cb * 512:(cb + 1) * 512],
                      in0=self.ps[b][:, :], in1=xsl[:, s, cb * 512:(cb + 1) * 512], op=ALU.add)
        return evac

    def qk_post(self, pviews, pkeys, G, H, gbc, gkey, rt, rtkey, out, outkeys):
        S = self.S
        n = G * H * 64
        sq = self.sq[:, 0:n]
        qg = self.qg[:, 0:n]
        off = 0
        for pv, pk in zip(pviews, pkeys):
            w = pv.shape[1]
            S.op("act", "activation", reads=[pk], writes=["sq"], out=sq[:, off:off + w], in_=pv, func=AF.Square)
            S.op("dve", "tensor_tensor", reads=[pk, gkey, "sq"], writes=["qg"], out=qg[:, off:off + w].rearrange("p (h d) -> p h d", d=64),
                 in0=pv.rearrange("p (h d) -> p h d", d=64), in1=gbc[:, :].unsqueeze(1).broadcast_to([128, w // 64, 64]), op=ALU.mult)
            off += w
        GH = G * H
        ssq = self.st[:, 16:16 + GH]
        rs = self.st[:, 32:32 + GH]
        S.op("dve", "tensor_reduce", reads=["sq"], writes=["ssq"], out=ssq, in_=sq.rearrange("p (h d) -> p h d", d=64),
             axis=AX.X, op=ALU.add)
        S.op("pool", "tensor_scalar", reads=["ssq"], writes=["rs"], out=rs, in0=ssq, scalar1=1.0 / HD, scalar2=RMS_EPS,
             op0=ALU.mult, op1=ALU.add)
        S.op("pool", "tensor_tensor", reads=["rs", "nhalf"], writes=["rs"], out=rs, in0=rs, in1=self.nhalf[:, 0:GH], op=ALU.pow)
        v6 = qg.rearrange("p (g h r f d) -> p g h r f d", g=G, h=H, r=2, f=2, d=16)
        shp = [128, G, H, 16]
        for half, eng in ((0, "dve"), (1, ROPE_ENG1)):
            Aa = v6[:, :, :, half, 0, :]
            Bb = v6[:, :, :, half, 1, :]
            if half == 0:
                C = rt[:, :, 0, :].unsqueeze(2).broadcast_to(shp)
                Sn = rt[:, :, 1, :].unsqueeze(2).broadcast_to(shp)
                tk = [rtkey]
            else:
                C = self.ct[:, 0, :].unsqueeze(1).unsqueeze(1).broadcast_to(shp)
                Sn = self.ct[:, 1, :].unsqueeze(1).unsqueeze(1).broadcast_to(shp)
                tk = ["ct"]
            t = [self.rtmp[:, (half * 3 + i) * 256:(half * 3 + i) * 256 + GH * 16].rearrange("p (g h d) -> p g h d", g=G, h=H)
                 for i in range(3)]
            k = [("rtmp", half, i) for i in range(3)]
            qk = ("qgh", half)
            S.op(eng, "tensor_tensor", reads=["qg", qk] + tk, writes=[k[0]], out=t[0], in0=Aa, in1=C, op=ALU.mult)
            S.op(eng, "tensor_tensor", reads=["qg", qk] + tk, writes=[k[1]], out=t[1], in0=Aa, in1=Sn, op=ALU.mult)
            S.op(eng, "tensor_tensor", reads=["qg", qk] + tk, writes=[k[2]], out=t[2], in0=Bb, in1=Sn, op=ALU.mult)
            S.op(eng, "tensor_tensor", reads=["qg", k[0], k[2]], writes=[qk], out=Aa, in0=t[0], in1=t[2], op=ALU.subtract)
            S.op(eng, "tensor_tensor", reads=["qg", qk] + tk, writes=[k[0]], out=t[0], in0=Bb, in1=C, op=ALU.mult)
            S.op(eng, "tensor_tensor", reads=["qg", k[0], k[1]], writes=[qk], out=Bb, in0=t[0], in1=t[1], op=ALU.add)
        S.op("dve", "tensor_tensor", reads=["qg", ("qgh", 0), ("qgh", 1), "rs"], writes=list(outkeys) + ["qg"],
             out=out.rearrange("p g (h d) -> p g h d", d=64), in0=qg.rearrange("p (g h d) -> p g h d", g=G, h=H),
             in1=rs.rearrange("p (g h) -> p g h", g=G).unsqueeze(3).broadcast_to([128, G, H, 64]), op=ALU.mult)

    def k_transposes(self, g, nsub):
        S = self.S
        T = nsub * 128
        kt = self.ktm2[g % 2]
        kk = ("ktm", g % 2)
        for m in range(2):
            b = self.bank()
            pk = ("ps", b)
            pv = self.ps[b][:, :].bitcast(BF16)
            for s in range(nsub):
                S.op("pe", "transpose", reads=[kk, "ident"], writes=[pk],
                     out=pv[:, s * 128:(s + 1) * 128], in_=kt[:, s, m * 128:(m + 1) * 128], identity=self.ident[:])
            S.op("dve", "tensor_copy", reads=[pk], writes=["KT"], out=self.KT[m][:, g * 512:g * 512 + T], in_=pv[:, 0:T])

    def stage1(self, xsrc, S_len):
        S = self.S
        ngrp = (S_len + 511) // 512
        pending = None

        def load_x(g):
            nsub_ = min(4, (S_len - g * 512) // 128)
            for s in range(nsub_):
                S.dma("sp", out=self.xsl[g % 2][:, s, :], in_=xsrc[g * 512 + s * 128:g * 512 + (s + 1) * 128, :],
                      writes=[self.xk(g % 2, s)], sem=("xld", g % 2, s))

        load_x(0)
        self.norm_scale(self.xsl[0], 0, min(4, S_len // 128), True)
        for g in range(ngrp):
            nsub = min(4, (S_len - g * 512) // 128)
            slot = g % 2
            xsl = self.xsl[slot]
            rtv, rtk = self.load_rt(self.rt_all_d, g * 4, nsub)
            if g + 1 < ngrp:
                load_x(g + 1)
            self.norm_transposes(nsub, "act")
            kbanks = []
            self.tm_proj(self.hT, "hT", [("kv", kc) for kc in range(8)], nsub, 1, lambda s, cb, b, kb=kbanks: kb.append(b))
            if pending is not None:
                self.k_transposes(*pending)
            if g + 1 < ngrp:
                self.norm_scale(self.xsl[(g + 1) % 2], (g + 1) % 2, min(4, (S_len - (g + 1) * 512) // 128), True)
            for s, b in enumerate(kbanks):
                pv = self.ps[b]
                pk = ("ps", b)
                j = g * 4 + s
                for (dst0, src0) in ((0, 256), (128, 320), (192, 448), (320, 384)):
                    S.op("act", "activation", reads=[pk], writes=["V"], out=self.V[:, j, dst0:dst0 + 64], in_=pv[:, src0:src0 + 64],
                         func=AF.Copy)
            self.qk_post([self.ps[bb][:, 0:256] for bb in kbanks], [("ps", bb) for bb in kbanks], nsub, 4, self.gk, "gk",
                         rtv[:, 0:nsub], rtk, self.ktm2[g % 2][:, 0:nsub, :], [("ktm", g % 2)])
            pending = (g, nsub)
            self.phase("s1")
        self.k_transposes(*pending)

    def ffn(self, l, xsl, slot, nsub):
        S = self.S
        T = nsub * 128
        self.norm_to_hT(xsl, slot, nsub)
        for fc in range(NFC):
            kg, wg = self.fetch(("g", l, fc))
            ku, wu = self.fetch(("u", l, fc))
            bg, bu = self.bank(), self.bank()
            for (kk, w, b) in ((kg, wg, bg), (ku, wu, bu)):
                for kc in range(8):
                    S.op("pe", "matmul", reads=["hT", kk], writes=[("ps", b)], signal=(kc == 7), out=self.ps[b][:, 0:T],
                         lhsT=w[:, kc * 128:(kc + 1) * 128], rhs=self.hT[:, kc, 0:T], start=(kc == 0), stop=(kc == 7))
            sgi = fc % 2
            S.op("act", "activation", reads=[("ps", bg)], writes=[("PT", 2 * sgi), ("PT", 2 * sgi + 1)], out=self.sg[sgi][:, 0:T], in_=self.ps[bg][:, 0:T],
                 func=AF.Silu)
            S.op("dve", "tensor_tensor", reads=[("ps", bu), ("PT", 2 * sgi), ("PT", 2 * sgi + 1)], writes=[self.hkey(fc)], out=self.hidT[:, fc, 0:T],
                 in0=self.ps[bu][:, 0:T], in1=self.sg[sgi][:, 0:T], op=ALU.mult)
        self.tm_proj(self.hidT, ["Ra", "Rb", "Rc", "Rd"], [("d", l, fc) for fc in range(NFC)], nsub, 2, self.resid_evac(xsl, slot))

    def attention(self, nsub, nkc, hook=None):
        S = self.S
        T = nsub * 128
        sbanks = [(0, 1), (2, 3), (4, 5)]
        oa, ob = 6, 7
        if nsub == 1:
            N = 512
            units = []
            for m in range(2):
                qa = self.Rf[0:64, m * 512:(m + 1) * 512]
                qb = self.Rf[64:128, m * 512:(m + 1) * 512]
                units.append((m, qa, qb, lambda lo, hi, m=m: self.Rf[lo:hi, 4096 + m * 512:4096 + (m + 1) * 512], "Rb"))
        else:
            N = T
            units = []
            for pr in range(8):
                units.append((pr // 4, self.qT[0:64, pr, 0:T], self.qT[64:128, pr, 0:T],
                              lambda lo, hi, pr=pr: self.oT[lo:hi, pr, 0:T], ("oT", pr)))
        items = [(u, c) for u in range(len(units)) for c in range(nkc)]

        def vviews(m, c):
            if m == 0:
                return self.V[:, c, 0:128], self.V[:, c, 64:192]
            return self.V[:, c, 256:384], self.V[:, c, 192:320]

        def s_mm(k):
            u, c = items[k]
            m, qa, qb, _, _ = units[u]
            ba, bb = sbanks[k % 3]
            qk = "Ra" if nsub == 1 else ("qT", u)
            if hook is not None and c == 0:
                hook(u, (ba, bb))
            S.op("pe", "matmul", reads=["KT", qk], writes=[("ps", ba)], out=self.ps[ba][:, 0:N],
                 lhsT=self.KT[m][0:64, c * 128:(c + 1) * 128], rhs=qa, start=True, stop=True, tile_position=(0, 0))
            S.op("pe", "matmul", reads=["KT", qk], writes=[("ps", bb)], out=self.ps[bb][:, 0:N],
                 lhsT=self.KT[m][64:128, c * 128:(c + 1) * 128], rhs=qb, start=True, stop=True, tile_position=(64, 0))

        def finish(u):
            m, _, _, oview, ok = units[u]
            a_o, b_o = (0, 64) if m == 0 else (64, 0)
            S.op("dve", "tensor_copy", reads=[("ps", oa)], writes=[("oraw", 0)], out=self.oraw[:, 0, 0:N], in_=self.ps[oa][:, 0:N])
            S.op("dve", "tensor_copy", reads=[("ps", ob)], writes=[("oraw", 1)], out=self.oraw[:, 1, 0:N], in_=self.ps[ob][:, 0:N])
            for h, o_off in enumerate((a_o, b_o)):
                s_off = 64 - o_off
                rk = ("rec", o_off)
                S.op("dve", "reciprocal", reads=[("oraw", h)], writes=[rk], out=self.rec[o_off:o_off + 64, 0:N],
                     in_=self.oraw[s_off:s_off + 64, h, 0:N])
                S.op("dve", "tensor_tensor", reads=[("oraw", h), rk], writes=[ok], out=oview(o_off, o_off + 64),
                     in0=self.oraw[o_off:o_off + 64, h, 0:N], in1=self.rec[o_off:o_off + 64, 0:N], op=ALU.mult)

        def pv_mm(k):
            u, c = items[k]
            m = units[u][0]
            ba, bb = sbanks[k % 3]
            pa = (k % 3) * 2
            va, vb = vviews(m, c)
            S.op("act", "activation", reads=[("ps", ba)], writes=[("PT", pa)], out=self.PT[:, pa, 0:N], in_=self.ps[ba][:, 0:N],
                 func=AF.Exp)
            S.op("act", "activation", reads=[("ps", bb)], writes=[("PT", pa + 1)], out=self.PT[:, pa + 1, 0:N],
                 in_=self.ps[bb][:, 0:N], func=AF.Exp)
            S.op("pe", "matmul", reads=["V", ("PT", pa)], writes=[("ps", oa)], out=self.ps[oa][:, 0:N],
                 lhsT=va, rhs=self.PT[:, pa, 0:N], start=(c == 0), stop=(c == nkc - 1))
            S.op("pe", "matmul", reads=["V", ("PT", pa + 1)], writes=[("ps", ob)], out=self.ps[ob][:, 0:N],
                 lhsT=vb, rhs=self.PT[:, pa + 1, 0:N], start=(c == 0), stop=(c == nkc - 1))
            if c == nkc - 1:
                finish(u)

        n = len(items)
        for k in range(min(2, n)):
            s_mm(k)
        for k in range(n):
            if k + 2 < n:
                s_mm(k + 2)
            pv_mm(k)

    def bank_t(self):
        b = self._t_rr % 6
        self._t_rr += 1
        return b

    def okey(self, pr):
        return "Rb" if pr < 4 else "Rc"

    def hkey(self, fc):
        return "Ra" if fc < 8 else ("Rb" if fc < 12 else ("Rc" if fc < 16 else "Rd"))

    def bank_o(self, pr):
        return (6, 7)

    def S_a(self, xsrc_ap, slot, nsub, nkc, rt_tab, rt_key, rt_j0, u_dst, u_key, u_col0):
        S = self.S
        T = nsub * 128
        xsl = self.xsl[slot]
        for s in range(nsub):
            S.dma("sp", out=xsl[:, s, :], in_=xsrc_ap[s * 128:(s + 1) * 128, :], writes=[self.xk(slot, s)], sem=("xld", slot, s))
        rtv, rtk = self.load_rt(rt_tab, rt_j0, nsub)
        self.norm_to_hT(xsl, slot, nsub)

        qb = {}
        self.tm_proj(self.hT, "hT", [("q", kc) for kc in range(8)], nsub, 2, lambda s_, cb, b: qb.__setitem__((s_, cb), b))

        def q_half(cb):
            for s_ in range(nsub):
                b = qb[(s_, cb)]
                self.qk_post([self.ps[b][:, :]], [("ps", b)], 1, 8, self.gq, "gq", rtv[:, s_:s_ + 1], rtk,
                             self.h_tm[:, s_:s_ + 1, cb * 512:(cb + 1) * 512], [("h_tmh", s_, cb)])

        def q_transposes(cb, free_banks=None):
            qdst = self.qTh if nsub == 1 else self.qT
            for c in range(4 * cb, 4 * cb + 4):
                if free_banks is None:
                    b = qb[((c - 4 * cb) % nsub, cb)]
                else:
                    b = free_banks[c % len(free_banks)]
                pk = ("ps", b)
                pv = self.ps[b][:, :].bitcast(BF16)
                for s_ in range(nsub):
                    S.op("pe", "transpose", reads=[("h_tmh", s_, cb), "ident"], writes=[pk], out=pv[:, s_ * 128:(s_ + 1) * 128],
                         in_=self.h_tm[:, s_, c * 128:(c + 1) * 128], identity=self.ident[:])
                S.op("dve", "tensor_copy", reads=[pk], writes=[("qT", c)], out=qdst[:, c, 0:T], in_=pv[:, 0:T])

        self._t_rr = 0
        q_half(0)
        q_transposes(0)
        q_half(1)
        self.phase("q")
        self._ps_rr = 0
        if nsub == 1:
            q_transposes(1)
            self.attention(nsub, nkc)
        else:
            self.attention(nsub, nkc, hook=lambda u, fb: q_transposes(1, fb) if u == 4 else None)
        self.phase("att")
        self.tm_proj(self.oTh if nsub == 1 else self.oT, ["Rb", "Rc"] if nsub == 1 else (lambda c: [("oT", c)]),
                     [("o", pr) for pr in range(8)], nsub, 2, self.resid_evac(xsl, slot))
        self.phase("wo")
        self.ffn(0, xsl, slot, nsub)
        self.phase("ffn0")
        self.norm_to_hT(xsl, slot, nsub)
        for cc in range(8):
            ka, wa = self.fetch(("ci", cc))
            kg, wg = self.fetch(("ci", 8 + cc))
            ba, bg = self.bank(), self.bank()
            for (kk, w, b) in ((ka, wa, ba), (kg, wg, bg)):
                for kc in range(8):
                    S.op("pe", "matmul", reads=["hT", kk], writes=[("ps", b)], signal=(kc == 7), out=self.ps[b][:, 0:T],
                         lhsT=w[:, kc * 128:(kc + 1) * 128], rhs=self.hT[:, kc, 0:T], start=(kc == 0), stop=(kc == 7))
            sgi = cc % 2
            S.op("act", "activation", reads=[("ps", bg), "fvec"], writes=[("PT", 2 * sgi), ("PT", 2 * sgi + 1)], out=self.sg[sgi][:, 0:T],
                 in_=self.ps[bg][:, 0:T], func=AF.Sigmoid, bias=self.fvec[:, 8 + cc:9 + cc])
            S.op("dve", "scalar_tensor_tensor", reads=[("ps", ba), ("PT", 2 * sgi), ("PT", 2 * sgi + 1), "fvec"], writes=[u_key],
                 out=u_dst[:, cc, u_col0:u_col0 + T], in0=self.ps[ba][:, 0:T], scalar=self.fvec[:, cc:cc + 1],
                 in1=self.sg[sgi][:, 0:T], op0=ALU.add, op1=ALU.mult)

    def S_b(self, slot, uslot, nsub, ydst_ap):
        S = self.S
        T = nsub * 128
        xsl = self.xsl[slot]
        U = self.U[uslot]
        ukey = ("U", uslot)
        for cc in range(8):
            dg = self.diag[cc % 2]
            dk = self.diag_keys[cc % 2]
            S.op("dve", "tensor_tensor", reads=["ident", "dww"], writes=dk, out=dg,
                 in0=self.ident[:, :].unsqueeze(1).broadcast_to([128, CW, 128]),
                 in1=self.dww[:, cc, :].unsqueeze(2).broadcast_to([128, CW, 128]), op=ALU.mult)
            b = self.bank()
            for j in range(CW):
                S.op("pe", "matmul", reads=dk + [ukey], writes=[("ps", b)], out=self.ps[b][:, 0:T],
                     lhsT=dg[:, j, :], rhs=U[:, cc, j:j + T], start=(j == 0), stop=(j == CW - 1))
            S.op("act", "activation", reads=[("ps", b), "fvec"], writes=["hT"], out=self.hT[:, cc, 0:T], in_=self.ps[b][:, 0:T],
                 func=AF.Identity, bias=self.fvec[:, 16 + cc:17 + cc])
        self.phase("conv")
        lnb = []
        for s in range(nsub):
            b = self.bank()
            pk = ("ps", b)
            pv = self.ps[b][:, :].bitcast(BF16)
            for cc in range(8):
                S.op("pe", "transpose", reads=["hT", "ident"], writes=[pk], out=pv[:, cc * 128:(cc + 1) * 128],
                     in_=self.hT[:, cc, s * 128:(s + 1) * 128], identity=self.ident[:])
            S.op("act", "activation", reads=[pk], writes=[("h_tm", s), ("lns", s)], out=self.h_tm[:, s, :], in_=pv, func=AF.Identity,
                 accum_out=self.st[:, 48 + s:49 + s])
            S.op("act", "activation", reads=[pk], writes=[("h_tm", s), ("lns", s)], out=self.h_tm[:, s, :], in_=pv, func=AF.Square,
                 accum_out=self.st[:, 52 + s:53 + s])
            lnb.append((pk, pv))
        lk = [("lns", s) for s in range(nsub)]
        mean = self.st[:, 56:56 + nsub]
        var = self.st[:, 60:60 + nsub]
        msq = self.st[:, 64:64 + nsub]
        S.op("dve", "tensor_scalar", reads=lk, writes=["lnm"], out=mean, in0=self.st[:, 48:48 + nsub], scalar1=1.0 / D,
             scalar2=None, op0=ALU.mult)
        S.op("dve", "tensor_tensor", reads=["lnm"], writes=["lnq"], out=msq, in0=mean, in1=mean, op=ALU.mult)
        S.op("dve", "scalar_tensor_tensor", reads=lk + ["lnq"], writes=["lnv"], out=var, in0=self.st[:, 52:52 + nsub], scalar=1.0 / D,
             in1=msq, op0=ALU.mult, op1=ALU.subtract)
        S.op("pool", "tensor_scalar", reads=["lnv"], writes=["lnv"], out=var, in0=var, scalar1=LN_EPS, scalar2=None, op0=ALU.add)
        S.op("pool", "tensor_tensor", reads=["lnv", "nhalf"], writes=["lnv"], out=var, in0=var, in1=self.nhalf[:, 0:nsub], op=ALU.pow)
        for s in range(nsub):
            pk, pv = lnb[s]
            S.op("dve", "tensor_scalar", reads=[pk, "lnm", "lnv"], writes=[("h_tm", s)], out=self.h_tm[:, s, :], in0=pv,
                 scalar1=self.st[:, 56 + s:57 + s], scalar2=self.st[:, 60 + s:61 + s], op0=ALU.subtract, op1=ALU.mult)
        self.transposes([self.h_tm[:, s, :] for s in range(nsub)], [("h_tm", s) for s in range(nsub)], 8, self.hT, "hT",
                        act_kw=lambda c: dict(func=AF.Silu, scale=self.fvec[:, 24 + c:25 + c], bias=self.fvec[:, 32 + c:33 + c],
                                              reads=["fvec"]))
        self.phase("ln")
        self.tm_proj(self.hT, "hT", [("co", cc) for cc in range(8)], nsub, 2, self.resid_evac(xsl, slot), bias=True)
        self.phase("co")
        self.ffn(1, xsl, slot, nsub)
        self.phase("ffn1")
        for s in range(nsub):
            S.op("act", "activation", reads=[self.xk(slot, s)], writes=[("h_tm", s), ("fs", s)], out=self.h_tm[:, s, :], in_=xsl[:, s, :],
                 func=AF.Square, accum_out=self.st[:, 68 + s:69 + s])
        fr = self.st[:, 72:72 + nsub]
        S.op("pool", "tensor_scalar", reads=[("fs", s) for s in range(nsub)], writes=["fr"], out=fr, in0=self.st[:, 68:68 + nsub],
             scalar1=1.0 / D, scalar2=RMS_EPS, op0=ALU.mult, op1=ALU.add)
        S.op("pool", "tensor_tensor", reads=["fr", "nhalf"], writes=["fr"], out=fr, in0=fr, in1=self.nhalf[:, 0:nsub], op=ALU.pow)
        for s in range(nsub):
            xkey = self.xk(slot, s)
            S.op("dve", "scalar_tensor_tensor", reads=[xkey, "fr", "gf"], writes=[xkey], out=xsl[:, s, :], in0=xsl[:, s, :],
                 scalar=self.st[:, 72 + s:73 + s], in1=self.gf[:], op0=ALU.mult, op1=ALU.mult)
            S.dma("sp", out=ydst_ap[s * 128:(s + 1) * 128, :], in_=xsl[:, s, :], reads=[xkey], sem=("yst", slot, s))
        self.phase("fin")

    def run_pass(self, xkv, S_len, xown, n_own, rt_tab, rt_key, ydst, has_halo):
        S = self.S
        nkc = S_len // 128
        self.stage1(xkv, S_len)
        ntile = n_own // 512
        if has_halo:
            self.S_a(xown[n_own:n_own + 128, :], 0, 1, nkc, rt_tab, rt_key, n_own // 128, self.UH, "UH", 0)
            S.op("pool", "tensor_scalar", reads=["UH", "flags"], writes=[("U", 0)], out=self.U[0][:, :, 0:CP],
                 in0=self.UH[:, :, 64 - CP:64], scalar1=self.flags[:, 0:1], scalar2=None, op0=ALU.mult)
        else:
            S.op("pool", "memset", writes=[("U", 0)], ap=self.U[0][:, :, 0:CP], constant=0.0)
        for i in range(ntile + 1):
            if i < ntile:
                us = i % 2
                self.extra_stage = (i == 0 and not has_halo)
                self.S_a(xown[i * 512:(i + 1) * 512, :], i % 2, 4, nkc, rt_tab, rt_key, i * 4, self.U[us], ("U", us), CP)
                self.extra_stage = False
                if i > 0:
                    S.op("pool", "tensor_copy", reads=[("U", 1 - us)], writes=[("U", us)], out=self.U[us][:, :, 0:CP],
                         in_=self.U[1 - us][:, :, 512:512 + CP])
                    S.op("pool", "tensor_copy", reads=[("U", us)], writes=[("U", 1 - us)], out=self.U[1 - us][:, :, 512 + CP:512 + 2 * CP],
                         in_=self.U[us][:, :, CP:2 * CP])
            if i == ntile:
                us = (ntile - 1) % 2
                if has_halo:
                    S.op("pool", "tensor_scalar", reads=["UH", "flags"], writes=[("U", us)], out=self.U[us][:, :, 512 + CP:512 + 2 * CP],
                         in0=self.UH[:, :, 64:64 + CP], scalar1=self.flags[:, 1:2], scalar2=None, op0=ALU.mult)
                else:
                    S.op("pool", "memset", writes=[("U", us)], ap=self.U[us][:, :, 512 + CP:512 + 2 * CP], constant=0.0)
            if i >= 1:
                j = i - 1
                self.S_b(j % 2, j % 2, 4, ydst[j * 512:(j + 1) * 512, :])

    def build(self):
        self.declare()
        self.alloc()
        self.make_chunks()
        self.setup()
        try:
            self.phase("setup")
            self.v_stage_on = True
            self.run_pass(self.xp, self.Sp, self.xp, self.Sp, self.rt_all_d, "rt", self.yp, False)
            self.v_stage_on = False
            if self.vstage:
                vk = ["V"] + [k for k, _ in self.vstage]
                c0 = self.Sp // 128
                self.S.op("pool", "memset", writes=vk, ap=self.V[:, c0:, 64:128], constant=1.0)
                self.S.op("pool", "memset", writes=vk, ap=self.V[:, c0:, 256:320], constant=1.0)
            self.phase("passP")
            self.run_pass(self.xs, self.Ss, self.xq, self.Q, self.rt_q_d, "rt", self.yq, True)
        except StopBuild:
            print("STOPPED at", self.stop_at)
        self.S.final_wait("pool")
        self.S.final_wait("sp")
        self.S.emit()
        print("instructions", self.S.nins, dict(self.S.nops), "signals", dict(self.S.cnt))
        return self.nc


def _rope_tabs(pos_rows):
    inv = (10000.0 ** (-np.arange(0, 32, 2, dtype=np.float64) / 32.0))
    ang = pos_rows[..., None].astype(np.float64) * inv
    return np.cos(ang).astype(np.float32), np.sin(ang).astype(np.float32)


def host_layout(inp, Sp, Ss):
    Q = Ss // 4
    f = lambda a: np.ascontiguousarray(np.asarray(a, dtype=np.float32))
    wqkv = f(inp["w_qkv"])[0]
    heads = [8 * m + 4 * hh + i for m in range(2) for i in range(4) for hh in range(2)]
    heads_o = []
    for m in range(2):
        for i in range(4):
            a, b = 8 * m + i, 8 * m + 4 + i
            heads_o += [a, b] if m == 0 else [b, a]
    qcols = np.concatenate([np.arange(h * 64, (h + 1) * 64) for h in heads])
    orows = np.concatenate([np.arange(h * 64, (h + 1) * 64) for h in heads_o])
    shared = {}
    shared["w_q"] = f(wqkv[:, :1024][:, qcols].reshape(8, 128, 1024))
    shared["w_kv"] = f(wqkv[:, 1024:].reshape(8, 128, 512))
    shared["w_o"] = f(f(inp["w_o"])[0][orows].reshape(8, 128, 1024))

    def up(w):
        Fo = w.shape[1]
        return f(w.reshape(8, 128, Fo // 128, 128).transpose(2, 1, 0, 3).reshape(Fo // 128, 128, 1024))
    shared["w_g"] = f(np.stack([up(f(inp["w_gate"])[l]) for l in range(2)]))
    shared["w_u"] = f(np.stack([up(f(inp["w_up"])[l]) for l in range(2)]))
    shared["w_d"] = f(f(inp["w_down"]).reshape(2, NFC, 128, 1024))
    shared["w_ci"] = up(f(inp["conv_w_in"])[0])
    shared["w_co"] = f(f(inp["conv_w_out"])[0].reshape(8, 128, 1024))
    col = lambda v: f(v).reshape(-1, 128).T
    gcols = np.stack([col(f(inp["attn_norm_g"])[0]), col(f(inp["ffn_norm_g"])[0]), col(f(inp["conv_norm_g"])[0]),
                      col(f(inp["ffn_norm_g"])[1])], axis=1)
    shared["gcols"] = f(gcols)
    shared["fvec"] = f(np.concatenate([col(f(inp["conv_b_in"])[0]), col(f(inp["dw_b"])[0]), col(f(inp["conv_ln_g"])[0]),
                                       col(f(inp["conv_ln_b"])[0])], axis=1))
    shared["dww"] = f(f(inp["dw_w"])[0].T.reshape(8, 128, CW).transpose(1, 0, 2))
    shared["gq"] = f(inp["q_norm_g"]).reshape(1, 64)
    shared["gk"] = f(inp["k_norm_g"]).reshape(1, 64)
    shared["gf"] = f(inp["final_norm_g"]).reshape(1, D)
    shared["bout"] = f(inp["conv_b_out"]).reshape(1, D)
    p = np.arange(128)
    j = np.arange(Ss // 128)
    rows = 2 * j[None, :] + (p[:, None] >= 64)
    c, s = _rope_tabs(rows)
    shared["rt_all"] = f(np.stack([c, s], axis=2))
    c, s = _rope_tabs(p % 64)
    shared["ct"] = f(np.stack([c, s], axis=1))
    xp = f(inp["x_prompt"])
    xs = f(inp["x_sample"])
    maps = []
    for core in range(NCORES):
        sb, qi = core // 4, core % 4
        qs = qi * Q
        m = dict(shared)
        m["xp"] = xp[core]
        m["xs"] = xs[sb]
        halo = np.zeros((128, D), np.float32)
        if qs > 0:
            halo[0:64] = xs[sb, qs - 64:qs]
        if qs + Q < Ss:
            halo[64:128] = xs[sb, qs + Q:qs + Q + 64]
        m["xq"] = f(np.concatenate([xs[sb, qs:qs + Q], halo], axis=0))
        jq = np.arange(Q // 128)
        rows = np.concatenate([2 * (jq[None, :] + qs // 128) + (p[:, None] >= 64),
                               np.where(p[:, None] >= 64, (qs + Q) // 64, qs // 64 - 1)], axis=1)
        c, s = _rope_tabs(rows)
        m["rt_q"] = f(np.stack([c, s], axis=2))
        fl = np.zeros((128, 2), np.float32)
        fl[:, 0] = 1.0 if qs > 0 else 0.0
        fl[:, 1] = 1.0 if qs + Q < Ss else 0.0
        m["flags"] = fl
        maps.append(m)
    return maps


_CACHE = {}


def kernel(**inputs):
    xp = np.asarray(inputs["x_prompt"])
    xs = np.asarray(inputs["x_sample"])
    Sp, Ss = xp.shape[1], xs.shape[1]
    Q = Ss // 4
    key = (Sp, Ss)
    if key not in _CACHE:
        _CACHE[key] = Builder(Sp, Ss).build()
    nc = _CACHE[key]
    maps = host_layout(inputs, Sp, Ss)
    res = run_bass_kernel_spmd(nc, maps, core_ids=list(range(NCORES)))
    yp = np.stack([np.asarray(res.results[c]["yp"], dtype=np.float32) for c in range(NCORES)], axis=0)
    ys = np.zeros((2, Ss, D), np.float32)
    for c in range(NCORES):
        sb, qi = c // 4, c % 4
        ys[sb, qi * Q:(qi + 1) * Q] = np.asarray(res.results[c]["yq"], dtype=np.float32)
    return yp, ys
```

```python
import math
import numpy as np
import concourse.bass as bass
import concourse.mybir as mybir
from concourse.bass_utils import run_bass_kernel_spmd

F32 = mybir.dt.float32
BF16 = mybir.dt.bfloat16
AF = mybir.ActivationFunctionType
ALU = mybir.AluOpType
AX = mybir.AxisListType

D = 1024
NH = 16
NKV = 4
HD = 64
FF = 2816
NFC = FF // 128
CW = 31
CP = 15
RMS_EPS = 1e-6
LN_EPS = 1e-5
NCORES = 8
import os
ROPE_ENG1 = os.environ.get("ROPE_ENG1", "dve")
DMA_Q2 = os.environ.get("DMA_Q2", "sp")


class Sched:
    def __init__(self, nc):
        self.nc = nc
        self.names = ["pe", "act", "dve", "pool", "sp"]
        self.sem = {n: nc.alloc_semaphore("s_" + n) for n in self.names}
        self.nops = {n: 0 for n in self.names}
        self.seen = {n: {} for n in self.names}
        self.prog = {n: [] for n in self.names}
        self.awaited = {n: set() for n in self.names}
        self.last_write = {}
        self.readers = {}
        self.dma_sems = {}
        self.nins = 0
        self.parent = {}
        self.children = {}

    def set_parent(self, fine, region):
        self.parent[fine] = region
        self.children.setdefault(region, []).append(fine)

    def _rel(self, k):
        out = [k]
        if k in self.parent:
            out.append(self.parent[k])
        out += self.children.get(k, [])
        return out

    def _deps(self, eng, reads, writes):
        deps = {}
        raw_self = [-1]

        def need(t, raw=False):
            src, v = t
            if src == eng:
                if raw_self[0] < v:
                    raw_self[0] = v
                return
            if deps.get(src, -1) < v:
                deps[src] = v

        for k0 in reads:
            for k in self._rel(k0):
                if k in self.last_write:
                    need(self.last_write[k], True)
        for k0 in writes:
            for k in self._rel(k0):
                if k in self.last_write:
                    need(self.last_write[k])
                for t in self.readers.get(k, {}).values():
                    need(t)
        if raw_self[0] >= 0 and eng not in ("pe", "sp"):
            deps[eng] = raw_self[0]
        out = []
        for src, v in deps.items():
            if self.seen[eng].get(src, -1) < v:
                self.seen[eng][src] = v
                out.append((src, v))
                if isinstance(src, str):
                    self.awaited[src].add(v)
        return out

    def _record(self, t, reads, writes):
        for k in reads:
            self.readers.setdefault(k, {})[t[0]] = t
        for k in writes:
            self.last_write[k] = t
            self.readers[k] = {}

    def op(self, eng, method, reads=(), writes=(), signal=True, **kw):
        waits = self._deps(eng, reads, writes)
        idx = self.nops[eng]
        self.nops[eng] = idx + 1
        self.prog[eng].append((waits, method, kw, ("e", idx)))
        self._record((eng, idx), reads, writes)
        self.nins += 1

    def dma(self, eng, out, in_, reads=(), writes=(), sem=None):
        waits = self._deps(eng, reads, writes)
        if sem not in self.dma_sems:
            self.dma_sems[sem] = [self.nc.alloc_semaphore("d_%d" % len(self.dma_sems)), 0]
        ent = self.dma_sems[sem]
        src = ("dma", sem)
        if ent[1] > 0 and self.seen[eng].get(src, -1) < ent[1]:
            self.seen[eng][src] = ent[1]
            waits.append((src, ent[1]))
        ent[1] += 16
        t = (src, ent[1])
        self.prog[eng].append((waits, "dma_start", dict(out=out, in_=in_), ("d", ent[0])))
        self.nops[eng] += 1
        self._record(t, reads, writes)
        self.nins += 1
        return t

    def final_wait(self, eng):
        ws = [(("dma", k), ent[1]) for k, ent in self.dma_sems.items() if ent[1] > 0]
        self.prog[eng].append((ws, None, None, None))

    def emit(self):
        nc = self.nc
        rank = {}
        for e in self.names:
            rank[e] = {idx: i + 1 for i, idx in enumerate(sorted(self.awaited[e]))}
        self.cnt = {e: len(rank[e]) for e in self.names}
        with nc.Block() as block:
            def run(name):
                def body(h):
                    for waits, method, kw, tag in self.prog[name]:
                        for src, v in waits:
                            if isinstance(src, str):
                                h.wait_ge(self.sem[src], rank[src][v])
                            else:
                                h.wait_ge(self.dma_sems[src[1]][0], v)
                        if method is not None:
                            ins = getattr(h, method)(**kw)
                            if tag[0] == "d":
                                ins.then_inc(tag[1], 16)
                            elif tag[1] in rank[name]:
                                ins.then_inc(self.sem[name], 1)
                return body
            block.tensor(run("pe"))
            block.scalar(run("act"))
            block.vector(run("dve"))
            block.gpsimd(run("pool"))
            block.sync(run("sp"))


class StopBuild(Exception):
    pass


class Builder:
    stop_at = None

    def phase(self, name):
        self.phases.append(name)
        if self.stop_at is not None and name == self.stop_at:
            raise StopBuild()

    def __init__(self, Sp, Ss, debug=False):
        self.phases = []
        self.Sp, self.Ss = Sp, Ss
        self.Q = Ss // 4
        self.debug = debug
        self.nc = bass.Bass("TRN2", target_bir_lowering=False)
        self.S = Sched(self.nc)
        self._ring_next = 0
        self._tp_next = 0
        self._ps_rr = 0

    def din(self, name, shape, dt=F32):
        return self.nc.dram_tensor(name, list(shape), dt, kind="ExternalInput").ap()

    def dout(self, name, shape, dt=F32):
        return self.nc.dram_tensor(name, list(shape), dt, kind="ExternalOutput").ap()

    def declare(self):
        Sp, Ss, Q = self.Sp, self.Ss, self.Q
        nc = self.nc
        self.xp = self.din("xp", [Sp, D])
        self.xs = self.din("xs", [Ss, D])
        self.xq = self.din("xq", [Q + 128, D])
        self.w_kv = self.din("w_kv", [8, 128, 512])
        self.w_q = self.din("w_q", [8, 128, 1024])
        self.w_o = self.din("w_o", [8, 128, 1024])
        self.w_g = self.din("w_g", [2, NFC, 128, 1024])
        self.w_u = self.din("w_u", [2, NFC, 128, 1024])
        self.w_d = self.din("w_d", [2, NFC, 128, 1024])
        self.w_ci = self.din("w_ci", [16, 128, 1024])
        self.w_co = self.din("w_co", [8, 128, 1024])
        self.gcols_d = self.din("gcols", [128, 4, 8])
        self.fvec_d = self.din("fvec", [128, 40])
        self.dww_d = self.din("dww", [128, 8, CW])
        self.gq_d = self.din("gq", [1, 64])
        self.gk_d = self.din("gk", [1, 64])
        self.gf_d = self.din("gf", [1, D])
        self.bout_d = self.din("bout", [1, D])
        self.rt_all_d = self.din("rt_all", [128, Ss // 128, 2, 16])
        self.rt_q_d = self.din("rt_q", [128, Q // 128 + 1, 2, 16])
        self.ct_d = self.din("ct", [128, 2, 16])
        self.flags_d = self.din("flags", [128, 2])
        self.yp = self.dout("yp", [Sp, D])
        self.yq = self.dout("yq", [Q, D])
        self.scr = nc.dram_tensor("wscr", [8 + 8 + 8 + 6 * NFC + 16 + 8, 128, 1024], BF16).ap()

    def alloc(self):
        nc = self.nc
        Ss = self.Ss
        A = nc.alloc_sbuf_tensor
        self.nkc_max = Ss // 128
        self.KT = [A("KT%d" % m, [128, Ss], BF16) for m in range(2)]
        self.V = A("Vst", [128, self.nkc_max, 384], BF16)
        self.xsl = [A("x%d" % i, [128, 4, D], F32) for i in range(2)]
        self.h_tm = A("h_tm", [128, 4, D], BF16)
        self.hT = A("hT", [128, 8, 512], BF16)
        self.R = A("R", [128, NFC * 512], BF16)
        self.PT = A("PT", [128, 6, 512], BF16)
        self.U = [A("U%d" % i, [128, 8, 512 + 2 * CP], BF16) for i in range(2)]
        self.UH = A("UH", [128, 8, 128], BF16)
        self.ring = A("ring", [128, 6, 1024], BF16)
        self.stage = A("stage", [128, 1024], F32)
        self.rtg = A("rtg", [128, 2, 4, 2, 16], F32)
        self._rtg_next = 0
        self.ct = A("ct_s", [128, 2, 16], F32)
        self.gf = A("gf_s", [128, D], F32)
        self.gq = A("gq_s", [128, 64], F32)
        self.gk = A("gk_s", [128, 64], F32)
        self.gcols = A("gcols_s", [128, 4, 8], F32)
        self.fvec = A("fvec_s", [128, 40], F32)
        self.dww = A("dww_s", [128, 8, CW], F32)
        self.flags = A("flags_s", [128, 2], F32)
        self.bout_b = A("bout_b", [1, D], BF16)
        self.ones_b = A("ones_b", [1, 128], BF16)
        self.ident = A("ident", [128, 128], BF16)
        self.nhalf = A("nhalf", [128, 16], F32)
        self.st = A("stats", [128, 128], F32)
        self.rec = A("rec", [128, 512], F32)
        self.oraw = A("oraw", [128, 2, 512], F32)
        self.ps = [nc.alloc_psum_tensor("ps%d" % i, [128, 512], F32) for i in range(8)]
        Rf = self.R[:, :]
        self.qT = Rf[:, 0:4096].rearrange("p (c t) -> p c t", c=8)
        self.oT = Rf[:, 4096:8192].rearrange("p (c t) -> p c t", c=8)
        self.hidT = Rf[:, 0:NFC * 512].rearrange("p (c t) -> p c t", c=NFC)
        self.cT = Rf[:, 0:4096].rearrange("p (c t) -> p c t", c=8)
        self.ktm2 = [Rf[:, i * 1024:(i + 1) * 1024].rearrange("p (s c) -> p s c", s=4) for i in range(2)]
        self.qTh = Rf[:, 0:1024].rearrange("p (c t) -> p c t", c=8)
        self.oTh = Rf[:, 4096:5120].rearrange("p (c t) -> p c t", c=8)
        self.Rf = Rf
        hi = Rf[:, 4096:NFC * 512].bitcast(F32)
        self.sq = hi[:, 0:1024]
        self.qg = hi[:, 1024:2048]
        self.rtmp = hi[:, 2048:3584]
        self.diag = [Rf[:, 0:CW * 128].rearrange("p (j c) -> p j c", j=CW),
                     Rf[:, 4096:4096 + CW * 128].rearrange("p (j c) -> p j c", j=CW)]
        self.diag_keys = [["Ra"], ["Rb", "Rc"]]
        self.extra_stage = False
        self.v_stage_on = False
        c0 = self.Sp // 128
        nfree = (self.nkc_max - c0) * 384 // 2
        self.vstage = []
        if nfree >= 1024 and os.environ.get("VSTAGE"):
            vf = self.V[:, c0:, :].rearrange("p a b -> p (a b)").bitcast(F32)
            self.vstage = [(("vst", i), vf[:, i * 1024:(i + 1) * 1024]) for i in range(min(12, nfree // 1024))]
        self._stage_rr = 0
        self._cast_rr = 0
        ptf = self.PT[:, :, :].rearrange("p a b -> p (a b)")
        self.sg = [ptf[:, 0:1024].bitcast(F32), ptf[:, 1024:2048].bitcast(F32)]
        S = self.S
        for ss in range(4):
            for cb in range(2):
                S.set_parent(("h_tmh", ss, cb), ("h_tm", ss))
        for c in range(8):
            S.set_parent(("qT", c), "Ra")
            S.set_parent(("oT", c), "Rb" if c < 4 else "Rc")
        S.set_parent(("ktm", 0), "Ra")
        S.set_parent(("ktm", 1), "Ra")
        self.tmp_keys = []
        for kname, reg in (("k_sq", "Rb"), ("k_qg", "Rc"), (("k_qgh", 0), "Rc"), (("k_qgh", 1), "Rc"),
                           (("q_sq", 0), "Rb"), (("q_sq", 1), "Rb"), (("q_sq", 2), "Rc"), (("q_sq", 3), "Rd"),
                           ("q_qg", "Rc"), (("q_qgh", 0), "Rc"), (("q_qgh", 1), "Rc")):
            S.set_parent(kname, reg)
            self.tmp_keys.append(kname)
        for j in range(6):
            for pfx in ("k_rt", "q_rt"):
                S.set_parent((pfx, j), "Rd")
                self.tmp_keys.append((pfx, j))
        for hf in range(2):
            for i in range(3):
                S.set_parent(("rtmp", hf, i), "Rd")
        print("sbuf bytes remaining", nc.sbuf_bytes_remaining)

    def make_chunks(self):
        self.chunks = {}
        idx = [0]

        def add(name, src, gain=None, scale=None, ncols=1024):
            self.chunks[name] = dict(src=src, gain=gain, scale=scale, ncols=ncols, scr=self.scr[idx[0]], done=False)
            idx[0] += 1

        for kc in range(8):
            add(("kv", kc), self.w_kv[kc], gain=("pp", 0, kc), ncols=512)
        for kc in range(8):
            add(("q", kc), self.w_q[kc], gain=("pp", 0, kc))
        for pr in range(8):
            add(("o", pr), self.w_o[pr])
        for l in range(2):
            for fc in range(NFC):
                add(("g", l, fc), self.w_g[l, fc], gain=("kc", 1 if l == 0 else 3))
                add(("u", l, fc), self.w_u[l, fc], gain=("kc", 1 if l == 0 else 3))
                add(("d", l, fc), self.w_d[l, fc])
        for oc in range(16):
            add(("ci", oc), self.w_ci[oc], gain=("kc", 2))
        for cc in range(8):
            add(("co", cc), self.w_co[cc])

    def fetch(self, name):
        S = self.S
        ch = self.chunks[name]
        slot = self._ring_next
        self._ring_next = (slot + 1) % 6
        n = ch["ncols"]
        dst = self.ring[:, slot, 0:n]
        key = ("ring", slot)
        if not ch["done"]:
            ch["done"] = True
            bufs = [("stage", self.stage[:, :])]
            if self.v_stage_on:
                bufs += self.vstage
            if self.extra_stage:
                bufs += [(("x", 1, j), self.xsl[1][:, j, :]) for j in range(4)]
            self._stage_rr = (self._stage_rr + 1) % len(bufs)
            skey, sbuf = bufs[self._stage_rr]
            self._cast_rr ^= 1
            ce = "pool" if self._cast_rr else "dve"
            S.dma("sp", out=sbuf[:, 0:n], in_=ch["src"], writes=[skey], sem=("stg", self._stage_rr))
            g = ch["gain"]
            if g is None:
                S.op(ce, "tensor_copy", reads=[skey], writes=[key], out=dst, in_=sbuf[:, 0:n])
            elif g[0] == "pp":
                S.op(ce, "tensor_scalar", reads=[skey, "gcols"], writes=[key], out=dst, in0=sbuf[:, 0:n],
                     scalar1=self.gcols[:, g[1], g[2]:g[2] + 1], scalar2=None, op0=ALU.mult)
            else:
                gb = self.gcols[:, g[1], :].unsqueeze(2).broadcast_to([128, 8, 128])
                S.op(ce, "tensor_tensor", reads=[skey, "gcols"], writes=[key],
                     out=dst.rearrange("p (k c) -> p k c", k=8), in0=sbuf.rearrange("p (k c) -> p k c", k=8),
                     in1=gb, op=ALU.mult)
            S.dma(DMA_Q2, out=ch["scr"][:, 0:n], in_=dst, reads=[key], writes=[("scr", name)], sem=("scr", slot))
        else:
            S.dma("sp", out=dst, in_=ch["scr"][:, 0:n], reads=[("scr", name)], writes=[key], sem=("ringld", slot))
        return key, dst

    def load_rt(self, src, j0, n):
        i = self._rtg_next
        self._rtg_next = 1 - i
        key = ("rtg", i)
        self.S.dma("sp", out=self.rtg[:, i, 0:n], in_=src[:, j0:j0 + n], writes=[key], sem=key)
        return self.rtg[:, i], key

    def bank(self):
        b = self._ps_rr
        self._ps_rr = (b + 1) % 8
        return b

    def setup(self):
        S = self.S
        ld = [("ct", self.ct, self.ct_d),
              ("gcols", self.gcols, self.gcols_d), ("fvec", self.fvec, self.fvec_d), ("dww", self.dww, self.dww_d),
              ("flags", self.flags, self.flags_d)]
        for i, (k, sb, dr) in enumerate(ld):
            S.dma("sp", out=sb[:], in_=dr, writes=[k], sem=("setup", i % 4))
        S.dma("sp", out=self.gf[:], in_=self.gf_d.partition_broadcast(128), writes=["gf"], sem=("setup", 0))
        S.dma("sp", out=self.gq[:], in_=self.gq_d.partition_broadcast(128), writes=["gq"], sem=("setup", 1))
        S.dma("sp", out=self.gk[:], in_=self.gk_d.partition_broadcast(128), writes=["gk"], sem=("setup", 2))
        S.dma("sp", out=self.stage[0:1, :], in_=self.bout_d, writes=["stage"], sem="stage")
        S.op("pool", "tensor_copy", reads=["stage"], writes=["bout_b"], out=self.bout_b[:], in_=self.stage[0:1, :])
        S.op("pool", "memset", writes=["ones_b"], ap=self.ones_b[:], constant=1.0)
        S.op("pool", "memset", writes=["nhalf"], ap=self.nhalf[:], constant=-0.5)
        identf = self.stage[:, 0:128]
        S.op("pool", "memset", reads=["stage"], writes=["stage"], ap=identf, constant=0.0)
        S.op("pool", "affine_select", reads=["stage"], writes=["stage"], out=identf, in_=identf,
             pattern=[[-1, 128]], compare_op=ALU.not_equal, fill=1.0, base=0, channel_multiplier=1)
        S.op("pool", "tensor_copy", reads=["stage"], writes=["ident"], out=self.ident[:], in_=identf)
        S.op("pool", "tensor_scalar", reads=["gq"], writes=["gq"], out=self.gq[:], in0=self.gq[:], scalar1=1.0 / math.sqrt(HD),
             scalar2=None, op0=ALU.mult)
        c1 = self.Sp // 128 if self.vstage else self.nkc_max
        S.op("pool", "memset", writes=["V"], ap=self.V[:, 0:c1, 64:128], constant=1.0)
        S.op("pool", "memset", writes=["V"], ap=self.V[:, 0:c1, 256:320], constant=1.0)

    def xk(self, slot, s):
        return ("x", slot, s)

    def norm_to_hT(self, xsl, slot, nsub, evac_eng="dve", scale_all_act=False):
        self.norm_scale(xsl, slot, nsub, scale_all_act)
        self.norm_transposes(nsub, evac_eng)

    def norm_scale(self, xsl, slot, nsub, scale_all_act=False):
        S = self.S
        for s in range(nsub):
            S.op("act", "activation", reads=[self.xk(slot, s)], writes=[("h_tm", s), ("ss", s)], out=self.h_tm[:, s, :], in_=xsl[:, s, :],
                 func=AF.Square, accum_out=self.st[:, s:s + 1])
        rr = self.st[:, 8:8 + nsub]
        S.op("pool", "tensor_scalar", reads=[("ss", s) for s in range(nsub)], writes=["rr"], out=rr, in0=self.st[:, 0:nsub],
             scalar1=1.0 / D, scalar2=RMS_EPS, op0=ALU.mult, op1=ALU.add)
        S.op("pool", "tensor_tensor", reads=["rr", "nhalf"], writes=["rr"], out=rr, in0=rr, in1=self.nhalf[:, 0:nsub], op=ALU.pow)
        for s in range(nsub):
            if s % 2 == 0 or scale_all_act:
                S.op("act", "activation", reads=[self.xk(slot, s), "rr"], writes=[("h_tm", s)], out=self.h_tm[:, s, :],
                     in_=xsl[:, s, :], func=AF.Copy, scale=self.st[:, 8 + s:9 + s])
            else:
                S.op("dve", "tensor_scalar", reads=[self.xk(slot, s), "rr"], writes=[("h_tm", s)], out=self.h_tm[:, s, :],
                     in0=xsl[:, s, :], scalar1=self.st[:, 8 + s:9 + s], scalar2=None, op0=ALU.mult)
        self.phase("n_scale")

    def norm_transposes(self, nsub, evac_eng="dve"):
        S = self.S
        for s in range(nsub):
            b = self.bank()
            pk = ("ps", b)
            pv = self.ps[b][:, :].bitcast(BF16)
            for c in range(8):
                S.op("pe", "transpose", reads=[("h_tm", s), "ident"], writes=[pk], out=pv[:, c * 128:(c + 1) * 128],
                     in_=self.h_tm[:, s, c * 128:(c + 1) * 128], identity=self.ident[:])
            dst = self.hT[:, :, s * 128:(s + 1) * 128]
            src = pv.rearrange("p (c t) -> p c t", c=8)
            if evac_eng == "dve":
                S.op("dve", "tensor_copy", reads=[pk], writes=["hT"], out=dst, in_=src)
            else:
                S.op("act", "activation", reads=[pk], writes=["hT"], out=dst, in_=src, func=AF.Copy)
        self.phase("n_hT")

    def transposes(self, srcs, srckeys, nchunks, dst, dstkey, evac_eng="dve", act_kw=None, col0=0):
        S = self.S
        nsub = len(srcs)
        T = nsub * 128
        for c in range(nchunks):
            b = self.bank()
            pk = ("ps", b)
            pv = self.ps[b][:, :].bitcast(BF16)
            for s in range(nsub):
                S.op("pe", "transpose", reads=[srckeys[s], "ident"], writes=[pk], signal=(s == nsub - 1),
                     out=pv[:, s * 128:(s + 1) * 128], in_=srcs[s][:, c * 128:(c + 1) * 128], identity=self.ident[:])
            if act_kw is not None:
                kw = act_kw(c)
                S.op("act", "activation", reads=[pk] + kw.pop("reads", []), writes=[dstkey], out=dst[:, c, col0:col0 + T],
                     in_=pv[:, 0:T], **kw)
            elif evac_eng == "dve":
                S.op("dve", "tensor_copy", reads=[pk], writes=[dstkey], out=dst[:, c, col0:col0 + T], in_=pv[:, 0:T])
            else:
                S.op("act", "activation", reads=[pk], writes=[dstkey], out=dst[:, c, col0:col0 + T], in_=pv[:, 0:T],
                     func=AF.Copy)

    def tm_proj(self, actT, actkey, chunk_names, nsub, ncb, evac, bias=False):
        S = self.S
        banks = [[self.bank() for cb in range(ncb)] for s in range(nsub)]
        nch = len(chunk_names)
        if bias:
            for s in range(nsub):
                for cb in range(ncb):
                    S.op("pe", "matmul", reads=["ones_b", "bout_b"], writes=[("ps", banks[s][cb])], signal=False,
                         out=self.ps[banks[s][cb]][:, :], lhsT=self.ones_b[0:1, :], rhs=self.bout_b[0:1, cb * 512:(cb + 1) * 512],
                         start=True, stop=False)
        for c, name in enumerate(chunk_names):
            wkey, w = self.fetch(name)
            for s in range(nsub):
                for cb in range(ncb):
                    last = (s == nsub - 1 and cb == ncb - 1)
                    ak = actkey(c) if callable(actkey) else (actkey if isinstance(actkey, list) else [actkey])
                    S.op("pe", "matmul", reads=ak + [wkey], writes=[("ps", banks[s][cb])], signal=last,
                         out=self.ps[banks[s][cb]][:, :], lhsT=actT[:, c, s * 128:(s + 1) * 128],
                         rhs=w[:, cb * 512:(cb + 1) * 512], start=(c == 0 and not bias), stop=(c == nch - 1))
        self.phase("tm_mm")
        for s in range(nsub):
            for cb in range(ncb):
                evac(s, cb, banks[s][cb])
                self.phase("tm_evac1")

    def resid_evac(self, xsl, slot):
        def evac(s, cb, b):
            xkey = self.xk(slot, s)
            self.S.op("dve", "tensor_tensor", reads=[("ps", b), xkey], writes=[xkey], out=xsl[:, s, cb * 512:(cb + 1) * 512],
                      in0=self.ps[b][:, :], in1=xsl[:, s, cb * 512:(cb + 1) * 512], op=ALU.add)
        return evac

    def tmp_fence(self):
        self.S.op("dve", "memset", writes=list(self.tmp_keys), ap=self.st[:, 120:121], constant=0.0)

    def qk_post(self, pviews, pkeys, G, H, gbc, gkey, rt, rtkey, out, outkeys, qsel=None):
        S = self.S
        n = G * H * 64
        GH = G * H
        if qsel is None:
            sq, ksq = self.sq[:, 0:n], "k_sq"
            qg, kqg, kh = self.qg[:, 0:n], "k_qg", "k_qgh"
            tbuf = lambda j: self.rtmp[:, j * 256:j * 256 + GH * 16]
            trk = "k_rt"
        else:
            assert n == 512
            sq = [self.sq[:, 0:512], self.sq[:, 512:1024], self.qg[:, 512:1024], self.rtmp[:, 1024:1536]][qsel]
            ksq = ("q_sq", qsel)
            qg, kqg, kh = self.qg[:, 0:512], "q_qg", "q_qgh"
            tbuf = lambda j: self.rtmp[:, j * 128:j * 128 + GH * 16]
            trk = "q_rt"
        off = 0
        for pv, pk in zip(pviews, pkeys):
            w = pv.shape[1]
            S.op("act", "activation", reads=[pk], writes=[ksq], out=sq[:, off:off + w], in_=pv, func=AF.Square)
            S.op("dve", "tensor_tensor", reads=[pk, gkey, ksq], writes=[kqg], out=qg[:, off:off + w].rearrange("p (h d) -> p h d", d=64),
                 in0=pv.rearrange("p (h d) -> p h d", d=64), in1=gbc[:, :].unsqueeze(1).broadcast_to([128, w // 64, 64]), op=ALU.mult)
            off += w
        ssq = self.st[:, 16:16 + GH]
        rs = self.st[:, 32:32 + GH]
        S.op("dve", "tensor_reduce", reads=[ksq], writes=["ssq"], out=ssq, in_=sq.rearrange("p (h d) -> p h d", d=64),
             axis=AX.X, op=ALU.add)
        S.op("pool", "tensor_scalar", reads=["ssq"], writes=["rs"], out=rs, in0=ssq, scalar1=1.0 / HD, scalar2=RMS_EPS,
             op0=ALU.mult, op1=ALU.add)
        S.op("pool", "tensor_tensor", reads=["rs", "nhalf"], writes=["rs"], out=rs, in0=rs, in1=self.nhalf[:, 0:GH], op=ALU.pow)
        v6 = qg.rearrange("p (g h r f d) -> p g h r f d", g=G, h=H, r=2, f=2, d=16)
        shp = [128, G, H, 16]
        for half, eng in ((0, "dve"), (1, ROPE_ENG1)):
            Aa = v6[:, :, :, half, 0, :]
            Bb = v6[:, :, :, half, 1, :]
            if half == 0:
                C = rt[:, :, 0, :].unsqueeze(2).broadcast_to(shp)
                Sn = rt[:, :, 1, :].unsqueeze(2).broadcast_to(shp)
                tk = [rtkey]
            else:
                C = self.ct[:, 0, :].unsqueeze(1).unsqueeze(1).broadcast_to(shp)
                Sn = self.ct[:, 1, :].unsqueeze(1).unsqueeze(1).broadcast_to(shp)
                tk = ["ct"]
            t = [tbuf(half * 3 + i).rearrange("p (g h d) -> p g h d", g=G, h=H) for i in range(3)]
            k = [(trk, half * 3 + i) for i in range(3)]
            qk = (kh, half)
            S.op(eng, "tensor_tensor", reads=[kqg, qk] + tk, writes=[k[0]], out=t[0], in0=Aa, in1=C, op=ALU.mult)
            S.op(eng, "tensor_tensor", reads=[kqg, qk] + tk, writes=[k[1]], out=t[1], in0=Aa, in1=Sn, op=ALU.mult)
            S.op(eng, "tensor_tensor", reads=[kqg, qk] + tk, writes=[k[2]], out=t[2], in0=Bb, in1=Sn, op=ALU.mult)
            S.op(eng, "tensor_tensor", reads=[kqg, k[0], k[2]], writes=[qk], out=Aa, in0=t[0], in1=t[2], op=ALU.subtract)
            S.op(eng, "tensor_tensor", reads=[kqg, qk] + tk, writes=[k[0]], out=t[0], in0=Bb, in1=C, op=ALU.mult)
            S.op(eng, "tensor_tensor", reads=[kqg, k[0], k[1]], writes=[qk], out=Bb, in0=t[0], in1=t[1], op=ALU.add)
        S.op("dve", "tensor_tensor", reads=[kqg, (kh, 0), (kh, 1), "rs"], writes=list(outkeys) + [kqg],
             out=out.rearrange("p g (h d) -> p g h d", d=64), in0=qg.rearrange("p (g h d) -> p g h d", g=G, h=H),
             in1=rs.rearrange("p (g h) -> p g h", g=G).unsqueeze(3).broadcast_to([128, G, H, 64]), op=ALU.mult)

    def k_transposes(self, g, nsub):
        S = self.S
        T = nsub * 128
        kt = self.ktm2[g % 2]
        kk = ("ktm", g % 2)
        for m in range(2):
            b = self.bank()
            pk = ("ps", b)
            pv = self.ps[b][:, :].bitcast(BF16)
            for s in range(nsub):
                S.op("pe", "transpose", reads=[kk, "ident"], writes=[pk],
                     out=pv[:, s * 128:(s + 1) * 128], in_=kt[:, s, m * 128:(m + 1) * 128], identity=self.ident[:])
            S.op("dve", "tensor_copy", reads=[pk], writes=["KT"], out=self.KT[m][:, g * 512:g * 512 + T], in_=pv[:, 0:T])

    def stage1(self, xsrc, S_len):
        S = self.S
        ngrp = (S_len + 511) // 512
        pending = None
        self.tmp_fence()

        def load_x(g):
            nsub_ = min(4, (S_len - g * 512) // 128)
            for s in range(nsub_):
                S.dma("sp", out=self.xsl[g % 2][:, s, :], in_=xsrc[g * 512 + s * 128:g * 512 + (s + 1) * 128, :],
                      writes=[self.xk(g % 2, s)], sem=("xld", g % 2, s))

        load_x(0)
        self.norm_scale(self.xsl[0], 0, min(4, S_len // 128), True)
        for g in range(ngrp):
            nsub = min(4, (S_len - g * 512) // 128)
            slot = g % 2
            xsl = self.xsl[slot]
            rtv, rtk = self.load_rt(self.rt_all_d, g * 4, nsub)
            if g + 1 < ngrp:
                load_x(g + 1)
            self.norm_transposes(nsub, "act")
            kbanks = []
            self.tm_proj(self.hT, "hT", [("kv", kc) for kc in range(8)], nsub, 1, lambda s, cb, b, kb=kbanks: kb.append(b))
            if pending is not None:
                self.k_transposes(*pending)
            if g + 1 < ngrp:
                self.norm_scale(self.xsl[(g + 1) % 2], (g + 1) % 2, min(4, (S_len - (g + 1) * 512) // 128), True)
            for s, b in enumerate(kbanks):
                pv = self.ps[b]
                pk = ("ps", b)
                j = g * 4 + s
                for (dst0, src0) in ((0, 256), (128, 320), (192, 448), (320, 384)):
                    S.op("act", "activation", reads=[pk], writes=["V"], out=self.V[:, j, dst0:dst0 + 64], in_=pv[:, src0:src0 + 64],
                         func=AF.Copy)
            self.qk_post([self.ps[bb][:, 0:256] for bb in kbanks], [("ps", bb) for bb in kbanks], nsub, 4, self.gk, "gk",
                         rtv[:, 0:nsub], rtk, self.ktm2[g % 2][:, 0:nsub, :], [("ktm", g % 2)])
            pending = (g, nsub)
            self.phase("s1")
        self.k_transposes(*pending)

    def ffn(self, l, xsl, slot, nsub):
        S = self.S
        T = nsub * 128
        self.norm_to_hT(xsl, slot, nsub)
        for fc in range(NFC):
            kg, wg = self.fetch(("g", l, fc))
            ku, wu = self.fetch(("u", l, fc))
            bg, bu = self.bank(), self.bank()
            for (kk, w, b) in ((kg, wg, bg), (ku, wu, bu)):
                for kc in range(8):
                    S.op("pe", "matmul", reads=["hT", kk], writes=[("ps", b)], signal=(kc == 7), out=self.ps[b][:, 0:T],
                         lhsT=w[:, kc * 128:(kc + 1) * 128], rhs=self.hT[:, kc, 0:T], start=(kc == 0), stop=(kc == 7))
            sgi = fc % 2
            S.op("act", "activation", reads=[("ps", bg)], writes=[("PT", 2 * sgi), ("PT", 2 * sgi + 1)], out=self.sg[sgi][:, 0:T], in_=self.ps[bg][:, 0:T],
                 func=AF.Silu)
            S.op("dve", "tensor_tensor", reads=[("ps", bu), ("PT", 2 * sgi), ("PT", 2 * sgi + 1)], writes=[self.hkey(fc)], out=self.hidT[:, fc, 0:T],
                 in0=self.ps[bu][:, 0:T], in1=self.sg[sgi][:, 0:T], op=ALU.mult)
        self.tm_proj(self.hidT, ["Ra", "Rb", "Rc", "Rd"], [("d", l, fc) for fc in range(NFC)], nsub, 2, self.resid_evac(xsl, slot))

    def attention(self, nsub, nkc, hook=None):
        S = self.S
        T = nsub * 128
        sbanks = [(0, 1), (2, 3), (4, 5)]
        oa, ob = 6, 7
        if nsub == 1:
            N = 512
            units = []
            for m in range(2):
                qa = self.Rf[0:64, m * 512:(m + 1) * 512]
                qb = self.Rf[64:128, m * 512:(m + 1) * 512]
                units.append((m, qa, qb, lambda lo, hi, m=m: self.Rf[lo:hi, 4096 + m * 512:4096 + (m + 1) * 512], "Rb"))
        else:
            N = T
            units = []
            for pr in range(8):
                units.append((pr // 4, self.qT[0:64, pr, 0:T], self.qT[64:128, pr, 0:T],
                              lambda lo, hi, pr=pr: self.oT[lo:hi, pr, 0:T], ("oT", pr)))
        items = [(u, c) for u in range(len(units)) for c in range(nkc)]

        def vviews(m, c):
            if m == 0:
                return self.V[:, c, 0:128], self.V[:, c, 64:192]
            return self.V[:, c, 256:384], self.V[:, c, 192:320]

        def s_mm(k):
            u, c = items[k]
            m, qa, qb, _, _ = units[u]
            ba, bb = sbanks[k % 3]
            qk = "Ra" if nsub == 1 else ("qT", u)
            if hook is not None and c == 0:
                hook(u, (ba, bb))
            S.op("pe", "matmul", reads=["KT", qk], writes=[("ps", ba)], out=self.ps[ba][:, 0:N],
                 lhsT=self.KT[m][0:64, c * 128:(c + 1) * 128], rhs=qa, start=True, stop=True, tile_position=(0, 0))
            S.op("pe", "matmul", reads=["KT", qk], writes=[("ps", bb)], out=self.ps[bb][:, 0:N],
                 lhsT=self.KT[m][64:128, c * 128:(c + 1) * 128], rhs=qb, start=True, stop=True, tile_position=(64, 0))

        def finish(u):
            m, _, _, oview, ok = units[u]
            a_o, b_o = (0, 64) if m == 0 else (64, 0)
            S.op("dve", "tensor_copy", reads=[("ps", oa)], writes=[("oraw", 0)], out=self.oraw[:, 0, 0:N], in_=self.ps[oa][:, 0:N])
            S.op("dve", "tensor_copy", reads=[("ps", ob)], writes=[("oraw", 1)], out=self.oraw[:, 1, 0:N], in_=self.ps[ob][:, 0:N])
            for h, o_off in enumerate((a_o, b_o)):
                s_off = 64 - o_off
                rk = ("rec", o_off)
                S.op("dve", "reciprocal", reads=[("oraw", h)], writes=[rk], out=self.rec[o_off:o_off + 64, 0:N],
                     in_=self.oraw[s_off:s_off + 64, h, 0:N])
                S.op("dve", "tensor_tensor", reads=[("oraw", h), rk], writes=[ok], out=oview(o_off, o_off + 64),
                     in0=self.oraw[o_off:o_off + 64, h, 0:N], in1=self.rec[o_off:o_off + 64, 0:N], op=ALU.mult)

        def pv_mm(k):
            u, c = items[k]
            m = units[u][0]
            ba, bb = sbanks[k % 3]
            pa = (k % 3) * 2
            va, vb = vviews(m, c)
            S.op("act", "activation", reads=[("ps", ba)], writes=[("PT", pa)], out=self.PT[:, pa, 0:N], in_=self.ps[ba][:, 0:N],
                 func=AF.Exp)
            S.op("act", "activation", reads=[("ps", bb)], writes=[("PT", pa + 1)], out=self.PT[:, pa + 1, 0:N],
                 in_=self.ps[bb][:, 0:N], func=AF.Exp)
            S.op("pe", "matmul", reads=["V", ("PT", pa)], writes=[("ps", oa)], out=self.ps[oa][:, 0:N],
                 lhsT=va, rhs=self.PT[:, pa, 0:N], start=(c == 0), stop=(c == nkc - 1))
            S.op("pe", "matmul", reads=["V", ("PT", pa + 1)], writes=[("ps", ob)], out=self.ps[ob][:, 0:N],
                 lhsT=vb, rhs=self.PT[:, pa + 1, 0:N], start=(c == 0), stop=(c == nkc - 1))
            if c == nkc - 1:
                finish(u)

        n = len(items)
        for k in range(min(2, n)):
            s_mm(k)
        for k in range(n):
            if k + 2 < n:
                s_mm(k + 2)
            pv_mm(k)

    def bank_t(self):
        b = self._t_rr % 6
        self._t_rr += 1
        return b

    def okey(self, pr):
        return "Rb" if pr < 4 else "Rc"

    def hkey(self, fc):
        return "Ra" if fc < 8 else ("Rb" if fc < 12 else ("Rc" if fc < 16 else "Rd"))

    def bank_o(self, pr):
        return (6, 7)

    def S_a(self, xsrc_ap, slot, nsub, nkc, rt_tab, rt_key, rt_j0, u_dst, u_key, u_col0):
        S = self.S
        T = nsub * 128
        xsl = self.xsl[slot]
        for s in range(nsub):
            S.dma("sp", out=xsl[:, s, :], in_=xsrc_ap[s * 128:(s + 1) * 128, :], writes=[self.xk(slot, s)], sem=("xld", slot, s))
        rtv, rtk = self.load_rt(rt_tab, rt_j0, nsub)
        self.norm_to_hT(xsl, slot, nsub)

        qb = {}
        self.tm_proj(self.hT, "hT", [("q", kc) for kc in range(8)], nsub, 2, lambda s_, cb, b: qb.__setitem__((s_, cb), b))

        def q_half(cb):
            for s_ in range(nsub):
                b = qb[(s_, cb)]
                self.qk_post([self.ps[b][:, :]], [("ps", b)], 1, 8, self.gq, "gq", rtv[:, s_:s_ + 1], rtk,
                             self.h_tm[:, s_:s_ + 1, cb * 512:(cb + 1) * 512], [("h_tmh", s_, cb)], qsel=s_)

        def q_transposes(cb, free_banks=None):
            qdst = self.qTh if nsub == 1 else self.qT
            for c in range(4 * cb, 4 * cb + 4):
                if free_banks is None:
                    b = qb[((c - 4 * cb) % nsub, cb)]
                else:
                    b = free_banks[c % len(free_banks)]
                pk = ("ps", b)
                pv = self.ps[b][:, :].bitcast(BF16)
                for s_ in range(nsub):
                    S.op("pe", "transpose", reads=[("h_tmh", s_, cb), "ident"], writes=[pk], out=pv[:, s_ * 128:(s_ + 1) * 128],
                         in_=self.h_tm[:, s_, c * 128:(c + 1) * 128], identity=self.ident[:])
                S.op("dve", "tensor_copy", reads=[pk], writes=[("qT", c)], out=qdst[:, c, 0:T], in_=pv[:, 0:T])

        self._t_rr = 0
        self.tmp_fence()
        q_half(0)
        q_transposes(0)
        q_half(1)
        self.phase("q")
        self._ps_rr = 0
        if nsub == 1:
            q_transposes(1)
            self.attention(nsub, nkc)
        else:
            self.attention(nsub, nkc, hook=lambda u, fb: q_transposes(1, fb) if u == 4 else None)
        self.phase("att")
        self.tm_proj(self.oTh if nsub == 1 else self.oT, ["Rb", "Rc"] if nsub == 1 else (lambda c: [("oT", c)]),
                     [("o", pr) for pr in range(8)], nsub, 2, self.resid_evac(xsl, slot))
        self.phase("wo")
        self.ffn(0, xsl, slot, nsub)
        self.phase("ffn0")
        self.norm_to_hT(xsl, slot, nsub)
        for cc in range(8):
            ka, wa = self.fetch(("ci", cc))
            kg, wg = self.fetch(("ci", 8 + cc))
            ba, bg = self.bank(), self.bank()
            for (kk, w, b) in ((ka, wa, ba), (kg, wg, bg)):
                for kc in range(8):
                    S.op("pe", "matmul", reads=["hT", kk], writes=[("ps", b)], signal=(kc == 7), out=self.ps[b][:, 0:T],
                         lhsT=w[:, kc * 128:(kc + 1) * 128], rhs=self.hT[:, kc, 0:T], start=(kc == 0), stop=(kc == 7))
            sgi = cc % 2
            S.op("act", "activation", reads=[("ps", bg), "fvec"], writes=[("PT", 2 * sgi), ("PT", 2 * sgi + 1)], out=self.sg[sgi][:, 0:T],
                 in_=self.ps[bg][:, 0:T], func=AF.Sigmoid, bias=self.fvec[:, 8 + cc:9 + cc])
            S.op("dve", "scalar_tensor_tensor", reads=[("ps", ba), ("PT", 2 * sgi), ("PT", 2 * sgi + 1), "fvec"], writes=[u_key],
                 out=u_dst[:, cc, u_col0:u_col0 + T], in0=self.ps[ba][:, 0:T], scalar=self.fvec[:, cc:cc + 1],
                 in1=self.sg[sgi][:, 0:T], op0=ALU.add, op1=ALU.mult)

    def S_b(self, slot, uslot, nsub, ydst_ap):
        S = self.S
        T = nsub * 128
        xsl = self.xsl[slot]
        U = self.U[uslot]
        ukey = ("U", uslot)
        for cc in range(8):
            dg = self.diag[cc % 2]
            dk = self.diag_keys[cc % 2]
            S.op("dve", "tensor_tensor", reads=["ident", "dww"], writes=dk, out=dg,
                 in0=self.ident[:, :].unsqueeze(1).broadcast_to([128, CW, 128]),
                 in1=self.dww[:, cc, :].unsqueeze(2).broadcast_to([128, CW, 128]), op=ALU.mult)
            b = self.bank()
            for j in range(CW):
                S.op("pe", "matmul", reads=dk + [ukey], writes=[("ps", b)], out=self.ps[b][:, 0:T],
                     lhsT=dg[:, j, :], rhs=U[:, cc, j:j + T], start=(j == 0), stop=(j == CW - 1))
            S.op("act", "activation", reads=[("ps", b), "fvec"], writes=["hT"], out=self.hT[:, cc, 0:T], in_=self.ps[b][:, 0:T],
                 func=AF.Identity, bias=self.fvec[:, 16 + cc:17 + cc])
        self.phase("conv")
        lnb = []
        for s in range(nsub):
            b = self.bank()
            pk = ("ps", b)
            pv = self.ps[b][:, :].bitcast(BF16)
            for cc in range(8):
                S.op("pe", "transpose", reads=["hT", "ident"], writes=[pk], out=pv[:, cc * 128:(cc + 1) * 128],
                     in_=self.hT[:, cc, s * 128:(s + 1) * 128], identity=self.ident[:])
            S.op("act", "activation", reads=[pk], writes=[("h_tm", s), ("lns", s)], out=self.h_tm[:, s, :], in_=pv, func=AF.Identity,
                 accum_out=self.st[:, 48 + s:49 + s])
            S.op("act", "activation", reads=[pk], writes=[("h_tm", s), ("lns", s)], out=self.h_tm[:, s, :], in_=pv, func=AF.Square,
                 accum_out=self.st[:, 52 + s:53 + s])
            lnb.append((pk, pv))
        lk = [("lns", s) for s in range(nsub)]
        mean = self.st[:, 56:56 + nsub]
        var = self.st[:, 60:60 + nsub]
        msq = self.st[:, 64:64 + nsub]
        S.op("dve", "tensor_scalar", reads=lk, writes=["lnm"], out=mean, in0=self.st[:, 48:48 + nsub], scalar1=1.0 / D,
             scalar2=None, op0=ALU.mult)
        S.op("dve", "tensor_tensor", reads=["lnm"], writes=["lnq"], out=msq, in0=mean, in1=mean, op=ALU.mult)
        S.op("dve", "scalar_tensor_tensor", reads=lk + ["lnq"], writes=["lnv"], out=var, in0=self.st[:, 52:52 + nsub], scalar=1.0 / D,
             in1=msq, op0=ALU.mult, op1=ALU.subtract)
        S.op("pool", "tensor_scalar", reads=["lnv"], writes=["lnv"], out=var, in0=var, scalar1=LN_EPS, scalar2=None, op0=ALU.add)
        S.op("pool", "tensor_tensor", reads=["lnv", "nhalf"], writes=["lnv"], out=var, in0=var, in1=self.nhalf[:, 0:nsub], op=ALU.pow)
        for s in range(nsub):
            pk, pv = lnb[s]
            S.op("dve", "tensor_scalar", reads=[pk, "lnm", "lnv"], writes=[("h_tm", s)], out=self.h_tm[:, s, :], in0=pv,
                 scalar1=self.st[:, 56 + s:57 + s], scalar2=self.st[:, 60 + s:61 + s], op0=ALU.subtract, op1=ALU.mult)
        self.transposes([self.h_tm[:, s, :] for s in range(nsub)], [("h_tm", s) for s in range(nsub)], 8, self.hT, "hT",
                        act_kw=lambda c: dict(func=AF.Silu, scale=self.fvec[:, 24 + c:25 + c], bias=self.fvec[:, 32 + c:33 + c],
                                              reads=["fvec"]))
        self.phase("ln")
        self.tm_proj(self.hT, "hT", [("co", cc) for cc in range(8)], nsub, 2, self.resid_evac(xsl, slot), bias=True)
        self.phase("co")
        self.ffn(1, xsl, slot, nsub)
        self.phase("ffn1")
        for s in range(nsub):
            S.op("act", "activation", reads=[self.xk(slot, s)], writes=[("h_tm", s), ("fs", s)], out=self.h_tm[:, s, :], in_=xsl[:, s, :],
                 func=AF.Square, accum_out=self.st[:, 68 + s:69 + s])
        fr = self.st[:, 72:72 + nsub]
        S.op("pool", "tensor_scalar", reads=[("fs", s) for s in range(nsub)], writes=["fr"], out=fr, in0=self.st[:, 68:68 + nsub],
             scalar1=1.0 / D, scalar2=RMS_EPS, op0=ALU.mult, op1=ALU.add)
        S.op("pool", "tensor_tensor", reads=["fr", "nhalf"], writes=["fr"], out=fr, in0=fr, in1=self.nhalf[:, 0:nsub], op=ALU.pow)
        for s in range(nsub):
            xkey = self.xk(slot, s)
            S.op("dve", "scalar_tensor_tensor", reads=[xkey, "fr", "gf"], writes=[xkey], out=xsl[:, s, :], in0=xsl[:, s, :],
                 scalar=self.st[:, 72 + s:73 + s], in1=self.gf[:], op0=ALU.mult, op1=ALU.mult)
            S.dma("sp", out=ydst_ap[s * 128:(s + 1) * 128, :], in_=xsl[:, s, :], reads=[xkey], sem=("yst", slot, s))
        self.phase("fin")

    def run_pass(self, xkv, S_len, xown, n_own, rt_tab, rt_key, ydst, has_halo):
        S = self.S
        nkc = S_len // 128
        self.stage1(xkv, S_len)
        ntile = n_own // 512
        if has_halo:
            self.S_a(xown[n_own:n_own + 128, :], 0, 1, nkc, rt_tab, rt_key, n_own // 128, self.UH, "UH", 0)
            S.op("pool", "tensor_scalar", reads=["UH", "flags"], writes=[("U", 0)], out=self.U[0][:, :, 0:CP],
                 in0=self.UH[:, :, 64 - CP:64], scalar1=self.flags[:, 0:1], scalar2=None, op0=ALU.mult)
        else:
            S.op("pool", "memset", writes=[("U", 0)], ap=self.U[0][:, :, 0:CP], constant=0.0)
        for i in range(ntile + 1):
            if i < ntile:
                us = i % 2
                self.extra_stage = (i == 0 and not has_halo)
                self.S_a(xown[i * 512:(i + 1) * 512, :], i % 2, 4, nkc, rt_tab, rt_key, i * 4, self.U[us], ("U", us), CP)
                self.extra_stage = False
                if i > 0:
                    S.op("pool", "tensor_copy", reads=[("U", 1 - us)], writes=[("U", us)], out=self.U[us][:, :, 0:CP],
                         in_=self.U[1 - us][:, :, 512:512 + CP])
                    S.op("pool", "tensor_copy", reads=[("U", us)], writes=[("U", 1 - us)], out=self.U[1 - us][:, :, 512 + CP:512 + 2 * CP],
                         in_=self.U[us][:, :, CP:2 * CP])
            if i == ntile:
                us = (ntile - 1) % 2
                if has_halo:
                    S.op("pool", "tensor_scalar", reads=["UH", "flags"], writes=[("U", us)], out=self.U[us][:, :, 512 + CP:512 + 2 * CP],
                         in0=self.UH[:, :, 64:64 + CP], scalar1=self.flags[:, 1:2], scalar2=None, op0=ALU.mult)
                else:
                    S.op("pool", "memset", writes=[("U", us)], ap=self.U[us][:, :, 512 + CP:512 + 2 * CP], constant=0.0)
            if i >= 1:
                j = i - 1
                self.S_b(j % 2, j % 2, 4, ydst[j * 512:(j + 1) * 512, :])

    def build(self):
        self.declare()
        self.alloc()
        self.make_chunks()
        self.setup()
        try:
            self.phase("setup")
            self.v_stage_on = True
            self.run_pass(self.xp, self.Sp, self.xp, self.Sp, self.rt_all_d, "rt", self.yp, False)
            self.v_stage_on = False
            if self.vstage:
                vk = ["V"] + [k for k, _ in self.vstage]
                c0 = self.Sp // 128
                self.S.op("pool", "memset", writes=vk, ap=self.V[:, c0:, 64:128], constant=1.0)
                self.S.op("pool", "memset", writes=vk, ap=self.V[:, c0:, 256:320], constant=1.0)
            self.phase("passP")
            self.run_pass(self.xs, self.Ss, self.xq, self.Q, self.rt_q_d, "rt", self.yq, True)
        except StopBuild:
            print("STOPPED at", self.stop_at)
        self.S.final_wait("pool")
        self.S.final_wait("sp")
        self.S.emit()
        print("instructions", self.S.nins, dict(self.S.nops), "signals", dict(self.S.cnt))
        return self.nc


def _rope_tabs(pos_rows):
    inv = (10000.0 ** (-np.arange(0, 32, 2, dtype=np.float64) / 32.0))
    ang = pos_rows[..., None].astype(np.float64) * inv
    return np.cos(ang).astype(np.float32), np.sin(ang).astype(np.float32)


def host_layout(inp, Sp, Ss):
    Q = Ss // 4
    f = lambda a: np.ascontiguousarray(np.asarray(a, dtype=np.float32))
    wqkv = f(inp["w_qkv"])[0]
    heads = [8 * m + 4 * hh + i for m in range(2) for i in range(4) for hh in range(2)]
    heads_o = []
    for m in range(2):
        for i in range(4):
            a, b = 8 * m + i, 8 * m + 4 + i
            heads_o += [a, b] if m == 0 else [b, a]
    qcols = np.concatenate([np.arange(h * 64, (h + 1) * 64) for h in heads])
    orows = np.concatenate([np.arange(h * 64, (h + 1) * 64) for h in heads_o])
    shared = {}
    shared["w_q"] = f(wqkv[:, :1024][:, qcols].reshape(8, 128, 1024))
    shared["w_kv"] = f(wqkv[:, 1024:].reshape(8, 128, 512))
    shared["w_o"] = f(f(inp["w_o"])[0][orows].reshape(8, 128, 1024))

    def up(w):
        Fo = w.shape[1]
        return f(w.reshape(8, 128, Fo // 128, 128).transpose(2, 1, 0, 3).reshape(Fo // 128, 128, 1024))
    shared["w_g"] = f(np.stack([up(f(inp["w_gate"])[l]) for l in range(2)]))
    shared["w_u"] = f(np.stack([up(f(inp["w_up"])[l]) for l in range(2)]))
    shared["w_d"] = f(f(inp["w_down"]).reshape(2, NFC, 128, 1024))
    shared["w_ci"] = up(f(inp["conv_w_in"])[0])
    shared["w_co"] = f(f(inp["conv_w_out"])[0].reshape(8, 128, 1024))
    col = lambda v: f(v).reshape(-1, 128).T
    gcols = np.stack([col(f(inp["attn_norm_g"])[0]), col(f(inp["ffn_norm_g"])[0]), col(f(inp["conv_norm_g"])[0]),
                      col(f(inp["ffn_norm_g"])[1])], axis=1)
    shared["gcols"] = f(gcols)
    shared["fvec"] = f(np.concatenate([col(f(inp["conv_b_in"])[0]), col(f(inp["dw_b"])[0]), col(f(inp["conv_ln_g"])[0]),
                                       col(f(inp["conv_ln_b"])[0])], axis=1))
    shared["dww"] = f(f(inp["dw_w"])[0].T.reshape(8, 128, CW).transpose(1, 0, 2))
    shared["gq"] = f(inp["q_norm_g"]).reshape(1, 64)
    shared["gk"] = f(inp["k_norm_g"]).reshape(1, 64)
    shared["gf"] = f(inp["final_norm_g"]).reshape(1, D)
    shared["bout"] = f(inp["conv_b_out"]).reshape(1, D)
    p = np.arange(128)
    j = np.arange(Ss // 128)
    rows = 2 * j[None, :] + (p[:, None] >= 64)
    c, s = _rope_tabs(rows)
    shared["rt_all"] = f(np.stack([c, s], axis=2))
    c, s = _rope_tabs(p % 64)
    shared["ct"] = f(np.stack([c, s], axis=1))
    xp = f(inp["x_prompt"])
    xs = f(inp["x_sample"])
    maps = []
    for core in range(NCORES):
        sb, qi = core // 4, core % 4
        qs = qi * Q
        m = dict(shared)
        m["xp"] = xp[core]
        m["xs"] = xs[sb]
        halo = np.zeros((128, D), np.float32)
        if qs > 0:
            halo[0:64] = xs[sb, qs - 64:qs]
        if qs + Q < Ss:
            halo[64:128] = xs[sb, qs + Q:qs + Q + 64]
        m["xq"] = f(np.concatenate([xs[sb, qs:qs + Q], halo], axis=0))
        jq = np.arange(Q // 128)
        rows = np.concatenate([2 * (jq[None, :] + qs // 128) + (p[:, None] >= 64),
                               np.where(p[:, None] >= 64, (qs + Q) // 64, qs // 64 - 1)], axis=1)
        c, s = _rope_tabs(rows)
        m["rt_q"] = f(np.stack([c, s], axis=2))
        fl = np.zeros((128, 2), np.float32)
        fl[:, 0] = 1.0 if qs > 0 else 0.0
        fl[:, 1] = 1.0 if qs + Q < Ss else 0.0
        m["flags"] = fl
        maps.append(m)
    return maps


_CACHE = {}


def kernel(**inputs):
    xp = np.asarray(inputs["x_prompt"])
    xs = np.asarray(inputs["x_sample"])
    Sp, Ss = xp.shape[1], xs.shape[1]
    Q = Ss // 4
    key = (Sp, Ss)
    if key not in _CACHE:
        _CACHE[key] = Builder(Sp, Ss).build()
    nc = _CACHE[key]
    maps = host_layout(inputs, Sp, Ss)
    res = run_bass_kernel_spmd(nc, maps, core_ids=list(range(NCORES)))
    yp = np.stack([np.asarray(res.results[c]["yp"], dtype=np.float32) for c in range(NCORES)], axis=0)
    ys = np.zeros((2, Ss, D), np.float32)
    for c in range(NCORES):
        sb, qi = c // 4, c % 4
        ys[sb, qi * Q:(qi + 1) * Q] = np.asarray(res.results[c]["yq"], dtype=np.float32)
    return yp, ys
```

```python
import math
import numpy as np
import concourse.bass as bass
import concourse.mybir as mybir
from concourse.bass_utils import run_bass_kernel_spmd

F32 = mybir.dt.float32
BF16 = mybir.dt.bfloat16
AF = mybir.ActivationFunctionType
ALU = mybir.AluOpType
AX = mybir.AxisListType

D = 1024
NH = 16
NKV = 4
HD = 64
FF = 2816
NFC = FF // 128
CW = 31
CP = 15
RMS_EPS = 1e-6
LN_EPS = 1e-5
NCORES = 8
import os
ROPE_ENG1 = os.environ.get("ROPE_ENG1", "dve")
DMA_Q2 = os.environ.get("DMA_Q2", "sp")


class Sched:
    def __init__(self, nc):
        self.nc = nc
        self.names = ["pe", "act", "dve", "pool", "sp"]
        self.sem = {n: nc.alloc_semaphore("s_" + n) for n in self.names}
        self.nops = {n: 0 for n in self.names}
        self.seen = {n: {} for n in self.names}
        self.prog = {n: [] for n in self.names}
        self.awaited = {n: set() for n in self.names}
        self.last_write = {}
        self.readers = {}
        self.dma_sems = {}
        self.nins = 0
        self.parent = {}
        self.children = {}

    def set_parent(self, fine, region):
        self.parent[fine] = region
        self.children.setdefault(region, []).append(fine)

    def _rel(self, k):
        out = [k]
        if k in self.parent:
            out.append(self.parent[k])
        out += self.children.get(k, [])
        return out

    def _deps(self, eng, reads, writes):
        deps = {}
        raw_self = [-1]

        def need(t, raw=False):
            src, v = t
            if src == eng:
                if raw_self[0] < v:
                    raw_self[0] = v
                return
            if deps.get(src, -1) < v:
                deps[src] = v

        for k0 in reads:
            for k in self._rel(k0):
                if k in self.last_write:
                    need(self.last_write[k], True)
        for k0 in writes:
            for k in self._rel(k0):
                if k in self.last_write:
                    need(self.last_write[k])
                for t in self.readers.get(k, {}).values():
                    need(t)
        if raw_self[0] >= 0 and eng not in ("pe", "sp"):
            deps[eng] = raw_self[0]
        out = []
        for src, v in deps.items():
            if self.seen[eng].get(src, -1) < v:
                self.seen[eng][src] = v
                out.append((src, v))
                if isinstance(src, str):
                    self.awaited[src].add(v)
        return out

    def _record(self, t, reads, writes):
        for k in reads:
            self.readers.setdefault(k, {})[t[0]] = t
        for k in writes:
            self.last_write[k] = t
            self.readers[k] = {}

    def op(self, eng, method, reads=(), writes=(), signal=True, **kw):
        waits = self._deps(eng, reads, writes)
        idx = self.nops[eng]
        self.nops[eng] = idx + 1
        self.prog[eng].append((waits, method, kw, ("e", idx)))
        self._record((eng, idx), reads, writes)
        self.nins += 1

    def dma(self, eng, out, in_, reads=(), writes=(), sem=None):
        waits = self._deps(eng, reads, writes)
        if sem not in self.dma_sems:
            self.dma_sems[sem] = [self.nc.alloc_semaphore("d_%d" % len(self.dma_sems)), 0]
        ent = self.dma_sems[sem]
        src = ("dma", sem)
        if ent[1] > 0 and self.seen[eng].get(src, -1) < ent[1]:
            self.seen[eng][src] = ent[1]
            waits.append((src, ent[1]))
        ent[1] += 16
        t = (src, ent[1])
        self.prog[eng].append((waits, "dma_start", dict(out=out, in_=in_), ("d", ent[0])))
        self.nops[eng] += 1
        self._record(t, reads, writes)
        self.nins += 1
        return t

    def final_wait(self, eng):
        ws = [(("dma", k), ent[1]) for k, ent in self.dma_sems.items() if ent[1] > 0]
        self.prog[eng].append((ws, None, None, None))

    def emit(self):
        nc = self.nc
        rank = {}
        for e in self.names:
            rank[e] = {idx: i + 1 for i, idx in enumerate(sorted(self.awaited[e]))}
        self.cnt = {e: len(rank[e]) for e in self.names}
        with nc.Block() as block:
            def run(name):
                def body(h):
                    for waits, method, kw, tag in self.prog[name]:
                        for src, v in waits:
                            if isinstance(src, str):
                                h.wait_ge(self.sem[src], rank[src][v])
                            else:
                                h.wait_ge(self.dma_sems[src[1]][0], v)
                        if method is not None:
                            ins = getattr(h, method)(**kw)
                            if tag[0] == "d":
                                ins.then_inc(tag[1], 16)
                            elif tag[1] in rank[name]:
                                ins.then_inc(self.sem[name], 1)
                return body
            block.tensor(run("pe"))
            block.scalar(run("act"))
            block.vector(run("dve"))
            block.gpsimd(run("pool"))
            block.sync(run("sp"))


class StopBuild(Exception):
    pass


class Builder:
    stop_at = None

    def phase(self, name):
        self.phases.append(name)
        if self.stop_at is not None and name == self.stop_at:
            raise StopBuild()

    def __init__(self, Sp, Ss, debug=False):
        self.phases = []
        self.Sp, self.Ss = Sp, Ss
        self.Q = Ss // 4
        self.debug = debug
        self.nc = bass.Bass("TRN2", target_bir_lowering=False)
        self.S = Sched(self.nc)
        self._ring_next = 0
        self._tp_next = 0
        self._ps_rr = 0

    def din(self, name, shape, dt=F32):
        return self.nc.dram_tensor(name, list(shape), dt, kind="ExternalInput").ap()

    def dout(self, name, shape, dt=F32):
        return self.nc.dram_tensor(name, list(shape), dt, kind="ExternalOutput").ap()

    def declare(self):
        Sp, Ss, Q = self.Sp, self.Ss, self.Q
        nc = self.nc
        self.xp = self.din("xp", [Sp, D])
        self.xs = self.din("xs", [Ss, D])
        self.xq = self.din("xq", [Q + 128, D])
        self.w_kv = self.din("w_kv", [8, 128, 512])
        self.w_q = self.din("w_q", [8, 128, 1024])
        self.w_o = self.din("w_o", [8, 128, 1024])
        self.w_g = self.din("w_g", [2, NFC, 128, 1024])
        self.w_u = self.din("w_u", [2, NFC, 128, 1024])
        self.w_d = self.din("w_d", [2, NFC, 128, 1024])
        self.w_ci = self.din("w_ci", [16, 128, 1024])
        self.w_co = self.din("w_co", [8, 128, 1024])
        self.gcols_d = self.din("gcols", [128, 4, 8])
        self.fvec_d = self.din("fvec", [128, 40])
        self.dww_d = self.din("dww", [128, 8, CW])
        self.gq_d = self.din("gq", [1, 64])
        self.gk_d = self.din("gk", [1, 64])
        self.gf_d = self.din("gf", [1, D])
        self.bout_d = self.din("bout", [1, D])
        self.rt_all_d = self.din("rt_all", [128, Ss // 128, 2, 16])
        self.rt_q_d = self.din("rt_q", [128, Q // 128 + 1, 2, 16])
        self.ct_d = self.din("ct", [128, 2, 16])
        self.flags_d = self.din("flags", [128, 2])
        self.yp = self.dout("yp", [Sp, D])
        self.yq = self.dout("yq", [Q, D])
        self.scr = nc.dram_tensor("wscr", [8 + 8 + 8 + 6 * NFC + 16 + 8, 128, 1024], BF16).ap()

    def alloc(self):
        nc = self.nc
        Ss = self.Ss
        A = nc.alloc_sbuf_tensor
        self.nkc_max = Ss // 128
        self.KT = [A("KT%d" % m, [128, Ss], BF16) for m in range(2)]
        self.V = A("Vst", [128, self.nkc_max, 384], BF16)
        self.xsl = [A("x%d" % i, [128, 4, D], F32) for i in range(2)]
        self.h_tm = A("h_tm", [128, 4, D], BF16)
        self.hT = A("hT", [128, 8, 512], BF16)
        self.R = A("R", [128, NFC * 512], BF16)
        self.PT = A("PT", [128, 6, 512], BF16)
        self.U = [A("U%d" % i, [128, 8, 512 + 2 * CP], BF16) for i in range(2)]
        self.UH = A("UH", [128, 8, 128], BF16)
        self.ring = A("ring", [128, 6, 1024], BF16)
        self.stage = A("stage", [128, 1024], F32)
        self.rtg = A("rtg", [128, 2, 4, 2, 16], F32)
        self._rtg_next = 0
        self.ct = A("ct_s", [128, 2, 16], F32)
        self.gf = A("gf_s", [128, D], F32)
        self.gq = A("gq_s", [128, 64], F32)
        self.gk = A("gk_s", [128, 64], F32)
        self.gcols = A("gcols_s", [128, 4, 8], F32)
        self.fvec = A("fvec_s", [128, 40], F32)
        self.dww = A("dww_s", [128, 8, CW], F32)
        self.flags = A("flags_s", [128, 2], F32)
        self.bout_b = A("bout_b", [1, D], BF16)
        self.ones_b = A("ones_b", [1, 128], BF16)
        self.ident = A("ident", [128, 128], BF16)
        self.nhalf = A("nhalf", [128, 16], F32)
        self.st = A("stats", [128, 128], F32)
        self.rec = A("rec", [128, 512], F32)
        self.oraw = A("oraw", [128, 2, 512], F32)
        self.ps = [nc.alloc_psum_tensor("ps%d" % i, [128, 512], F32) for i in range(8)]
        Rf = self.R[:, :]
        self.qT = Rf[:, 0:4096].rearrange("p (c t) -> p c t", c=8)
        self.oT = Rf[:, 4096:8192].rearrange("p (c t) -> p c t", c=8)
        self.hidT = Rf[:, 0:NFC * 512].rearrange("p (c t) -> p c t", c=NFC)
        self.cT = Rf[:, 0:4096].rearrange("p (c t) -> p c t", c=8)
        self.ktm2 = [Rf[:, i * 1024:(i + 1) * 1024].rearrange("p (s c) -> p s c", s=4) for i in range(2)]
        self.qTh = Rf[:, 0:1024].rearrange("p (c t) -> p c t", c=8)
        self.oTh = Rf[:, 4096:5120].rearrange("p (c t) -> p c t", c=8)
        self.Rf = Rf
        hi = Rf[:, 4096:NFC * 512].bitcast(F32)
        self.sq = hi[:, 0:1024]
        self.qg = hi[:, 1024:2048]
        self.rtmp = hi[:, 2048:3584]
        self.diag = [Rf[:, 0:CW * 128].rearrange("p (j c) -> p j c", j=CW),
                     Rf[:, 4096:4096 + CW * 128].rearrange("p (j c) -> p j c", j=CW)]
        self.diag_keys = [["Ra"], ["Rb", "Rc"]]
        self.extra_stage = False
        self.v_stage_on = False
        c0 = self.Sp // 128
        nfree = (self.nkc_max - c0) * 384 // 2
        self.vstage = []
        if nfree >= 1024 and os.environ.get("VSTAGE"):
            vf = self.V[:, c0:, :].rearrange("p a b -> p (a b)").bitcast(F32)
            self.vstage = [(("vst", i), vf[:, i * 1024:(i + 1) * 1024]) for i in range(min(12, nfree // 1024))]
        self._stage_rr = 0
        self._cast_rr = 0
        ptf = self.PT[:, :, :].rearrange("p a b -> p (a b)")
        self.sg = [ptf[:, 0:1024].bitcast(F32), ptf[:, 1024:2048].bitcast(F32)]
        S = self.S
        for ss in range(4):
            for cb in range(2):
                S.set_parent(("h_tmh", ss, cb), ("h_tm", ss))
        for c in range(8):
            S.set_parent(("qT", c), "Ra")
            S.set_parent(("oT", c), "Rb" if c < 4 else "Rc")
        S.set_parent(("ktm", 0), "Ra")
        S.set_parent(("ktm", 1), "Ra")
        self.tmp_keys = []
        for kname, reg in (("k_sq", "Rb"), ("k_qg", "Rc"), (("k_qgh", 0), "Rc"), (("k_qgh", 1), "Rc"),
                           (("q_sq", 0), "Rb"), (("q_sq", 1), "Rb"), (("q_sq", 2), "Rc"), (("q_sq", 3), "Rd"),
                           ("q_qg", "Rc"), (("q_qgh", 0), "Rc"), (("q_qgh", 1), "Rc")):
            S.set_parent(kname, reg)
            self.tmp_keys.append(kname)
        self.hi = hi
        self.hTf = self.hT[:, :, :].rearrange("p a b -> p (a b)").bitcast(F32)
        for i, reg in ((0, "Rb"), (1, "Rb"), (2, "Rc"), (3, "Rc")):
            S.set_parent(("qq_sq1", i), reg)
            self.tmp_keys.append(("qq_sq1", i))
            S.set_parent(("qq_qg1", i), "hT")
            S.set_parent((("qq_qgh1", i), 0), "hT")
            S.set_parent((("qq_qgh1", i), 1), "hT")
        for kname in ("qq_sq0", "qq_qg0", ("qq_qgh0", 0), ("qq_qgh0", 1), ("qq_rt", 0), ("qq_rt", 1), ("qq_rt", 2)):
            S.set_parent(kname, "Rd")
            self.tmp_keys.append(kname)
        for j in range(6):
            for pfx in ("k_rt", "q_rt"):
                S.set_parent((pfx, j), "Rd")
                self.tmp_keys.append((pfx, j))
        for hf in range(2):
            for i in range(3):
                S.set_parent(("rtmp", hf, i), "Rd")
        print("sbuf bytes remaining", nc.sbuf_bytes_remaining)

    def make_chunks(self):
        self.chunks = {}
        idx = [0]

        def add(name, src, gain=None, scale=None, ncols=1024):
            self.chunks[name] = dict(src=src, gain=gain, scale=scale, ncols=ncols, scr=self.scr[idx[0]], done=False)
            idx[0] += 1

        for kc in range(8):
            add(("kv", kc), self.w_kv[kc], gain=("pp", 0, kc), ncols=512)
        for kc in range(8):
            add(("q", kc), self.w_q[kc], gain=("pp", 0, kc))
        for pr in range(8):
            add(("o", pr), self.w_o[pr])
        for l in range(2):
            for fc in range(NFC):
                add(("g", l, fc), self.w_g[l, fc], gain=("kc", 1 if l == 0 else 3))
                add(("u", l, fc), self.w_u[l, fc], gain=("kc", 1 if l == 0 else 3))
                add(("d", l, fc), self.w_d[l, fc])
        for oc in range(16):
            add(("ci", oc), self.w_ci[oc], gain=("kc", 2))
        for cc in range(8):
            add(("co", cc), self.w_co[cc])

    def fetch(self, name):
        S = self.S
        ch = self.chunks[name]
        slot = self._ring_next
        self._ring_next = (slot + 1) % 6
        n = ch["ncols"]
        dst = self.ring[:, slot, 0:n]
        key = ("ring", slot)
        if not ch["done"]:
            ch["done"] = True
            bufs = [("stage", self.stage[:, :])]
            if self.v_stage_on:
                bufs += self.vstage
            if self.extra_stage:
                bufs += [(("x", 1, j), self.xsl[1][:, j, :]) for j in range(4)]
            self._stage_rr = (self._stage_rr + 1) % len(bufs)
            skey, sbuf = bufs[self._stage_rr]
            self._cast_rr ^= 1
            ce = "pool" if self._cast_rr else "dve"
            S.dma("sp", out=sbuf[:, 0:n], in_=ch["src"], writes=[skey], sem=("stg", self._stage_rr))
            g = ch["gain"]
            if g is None:
                S.op(ce, "tensor_copy", reads=[skey], writes=[key], out=dst, in_=sbuf[:, 0:n])
            elif g[0] == "pp":
                S.op(ce, "tensor_scalar", reads=[skey, "gcols"], writes=[key], out=dst, in0=sbuf[:, 0:n],
                     scalar1=self.gcols[:, g[1], g[2]:g[2] + 1], scalar2=None, op0=ALU.mult)
            else:
                gb = self.gcols[:, g[1], :].unsqueeze(2).broadcast_to([128, 8, 128])
                S.op(ce, "tensor_tensor", reads=[skey, "gcols"], writes=[key],
                     out=dst.rearrange("p (k c) -> p k c", k=8), in0=sbuf.rearrange("p (k c) -> p k c", k=8),
                     in1=gb, op=ALU.mult)
            S.dma(DMA_Q2, out=ch["scr"][:, 0:n], in_=dst, reads=[key], writes=[("scr", name)], sem=("scr", slot))
        else:
            S.dma("sp", out=dst, in_=ch["scr"][:, 0:n], reads=[("scr", name)], writes=[key], sem=("ringld", slot))
        return key, dst

    def load_rt(self, src, j0, n):
        i = self._rtg_next
        self._rtg_next = 1 - i
        key = ("rtg", i)
        self.S.dma("sp", out=self.rtg[:, i, 0:n], in_=src[:, j0:j0 + n], writes=[key], sem=key)
        return self.rtg[:, i], key

    def bank(self):
        b = self._ps_rr
        self._ps_rr = (b + 1) % 8
        return b

    def setup(self):
        S = self.S
        ld = [("ct", self.ct, self.ct_d),
              ("gcols", self.gcols, self.gcols_d), ("fvec", self.fvec, self.fvec_d), ("dww", self.dww, self.dww_d),
              ("flags", self.flags, self.flags_d)]
        for i, (k, sb, dr) in enumerate(ld):
            S.dma("sp", out=sb[:], in_=dr, writes=[k], sem=("setup", i % 4))
        S.dma("sp", out=self.gf[:], in_=self.gf_d.partition_broadcast(128), writes=["gf"], sem=("setup", 0))
        S.dma("sp", out=self.gq[:], in_=self.gq_d.partition_broadcast(128), writes=["gq"], sem=("setup", 1))
        S.dma("sp", out=self.gk[:], in_=self.gk_d.partition_broadcast(128), writes=["gk"], sem=("setup", 2))
        S.dma("sp", out=self.stage[0:1, :], in_=self.bout_d, writes=["stage"], sem="stage")
        S.op("pool", "tensor_copy", reads=["stage"], writes=["bout_b"], out=self.bout_b[:], in_=self.stage[0:1, :])
        S.op("pool", "memset", writes=["ones_b"], ap=self.ones_b[:], constant=1.0)
        S.op("pool", "memset", writes=["nhalf"], ap=self.nhalf[:], constant=-0.5)
        identf = self.stage[:, 0:128]
        S.op("pool", "memset", reads=["stage"], writes=["stage"], ap=identf, constant=0.0)
        S.op("pool", "affine_select", reads=["stage"], writes=["stage"], out=identf, in_=identf,
             pattern=[[-1, 128]], compare_op=ALU.not_equal, fill=1.0, base=0, channel_multiplier=1)
        S.op("pool", "tensor_copy", reads=["stage"], writes=["ident"], out=self.ident[:], in_=identf)
        S.op("pool", "tensor_scalar", reads=["gq"], writes=["gq"], out=self.gq[:], in0=self.gq[:], scalar1=1.0 / math.sqrt(HD),
             scalar2=None, op0=ALU.mult)
        c1 = self.Sp // 128 if self.vstage else self.nkc_max
        S.op("pool", "memset", writes=["V"], ap=self.V[:, 0:c1, 64:128], constant=1.0)
        S.op("pool", "memset", writes=["V"], ap=self.V[:, 0:c1, 256:320], constant=1.0)

    def xk(self, slot, s):
        return ("x", slot, s)

    def norm_to_hT(self, xsl, slot, nsub, evac_eng="dve", scale_all_act=False):
        self.norm_scale(xsl, slot, nsub, scale_all_act)
        self.norm_transposes(nsub, evac_eng)

    def norm_scale(self, xsl, slot, nsub, scale_all_act=False):
        S = self.S
        for s in range(nsub):
            S.op("act", "activation", reads=[self.xk(slot, s)], writes=[("h_tm", s), ("ss", s)], out=self.h_tm[:, s, :], in_=xsl[:, s, :],
                 func=AF.Square, accum_out=self.st[:, s:s + 1])
        rr = self.st[:, 8:8 + nsub]
        S.op("pool", "tensor_scalar", reads=[("ss", s) for s in range(nsub)], writes=["rr"], out=rr, in0=self.st[:, 0:nsub],
             scalar1=1.0 / D, scalar2=RMS_EPS, op0=ALU.mult, op1=ALU.add)
        S.op("pool", "tensor_tensor", reads=["rr", "nhalf"], writes=["rr"], out=rr, in0=rr, in1=self.nhalf[:, 0:nsub], op=ALU.pow)
        for s in range(nsub):
            if s % 2 == 0 or scale_all_act:
                S.op("act", "activation", reads=[self.xk(slot, s), "rr"], writes=[("h_tm", s)], out=self.h_tm[:, s, :],
                     in_=xsl[:, s, :], func=AF.Copy, scale=self.st[:, 8 + s:9 + s])
            else:
                S.op("dve", "tensor_scalar", reads=[self.xk(slot, s), "rr"], writes=[("h_tm", s)], out=self.h_tm[:, s, :],
                     in0=xsl[:, s, :], scalar1=self.st[:, 8 + s:9 + s], scalar2=None, op0=ALU.mult)
        self.phase("n_scale")

    def norm_transposes(self, nsub, evac_eng="dve"):
        S = self.S
        for s in range(nsub):
            b = self.bank()
            pk = ("ps", b)
            pv = self.ps[b][:, :].bitcast(BF16)
            for c in range(8):
                S.op("pe", "transpose", reads=[("h_tm", s), "ident"], writes=[pk], out=pv[:, c * 128:(c + 1) * 128],
                     in_=self.h_tm[:, s, c * 128:(c + 1) * 128], identity=self.ident[:])
            dst = self.hT[:, :, s * 128:(s + 1) * 128]
            src = pv.rearrange("p (c t) -> p c t", c=8)
            if evac_eng == "dve":
                S.op("dve", "tensor_copy", reads=[pk], writes=["hT"], out=dst, in_=src)
            else:
                S.op("act", "activation", reads=[pk], writes=["hT"], out=dst, in_=src, func=AF.Copy)
        self.phase("n_hT")

    def transposes(self, srcs, srckeys, nchunks, dst, dstkey, evac_eng="dve", act_kw=None, col0=0):
        S = self.S
        nsub = len(srcs)
        T = nsub * 128
        for c in range(nchunks):
            b = self.bank()
            pk = ("ps", b)
            pv = self.ps[b][:, :].bitcast(BF16)
            for s in range(nsub):
                S.op("pe", "transpose", reads=[srckeys[s], "ident"], writes=[pk], signal=(s == nsub - 1),
                     out=pv[:, s * 128:(s + 1) * 128], in_=srcs[s][:, c * 128:(c + 1) * 128], identity=self.ident[:])
            if act_kw is not None:
                kw = act_kw(c)
                S.op("act", "activation", reads=[pk] + kw.pop("reads", []), writes=[dstkey], out=dst[:, c, col0:col0 + T],
                     in_=pv[:, 0:T], **kw)
            elif evac_eng == "dve":
                S.op("dve", "tensor_copy", reads=[pk], writes=[dstkey], out=dst[:, c, col0:col0 + T], in_=pv[:, 0:T])
            else:
                S.op("act", "activation", reads=[pk], writes=[dstkey], out=dst[:, c, col0:col0 + T], in_=pv[:, 0:T],
                     func=AF.Copy)

    def tm_proj(self, actT, actkey, chunk_names, nsub, ncb, evac, bias=False):
        S = self.S
        banks = [[self.bank() for cb in range(ncb)] for s in range(nsub)]
        nch = len(chunk_names)
        if bias:
            for s in range(nsub):
                for cb in range(ncb):
                    S.op("pe", "matmul", reads=["ones_b", "bout_b"], writes=[("ps", banks[s][cb])], signal=False,
                         out=self.ps[banks[s][cb]][:, :], lhsT=self.ones_b[0:1, :], rhs=self.bout_b[0:1, cb * 512:(cb + 1) * 512],
                         start=True, stop=False)
        for c, name in enumerate(chunk_names):
            wkey, w = self.fetch(name)
            for s in range(nsub):
                for cb in range(ncb):
                    last = (s == nsub - 1 and cb == ncb - 1)
                    ak = actkey(c) if callable(actkey) else (actkey if isinstance(actkey, list) else [actkey])
                    S.op("pe", "matmul", reads=ak + [wkey], writes=[("ps", banks[s][cb])], signal=last,
                         out=self.ps[banks[s][cb]][:, :], lhsT=actT[:, c, s * 128:(s + 1) * 128],
                         rhs=w[:, cb * 512:(cb + 1) * 512], start=(c == 0 and not bias), stop=(c == nch - 1))
        self.phase("tm_mm")
        for s in range(nsub):
            for cb in range(ncb):
                evac(s, cb, banks[s][cb])
                self.phase("tm_evac1")

    def resid_evac(self, xsl, slot):
        def evac(s, cb, b):
            xkey = self.xk(slot, s)
            self.S.op("dve", "tensor_tensor", reads=[("ps", b), xkey], writes=[xkey], out=xsl[:, s, cb * 512:(cb + 1) * 512],
                      in0=self.ps[b][:, :], in1=xsl[:, s, cb * 512:(cb + 1) * 512], op=ALU.add)
        return evac

    def tmp_fence(self):
        self.S.op("dve", "memset", writes=list(self.tmp_keys), ap=self.st[:, 120:121], constant=0.0)

    def qk_front(self, pviews, pkeys, sq, ksq, qg, kqg, gbc, gkey):
        S = self.S
        off = 0
        for pv, pk in zip(pviews, pkeys):
            w = pv.shape[1]
            S.op("act", "activation", reads=[pk], writes=[ksq], out=sq[:, off:off + w], in_=pv, func=AF.Square)
            S.op("dve", "tensor_tensor", reads=[pk, gkey, ksq], writes=[kqg], out=qg[:, off:off + w].rearrange("p (h d) -> p h d", d=64),
                 in0=pv.rearrange("p (h d) -> p h d", d=64), in1=gbc[:, :].unsqueeze(1).broadcast_to([128, w // 64, 64]), op=ALU.mult)
            off += w

    def qk_chain(self, G, H, sq, ksq, qg, kqg, kh, tbufs, tkeys, rt, rtkey, out, outkeys):
        S = self.S
        GH = G * H
        ssq = self.st[:, 16:16 + GH]
        rs = self.st[:, 32:32 + GH]
        S.op("dve", "tensor_reduce", reads=[ksq], writes=["ssq"], out=ssq, in_=sq.rearrange("p (h d) -> p h d", d=64),
             axis=AX.X, op=ALU.add)
        S.op("pool", "tensor_scalar", reads=["ssq"], writes=["rs"], out=rs, in0=ssq, scalar1=1.0 / HD, scalar2=RMS_EPS,
             op0=ALU.mult, op1=ALU.add)
        S.op("pool", "tensor_tensor", reads=["rs", "nhalf"], writes=["rs"], out=rs, in0=rs, in1=self.nhalf[:, 0:GH], op=ALU.pow)
        v6 = qg.rearrange("p (g h r f d) -> p g h r f d", g=G, h=H, r=2, f=2, d=16)
        shp = [128, G, H, 16]
        for half, eng in ((0, "dve"), (1, ROPE_ENG1)):
            Aa = v6[:, :, :, half, 0, :]
            Bb = v6[:, :, :, half, 1, :]
            if half == 0:
                C = rt[:, :, 0, :].unsqueeze(2).broadcast_to(shp)
                Sn = rt[:, :, 1, :].unsqueeze(2).broadcast_to(shp)
                tk = [rtkey]
            else:
                C = self.ct[:, 0, :].unsqueeze(1).unsqueeze(1).broadcast_to(shp)
                Sn = self.ct[:, 1, :].unsqueeze(1).unsqueeze(1).broadcast_to(shp)
                tk = ["ct"]
            t = [b.rearrange("p (g h d) -> p g h d", g=G, h=H) for b in tbufs[half]]
            k = tkeys[half]
            qk = (kh, half)
            S.op(eng, "tensor_tensor", reads=[kqg, qk] + tk, writes=[k[0]], out=t[0], in0=Aa, in1=C, op=ALU.mult)
            S.op(eng, "tensor_tensor", reads=[kqg, qk] + tk, writes=[k[1]], out=t[1], in0=Aa, in1=Sn, op=ALU.mult)
            S.op(eng, "tensor_tensor", reads=[kqg, qk] + tk, writes=[k[2]], out=t[2], in0=Bb, in1=Sn, op=ALU.mult)
            S.op(eng, "tensor_tensor", reads=[kqg, k[0], k[2]], writes=[qk], out=Aa, in0=t[0], in1=t[2], op=ALU.subtract)
            S.op(eng, "tensor_tensor", reads=[kqg, qk] + tk, writes=[k[0]], out=t[0], in0=Bb, in1=C, op=ALU.mult)
            S.op(eng, "tensor_tensor", reads=[kqg, k[0], k[1]], writes=[qk], out=Bb, in0=t[0], in1=t[1], op=ALU.add)
        S.op("dve", "tensor_tensor", reads=[kqg, (kh, 0), (kh, 1), "rs"], writes=list(outkeys) + [kqg],
             out=out.rearrange("p g (h d) -> p g h d", d=64), in0=qg.rearrange("p (g h d) -> p g h d", g=G, h=H),
             in1=rs.rearrange("p (g h) -> p g h", g=G).unsqueeze(3).broadcast_to([128, G, H, 64]), op=ALU.mult)

    def qk_post(self, pviews, pkeys, G, H, gbc, gkey, rt, rtkey, out, outkeys):
        n = G * H * 64
        GH = G * H
        sq, qg = self.sq[:, 0:n], self.qg[:, 0:n]
        self.qk_front(pviews, pkeys, sq, "k_sq", qg, "k_qg", gbc, gkey)
        tb = [[self.rtmp[:, (hf * 3 + i) * 256:(hf * 3 + i) * 256 + GH * 16] for i in range(3)] for hf in range(2)]
        tkk = [[("k_rt", hf * 3 + i) for i in range(3)] for hf in range(2)]
        self.qk_chain(G, H, sq, "k_sq", qg, "k_qg", "k_qgh", tb, tkk, rt, rtkey, out, outkeys)

    def q_bufs(self, s_, cb):
        hi = self.hi
        tb = [hi[:, 3072 + i * 128:3072 + (i + 1) * 128] for i in range(3)]
        tk = [("qq_rt", i) for i in range(3)]
        if cb == 1:
            return (hi[:, s_ * 512:(s_ + 1) * 512], ("qq_sq1", s_), self.hTf[:, s_ * 512:(s_ + 1) * 512], ("qq_qg1", s_),
                    ("qq_qgh1", s_), [tb, tb], [tk, tk])
        return (hi[:, 2048:2560], "qq_sq0", hi[:, 2560:3072], "qq_qg0", "qq_qgh0", [tb, tb], [tk, tk])

    def k_transposes(self, g, nsub):
        S = self.S
        T = nsub * 128
        kt = self.ktm2[g % 2]
        kk = ("ktm", g % 2)
        for m in range(2):
            b = self.bank()
            pk = ("ps", b)
            pv = self.ps[b][:, :].bitcast(BF16)
            for s in range(nsub):
                S.op("pe", "transpose", reads=[kk, "ident"], writes=[pk],
                     out=pv[:, s * 128:(s + 1) * 128], in_=kt[:, s, m * 128:(m + 1) * 128], identity=self.ident[:])
            S.op("dve", "tensor_copy", reads=[pk], writes=["KT"], out=self.KT[m][:, g * 512:g * 512 + T], in_=pv[:, 0:T])

    def stage1(self, xsrc, S_len):
        S = self.S
        ngrp = (S_len + 511) // 512
        pending = None
        self.tmp_fence()

        def load_x(g):
            nsub_ = min(4, (S_len - g * 512) // 128)
            for s in range(nsub_):
                S.dma("sp", out=self.xsl[g % 2][:, s, :], in_=xsrc[g * 512 + s * 128:g * 512 + (s + 1) * 128, :],
                      writes=[self.xk(g % 2, s)], sem=("xld", g % 2, s))

        load_x(0)
        self.norm_scale(self.xsl[0], 0, min(4, S_len // 128), True)
        for g in range(ngrp):
            nsub = min(4, (S_len - g * 512) // 128)
            slot = g % 2
            xsl = self.xsl[slot]
            rtv, rtk = self.load_rt(self.rt_all_d, g * 4, nsub)
            if g + 1 < ngrp:
                load_x(g + 1)
            self.norm_transposes(nsub, "act")
            kbanks = []
            self.tm_proj(self.hT, "hT", [("kv", kc) for kc in range(8)], nsub, 1, lambda s, cb, b, kb=kbanks: kb.append(b))
            if pending is not None:
                self.k_transposes(*pending)
            if g + 1 < ngrp:
                self.norm_scale(self.xsl[(g + 1) % 2], (g + 1) % 2, min(4, (S_len - (g + 1) * 512) // 128), True)
            for s, b in enumerate(kbanks):
                pv = self.ps[b]
                pk = ("ps", b)
                j = g * 4 + s
                for (dst0, src0) in ((0, 256), (128, 320), (192, 448), (320, 384)):
                    S.op("act", "activation", reads=[pk], writes=["V"], out=self.V[:, j, dst0:dst0 + 64], in_=pv[:, src0:src0 + 64],
                         func=AF.Copy)
            self.qk_post([self.ps[bb][:, 0:256] for bb in kbanks], [("ps", bb) for bb in kbanks], nsub, 4, self.gk, "gk",
                         rtv[:, 0:nsub], rtk, self.ktm2[g % 2][:, 0:nsub, :], [("ktm", g % 2)])
            pending = (g, nsub)
            self.phase("s1")
        self.k_transposes(*pending)

    def ffn(self, l, xsl, slot, nsub):
        S = self.S
        T = nsub * 128
        self.norm_to_hT(xsl, slot, nsub)
        for fc in range(NFC):
            kg, wg = self.fetch(("g", l, fc))
            ku, wu = self.fetch(("u", l, fc))
            bg, bu = self.bank(), self.bank()
            for (kk, w, b) in ((kg, wg, bg), (ku, wu, bu)):
                for kc in range(8):
                    S.op("pe", "matmul", reads=["hT", kk], writes=[("ps", b)], signal=(kc == 7), out=self.ps[b][:, 0:T],
                         lhsT=w[:, kc * 128:(kc + 1) * 128], rhs=self.hT[:, kc, 0:T], start=(kc == 0), stop=(kc == 7))
            sgi = fc % 2
            S.op("act", "activation", reads=[("ps", bg)], writes=[("PT", 2 * sgi), ("PT", 2 * sgi + 1)], out=self.sg[sgi][:, 0:T], in_=self.ps[bg][:, 0:T],
                 func=AF.Silu)
            S.op("dve", "tensor_tensor", reads=[("ps", bu), ("PT", 2 * sgi), ("PT", 2 * sgi + 1)], writes=[self.hkey(fc)], out=self.hidT[:, fc, 0:T],
                 in0=self.ps[bu][:, 0:T], in1=self.sg[sgi][:, 0:T], op=ALU.mult)
        self.tm_proj(self.hidT, ["Ra", "Rb", "Rc", "Rd"], [("d", l, fc) for fc in range(NFC)], nsub, 2, self.resid_evac(xsl, slot))

    def attention(self, nsub, nkc, hook=None):
        S = self.S
        T = nsub * 128
        sbanks = [(0, 1), (2, 3), (4, 5)]
        oa, ob = 6, 7
        if nsub == 1:
            N = 512
            units = []
            for m in range(2):
                qa = self.Rf[0:64, m * 512:(m + 1) * 512]
                qb = self.Rf[64:128, m * 512:(m + 1) * 512]
                units.append((m, qa, qb, lambda lo, hi, m=m: self.Rf[lo:hi, 4096 + m * 512:4096 + (m + 1) * 512], "Rb"))
        else:
            N = T
            units = []
            for pr in range(8):
                units.append((pr // 4, self.qT[0:64, pr, 0:T], self.qT[64:128, pr, 0:T],
                              lambda lo, hi, pr=pr: self.oT[lo:hi, pr, 0:T], ("oT", pr)))
        items = [(u, c) for u in range(len(units)) for c in range(nkc)]

        def vviews(m, c):
            if m == 0:
                return self.V[:, c, 0:128], self.V[:, c, 64:192]
            return self.V[:, c, 256:384], self.V[:, c, 192:320]

        def s_mm(k):
            u, c = items[k]
            m, qa, qb, _, _ = units[u]
            ba, bb = sbanks[k % 3]
            qk = "Ra" if nsub == 1 else ("qT", u)
            if hook is not None and c == 0:
                hook(u, (ba, bb))
            S.op("pe", "matmul", reads=["KT", qk], writes=[("ps", ba)], out=self.ps[ba][:, 0:N],
                 lhsT=self.KT[m][0:64, c * 128:(c + 1) * 128], rhs=qa, start=True, stop=True, tile_position=(0, 0))
            S.op("pe", "matmul", reads=["KT", qk], writes=[("ps", bb)], out=self.ps[bb][:, 0:N],
                 lhsT=self.KT[m][64:128, c * 128:(c + 1) * 128], rhs=qb, start=True, stop=True, tile_position=(64, 0))

        def finish(u):
            m, _, _, oview, ok = units[u]
            a_o, b_o = (0, 64) if m == 0 else (64, 0)
            S.op("dve", "tensor_copy", reads=[("ps", oa)], writes=[("oraw", 0)], out=self.oraw[:, 0, 0:N], in_=self.ps[oa][:, 0:N])
            S.op("dve", "tensor_copy", reads=[("ps", ob)], writes=[("oraw", 1)], out=self.oraw[:, 1, 0:N], in_=self.ps[ob][:, 0:N])
            for h, o_off in enumerate((a_o, b_o)):
                s_off = 64 - o_off
                rk = ("rec", o_off)
                S.op("dve", "reciprocal", reads=[("oraw", h)], writes=[rk], out=self.rec[o_off:o_off + 64, 0:N],
                     in_=self.oraw[s_off:s_off + 64, h, 0:N])
                S.op("dve", "tensor_tensor", reads=[("oraw", h), rk], writes=[ok], out=oview(o_off, o_off + 64),
                     in0=self.oraw[o_off:o_off + 64, h, 0:N], in1=self.rec[o_off:o_off + 64, 0:N], op=ALU.mult)

        def pv_mm(k):
            u, c = items[k]
            m = units[u][0]
            ba, bb = sbanks[k % 3]
            pa = (k % 3) * 2
            va, vb = vviews(m, c)
            S.op("act", "activation", reads=[("ps", ba)], writes=[("PT", pa)], out=self.PT[:, pa, 0:N], in_=self.ps[ba][:, 0:N],
                 func=AF.Exp)
            S.op("act", "activation", reads=[("ps", bb)], writes=[("PT", pa + 1)], out=self.PT[:, pa + 1, 0:N],
                 in_=self.ps[bb][:, 0:N], func=AF.Exp)
            S.op("pe", "matmul", reads=["V", ("PT", pa)], writes=[("ps", oa)], out=self.ps[oa][:, 0:N],
                 lhsT=va, rhs=self.PT[:, pa, 0:N], start=(c == 0), stop=(c == nkc - 1))
            S.op("pe", "matmul", reads=["V", ("PT", pa + 1)], writes=[("ps", ob)], out=self.ps[ob][:, 0:N],
                 lhsT=vb, rhs=self.PT[:, pa + 1, 0:N], start=(c == 0), stop=(c == nkc - 1))
            if c == nkc - 1:
                finish(u)

        n = len(items)
        for k in range(min(2, n)):
            s_mm(k)
        for k in range(n):
            if k + 2 < n:
                s_mm(k + 2)
            pv_mm(k)

    def bank_t(self):
        b = self._t_rr % 6
        self._t_rr += 1
        return b

    def okey(self, pr):
        return "Rb" if pr < 4 else "Rc"

    def hkey(self, fc):
        return "Ra" if fc < 8 else ("Rb" if fc < 12 else ("Rc" if fc < 16 else "Rd"))

    def bank_o(self, pr):
        return (6, 7)

    def S_a(self, xsrc_ap, slot, nsub, nkc, rt_tab, rt_key, rt_j0, u_dst, u_key, u_col0):
        S = self.S
        T = nsub * 128
        xsl = self.xsl[slot]
        for s in range(nsub):
            S.dma("sp", out=xsl[:, s, :], in_=xsrc_ap[s * 128:(s + 1) * 128, :], writes=[self.xk(slot, s)], sem=("xld", slot, s))
        rtv, rtk = self.load_rt(rt_tab, rt_j0, nsub)
        self.norm_to_hT(xsl, slot, nsub)

        qb = {}
        self.tm_proj(self.hT, "hT", [("q", kc) for kc in range(8)], nsub, 2, lambda s_, cb, b: qb.__setitem__((s_, cb), b))

        def q_front(s_, cb):
            b = qb[(s_, cb)]
            sq, ksq, qg, kqg, kh, tb, tk = self.q_bufs(s_, cb)
            self.qk_front([self.ps[b][:, :]], [("ps", b)], sq, ksq, qg, kqg, self.gq, "gq")

        def q_chain(s_, cb):
            sq, ksq, qg, kqg, kh, tb, tk = self.q_bufs(s_, cb)
            self.qk_chain(1, 8, sq, ksq, qg, kqg, kh, tb, tk, rtv[:, s_:s_ + 1], rtk,
                          self.h_tm[:, s_:s_ + 1, cb * 512:(cb + 1) * 512], [("h_tmh", s_, cb)])

        def q_transposes(cb, free_banks=None):
            qdst = self.qTh if nsub == 1 else self.qT
            for c in range(4 * cb, 4 * cb + 4):
                if free_banks is None:
                    b = qb[((c - 4 * cb) % nsub, cb)]
                else:
                    b = free_banks[c % len(free_banks)]
                pk = ("ps", b)
                pv = self.ps[b][:, :].bitcast(BF16)
                for s_ in range(nsub):
                    S.op("pe", "transpose", reads=[("h_tmh", s_, cb), "ident"], writes=[pk], out=pv[:, s_ * 128:(s_ + 1) * 128],
                         in_=self.h_tm[:, s_, c * 128:(c + 1) * 128], identity=self.ident[:])
                S.op("dve", "tensor_copy", reads=[pk], writes=[("qT", c)], out=qdst[:, c, 0:T], in_=pv[:, 0:T])

        self._t_rr = 0
        self.tmp_fence()
        for s_ in range(nsub):
            q_front(s_, 1)
        for s_ in range(nsub):
            q_front(s_, 0)
            q_chain(s_, 0)
        q_transposes(0)
        for s_ in range(nsub):
            q_chain(s_, 1)
        self.phase("q")
        self._ps_rr = 0
        if nsub == 1:
            q_transposes(1)
            self.attention(nsub, nkc)
        else:
            self.attention(nsub, nkc, hook=lambda u, fb: q_transposes(1, fb) if u == 4 else None)
        self.phase("att")
        self.tm_proj(self.oTh if nsub == 1 else self.oT, ["Rb", "Rc"] if nsub == 1 else (lambda c: [("oT", c)]),
                     [("o", pr) for pr in range(8)], nsub, 2, self.resid_evac(xsl, slot))
        self.phase("wo")
        self.ffn(0, xsl, slot, nsub)
        self.phase("ffn0")
        self.norm_to_hT(xsl, slot, nsub)
        for cc in range(8):
            ka, wa = self.fetch(("ci", cc))
            kg, wg = self.fetch(("ci", 8 + cc))
            ba, bg = self.bank(), self.bank()
            for (kk, w, b) in ((ka, wa, ba), (kg, wg, bg)):
                for kc in range(8):
                    S.op("pe", "matmul", reads=["hT", kk], writes=[("ps", b)], signal=(kc == 7), out=self.ps[b][:, 0:T],
                         lhsT=w[:, kc * 128:(kc + 1) * 128], rhs=self.hT[:, kc, 0:T], start=(kc == 0), stop=(kc == 7))
            sgi = cc % 2
            S.op("act", "activation", reads=[("ps", bg), "fvec"], writes=[("PT", 2 * sgi), ("PT", 2 * sgi + 1)], out=self.sg[sgi][:, 0:T],
                 in_=self.ps[bg][:, 0:T], func=AF.Sigmoid, bias=self.fvec[:, 8 + cc:9 + cc])
            S.op("dve", "scalar_tensor_tensor", reads=[("ps", ba), ("PT", 2 * sgi), ("PT", 2 * sgi + 1), "fvec"], writes=[u_key],
                 out=u_dst[:, cc, u_col0:u_col0 + T], in0=self.ps[ba][:, 0:T], scalar=self.fvec[:, cc:cc + 1],
                 in1=self.sg[sgi][:, 0:T], op0=ALU.add, op1=ALU.mult)

    def S_b(self, slot, uslot, nsub, ydst_ap):
        S = self.S
        T = nsub * 128
        xsl = self.xsl[slot]
        U = self.U[uslot]
        ukey = ("U", uslot)
        for cc in range(8):
            dg = self.diag[cc % 2]
            dk = self.diag_keys[cc % 2]
            S.op("dve", "tensor_tensor", reads=["ident", "dww"], writes=dk, out=dg,
                 in0=self.ident[:, :].unsqueeze(1).broadcast_to([128, CW, 128]),
                 in1=self.dww[:, cc, :].unsqueeze(2).broadcast_to([128, CW, 128]), op=ALU.mult)
            b = self.bank()
            for j in range(CW):
                S.op("pe", "matmul", reads=dk + [ukey], writes=[("ps", b)], out=self.ps[b][:, 0:T],
                     lhsT=dg[:, j, :], rhs=U[:, cc, j:j + T], start=(j == 0), stop=(j == CW - 1))
            S.op("act", "activation", reads=[("ps", b), "fvec"], writes=["hT"], out=self.hT[:, cc, 0:T], in_=self.ps[b][:, 0:T],
                 func=AF.Identity, bias=self.fvec[:, 16 + cc:17 + cc])
        self.phase("conv")
        lnb = []
        for s in range(nsub):
            b = self.bank()
            pk = ("ps", b)
            pv = self.ps[b][:, :].bitcast(BF16)
            for cc in range(8):
                S.op("pe", "transpose", reads=["hT", "ident"], writes=[pk], out=pv[:, cc * 128:(cc + 1) * 128],
                     in_=self.hT[:, cc, s * 128:(s + 1) * 128], identity=self.ident[:])
            S.op("act", "activation", reads=[pk], writes=[("h_tm", s), ("lns", s)], out=self.h_tm[:, s, :], in_=pv, func=AF.Identity,
                 accum_out=self.st[:, 48 + s:49 + s])
            S.op("act", "activation", reads=[pk], writes=[("h_tm", s), ("lns", s)], out=self.h_tm[:, s, :], in_=pv, func=AF.Square,
                 accum_out=self.st[:, 52 + s:53 + s])
            lnb.append((pk, pv))
        lk = [("lns", s) for s in range(nsub)]
        mean = self.st[:, 56:56 + nsub]
        var = self.st[:, 60:60 + nsub]
        msq = self.st[:, 64:64 + nsub]
        S.op("dve", "tensor_scalar", reads=lk, writes=["lnm"], out=mean, in0=self.st[:, 48:48 + nsub], scalar1=1.0 / D,
             scalar2=None, op0=ALU.mult)
        S.op("dve", "tensor_tensor", reads=["lnm"], writes=["lnq"], out=msq, in0=mean, in1=mean, op=ALU.mult)
        S.op("dve", "scalar_tensor_tensor", reads=lk + ["lnq"], writes=["lnv"], out=var, in0=self.st[:, 52:52 + nsub], scalar=1.0 / D,
             in1=msq, op0=ALU.mult, op1=ALU.subtract)
        S.op("pool", "tensor_scalar", reads=["lnv"], writes=["lnv"], out=var, in0=var, scalar1=LN_EPS, scalar2=None, op0=ALU.add)
        S.op("pool", "tensor_tensor", reads=["lnv", "nhalf"], writes=["lnv"], out=var, in0=var, in1=self.nhalf[:, 0:nsub], op=ALU.pow)
        for s in range(nsub):
            pk, pv = lnb[s]
            S.op("dve", "tensor_scalar", reads=[pk, "lnm", "lnv"], writes=[("h_tm", s)], out=self.h_tm[:, s, :], in0=pv,
                 scalar1=self.st[:, 56 + s:57 + s], scalar2=self.st[:, 60 + s:61 + s], op0=ALU.subtract, op1=ALU.mult)
        self.transposes([self.h_tm[:, s, :] for s in range(nsub)], [("h_tm", s) for s in range(nsub)], 8, self.hT, "hT",
                        act_kw=lambda c: dict(func=AF.Silu, scale=self.fvec[:, 24 + c:25 + c], bias=self.fvec[:, 32 + c:33 + c],
                                              reads=["fvec"]))
        self.phase("ln")
        self.tm_proj(self.hT, "hT", [("co", cc) for cc in range(8)], nsub, 2, self.resid_evac(xsl, slot), bias=True)
        self.phase("co")
        self.ffn(1, xsl, slot, nsub)
        self.phase("ffn1")
        for s in range(nsub):
            S.op("act", "activation", reads=[self.xk(slot, s)], writes=[("h_tm", s), ("fs", s)], out=self.h_tm[:, s, :], in_=xsl[:, s, :],
                 func=AF.Square, accum_out=self.st[:, 68 + s:69 + s])
        fr = self.st[:, 72:72 + nsub]
        S.op("pool", "tensor_scalar", reads=[("fs", s) for s in range(nsub)], writes=["fr"], out=fr, in0=self.st[:, 68:68 + nsub],
             scalar1=1.0 / D, scalar2=RMS_EPS, op0=ALU.mult, op1=ALU.add)
        S.op("pool", "tensor_tensor", reads=["fr", "nhalf"], writes=["fr"], out=fr, in0=fr, in1=self.nhalf[:, 0:nsub], op=ALU.pow)
        for s in range(nsub):
            xkey = self.xk(slot, s)
            S.op("dve", "scalar_tensor_tensor", reads=[xkey, "fr", "gf"], writes=[xkey], out=xsl[:, s, :], in0=xsl[:, s, :],
                 scalar=self.st[:, 72 + s:73 + s], in1=self.gf[:], op0=ALU.mult, op1=ALU.mult)
            S.dma("sp", out=ydst_ap[s * 128:(s + 1) * 128, :], in_=xsl[:, s, :], reads=[xkey], sem=("yst", slot, s))
        self.phase("fin")

    def run_pass(self, xkv, S_len, xown, n_own, rt_tab, rt_key, ydst, has_halo):
        S = self.S
        nkc = S_len // 128
        self.stage1(xkv, S_len)
        ntile = n_own // 512
        if has_halo:
            self.S_a(xown[n_own:n_own + 128, :], 0, 1, nkc, rt_tab, rt_key, n_own // 128, self.UH, "UH", 0)
            S.op("pool", "tensor_scalar", reads=["UH", "flags"], writes=[("U", 0)], out=self.U[0][:, :, 0:CP],
                 in0=self.UH[:, :, 64 - CP:64], scalar1=self.flags[:, 0:1], scalar2=None, op0=ALU.mult)
        else:
            S.op("pool", "memset", writes=[("U", 0)], ap=self.U[0][:, :, 0:CP], constant=0.0)
        for i in range(ntile + 1):
            if i < ntile:
                us = i % 2
                self.extra_stage = (i == 0 and not has_halo)
                self.S_a(xown[i * 512:(i + 1) * 512, :], i % 2, 4, nkc, rt_tab, rt_key, i * 4, self.U[us], ("U", us), CP)
                self.extra_stage = False
                if i > 0:
                    S.op("pool", "tensor_copy", reads=[("U", 1 - us)], writes=[("U", us)], out=self.U[us][:, :, 0:CP],
                         in_=self.U[1 - us][:, :, 512:512 + CP])
                    S.op("pool", "tensor_copy", reads=[("U", us)], writes=[("U", 1 - us)], out=self.U[1 - us][:, :, 512 + CP:512 + 2 * CP],
                         in_=self.U[us][:, :, CP:2 * CP])
            if i == ntile:
                us = (ntile - 1) % 2
                if has_halo:
                    S.op("pool", "tensor_scalar", reads=["UH", "flags"], writes=[("U", us)], out=self.U[us][:, :, 512 + CP:512 + 2 * CP],
                         in0=self.UH[:, :, 64:64 + CP], scalar1=self.flags[:, 1:2], scalar2=None, op0=ALU.mult)
                else:
                    S.op("pool", "memset", writes=[("U", us)], ap=self.U[us][:, :, 512 + CP:512 + 2 * CP], constant=0.0)
            if i >= 1:
                j = i - 1
                self.S_b(j % 2, j % 2, 4, ydst[j * 512:(j + 1) * 512, :])

    def build(self):
        self.declare()
        self.alloc()
        self.make_chunks()
        self.setup()
        try:
            self.phase("setup")
            self.v_stage_on = True
            self.run_pass(self.xp, self.Sp, self.xp, self.Sp, self.rt_all_d, "rt", self.yp, False)
            self.v_stage_on = False
            if self.vstage:
                vk = ["V"] + [k for k, _ in self.vstage]
                c0 = self.Sp // 128
                self.S.op("pool", "memset", writes=vk, ap=self.V[:, c0:, 64:128], constant=1.0)
                self.S.op("pool", "memset", writes=vk, ap=self.V[:, c0:, 256:320], constant=1.0)
            self.phase("passP")
            self.run_pass(self.xs, self.Ss, self.xq, self.Q, self.rt_q_d, "rt", self.yq, True)
        except StopBuild:
            print("STOPPED at", self.stop_at)
        self.S.final_wait("pool")
        self.S.final_wait("sp")
        self.S.emit()
        print("instructions", self.S.nins, dict(self.S.nops), "signals", dict(self.S.cnt))
        return self.nc


def _rope_tabs(pos_rows):
    inv = (10000.0 ** (-np.arange(0, 32, 2, dtype=np.float64) / 32.0))
    ang = pos_rows[..., None].astype(np.float64) * inv
    return np.cos(ang).astype(np.float32), np.sin(ang).astype(np.float32)


def host_layout(inp, Sp, Ss):
    Q = Ss // 4
    f = lambda a: np.ascontiguousarray(np.asarray(a, dtype=np.float32))
    wqkv = f(inp["w_qkv"])[0]
    heads = [8 * m + 4 * hh + i for m in range(2) for i in range(4) for hh in range(2)]
    heads_o = []
    for m in range(2):
        for i in range(4):
            a, b = 8 * m + i, 8 * m + 4 + i
            heads_o += [a, b] if m == 0 else [b, a]
    qcols = np.concatenate([np.arange(h * 64, (h + 1) * 64) for h in heads])
    orows = np.concatenate([np.arange(h * 64, (h + 1) * 64) for h in heads_o])
    shared = {}
    shared["w_q"] = f(wqkv[:, :1024][:, qcols].reshape(8, 128, 1024))
    shared["w_kv"] = f(wqkv[:, 1024:].reshape(8, 128, 512))
    shared["w_o"] = f(f(inp["w_o"])[0][orows].reshape(8, 128, 1024))

    def up(w):
        Fo = w.shape[1]
        return f(w.reshape(8, 128, Fo // 128, 128).transpose(2, 1, 0, 3).reshape(Fo // 128, 128, 1024))
    shared["w_g"] = f(np.stack([up(f(inp["w_gate"])[l]) for l in range(2)]))
    shared["w_u"] = f(np.stack([up(f(inp["w_up"])[l]) for l in range(2)]))
    shared["w_d"] = f(f(inp["w_down"]).reshape(2, NFC, 128, 1024))
    shared["w_ci"] = up(f(inp["conv_w_in"])[0])
    shared["w_co"] = f(f(inp["conv_w_out"])[0].reshape(8, 128, 1024))
    col = lambda v: f(v).reshape(-1, 128).T
    gcols = np.stack([col(f(inp["attn_norm_g"])[0]), col(f(inp["ffn_norm_g"])[0]), col(f(inp["conv_norm_g"])[0]),
                      col(f(inp["ffn_norm_g"])[1])], axis=1)
    shared["gcols"] = f(gcols)
    shared["fvec"] = f(np.concatenate([col(f(inp["conv_b_in"])[0]), col(f(inp["dw_b"])[0]), col(f(inp["conv_ln_g"])[0]),
                                       col(f(inp["conv_ln_b"])[0])], axis=1))
    shared["dww"] = f(f(inp["dw_w"])[0].T.reshape(8, 128, CW).transpose(1, 0, 2))
    shared["gq"] = f(inp["q_norm_g"]).reshape(1, 64)
    shared["gk"] = f(inp["k_norm_g"]).reshape(1, 64)
    shared["gf"] = f(inp["final_norm_g"]).reshape(1, D)
    shared["bout"] = f(inp["conv_b_out"]).reshape(1, D)
    p = np.arange(128)
    j = np.arange(Ss // 128)
    rows = 2 * j[None, :] + (p[:, None] >= 64)
    c, s = _rope_tabs(rows)
    shared["rt_all"] = f(np.stack([c, s], axis=2))
    c, s = _rope_tabs(p % 64)
    shared["ct"] = f(np.stack([c, s], axis=1))
    xp = f(inp["x_prompt"])
    xs = f(inp["x_sample"])
    maps = []
    for core in range(NCORES):
        sb, qi = core // 4, core % 4
        qs = qi * Q
        m = dict(shared)
        m["xp"] = xp[core]
        m["xs"] = xs[sb]
        halo = np.zeros((128, D), np.float32)
        if qs > 0:
            halo[0:64] = xs[sb, qs - 64:qs]
        if qs + Q < Ss:
            halo[64:128] = xs[sb, qs + Q:qs + Q + 64]
        m["xq"] = f(np.concatenate([xs[sb, qs:qs + Q], halo], axis=0))
        jq = np.arange(Q // 128)
        rows = np.concatenate([2 * (jq[None, :] + qs // 128) + (p[:, None] >= 64),
                               np.where(p[:, None] >= 64, (qs + Q) // 64, qs // 64 - 1)], axis=1)
        c, s = _rope_tabs(rows)
        m["rt_q"] = f(np.stack([c, s], axis=2))
        fl = np.zeros((128, 2), np.float32)
        fl[:, 0] = 1.0 if qs > 0 else 0.0
        fl[:, 1] = 1.0 if qs + Q < Ss else 0.0
        m["flags"] = fl
        maps.append(m)
    return maps


_CACHE = {}


def kernel(**inputs):
    xp = np.asarray(inputs["x_prompt"])
    xs = np.asarray(inputs["x_sample"])
    Sp, Ss = xp.shape[1], xs.shape[1]
    Q = Ss // 4
    key = (Sp, Ss)
    if key not in _CACHE:
        _CACHE[key] = Builder(Sp, Ss).build()
    nc = _CACHE[key]
    maps = host_layout(inputs, Sp, Ss)
    res = run_bass_kernel_spmd(nc, maps, core_ids=list(range(NCORES)))
    yp = np.stack([np.asarray(res.results[c]["yp"], dtype=np.float32) for c in range(NCORES)], axis=0)
    ys = np.zeros((2, Ss, D), np.float32)
    for c in range(NCORES):
        sb, qi = c // 4, c % 4
        ys[sb, qi * Q:(qi + 1) * Q] = np.asarray(res.results[c]["yq"], dtype=np.float32)
    return yp, ys
```

```python
import math
import numpy as np
import concourse.bass as bass
import concourse.mybir as mybir
from concourse.bass_utils import run_bass_kernel_spmd

F32 = mybir.dt.float32
BF16 = mybir.dt.bfloat16
AF = mybir.ActivationFunctionType
ALU = mybir.AluOpType
AX = mybir.AxisListType

D = 1024
NH = 16
NKV = 4
HD = 64
FF = 2816
NFC = FF // 128
CW = 31
CP = 15
RMS_EPS = 1e-6
LN_EPS = 1e-5
NCORES = 8
import os
ROPE_ENG1 = os.environ.get("ROPE_ENG1", "dve")
DMA_Q2 = os.environ.get("DMA_Q2", "sp")


class Sched:
    def __init__(self, nc):
        self.nc = nc
        self.names = ["pe", "act", "dve", "pool", "sp"]
        self.sem = {n: nc.alloc_semaphore("s_" + n) for n in self.names}
        self.nops = {n: 0 for n in self.names}
        self.seen = {n: {} for n in self.names}
        self.prog = {n: [] for n in self.names}
        self.awaited = {n: set() for n in self.names}
        self.last_write = {}
        self.readers = {}
        self.dma_sems = {}
        self.nins = 0
        self.parent = {}
        self.children = {}

    def set_parent(self, fine, region):
        self.parent[fine] = region
        self.children.setdefault(region, []).append(fine)

    def _rel(self, k):
        out = [k]
        if k in self.parent:
            out.append(self.parent[k])
        out += self.children.get(k, [])
        return out

    def _deps(self, eng, reads, writes):
        deps = {}
        raw_self = [-1]

        def need(t, raw=False):
            src, v = t
            if src == eng:
                if raw_self[0] < v:
                    raw_self[0] = v
                return
            if deps.get(src, -1) < v:
                deps[src] = v

        for k0 in reads:
            for k in self._rel(k0):
                if k in self.last_write:
                    need(self.last_write[k], True)
        for k0 in writes:
            for k in self._rel(k0):
                if k in self.last_write:
                    need(self.last_write[k])
                for t in self.readers.get(k, {}).values():
                    need(t)
        if raw_self[0] >= 0 and eng not in ("pe", "sp"):
            deps[eng] = raw_self[0]
        out = []
        for src, v in deps.items():
            if self.seen[eng].get(src, -1) < v:
                self.seen[eng][src] = v
                out.append((src, v))
                if isinstance(src, str):
                    self.awaited[src].add(v)
        return out

    def _record(self, t, reads, writes):
        for k in reads:
            self.readers.setdefault(k, {})[t[0]] = t
        for k in writes:
            self.last_write[k] = t
            self.readers[k] = {}

    def op(self, eng, method, reads=(), writes=(), signal=True, **kw):
        waits = self._deps(eng, reads, writes)
        idx = self.nops[eng]
        self.nops[eng] = idx + 1
        self.prog[eng].append((waits, method, kw, ("e", idx)))
        self._record((eng, idx), reads, writes)
        self.nins += 1

    def dma(self, eng, out, in_, reads=(), writes=(), sem=None):
        waits = self._deps(eng, reads, writes)
        if sem not in self.dma_sems:
            self.dma_sems[sem] = [self.nc.alloc_semaphore("d_%d" % len(self.dma_sems)), 0]
        ent = self.dma_sems[sem]
        src = ("dma", sem)
        if ent[1] > 0 and self.seen[eng].get(src, -1) < ent[1]:
            self.seen[eng][src] = ent[1]
            waits.append((src, ent[1]))
        ent[1] += 16
        t = (src, ent[1])
        self.prog[eng].append((waits, "dma_start", dict(out=out, in_=in_), ("d", ent[0])))
        self.nops[eng] += 1
        self._record(t, reads, writes)
        self.nins += 1
        return t

    def final_wait(self, eng):
        ws = [(("dma", k), ent[1]) for k, ent in self.dma_sems.items() if ent[1] > 0]
        self.prog[eng].append((ws, None, None, None))

    def emit(self):
        nc = self.nc
        rank = {}
        for e in self.names:
            rank[e] = {idx: i + 1 for i, idx in enumerate(sorted(self.awaited[e]))}
        self.cnt = {e: len(rank[e]) for e in self.names}
        with nc.Block() as block:
            def run(name):
                def body(h):
                    for waits, method, kw, tag in self.prog[name]:
                        for src, v in waits:
                            if isinstance(src, str):
                                h.wait_ge(self.sem[src], rank[src][v])
                            else:
                                h.wait_ge(self.dma_sems[src[1]][0], v)
                        if method is not None:
                            ins = getattr(h, method)(**kw)
                            if tag[0] == "d":
                                ins.then_inc(tag[1], 16)
                            elif tag[1] in rank[name]:
                                ins.then_inc(self.sem[name], 1)
                return body
            block.tensor(run("pe"))
            block.scalar(run("act"))
            block.vector(run("dve"))
            block.gpsimd(run("pool"))
            block.sync(run("sp"))


class StopBuild(Exception):
    pass


class Builder:
    stop_at = None

    def phase(self, name):
        self.phases.append(name)
        if self.stop_at is not None and name == self.stop_at:
            raise StopBuild()

    def __init__(self, Sp, Ss, debug=False):
        self.phases = []
        self.Sp, self.Ss = Sp, Ss
        self.Q = Ss // 4
        self.debug = debug
        self.nc = bass.Bass("TRN2", target_bir_lowering=False)
        self.S = Sched(self.nc)
        self._ring_next = 0
        self._tp_next = 0
        self._ps_rr = 0

    def din(self, name, shape, dt=F32):
        return self.nc.dram_tensor(name, list(shape), dt, kind="ExternalInput").ap()

    def dout(self, name, shape, dt=F32):
        return self.nc.dram_tensor(name, list(shape), dt, kind="ExternalOutput").ap()

    def declare(self):
        Sp, Ss, Q = self.Sp, self.Ss, self.Q
        nc = self.nc
        self.xp = self.din("xp", [Sp, D])
        self.xs = self.din("xs", [Ss, D])
        self.xq = self.din("xq", [Q + 128, D])
        self.w_kv = self.din("w_kv", [8, 128, 512])
        self.w_q = self.din("w_q", [8, 128, 1024])
        self.w_o = self.din("w_o", [8, 128, 1024])
        self.w_g = self.din("w_g", [2, NFC, 128, 1024])
        self.w_u = self.din("w_u", [2, NFC, 128, 1024])
        self.w_d = self.din("w_d", [2, NFC, 128, 1024])
        self.w_ci = self.din("w_ci", [16, 128, 1024])
        self.w_co = self.din("w_co", [8, 128, 1024])
        self.gcols_d = self.din("gcols", [128, 4, 8])
        self.fvec_d = self.din("fvec", [128, 40])
        self.dww_d = self.din("dww", [128, 8, CW])
        self.gq_d = self.din("gq", [1, 64])
        self.gk_d = self.din("gk", [1, 64])
        self.gf_d = self.din("gf", [1, D])
        self.bout_d = self.din("bout", [1, D])
        self.rt_all_d = self.din("rt_all", [128, Ss // 128, 2, 16])
        self.rt_q_d = self.din("rt_q", [128, Q // 128 + 1, 2, 16])
        self.ct_d = self.din("ct", [128, 2, 16])
        self.flags_d = self.din("flags", [128, 2])
        self.yp = self.dout("yp", [Sp, D])
        self.yq = self.dout("yq", [Q, D])
        self.scr = nc.dram_tensor("wscr", [8 + 8 + 8 + 6 * NFC + 16 + 8, 128, 1024], BF16).ap()

    def alloc(self):
        nc = self.nc
        Ss = self.Ss
        A = nc.alloc_sbuf_tensor
        self.nkc_max = Ss // 128
        self.KT = [A("KT%d" % m, [128, Ss], BF16) for m in range(2)]
        self.V = A("Vst", [128, self.nkc_max, 384], BF16)
        self.xsl = [A("x%d" % i, [128, 4, D], F32) for i in range(2)]
        self.h_tm = A("h_tm", [128, 4, D], BF16)
        self.hT = A("hT", [128, 8, 512], BF16)
        self.R = A("R", [128, NFC * 512], BF16)
        self.PT = A("PT", [128, 6, 512], BF16)
        self.U = [A("U%d" % i, [128, 8, 512 + 2 * CP], BF16) for i in range(2)]
        self.UH = A("UH", [128, 8, 128], BF16)
        self.ring = A("ring", [128, 6, 1024], BF16)
        self.stage = A("stage", [128, 1024], F32)
        self.rtg = A("rtg", [128, 2, 4, 2, 16], F32)
        self._rtg_next = 0
        self.ct = A("ct_s", [128, 2, 16], F32)
        self.gf = A("gf_s", [128, D], F32)
        self.gq = A("gq_s", [128, 64], F32)
        self.gk = A("gk_s", [128, 64], F32)
        self.gcols = A("gcols_s", [128, 4, 8], F32)
        self.fvec = A("fvec_s", [128, 40], F32)
        self.dww = A("dww_s", [128, 8, CW], F32)
        self.flags = A("flags_s", [128, 2], F32)
        self.bout_b = A("bout_b", [1, D], BF16)
        self.ones_b = A("ones_b", [1, 128], BF16)
        self.ident = A("ident", [128, 128], BF16)
        self.nhalf = A("nhalf", [128, 16], F32)
        self.st = A("stats", [128, 128], F32)
        self.rec = A("rec", [128, 512], F32)
        self.oraw = A("oraw", [128, 2, 512], F32)
        self.ps = [nc.alloc_psum_tensor("ps%d" % i, [128, 512], F32) for i in range(8)]
        Rf = self.R[:, :]
        self.qT = Rf[:, 0:4096].rearrange("p (c t) -> p c t", c=8)
        self.oT = Rf[:, 4096:8192].rearrange("p (c t) -> p c t", c=8)
        self.hidT = Rf[:, 0:NFC * 512].rearrange("p (c t) -> p c t", c=NFC)
        self.cT = Rf[:, 0:4096].rearrange("p (c t) -> p c t", c=8)
        self.ktm2 = [Rf[:, i * 1024:(i + 1) * 1024].rearrange("p (s c) -> p s c", s=4) for i in range(2)]
        self.qTh = Rf[:, 0:1024].rearrange("p (c t) -> p c t", c=8)
        self.oTh = Rf[:, 4096:5120].rearrange("p (c t) -> p c t", c=8)
        self.Rf = Rf
        hi = Rf[:, 4096:NFC * 512].bitcast(F32)
        self.sq = hi[:, 0:1024]
        self.qg = hi[:, 1024:2048]
        self.rtmp = hi[:, 2048:3584]
        self.diag = [Rf[:, 0:CW * 128].rearrange("p (j c) -> p j c", j=CW),
                     Rf[:, 4096:4096 + CW * 128].rearrange("p (j c) -> p j c", j=CW)]
        self.diag_keys = [["Ra"], ["Rb", "Rc"]]
        self.extra_stage = False
        self.pending_st = []
        self.v_stage_on = False
        c0 = self.Sp // 128
        nfree = (self.nkc_max - c0) * 384 // 2
        self.vstage = []
        if nfree >= 1024 and not os.environ.get("NO_VSTAGE"):
            vf = self.V[:, c0:, :].rearrange("p a b -> p (a b)").bitcast(F32)
            self.vstage = [(("vst", i), vf[:, i * 1024:(i + 1) * 1024]) for i in range(min(12, nfree // 1024))]
        self._stage_rr = 0
        self._cast_rr = 0
        ptf = self.PT[:, :, :].rearrange("p a b -> p (a b)")
        self.sg = [ptf[:, 0:1024].bitcast(F32), ptf[:, 1024:2048].bitcast(F32)]
        S = self.S
        for ss in range(4):
            for cb in range(2):
                S.set_parent(("h_tmh", ss, cb), ("h_tm", ss))
        for c in range(8):
            S.set_parent(("qT", c), "Ra")
            S.set_parent(("oT", c), "Rb" if c < 4 else "Rc")
        S.set_parent(("ktm", 0), "Ra")
        S.set_parent(("ktm", 1), "Ra")
        self.tmp_keys = []
        for kname, reg in (("k_sq", "Rb"), ("k_qg", "Rc"), (("k_qgh", 0), "Rc"), (("k_qgh", 1), "Rc"),
                           (("q_sq", 0), "Rb"), (("q_sq", 1), "Rb"), (("q_sq", 2), "Rc"), (("q_sq", 3), "Rd"),
                           ("q_qg", "Rc"), (("q_qgh", 0), "Rc"), (("q_qgh", 1), "Rc")):
            S.set_parent(kname, reg)
            self.tmp_keys.append(kname)
        self.hi = hi
        self.hTf = self.hT[:, :, :].rearrange("p a b -> p (a b)").bitcast(F32)
        for i, reg in ((0, "Rb"), (1, "Rb"), (2, "Rc"), (3, "Rc")):
            S.set_parent(("qq_sq1", i), reg)
            self.tmp_keys.append(("qq_sq1", i))
            S.set_parent(("qq_qg1", i), "hT")
            S.set_parent((("qq_qgh1", i), 0), "hT")
            S.set_parent((("qq_qgh1", i), 1), "hT")
        for kname in ("qq_sq0", "qq_qg0", ("qq_qgh0", 0), ("qq_qgh0", 1), ("qq_rt", 0), ("qq_rt", 1), ("qq_rt", 2)):
            S.set_parent(kname, "Rd")
            self.tmp_keys.append(kname)
        for j in range(6):
            for pfx in ("k_rt", "q_rt"):
                S.set_parent((pfx, j), "Rd")
                self.tmp_keys.append((pfx, j))
        for hf in range(2):
            for i in range(3):
                S.set_parent(("rtmp", hf, i), "Rd")
        print("sbuf bytes remaining", nc.sbuf_bytes_remaining)

    def make_chunks(self):
        self.chunks = {}
        idx = [0]

        def add(name, src, gain=None, scale=None, ncols=1024):
            self.chunks[name] = dict(src=src, gain=gain, scale=scale, ncols=ncols, scr=self.scr[idx[0]], done=False)
            idx[0] += 1

        for kc in range(8):
            add(("kv", kc), self.w_kv[kc], gain=("pp", 0, kc), ncols=512)
        for kc in range(8):
            add(("q", kc), self.w_q[kc], gain=("pp", 0, kc))
        for pr in range(8):
            add(("o", pr), self.w_o[pr])
        for l in range(2):
            for fc in range(NFC):
                add(("g", l, fc), self.w_g[l, fc], gain=("kc", 1 if l == 0 else 3))
                add(("u", l, fc), self.w_u[l, fc], gain=("kc", 1 if l == 0 else 3))
                add(("d", l, fc), self.w_d[l, fc])
        for oc in range(16):
            add(("ci", oc), self.w_ci[oc], gain=("kc", 2))
        for cc in range(8):
            add(("co", cc), self.w_co[cc])

    def flush_stores(self, keep=0):
        while len(self.pending_st) > keep:
            scr_ap, src_ap, key, name, slot = self.pending_st.pop(0)
            self.S.dma("sp", out=scr_ap, in_=src_ap, reads=[key], writes=[("scr", name)], sem=("scr", slot))
            self.chunks[name]["stored"] = True

    def fetch(self, name):
        S = self.S
        ch = self.chunks[name]
        self.flush_stores(0 if ch["done"] else 2)
        slot = self._ring_next
        self._ring_next = (slot + 1) % 6
        n = ch["ncols"]
        dst = self.ring[:, slot, 0:n]
        key = ("ring", slot)
        if not ch["done"]:
            ch["done"] = True
            bufs = [("stage", self.stage[:, :])]
            if self.v_stage_on:
                bufs += self.vstage
            if self.extra_stage:
                bufs += [(("x", 1, j), self.xsl[1][:, j, :]) for j in range(4)]
            self._stage_rr = (self._stage_rr + 1) % len(bufs)
            skey, sbuf = bufs[self._stage_rr]
            self._cast_rr ^= 1
            ce = "pool" if self._cast_rr else "dve"
            S.dma("sp", out=sbuf[:, 0:n], in_=ch["src"], writes=[skey], sem=("stg", self._stage_rr))
            g = ch["gain"]
            if g is None:
                S.op(ce, "tensor_copy", reads=[skey], writes=[key], out=dst, in_=sbuf[:, 0:n])
            elif g[0] == "pp":
                S.op(ce, "tensor_scalar", reads=[skey, "gcols"], writes=[key], out=dst, in0=sbuf[:, 0:n],
                     scalar1=self.gcols[:, g[1], g[2]:g[2] + 1], scalar2=None, op0=ALU.mult)
            else:
                gb = self.gcols[:, g[1], :].unsqueeze(2).broadcast_to([128, 8, 128])
                S.op(ce, "tensor_tensor", reads=[skey, "gcols"], writes=[key],
                     out=dst.rearrange("p (k c) -> p k c", k=8), in0=sbuf.rearrange("p (k c) -> p k c", k=8),
                     in1=gb, op=ALU.mult)
            self.pending_st.append((ch["scr"][:, 0:n], dst, key, name, slot))
        else:
            S.dma("sp", out=dst, in_=ch["scr"][:, 0:n], reads=[("scr", name)], writes=[key], sem=("ringld", slot))
        return key, dst

    def load_rt(self, src, j0, n):
        i = self._rtg_next
        self._rtg_next = 1 - i
        key = ("rtg", i)
        self.S.dma("sp", out=self.rtg[:, i, 0:n], in_=src[:, j0:j0 + n], writes=[key], sem=key)
        return self.rtg[:, i], key

    def bank(self):
        b = self._ps_rr
        self._ps_rr = (b + 1) % 8
        return b

    def setup(self):
        S = self.S
        ld = [("ct", self.ct, self.ct_d),
              ("gcols", self.gcols, self.gcols_d), ("fvec", self.fvec, self.fvec_d), ("dww", self.dww, self.dww_d),
              ("flags", self.flags, self.flags_d)]
        for i, (k, sb, dr) in enumerate(ld):
            S.dma("sp", out=sb[:], in_=dr, writes=[k], sem=("setup", i % 4))
        S.dma("sp", out=self.gf[:], in_=self.gf_d.partition_broadcast(128), writes=["gf"], sem=("setup", 0))
        S.dma("sp", out=self.gq[:], in_=self.gq_d.partition_broadcast(128), writes=["gq"], sem=("setup", 1))
        S.dma("sp", out=self.gk[:], in_=self.gk_d.partition_broadcast(128), writes=["gk"], sem=("setup", 2))
        S.dma("sp", out=self.stage[0:1, :], in_=self.bout_d, writes=["stage"], sem="stage")
        S.op("pool", "tensor_copy", reads=["stage"], writes=["bout_b"], out=self.bout_b[:], in_=self.stage[0:1, :])
        S.op("pool", "memset", writes=["ones_b"], ap=self.ones_b[:], constant=1.0)
        S.op("pool", "memset", writes=["nhalf"], ap=self.nhalf[:], constant=-0.5)
        identf = self.stage[:, 0:128]
        S.op("pool", "memset", reads=["stage"], writes=["stage"], ap=identf, constant=0.0)
        S.op("pool", "affine_select", reads=["stage"], writes=["stage"], out=identf, in_=identf,
             pattern=[[-1, 128]], compare_op=ALU.not_equal, fill=1.0, base=0, channel_multiplier=1)
        S.op("pool", "tensor_copy", reads=["stage"], writes=["ident"], out=self.ident[:], in_=identf)
        S.op("pool", "tensor_scalar", reads=["gq"], writes=["gq"], out=self.gq[:], in0=self.gq[:], scalar1=1.0 / math.sqrt(HD),
             scalar2=None, op0=ALU.mult)
        c1 = self.Sp // 128 if self.vstage else self.nkc_max
        S.op("pool", "memset", writes=["V"], ap=self.V[:, 0:c1, 64:128], constant=1.0)
        S.op("pool", "memset", writes=["V"], ap=self.V[:, 0:c1, 256:320], constant=1.0)

    def xk(self, slot, s):
        return ("x", slot, s)

    def norm_to_hT(self, xsl, slot, nsub, evac_eng="dve", scale_all_act=False):
        self.norm_scale(xsl, slot, nsub, scale_all_act)
        self.norm_transposes(nsub, evac_eng)

    def norm_scale(self, xsl, slot, nsub, scale_all_act=False):
        S = self.S
        for s in range(nsub):
            S.op("act", "activation", reads=[self.xk(slot, s)], writes=[("h_tm", s), ("ss", s)], out=self.h_tm[:, s, :], in_=xsl[:, s, :],
                 func=AF.Square, accum_out=self.st[:, s:s + 1])
        rr = self.st[:, 8:8 + nsub]
        S.op("pool", "tensor_scalar", reads=[("ss", s) for s in range(nsub)], writes=["rr"], out=rr, in0=self.st[:, 0:nsub],
             scalar1=1.0 / D, scalar2=RMS_EPS, op0=ALU.mult, op1=ALU.add)
        S.op("pool", "tensor_tensor", reads=["rr", "nhalf"], writes=["rr"], out=rr, in0=rr, in1=self.nhalf[:, 0:nsub], op=ALU.pow)
        for s in range(nsub):
            if s % 2 == 0 or scale_all_act:
                S.op("act", "activation", reads=[self.xk(slot, s), "rr"], writes=[("h_tm", s)], out=self.h_tm[:, s, :],
                     in_=xsl[:, s, :], func=AF.Copy, scale=self.st[:, 8 + s:9 + s])
            else:
                S.op("dve", "tensor_scalar", reads=[self.xk(slot, s), "rr"], writes=[("h_tm", s)], out=self.h_tm[:, s, :],
                     in0=xsl[:, s, :], scalar1=self.st[:, 8 + s:9 + s], scalar2=None, op0=ALU.mult)
        self.phase("n_scale")

    def norm_transposes(self, nsub, evac_eng="dve"):
        S = self.S
        for s in range(nsub):
            b = self.bank()
            pk = ("ps", b)
            pv = self.ps[b][:, :].bitcast(BF16)
            for c in range(8):
                S.op("pe", "transpose", reads=[("h_tm", s), "ident"], writes=[pk], out=pv[:, c * 128:(c + 1) * 128],
                     in_=self.h_tm[:, s, c * 128:(c + 1) * 128], identity=self.ident[:])
            dst = self.hT[:, :, s * 128:(s + 1) * 128]
            src = pv.rearrange("p (c t) -> p c t", c=8)
            if evac_eng == "dve":
                S.op("dve", "tensor_copy", reads=[pk], writes=["hT"], out=dst, in_=src)
            else:
                S.op("act", "activation", reads=[pk], writes=["hT"], out=dst, in_=src, func=AF.Copy)
        self.phase("n_hT")

    def transposes(self, srcs, srckeys, nchunks, dst, dstkey, evac_eng="dve", act_kw=None, col0=0):
        S = self.S
        nsub = len(srcs)
        T = nsub * 128
        for c in range(nchunks):
            b = self.bank()
            pk = ("ps", b)
            pv = self.ps[b][:, :].bitcast(BF16)
            for s in range(nsub):
                S.op("pe", "transpose", reads=[srckeys[s], "ident"], writes=[pk], signal=(s == nsub - 1),
                     out=pv[:, s * 128:(s + 1) * 128], in_=srcs[s][:, c * 128:(c + 1) * 128], identity=self.ident[:])
            if act_kw is not None:
                kw = act_kw(c)
                S.op("act", "activation", reads=[pk] + kw.pop("reads", []), writes=[dstkey], out=dst[:, c, col0:col0 + T],
                     in_=pv[:, 0:T], **kw)
            elif evac_eng == "dve":
                S.op("dve", "tensor_copy", reads=[pk], writes=[dstkey], out=dst[:, c, col0:col0 + T], in_=pv[:, 0:T])
            else:
                S.op("act", "activation", reads=[pk], writes=[dstkey], out=dst[:, c, col0:col0 + T], in_=pv[:, 0:T],
                     func=AF.Copy)

    def tm_proj(self, actT, actkey, chunk_names, nsub, ncb, evac, bias=False):
        S = self.S
        banks = [[self.bank() for cb in range(ncb)] for s in range(nsub)]
        nch = len(chunk_names)
        if bias:
            for s in range(nsub):
                for cb in range(ncb):
                    S.op("pe", "matmul", reads=["ones_b", "bout_b"], writes=[("ps", banks[s][cb])], signal=False,
                         out=self.ps[banks[s][cb]][:, :], lhsT=self.ones_b[0:1, :], rhs=self.bout_b[0:1, cb * 512:(cb + 1) * 512],
                         start=True, stop=False)
        for c, name in enumerate(chunk_names):
            wkey, w = self.fetch(name)
            for s in range(nsub):
                for cb in range(ncb):
                    last = (s == nsub - 1 and cb == ncb - 1)
                    ak = actkey(c) if callable(actkey) else (actkey if isinstance(actkey, list) else [actkey])
                    S.op("pe", "matmul", reads=ak + [wkey], writes=[("ps", banks[s][cb])], signal=last,
                         out=self.ps[banks[s][cb]][:, :], lhsT=actT[:, c, s * 128:(s + 1) * 128],
                         rhs=w[:, cb * 512:(cb + 1) * 512], start=(c == 0 and not bias), stop=(c == nch - 1))
        self.phase("tm_mm")
        for s in range(nsub):
            for cb in range(ncb):
                evac(s, cb, banks[s][cb])
                self.phase("tm_evac1")

    def resid_evac(self, xsl, slot):
        def evac(s, cb, b):
            xkey = self.xk(slot, s)
            self.S.op("dve", "tensor_tensor", reads=[("ps", b), xkey], writes=[xkey], out=xsl[:, s, cb * 512:(cb + 1) * 512],
                      in0=self.ps[b][:, :], in1=xsl[:, s, cb * 512:(cb + 1) * 512], op=ALU.add)
        return evac

    def tmp_fence(self):
        self.S.op("dve", "memset", writes=list(self.tmp_keys), ap=self.st[:, 120:121], constant=0.0)

    def qk_front(self, pviews, pkeys, sq, ksq, qg, kqg, gbc, gkey):
        S = self.S
        off = 0
        for pv, pk in zip(pviews, pkeys):
            w = pv.shape[1]
            S.op("act", "activation", reads=[pk], writes=[ksq], out=sq[:, off:off + w], in_=pv, func=AF.Square)
            S.op("dve", "tensor_tensor", reads=[pk, gkey, ksq], writes=[kqg], out=qg[:, off:off + w].rearrange("p (h d) -> p h d", d=64),
                 in0=pv.rearrange("p (h d) -> p h d", d=64), in1=gbc[:, :].unsqueeze(1).broadcast_to([128, w // 64, 64]), op=ALU.mult)
            off += w

    def qk_chain(self, G, H, sq, ksq, qg, kqg, kh, tbufs, tkeys, rt, rtkey, out, outkeys):
        S = self.S
        GH = G * H
        ssq = self.st[:, 16:16 + GH]
        rs = self.st[:, 32:32 + GH]
        S.op("dve", "tensor_reduce", reads=[ksq], writes=["ssq"], out=ssq, in_=sq.rearrange("p (h d) -> p h d", d=64),
             axis=AX.X, op=ALU.add)
        S.op("pool", "tensor_scalar", reads=["ssq"], writes=["rs"], out=rs, in0=ssq, scalar1=1.0 / HD, scalar2=RMS_EPS,
             op0=ALU.mult, op1=ALU.add)
        S.op("pool", "tensor_tensor", reads=["rs", "nhalf"], writes=["rs"], out=rs, in0=rs, in1=self.nhalf[:, 0:GH], op=ALU.pow)
        v6 = qg.rearrange("p (g h r f d) -> p g h r f d", g=G, h=H, r=2, f=2, d=16)
        shp = [128, G, H, 16]
        for half, eng in ((0, "dve"), (1, ROPE_ENG1)):
            Aa = v6[:, :, :, half, 0, :]
            Bb = v6[:, :, :, half, 1, :]
            if half == 0:
                C = rt[:, :, 0, :].unsqueeze(2).broadcast_to(shp)
                Sn = rt[:, :, 1, :].unsqueeze(2).broadcast_to(shp)
                tk = [rtkey]
            else:
                C = self.ct[:, 0, :].unsqueeze(1).unsqueeze(1).broadcast_to(shp)
                Sn = self.ct[:, 1, :].unsqueeze(1).unsqueeze(1).broadcast_to(shp)
                tk = ["ct"]
            t = [b.rearrange("p (g h d) -> p g h d", g=G, h=H) for b in tbufs[half]]
            k = tkeys[half]
            qk = (kh, half)
            S.op(eng, "tensor_tensor", reads=[kqg, qk] + tk, writes=[k[0]], out=t[0], in0=Aa, in1=C, op=ALU.mult)
            S.op(eng, "tensor_tensor", reads=[kqg, qk] + tk, writes=[k[1]], out=t[1], in0=Aa, in1=Sn, op=ALU.mult)
            S.op(eng, "tensor_tensor", reads=[kqg, qk] + tk, writes=[k[2]], out=t[2], in0=Bb, in1=Sn, op=ALU.mult)
            S.op(eng, "tensor_tensor", reads=[kqg, k[0], k[2]], writes=[qk], out=Aa, in0=t[0], in1=t[2], op=ALU.subtract)
            S.op(eng, "tensor_tensor", reads=[kqg, qk] + tk, writes=[k[0]], out=t[0], in0=Bb, in1=C, op=ALU.mult)
            S.op(eng, "tensor_tensor", reads=[kqg, k[0], k[1]], writes=[qk], out=Bb, in0=t[0], in1=t[1], op=ALU.add)
        S.op("dve", "tensor_tensor", reads=[kqg, (kh, 0), (kh, 1), "rs"], writes=list(outkeys) + [kqg],
             out=out.rearrange("p g (h d) -> p g h d", d=64), in0=qg.rearrange("p (g h d) -> p g h d", g=G, h=H),
             in1=rs.rearrange("p (g h) -> p g h", g=G).unsqueeze(3).broadcast_to([128, G, H, 64]), op=ALU.mult)

    def qk_post(self, pviews, pkeys, G, H, gbc, gkey, rt, rtkey, out, outkeys):
        n = G * H * 64
        GH = G * H
        sq, qg = self.sq[:, 0:n], self.qg[:, 0:n]
        self.qk_front(pviews, pkeys, sq, "k_sq", qg, "k_qg", gbc, gkey)
        tb = [[self.rtmp[:, (hf * 3 + i) * 256:(hf * 3 + i) * 256 + GH * 16] for i in range(3)] for hf in range(2)]
        tkk = [[("k_rt", hf * 3 + i) for i in range(3)] for hf in range(2)]
        self.qk_chain(G, H, sq, "k_sq", qg, "k_qg", "k_qgh", tb, tkk, rt, rtkey, out, outkeys)

    def q_bufs(self, s_, cb):
        hi = self.hi
        tb = [hi[:, 3072 + i * 128:3072 + (i + 1) * 128] for i in range(3)]
        tk = [("qq_rt", i) for i in range(3)]
        if cb == 1:
            return (hi[:, s_ * 512:(s_ + 1) * 512], ("qq_sq1", s_), self.hTf[:, s_ * 512:(s_ + 1) * 512], ("qq_qg1", s_),
                    ("qq_qgh1", s_), [tb, tb], [tk, tk])
        return (hi[:, 2048:2560], "qq_sq0", hi[:, 2560:3072], "qq_qg0", "qq_qgh0", [tb, tb], [tk, tk])

    def k_transposes(self, g, nsub):
        S = self.S
        T = nsub * 128
        kt = self.ktm2[g % 2]
        kk = ("ktm", g % 2)
        for m in range(2):
            b = self.bank()
            pk = ("ps", b)
            pv = self.ps[b][:, :].bitcast(BF16)
            for s in range(nsub):
                S.op("pe", "transpose", reads=[kk, "ident"], writes=[pk],
                     out=pv[:, s * 128:(s + 1) * 128], in_=kt[:, s, m * 128:(m + 1) * 128], identity=self.ident[:])
            S.op("dve", "tensor_copy", reads=[pk], writes=["KT"], out=self.KT[m][:, g * 512:g * 512 + T], in_=pv[:, 0:T])

    def stage1(self, xsrc, S_len):
        S = self.S
        ngrp = (S_len + 511) // 512
        pending = None
        self.tmp_fence()

        def load_x(g):
            nsub_ = min(4, (S_len - g * 512) // 128)
            for s in range(nsub_):
                S.dma("sp", out=self.xsl[g % 2][:, s, :], in_=xsrc[g * 512 + s * 128:g * 512 + (s + 1) * 128, :],
                      writes=[self.xk(g % 2, s)], sem=("xld", g % 2, s))

        load_x(0)
        self.norm_scale(self.xsl[0], 0, min(4, S_len // 128), True)
        for g in range(ngrp):
            nsub = min(4, (S_len - g * 512) // 128)
            slot = g % 2
            xsl = self.xsl[slot]
            rtv, rtk = self.load_rt(self.rt_all_d, g * 4, nsub)
            if g + 1 < ngrp:
                load_x(g + 1)
            self.norm_transposes(nsub, "act")
            kbanks = []
            self.tm_proj(self.hT, "hT", [("kv", kc) for kc in range(8)], nsub, 1, lambda s, cb, b, kb=kbanks: kb.append(b))
            if pending is not None:
                self.k_transposes(*pending)
            if g + 1 < ngrp:
                self.norm_scale(self.xsl[(g + 1) % 2], (g + 1) % 2, min(4, (S_len - (g + 1) * 512) // 128), True)
            for s, b in enumerate(kbanks):
                pv = self.ps[b]
                pk = ("ps", b)
                j = g * 4 + s
                for (dst0, src0) in ((0, 256), (128, 320), (192, 448), (320, 384)):
                    S.op("act", "activation", reads=[pk], writes=["V"], out=self.V[:, j, dst0:dst0 + 64], in_=pv[:, src0:src0 + 64],
                         func=AF.Copy)
            self.qk_post([self.ps[bb][:, 0:256] for bb in kbanks], [("ps", bb) for bb in kbanks], nsub, 4, self.gk, "gk",
                         rtv[:, 0:nsub], rtk, self.ktm2[g % 2][:, 0:nsub, :], [("ktm", g % 2)])
            pending = (g, nsub)
            self.phase("s1")
        self.k_transposes(*pending)

    def ffn(self, l, xsl, slot, nsub):
        S = self.S
        T = nsub * 128
        self.norm_to_hT(xsl, slot, nsub)
        for fc in range(NFC):
            kg, wg = self.fetch(("g", l, fc))
            ku, wu = self.fetch(("u", l, fc))
            bg, bu = self.bank(), self.bank()
            for (kk, w, b) in ((kg, wg, bg), (ku, wu, bu)):
                for kc in range(8):
                    S.op("pe", "matmul", reads=["hT", kk], writes=[("ps", b)], signal=(kc == 7), out=self.ps[b][:, 0:T],
                         lhsT=w[:, kc * 128:(kc + 1) * 128], rhs=self.hT[:, kc, 0:T], start=(kc == 0), stop=(kc == 7))
            sgi = fc % 2
            S.op("act", "activation", reads=[("ps", bg)], writes=[("PT", 2 * sgi), ("PT", 2 * sgi + 1)], out=self.sg[sgi][:, 0:T], in_=self.ps[bg][:, 0:T],
                 func=AF.Silu)
            S.op("dve", "tensor_tensor", reads=[("ps", bu), ("PT", 2 * sgi), ("PT", 2 * sgi + 1)], writes=[self.hkey(fc)], out=self.hidT[:, fc, 0:T],
                 in0=self.ps[bu][:, 0:T], in1=self.sg[sgi][:, 0:T], op=ALU.mult)
        self.tm_proj(self.hidT, ["Ra", "Rb", "Rc", "Rd"], [("d", l, fc) for fc in range(NFC)], nsub, 2, self.resid_evac(xsl, slot))

    def attention(self, nsub, nkc, hook=None):
        S = self.S
        T = nsub * 128
        sbanks = [(0, 1), (2, 3), (4, 5)]
        oa, ob = 6, 7
        if nsub == 1:
            N = 512
            units = []
            for m in range(2):
                qa = self.Rf[0:64, m * 512:(m + 1) * 512]
                qb = self.Rf[64:128, m * 512:(m + 1) * 512]
                units.append((m, qa, qb, lambda lo, hi, m=m: self.Rf[lo:hi, 4096 + m * 512:4096 + (m + 1) * 512], "Rb"))
        else:
            N = T
            units = []
            for pr in range(8):
                units.append((pr // 4, self.qT[0:64, pr, 0:T], self.qT[64:128, pr, 0:T],
                              lambda lo, hi, pr=pr: self.oT[lo:hi, pr, 0:T], ("oT", pr)))
        items = [(u, c) for u in range(len(units)) for c in range(nkc)]

        def vviews(m, c):
            if m == 0:
                return self.V[:, c, 0:128], self.V[:, c, 64:192]
            return self.V[:, c, 256:384], self.V[:, c, 192:320]

        def s_mm(k):
            u, c = items[k]
            m, qa, qb, _, _ = units[u]
            ba, bb = sbanks[k % 3]
            qk = "Ra" if nsub == 1 else ("qT", u)
            if hook is not None and c == 0:
                hook(u, (ba, bb))
            S.op("pe", "matmul", reads=["KT", qk], writes=[("ps", ba)], out=self.ps[ba][:, 0:N],
                 lhsT=self.KT[m][0:64, c * 128:(c + 1) * 128], rhs=qa, start=True, stop=True, tile_position=(0, 0))
            S.op("pe", "matmul", reads=["KT", qk], writes=[("ps", bb)], out=self.ps[bb][:, 0:N],
                 lhsT=self.KT[m][64:128, c * 128:(c + 1) * 128], rhs=qb, start=True, stop=True, tile_position=(64, 0))

        def finish(u):
            m, _, _, oview, ok = units[u]
            a_o, b_o = (0, 64) if m == 0 else (64, 0)
            S.op("dve", "tensor_copy", reads=[("ps", oa)], writes=[("oraw", 0)], out=self.oraw[:, 0, 0:N], in_=self.ps[oa][:, 0:N])
            S.op("dve", "tensor_copy", reads=[("ps", ob)], writes=[("oraw", 1)], out=self.oraw[:, 1, 0:N], in_=self.ps[ob][:, 0:N])
            for h, o_off in enumerate((a_o, b_o)):
                s_off = 64 - o_off
                rk = ("rec", o_off)
                S.op("dve", "reciprocal", reads=[("oraw", h)], writes=[rk], out=self.rec[o_off:o_off + 64, 0:N],
                     in_=self.oraw[s_off:s_off + 64, h, 0:N])
                S.op("dve", "tensor_tensor", reads=[("oraw", h), rk], writes=[ok], out=oview(o_off, o_off + 64),
                     in0=self.oraw[o_off:o_off + 64, h, 0:N], in1=self.rec[o_off:o_off + 64, 0:N], op=ALU.mult)

        def pv_mm(k):
            u, c = items[k]
            m = units[u][0]
            ba, bb = sbanks[k % 3]
            pa = (k % 3) * 2
            va, vb = vviews(m, c)
            S.op("act", "activation", reads=[("ps", ba)], writes=[("PT", pa)], out=self.PT[:, pa, 0:N], in_=self.ps[ba][:, 0:N],
                 func=AF.Exp)
            S.op("act", "activation", reads=[("ps", bb)], writes=[("PT", pa + 1)], out=self.PT[:, pa + 1, 0:N],
                 in_=self.ps[bb][:, 0:N], func=AF.Exp)
            S.op("pe", "matmul", reads=["V", ("PT", pa)], writes=[("ps", oa)], out=self.ps[oa][:, 0:N],
                 lhsT=va, rhs=self.PT[:, pa, 0:N], start=(c == 0), stop=(c == nkc - 1))
            S.op("pe", "matmul", reads=["V", ("PT", pa + 1)], writes=[("ps", ob)], out=self.ps[ob][:, 0:N],
                 lhsT=vb, rhs=self.PT[:, pa + 1, 0:N], start=(c == 0), stop=(c == nkc - 1))
            if c == nkc - 1:
                finish(u)

        n = len(items)
        for k in range(min(2, n)):
            s_mm(k)
        for k in range(n):
            if k + 2 < n:
                s_mm(k + 2)
            pv_mm(k)

    def bank_t(self):
        b = self._t_rr % 6
        self._t_rr += 1
        return b

    def okey(self, pr):
        return "Rb" if pr < 4 else "Rc"

    def hkey(self, fc):
        return "Ra" if fc < 8 else ("Rb" if fc < 12 else ("Rc" if fc < 16 else "Rd"))

    def bank_o(self, pr):
        return (6, 7)

    def S_a(self, xsrc_ap, slot, nsub, nkc, rt_tab, rt_key, rt_j0, u_dst, u_key, u_col0):
        S = self.S
        T = nsub * 128
        xsl = self.xsl[slot]
        for s in range(nsub):
            S.dma("sp", out=xsl[:, s, :], in_=xsrc_ap[s * 128:(s + 1) * 128, :], writes=[self.xk(slot, s)], sem=("xld", slot, s))
        rtv, rtk = self.load_rt(rt_tab, rt_j0, nsub)
        self.norm_to_hT(xsl, slot, nsub)

        qb = {}
        self.tm_proj(self.hT, "hT", [("q", kc) for kc in range(8)], nsub, 2, lambda s_, cb, b: qb.__setitem__((s_, cb), b))

        def q_front(s_, cb):
            b = qb[(s_, cb)]
            sq, ksq, qg, kqg, kh, tb, tk = self.q_bufs(s_, cb)
            self.qk_front([self.ps[b][:, :]], [("ps", b)], sq, ksq, qg, kqg, self.gq, "gq")

        def q_chain(s_, cb):
            sq, ksq, qg, kqg, kh, tb, tk = self.q_bufs(s_, cb)
            self.qk_chain(1, 8, sq, ksq, qg, kqg, kh, tb, tk, rtv[:, s_:s_ + 1], rtk,
                          self.h_tm[:, s_:s_ + 1, cb * 512:(cb + 1) * 512], [("h_tmh", s_, cb)])

        def q_transposes(cb, free_banks=None):
            qdst = self.qTh if nsub == 1 else self.qT
            for c in range(4 * cb, 4 * cb + 4):
                if free_banks is None:
                    b = qb[((c - 4 * cb) % nsub, cb)]
                else:
                    b = free_banks[c % len(free_banks)]
                pk = ("ps", b)
                pv = self.ps[b][:, :].bitcast(BF16)
                for s_ in range(nsub):
                    S.op("pe", "transpose", reads=[("h_tmh", s_, cb), "ident"], writes=[pk], out=pv[:, s_ * 128:(s_ + 1) * 128],
                         in_=self.h_tm[:, s_, c * 128:(c + 1) * 128], identity=self.ident[:])
                S.op("dve", "tensor_copy", reads=[pk], writes=[("qT", c)], out=qdst[:, c, 0:T], in_=pv[:, 0:T])

        self._t_rr = 0
        self.tmp_fence()
        for s_ in range(nsub):
            q_front(s_, 1)
        for s_ in range(nsub):
            q_front(s_, 0)
            q_chain(s_, 0)
        q_transposes(0)
        for s_ in range(nsub):
            q_chain(s_, 1)
        self.phase("q")
        self._ps_rr = 0
        if nsub == 1:
            q_transposes(1)
            self.attention(nsub, nkc)
        else:
            self.attention(nsub, nkc, hook=lambda u, fb: q_transposes(1, fb) if u == 4 else None)
        self.phase("att")
        self.tm_proj(self.oTh if nsub == 1 else self.oT, ["Rb", "Rc"] if nsub == 1 else (lambda c: [("oT", c)]),
                     [("o", pr) for pr in range(8)], nsub, 2, self.resid_evac(xsl, slot))
        self.phase("wo")
        self.ffn(0, xsl, slot, nsub)
        self.phase("ffn0")
        self.norm_to_hT(xsl, slot, nsub)
        for cc in range(8):
            ka, wa = self.fetch(("ci", cc))
            kg, wg = self.fetch(("ci", 8 + cc))
            ba, bg = self.bank(), self.bank()
            for (kk, w, b) in ((ka, wa, ba), (kg, wg, bg)):
                for kc in range(8):
                    S.op("pe", "matmul", reads=["hT", kk], writes=[("ps", b)], signal=(kc == 7), out=self.ps[b][:, 0:T],
                         lhsT=w[:, kc * 128:(kc + 1) * 128], rhs=self.hT[:, kc, 0:T], start=(kc == 0), stop=(kc == 7))
            sgi = cc % 2
            S.op("act", "activation", reads=[("ps", bg), "fvec"], writes=[("PT", 2 * sgi), ("PT", 2 * sgi + 1)], out=self.sg[sgi][:, 0:T],
                 in_=self.ps[bg][:, 0:T], func=AF.Sigmoid, bias=self.fvec[:, 8 + cc:9 + cc])
            S.op("dve", "scalar_tensor_tensor", reads=[("ps", ba), ("PT", 2 * sgi), ("PT", 2 * sgi + 1), "fvec"], writes=[u_key],
                 out=u_dst[:, cc, u_col0:u_col0 + T], in0=self.ps[ba][:, 0:T], scalar=self.fvec[:, cc:cc + 1],
                 in1=self.sg[sgi][:, 0:T], op0=ALU.add, op1=ALU.mult)

    def S_b(self, slot, uslot, nsub, ydst_ap):
        S = self.S
        T = nsub * 128
        xsl = self.xsl[slot]
        U = self.U[uslot]
        ukey = ("U", uslot)
        for cc in range(8):
            dg = self.diag[cc % 2]
            dk = self.diag_keys[cc % 2]
            S.op("dve", "tensor_tensor", reads=["ident", "dww"], writes=dk, out=dg,
                 in0=self.ident[:, :].unsqueeze(1).broadcast_to([128, CW, 128]),
                 in1=self.dww[:, cc, :].unsqueeze(2).broadcast_to([128, CW, 128]), op=ALU.mult)
            b = self.bank()
            for j in range(CW):
                S.op("pe", "matmul", reads=dk + [ukey], writes=[("ps", b)], out=self.ps[b][:, 0:T],
                     lhsT=dg[:, j, :], rhs=U[:, cc, j:j + T], start=(j == 0), stop=(j == CW - 1))
            S.op("act", "activation", reads=[("ps", b), "fvec"], writes=["hT"], out=self.hT[:, cc, 0:T], in_=self.ps[b][:, 0:T],
                 func=AF.Identity, bias=self.fvec[:, 16 + cc:17 + cc])
        self.phase("conv")
        lnb = []
        for s in range(nsub):
            b = self.bank()
            pk = ("ps", b)
            pv = self.ps[b][:, :].bitcast(BF16)
            for cc in range(8):
                S.op("pe", "transpose", reads=["hT", "ident"], writes=[pk], out=pv[:, cc * 128:(cc + 1) * 128],
                     in_=self.hT[:, cc, s * 128:(s + 1) * 128], identity=self.ident[:])
            S.op("act", "activation", reads=[pk], writes=[("h_tm", s), ("lns", s)], out=self.h_tm[:, s, :], in_=pv, func=AF.Identity,
                 accum_out=self.st[:, 48 + s:49 + s])
            S.op("act", "activation", reads=[pk], writes=[("h_tm", s), ("lns", s)], out=self.h_tm[:, s, :], in_=pv, func=AF.Square,
                 accum_out=self.st[:, 52 + s:53 + s])
            lnb.append((pk, pv))
        lk = [("lns", s) for s in range(nsub)]
        mean = self.st[:, 56:56 + nsub]
        var = self.st[:, 60:60 + nsub]
        msq = self.st[:, 64:64 + nsub]
        S.op("dve", "tensor_scalar", reads=lk, writes=["lnm"], out=mean, in0=self.st[:, 48:48 + nsub], scalar1=1.0 / D,
             scalar2=None, op0=ALU.mult)
        S.op("dve", "tensor_tensor", reads=["lnm"], writes=["lnq"], out=msq, in0=mean, in1=mean, op=ALU.mult)
        S.op("dve", "scalar_tensor_tensor", reads=lk + ["lnq"], writes=["lnv"], out=var, in0=self.st[:, 52:52 + nsub], scalar=1.0 / D,
             in1=msq, op0=ALU.mult, op1=ALU.subtract)
        S.op("pool", "tensor_scalar", reads=["lnv"], writes=["lnv"], out=var, in0=var, scalar1=LN_EPS, scalar2=None, op0=ALU.add)
        S.op("pool", "tensor_tensor", reads=["lnv", "nhalf"], writes=["lnv"], out=var, in0=var, in1=self.nhalf[:, 0:nsub], op=ALU.pow)
        for s in range(nsub):
            pk, pv = lnb[s]
            S.op("dve", "tensor_scalar", reads=[pk, "lnm", "lnv"], writes=[("h_tm", s)], out=self.h_tm[:, s, :], in0=pv,
                 scalar1=self.st[:, 56 + s:57 + s], scalar2=self.st[:, 60 + s:61 + s], op0=ALU.subtract, op1=ALU.mult)
        self.transposes([self.h_tm[:, s, :] for s in range(nsub)], [("h_tm", s) for s in range(nsub)], 8, self.hT, "hT",
                        act_kw=lambda c: dict(func=AF.Silu, scale=self.fvec[:, 24 + c:25 + c], bias=self.fvec[:, 32 + c:33 + c],
                                              reads=["fvec"]))
        self.phase("ln")
        self.tm_proj(self.hT, "hT", [("co", cc) for cc in range(8)], nsub, 2, self.resid_evac(xsl, slot), bias=True)
        self.phase("co")
        self.ffn(1, xsl, slot, nsub)
        self.phase("ffn1")
        for s in range(nsub):
            S.op("act", "activation", reads=[self.xk(slot, s)], writes=[("h_tm", s), ("fs", s)], out=self.h_tm[:, s, :], in_=xsl[:, s, :],
                 func=AF.Square, accum_out=self.st[:, 68 + s:69 + s])
        fr = self.st[:, 72:72 + nsub]
        S.op("pool", "tensor_scalar", reads=[("fs", s) for s in range(nsub)], writes=["fr"], out=fr, in0=self.st[:, 68:68 + nsub],
             scalar1=1.0 / D, scalar2=RMS_EPS, op0=ALU.mult, op1=ALU.add)
        S.op("pool", "tensor_tensor", reads=["fr", "nhalf"], writes=["fr"], out=fr, in0=fr, in1=self.nhalf[:, 0:nsub], op=ALU.pow)
        for s in range(nsub):
            xkey = self.xk(slot, s)
            S.op("dve", "scalar_tensor_tensor", reads=[xkey, "fr", "gf"], writes=[xkey], out=xsl[:, s, :], in0=xsl[:, s, :],
                 scalar=self.st[:, 72 + s:73 + s], in1=self.gf[:], op0=ALU.mult, op1=ALU.mult)
            S.dma("sp", out=ydst_ap[s * 128:(s + 1) * 128, :], in_=xsl[:, s, :], reads=[xkey], sem=("yst", slot, s))
        self.phase("fin")

    def run_pass(self, xkv, S_len, xown, n_own, rt_tab, rt_key, ydst, has_halo):
        S = self.S
        nkc = S_len // 128
        self.stage1(xkv, S_len)
        ntile = n_own // 512
        if has_halo:
            self.S_a(xown[n_own:n_own + 128, :], 0, 1, nkc, rt_tab, rt_key, n_own // 128, self.UH, "UH", 0)
            S.op("pool", "tensor_scalar", reads=["UH", "flags"], writes=[("U", 0)], out=self.U[0][:, :, 0:CP],
                 in0=self.UH[:, :, 64 - CP:64], scalar1=self.flags[:, 0:1], scalar2=None, op0=ALU.mult)
        else:
            S.op("pool", "memset", writes=[("U", 0)], ap=self.U[0][:, :, 0:CP], constant=0.0)
        for i in range(ntile + 1):
            if i < ntile:
                us = i % 2
                self.extra_stage = (i == 0 and not has_halo)
                self.S_a(xown[i * 512:(i + 1) * 512, :], i % 2, 4, nkc, rt_tab, rt_key, i * 4, self.U[us], ("U", us), CP)
                self.extra_stage = False
                if i > 0:
                    S.op("pool", "tensor_copy", reads=[("U", 1 - us)], writes=[("U", us)], out=self.U[us][:, :, 0:CP],
                         in_=self.U[1 - us][:, :, 512:512 + CP])
                    S.op("pool", "tensor_copy", reads=[("U", us)], writes=[("U", 1 - us)], out=self.U[1 - us][:, :, 512 + CP:512 + 2 * CP],
                         in_=self.U[us][:, :, CP:2 * CP])
            if i == ntile:
                us = (ntile - 1) % 2
                if has_halo:
                    S.op("pool", "tensor_scalar", reads=["UH", "flags"], writes=[("U", us)], out=self.U[us][:, :, 512 + CP:512 + 2 * CP],
                         in0=self.UH[:, :, 64:64 + CP], scalar1=self.flags[:, 1:2], scalar2=None, op0=ALU.mult)
                else:
                    S.op("pool", "memset", writes=[("U", us)], ap=self.U[us][:, :, 512 + CP:512 + 2 * CP], constant=0.0)
            if i >= 1:
                j = i - 1
                self.S_b(j % 2, j % 2, 4, ydst[j * 512:(j + 1) * 512, :])

    def build(self):
        self.declare()
        self.alloc()
        self.make_chunks()
        self.setup()
        try:
            self.phase("setup")
            self.v_stage_on = True
            self.run_pass(self.xp, self.Sp, self.xp, self.Sp, self.rt_all_d, "rt", self.yp, False)
            self.v_stage_on = False
            if self.vstage:
                vk = ["V"] + [k for k, _ in self.vstage]
                c0 = self.Sp // 128
                self.S.op("pool", "memset", writes=vk, ap=self.V[:, c0:, 64:128], constant=1.0)
                self.S.op("pool", "memset", writes=vk, ap=self.V[:, c0:, 256:320], constant=1.0)
            self.phase("passP")
            self.run_pass(self.xs, self.Ss, self.xq, self.Q, self.rt_q_d, "rt", self.yq, True)
        except StopBuild:
            print("STOPPED at", self.stop_at)
        self.flush_stores(0)
        self.S.final_wait("pool")
        self.S.final_wait("sp")
        self.S.emit()
        print("instructions", self.S.nins, dict(self.S.nops), "signals", dict(self.S.cnt))
        return self.nc


def _rope_tabs(pos_rows):
    inv = (10000.0 ** (-np.arange(0, 32, 2, dtype=np.float64) / 32.0))
    ang = pos_rows[..., None].astype(np.float64) * inv
    return np.cos(ang).astype(np.float32), np.sin(ang).astype(np.float32)


def host_layout(inp, Sp, Ss):
    Q = Ss // 4
    f = lambda a: np.ascontiguousarray(np.asarray(a, dtype=np.float32))
    wqkv = f(inp["w_qkv"])[0]
    heads = [8 * m + 4 * hh + i for m in range(2) for i in range(4) for hh in range(2)]
    heads_o = []
    for m in range(2):
        for i in range(4):
            a, b = 8 * m + i, 8 * m + 4 + i
            heads_o += [a, b] if m == 0 else [b, a]
    qcols = np.concatenate([np.arange(h * 64, (h + 1) * 64) for h in heads])
    orows = np.concatenate([np.arange(h * 64, (h + 1) * 64) for h in heads_o])
    shared = {}
    shared["w_q"] = f(wqkv[:, :1024][:, qcols].reshape(8, 128, 1024))
    shared["w_kv"] = f(wqkv[:, 1024:].reshape(8, 128, 512))
    shared["w_o"] = f(f(inp["w_o"])[0][orows].reshape(8, 128, 1024))

    def up(w):
        Fo = w.shape[1]
        return f(w.reshape(8, 128, Fo // 128, 128).transpose(2, 1, 0, 3).reshape(Fo // 128, 128, 1024))
    shared["w_g"] = f(np.stack([up(f(inp["w_gate"])[l]) for l in range(2)]))
    shared["w_u"] = f(np.stack([up(f(inp["w_up"])[l]) for l in range(2)]))
    shared["w_d"] = f(f(inp["w_down"]).reshape(2, NFC, 128, 1024))
    shared["w_ci"] = up(f(inp["conv_w_in"])[0])
    shared["w_co"] = f(f(inp["conv_w_out"])[0].reshape(8, 128, 1024))
    col = lambda v: f(v).reshape(-1, 128).T
    gcols = np.stack([col(f(inp["attn_norm_g"])[0]), col(f(inp["ffn_norm_g"])[0]), col(f(inp["conv_norm_g"])[0]),
                      col(f(inp["ffn_norm_g"])[1])], axis=1)
    shared["gcols"] = f(gcols)
    shared["fvec"] = f(np.concatenate([col(f(inp["conv_b_in"])[0]), col(f(inp["dw_b"])[0]), col(f(inp["conv_ln_g"])[0]),
                                       col(f(inp["conv_ln_b"])[0])], axis=1))
    shared["dww"] = f(f(inp["dw_w"])[0].T.reshape(8, 128, CW).transpose(1, 0, 2))
    shared["gq"] = f(inp["q_norm_g"]).reshape(1, 64)
    shared["gk"] = f(inp["k_norm_g"]).reshape(1, 64)
    shared["gf"] = f(inp["final_norm_g"]).reshape(1, D)
    shared["bout"] = f(inp["conv_b_out"]).reshape(1, D)
    p = np.arange(128)
    j = np.arange(Ss // 128)
    rows = 2 * j[None, :] + (p[:, None] >= 64)
    c, s = _rope_tabs(rows)
    shared["rt_all"] = f(np.stack([c, s], axis=2))
    c, s = _rope_tabs(p % 64)
    shared["ct"] = f(np.stack([c, s], axis=1))
    xp = f(inp["x_prompt"])
    xs = f(inp["x_sample"])
    maps = []
    for core in range(NCORES):
        sb, qi = core // 4, core % 4
        qs = qi * Q
        m = dict(shared)
        m["xp"] = xp[core]
        m["xs"] = xs[sb]
        halo = np.zeros((128, D), np.float32)
        if qs > 0:
            halo[0:64] = xs[sb, qs - 64:qs]
        if qs + Q < Ss:
            halo[64:128] = xs[sb, qs + Q:qs + Q + 64]
        m["xq"] = f(np.concatenate([xs[sb, qs:qs + Q], halo], axis=0))
        jq = np.arange(Q // 128)
        rows = np.concatenate([2 * (jq[None, :] + qs // 128) + (p[:, None] >= 64),
                               np.where(p[:, None] >= 64, (qs + Q) // 64, qs // 64 - 1)], axis=1)
        c, s = _rope_tabs(rows)
        m["rt_q"] = f(np.stack([c, s], axis=2))
        fl = np.zeros((128, 2), np.float32)
        fl[:, 0] = 1.0 if qs > 0 else 0.0
        fl[:, 1] = 1.0 if qs + Q < Ss else 0.0
        m["flags"] = fl
        maps.append(m)
    return maps


_CACHE = {}


def kernel(**inputs):
    xp = np.asarray(inputs["x_prompt"])
    xs = np.asarray(inputs["x_sample"])
    Sp, Ss = xp.shape[1], xs.shape[1]
    Q = Ss // 4
    key = (Sp, Ss)
    if key not in _CACHE:
        _CACHE[key] = Builder(Sp, Ss).build()
    nc = _CACHE[key]
    maps = host_layout(inputs, Sp, Ss)
    res = run_bass_kernel_spmd(nc, maps, core_ids=list(range(NCORES)))
    yp = np.stack([np.asarray(res.results[c]["yp"], dtype=np.float32) for c in range(NCORES)], axis=0)
    ys = np.zeros((2, Ss, D), np.float32)
    for c in range(NCORES):
        sb, qi = c // 4, c % 4
        ys[sb, qi * Q:(qi + 1) * Q] = np.asarray(res.results[c]["yq"], dtype=np.float32)
    return yp, ys
```

```python
import math
import numpy as np
import concourse.bass as bass
import concourse.mybir as mybir
from concourse.bass_utils import run_bass_kernel_spmd

F32 = mybir.dt.float32
BF16 = mybir.dt.bfloat16
AF = mybir.ActivationFunctionType
ALU = mybir.AluOpType
AX = mybir.AxisListType

D = 1024
NH = 16
NKV = 4
HD = 64
FF = 2816
NFC = FF // 128
CW = 31
CP = 15
RMS_EPS = 1e-6
LN_EPS = 1e-5
NCORES = 8
import os
ROPE_ENG1 = os.environ.get("ROPE_ENG1", "dve")
DMA_Q2 = os.environ.get("DMA_Q2", "sp")


class Sched:
    def __init__(self, nc):
        self.nc = nc
        self.names = ["pe", "act", "dve", "pool", "sp"]
        self.sem = {n: nc.alloc_semaphore("s_" + n) for n in self.names}
        self.nops = {n: 0 for n in self.names}
        self.seen = {n: {} for n in self.names}
        self.prog = {n: [] for n in self.names}
        self.awaited = {n: set() for n in self.names}
        self.last_write = {}
        self.readers = {}
        self.dma_sems = {}
        self.nins = 0
        self.parent = {}
        self.children = {}

    def set_parent(self, fine, region):
        self.parent[fine] = region
        self.children.setdefault(region, []).append(fine)

    def _rel(self, k):
        out = [k]
        if k in self.parent:
            out.append(self.parent[k])
        out += self.children.get(k, [])
        return out

    def _deps(self, eng, reads, writes):
        deps = {}
        raw_self = [-1]

        def need(t, raw=False):
            src, v = t
            if src == eng:
                if raw_self[0] < v:
                    raw_self[0] = v
                return
            if deps.get(src, -1) < v:
                deps[src] = v

        for k0 in reads:
            for k in self._rel(k0):
                if k in self.last_write:
                    need(self.last_write[k], True)
        for k0 in writes:
            for k in self._rel(k0):
                if k in self.last_write:
                    need(self.last_write[k])
                for t in self.readers.get(k, {}).values():
                    need(t)
        if raw_self[0] >= 0 and eng not in ("pe", "sp"):
            deps[eng] = raw_self[0]
        out = []
        for src, v in deps.items():
            if self.seen[eng].get(src, -1) < v:
                self.seen[eng][src] = v
                out.append((src, v))
                if isinstance(src, str):
                    self.awaited[src].add(v)
        return out

    def _record(self, t, reads, writes):
        for k in reads:
            self.readers.setdefault(k, {})[t[0]] = t
        for k in writes:
            self.last_write[k] = t
            self.readers[k] = {}

    def op(self, eng, method, reads=(), writes=(), signal=True, **kw):
        waits = self._deps(eng, reads, writes)
        idx = self.nops[eng]
        self.nops[eng] = idx + 1
        self.prog[eng].append((waits, method, kw, ("e", idx)))
        self._record((eng, idx), reads, writes)
        self.nins += 1

    def dma(self, eng, out, in_, reads=(), writes=(), sem=None):
        waits = self._deps(eng, reads, writes)
        if sem not in self.dma_sems:
            self.dma_sems[sem] = [self.nc.alloc_semaphore("d_%d" % len(self.dma_sems)), 0]
        ent = self.dma_sems[sem]
        src = ("dma", sem)
        if ent[1] > 0 and self.seen[eng].get(src, -1) < ent[1]:
            self.seen[eng][src] = ent[1]
            waits.append((src, ent[1]))
        ent[1] += 16
        t = (src, ent[1])
        self.prog[eng].append((waits, "dma_start", dict(out=out, in_=in_), ("d", ent[0])))
        self.nops[eng] += 1
        self._record(t, reads, writes)
        self.nins += 1
        return t

    def final_wait(self, eng):
        ws = [(("dma", k), ent[1]) for k, ent in self.dma_sems.items() if ent[1] > 0]
        self.prog[eng].append((ws, None, None, None))

    def emit(self):
        nc = self.nc
        rank = {}
        for e in self.names:
            rank[e] = {idx: i + 1 for i, idx in enumerate(sorted(self.awaited[e]))}
        self.cnt = {e: len(rank[e]) for e in self.names}
        with nc.Block() as block:
            def run(name):
                def body(h):
                    for waits, method, kw, tag in self.prog[name]:
                        for src, v in waits:
                            if isinstance(src, str):
                                h.wait_ge(self.sem[src], rank[src][v])
                            else:
                                h.wait_ge(self.dma_sems[src[1]][0], v)
                        if method is not None:
                            ins = getattr(h, method)(**kw)
                            if tag[0] == "d":
                                ins.then_inc(tag[1], 16)
                            elif tag[1] in rank[name]:
                                ins.then_inc(self.sem[name], 1)
                return body
            block.tensor(run("pe"))
            block.scalar(run("act"))
            block.vector(run("dve"))
            block.gpsimd(run("pool"))
            block.sync(run("sp"))


class StopBuild(Exception):
    pass


class Builder:
    stop_at = None

    def phase(self, name):
        self.phases.append(name)
        if self.stop_at is not None and name == self.stop_at:
            raise StopBuild()

    def __init__(self, Sp, Ss, debug=False):
        self.phases = []
        self.Sp, self.Ss = Sp, Ss
        self.Q = Ss // 4
        self.debug = debug
        self.nc = bass.Bass("TRN2", target_bir_lowering=False)
        self.S = Sched(self.nc)
        self._ring_next = 0
        self._tp_next = 0
        self._ps_rr = 0

    def din(self, name, shape, dt=F32):
        return self.nc.dram_tensor(name, list(shape), dt, kind="ExternalInput").ap()

    def dout(self, name, shape, dt=F32):
        return self.nc.dram_tensor(name, list(shape), dt, kind="ExternalOutput").ap()

    def declare(self):
        Sp, Ss, Q = self.Sp, self.Ss, self.Q
        nc = self.nc
        self.xp = self.din("xp", [Sp, D])
        self.xs = self.din("xs", [Ss, D])
        self.xq = self.din("xq", [Q + 128, D])
        self.w_kv = self.din("w_kv", [8, 128, 512])
        self.w_q = self.din("w_q", [8, 128, 1024])
        self.w_o = self.din("w_o", [8, 128, 1024])
        self.w_g = self.din("w_g", [2, NFC, 128, 1024])
        self.w_u = self.din("w_u", [2, NFC, 128, 1024])
        self.w_d = self.din("w_d", [2, NFC, 128, 1024])
        self.w_ci = self.din("w_ci", [16, 128, 1024])
        self.w_co = self.din("w_co", [8, 128, 1024])
        self.gcols_d = self.din("gcols", [128, 4, 8])
        self.fvec_d = self.din("fvec", [128, 40])
        self.dww_d = self.din("dww", [128, 8, CW])
        self.gq_d = self.din("gq", [1, 64])
        self.gk_d = self.din("gk", [1, 64])
        self.gf_d = self.din("gf", [1, D])
        self.bout_d = self.din("bout", [1, D])
        self.rt_all_d = self.din("rt_all", [128, Ss // 128, 2, 16])
        self.rt_q_d = self.din("rt_q", [128, Q // 128 + 1, 2, 16])
        self.ct_d = self.din("ct", [128, 2, 16])
        self.flags_d = self.din("flags", [128, 2])
        self.yp = self.dout("yp", [Sp, D])
        self.yq = self.dout("yq", [Q, D])
        self.scr = nc.dram_tensor("wscr", [8 + 8 + 8 + 6 * NFC + 16 + 8, 128, 1024], BF16).ap()

    def alloc(self):
        nc = self.nc
        Ss = self.Ss
        A = nc.alloc_sbuf_tensor
        self.nkc_max = Ss // 128
        self.KT = [A("KT%d" % m, [128, Ss], BF16) for m in range(2)]
        self.V = A("Vst", [128, self.nkc_max, 384], BF16)
        self.xsl = [A("x%d" % i, [128, 4, D], F32) for i in range(2)]
        self.h_tm = A("h_tm", [128, 4, D], BF16)
        self.hT = A("hT", [128, 8, 512], BF16)
        self.R = A("R", [128, NFC * 512], BF16)
        self.PT = A("PT", [128, 6, 512], BF16)
        self.U = [A("U%d" % i, [128, 8, 512 + 2 * CP], BF16) for i in range(2)]
        self.UH = A("UH", [128, 8, 128], BF16)
        self.ring = A("ring", [128, 6, 1024], BF16)
        self.stage = A("stage", [128, 1024], F32)
        self.rtg = A("rtg", [128, 2, 4, 2, 16], F32)
        self._rtg_next = 0
        self.ct = A("ct_s", [128, 2, 16], F32)
        self.gf = A("gf_s", [128, D], F32)
        self.gq = A("gq_s", [128, 64], F32)
        self.gk = A("gk_s", [128, 64], F32)
        self.gcols = A("gcols_s", [128, 4, 8], F32)
        self.fvec = A("fvec_s", [128, 40], F32)
        self.dww = A("dww_s", [128, 8, CW], F32)
        self.flags = A("flags_s", [128, 2], F32)
        self.bout_b = A("bout_b", [1, D], BF16)
        self.ones_b = A("ones_b", [1, 128], BF16)
        self.ident = A("ident", [128, 128], BF16)
        self.nhalf = A("nhalf", [128, 16], F32)
        self.st = A("stats", [128, 128], F32)
        self.rec = A("rec", [128, 512], F32)
        self.oraw = A("oraw", [128, 2, 512], F32)
        self.psall = nc.alloc_psum_tensor("psall", [128, 8 * 512], F32)
        self.ps = [self.psall[:, i * 512:(i + 1) * 512] for i in range(8)]
        Rf = self.R[:, :]
        self.qT = Rf[:, 0:4096].rearrange("p (c t) -> p c t", c=8)
        self.oT = Rf[:, 4096:8192].rearrange("p (c t) -> p c t", c=8)
        self.hidT = Rf[:, 0:NFC * 512].rearrange("p (c t) -> p c t", c=NFC)
        self.cT = Rf[:, 0:4096].rearrange("p (c t) -> p c t", c=8)
        self.ktm2 = [Rf[:, i * 1024:(i + 1) * 1024].rearrange("p (s c) -> p s c", s=4) for i in range(2)]
        self.qTh = Rf[:, 0:1024].rearrange("p (c t) -> p c t", c=8)
        self.oTh = Rf[:, 4096:5120].rearrange("p (c t) -> p c t", c=8)
        self.Rf = Rf
        hi = Rf[:, 4096:NFC * 512].bitcast(F32)
        self.sq = hi[:, 0:1024]
        self.qg = hi[:, 1024:2048]
        self.rtmp = hi[:, 2048:3584]
        self.diag = [Rf[:, 0:CW * 128].rearrange("p (j c) -> p j c", j=CW),
                     Rf[:, 4096:4096 + CW * 128].rearrange("p (j c) -> p j c", j=CW)]
        self.diag_keys = [["Ra"], ["Rb", "Rc"]]
        self.extra_stage = False
        self.pending_st = []
        self.v_stage_on = False
        c0 = self.Sp // 128
        nfree = (self.nkc_max - c0) * 384 // 2
        self.vstage = []
        if nfree >= 1024 and not os.environ.get("NO_VSTAGE"):
            vf = self.V[:, c0:, :].rearrange("p a b -> p (a b)").bitcast(F32)
            self.vstage = [(("vst", i), vf[:, i * 1024:(i + 1) * 1024]) for i in range(min(12, nfree // 1024))]
        self._stage_rr = 0
        self._cast_rr = 0
        ptf = self.PT[:, :, :].rearrange("p a b -> p (a b)")
        self.sg = [ptf[:, 0:1024].bitcast(F32), ptf[:, 1024:2048].bitcast(F32)]
        S = self.S
        for ss in range(4):
            for cb in range(2):
                S.set_parent(("h_tmh", ss, cb), ("h_tm", ss))
        for c in range(8):
            S.set_parent(("qT", c), "Ra")
            S.set_parent(("oT", c), "Rb" if c < 4 else "Rc")
        S.set_parent(("ktm", 0), "Ra")
        S.set_parent(("ktm", 1), "Ra")
        self.tmp_keys = []
        for kname, reg in (("k_sq", "Rb"), ("k_qg", "Rc"), (("k_qgh", 0), "Rc"), (("k_qgh", 1), "Rc"),
                           (("q_sq", 0), "Rb"), (("q_sq", 1), "Rb"), (("q_sq", 2), "Rc"), (("q_sq", 3), "Rd"),
                           ("q_qg", "Rc"), (("q_qgh", 0), "Rc"), (("q_qgh", 1), "Rc")):
            S.set_parent(kname, reg)
            self.tmp_keys.append(kname)
        self.hi = hi
        self.hTf = self.hT[:, :, :].rearrange("p a b -> p (a b)").bitcast(F32)
        for i, reg in ((0, "Rb"), (1, "Rb"), (2, "Rc"), (3, "Rc")):
            S.set_parent(("qq_sq1", i), reg)
            self.tmp_keys.append(("qq_sq1", i))
            S.set_parent(("qq_qg1", i), "hT")
            S.set_parent((("qq_qgh1", i), 0), "hT")
            S.set_parent((("qq_qgh1", i), 1), "hT")
        for kname in ("qq_sq0", "qq_qg0", ("qq_qgh0", 0), ("qq_qgh0", 1), ("qq_rt", 0), ("qq_rt", 1), ("qq_rt", 2)):
            S.set_parent(kname, "Rd")
            self.tmp_keys.append(kname)
        for j in range(6):
            for pfx in ("k_rt", "q_rt"):
                S.set_parent((pfx, j), "Rd")
                self.tmp_keys.append((pfx, j))
        for hf in range(2):
            for i in range(3):
                S.set_parent(("rtmp", hf, i), "Rd")
        print("sbuf bytes remaining", nc.sbuf_bytes_remaining)

    def make_chunks(self):
        self.chunks = {}
        idx = [0]

        def add(name, src, gain=None, scale=None, ncols=1024):
            self.chunks[name] = dict(src=src, gain=gain, scale=scale, ncols=ncols, scr=self.scr[idx[0]], done=False)
            idx[0] += 1

        for kc in range(8):
            add(("kv", kc), self.w_kv[kc], gain=("pp", 0, kc), ncols=512)
        for kc in range(8):
            add(("q", kc), self.w_q[kc], gain=("pp", 0, kc))
        for pr in range(8):
            add(("o", pr), self.w_o[pr])
        for l in range(2):
            for fc in range(NFC):
                add(("g", l, fc), self.w_g[l, fc], gain=("kc", 1 if l == 0 else 3))
                add(("u", l, fc), self.w_u[l, fc], gain=("kc", 1 if l == 0 else 3))
                add(("d", l, fc), self.w_d[l, fc])
        for oc in range(16):
            add(("ci", oc), self.w_ci[oc], gain=("kc", 2))
        for cc in range(8):
            add(("co", cc), self.w_co[cc])

    def flush_stores(self, keep=0):
        while len(self.pending_st) > keep:
            scr_ap, src_ap, key, name, slot = self.pending_st.pop(0)
            self.S.dma("sp", out=scr_ap, in_=src_ap, reads=[key], writes=[("scr", name)], sem=("scr", slot))
            self.chunks[name]["stored"] = True

    def fetch(self, name):
        S = self.S
        ch = self.chunks[name]
        self.flush_stores(0 if ch["done"] else 2)
        slot = self._ring_next
        self._ring_next = (slot + 1) % 6
        n = ch["ncols"]
        dst = self.ring[:, slot, 0:n]
        key = ("ring", slot)
        if not ch["done"]:
            ch["done"] = True
            bufs = [("stage", self.stage[:, :])]
            if self.v_stage_on:
                bufs += self.vstage
            if self.extra_stage:
                bufs += [(("x", 1, j), self.xsl[1][:, j, :]) for j in range(4)]
            self._stage_rr = (self._stage_rr + 1) % len(bufs)
            skey, sbuf = bufs[self._stage_rr]
            self._cast_rr ^= 1
            ce = "pool" if self._cast_rr else "dve"
            S.dma("sp", out=sbuf[:, 0:n], in_=ch["src"], writes=[skey], sem=("stg", self._stage_rr))
            g = ch["gain"]
            if g is None:
                S.op(ce, "tensor_copy", reads=[skey], writes=[key], out=dst, in_=sbuf[:, 0:n])
            elif g[0] == "pp":
                S.op(ce, "tensor_scalar", reads=[skey, "gcols"], writes=[key], out=dst, in0=sbuf[:, 0:n],
                     scalar1=self.gcols[:, g[1], g[2]:g[2] + 1], scalar2=None, op0=ALU.mult)
            else:
                gb = self.gcols[:, g[1], :].unsqueeze(2).broadcast_to([128, 8, 128])
                S.op(ce, "tensor_tensor", reads=[skey, "gcols"], writes=[key],
                     out=dst.rearrange("p (k c) -> p k c", k=8), in0=sbuf.rearrange("p (k c) -> p k c", k=8),
                     in1=gb, op=ALU.mult)
            self.pending_st.append((ch["scr"][:, 0:n], dst, key, name, slot))
        else:
            S.dma("sp", out=dst, in_=ch["scr"][:, 0:n], reads=[("scr", name)], writes=[key], sem=("ringld", slot))
        return key, dst

    def load_rt(self, src, j0, n):
        i = self._rtg_next
        self._rtg_next = 1 - i
        key = ("rtg", i)
        self.S.dma("sp", out=self.rtg[:, i, 0:n], in_=src[:, j0:j0 + n], writes=[key], sem=key)
        return self.rtg[:, i], key

    def bank(self):
        b = self._ps_rr
        self._ps_rr = (b + 1) % 8
        return b

    def setup(self):
        S = self.S
        ld = [("ct", self.ct, self.ct_d),
              ("gcols", self.gcols, self.gcols_d), ("fvec", self.fvec, self.fvec_d), ("dww", self.dww, self.dww_d),
              ("flags", self.flags, self.flags_d)]
        for i, (k, sb, dr) in enumerate(ld):
            S.dma("sp", out=sb[:], in_=dr, writes=[k], sem=("setup", i % 4))
        S.dma("sp", out=self.gf[:], in_=self.gf_d.partition_broadcast(128), writes=["gf"], sem=("setup", 0))
        S.dma("sp", out=self.gq[:], in_=self.gq_d.partition_broadcast(128), writes=["gq"], sem=("setup", 1))
        S.dma("sp", out=self.gk[:], in_=self.gk_d.partition_broadcast(128), writes=["gk"], sem=("setup", 2))
        S.dma("sp", out=self.stage[0:1, :], in_=self.bout_d, writes=["stage"], sem="stage")
        S.op("pool", "tensor_copy", reads=["stage"], writes=["bout_b"], out=self.bout_b[:], in_=self.stage[0:1, :])
        S.op("pool", "memset", writes=["ones_b"], ap=self.ones_b[:], constant=1.0)
        S.op("pool", "memset", writes=["nhalf"], ap=self.nhalf[:], constant=-0.5)
        identf = self.stage[:, 0:128]
        S.op("pool", "memset", reads=["stage"], writes=["stage"], ap=identf, constant=0.0)
        S.op("pool", "affine_select", reads=["stage"], writes=["stage"], out=identf, in_=identf,
             pattern=[[-1, 128]], compare_op=ALU.not_equal, fill=1.0, base=0, channel_multiplier=1)
        S.op("pool", "tensor_copy", reads=["stage"], writes=["ident"], out=self.ident[:], in_=identf)
        S.op("pool", "tensor_scalar", reads=["gq"], writes=["gq"], out=self.gq[:], in0=self.gq[:], scalar1=1.0 / math.sqrt(HD),
             scalar2=None, op0=ALU.mult)
        c1 = self.Sp // 128 if self.vstage else self.nkc_max
        S.op("pool", "memset", writes=["V"], ap=self.V[:, 0:c1, 64:128], constant=1.0)
        S.op("pool", "memset", writes=["V"], ap=self.V[:, 0:c1, 256:320], constant=1.0)

    def xk(self, slot, s):
        return ("x", slot, s)

    def norm_to_hT(self, xsl, slot, nsub, evac_eng="dve", scale_all_act=False):
        self.norm_scale(xsl, slot, nsub, scale_all_act)
        self.norm_transposes(nsub, evac_eng)

    def norm_scale(self, xsl, slot, nsub, scale_all_act=False):
        S = self.S
        for s in range(nsub):
            S.op("act", "activation", reads=[self.xk(slot, s)], writes=[("h_tm", s), ("ss", s)], out=self.h_tm[:, s, :], in_=xsl[:, s, :],
                 func=AF.Square, accum_out=self.st[:, s:s + 1])
        rr = self.st[:, 8:8 + nsub]
        S.op("pool", "tensor_scalar", reads=[("ss", s) for s in range(nsub)], writes=["rr"], out=rr, in0=self.st[:, 0:nsub],
             scalar1=1.0 / D, scalar2=RMS_EPS, op0=ALU.mult, op1=ALU.add)
        S.op("pool", "tensor_tensor", reads=["rr", "nhalf"], writes=["rr"], out=rr, in0=rr, in1=self.nhalf[:, 0:nsub], op=ALU.pow)
        for s in range(nsub):
            if s % 2 == 0 or scale_all_act:
                S.op("act", "activation", reads=[self.xk(slot, s), "rr"], writes=[("h_tm", s)], out=self.h_tm[:, s, :],
                     in_=xsl[:, s, :], func=AF.Copy, scale=self.st[:, 8 + s:9 + s])
            else:
                S.op("dve", "tensor_scalar", reads=[self.xk(slot, s), "rr"], writes=[("h_tm", s)], out=self.h_tm[:, s, :],
                     in0=xsl[:, s, :], scalar1=self.st[:, 8 + s:9 + s], scalar2=None, op0=ALU.mult)
        self.phase("n_scale")

    def norm_transposes(self, nsub, evac_eng="dve"):
        S = self.S
        for s in range(nsub):
            b = self.bank()
            pk = ("ps", b)
            pv = self.ps[b][:, :].bitcast(BF16)
            for c in range(8):
                S.op("pe", "transpose", reads=[("h_tm", s), "ident"], writes=[pk], out=pv[:, c * 128:(c + 1) * 128],
                     in_=self.h_tm[:, s, c * 128:(c + 1) * 128], identity=self.ident[:])
            dst = self.hT[:, :, s * 128:(s + 1) * 128]
            src = pv.rearrange("p (c t) -> p c t", c=8)
            if evac_eng == "dve":
                S.op("dve", "tensor_copy", reads=[pk], writes=["hT"], out=dst, in_=src)
            else:
                S.op("act", "activation", reads=[pk], writes=["hT"], out=dst, in_=src, func=AF.Copy)
        self.phase("n_hT")

    def transposes(self, srcs, srckeys, nchunks, dst, dstkey, evac_eng="dve", act_kw=None, col0=0):
        S = self.S
        nsub = len(srcs)
        T = nsub * 128
        for c in range(nchunks):
            b = self.bank()
            pk = ("ps", b)
            pv = self.ps[b][:, :].bitcast(BF16)
            for s in range(nsub):
                S.op("pe", "transpose", reads=[srckeys[s], "ident"], writes=[pk], signal=(s == nsub - 1),
                     out=pv[:, s * 128:(s + 1) * 128], in_=srcs[s][:, c * 128:(c + 1) * 128], identity=self.ident[:])
            if act_kw is not None:
                kw = act_kw(c)
                S.op("act", "activation", reads=[pk] + kw.pop("reads", []), writes=[dstkey], out=dst[:, c, col0:col0 + T],
                     in_=pv[:, 0:T], **kw)
            elif evac_eng == "dve":
                S.op("dve", "tensor_copy", reads=[pk], writes=[dstkey], out=dst[:, c, col0:col0 + T], in_=pv[:, 0:T])
            else:
                S.op("act", "activation", reads=[pk], writes=[dstkey], out=dst[:, c, col0:col0 + T], in_=pv[:, 0:T],
                     func=AF.Copy)

    def tm_proj(self, actT, actkey, chunk_names, nsub, ncb, evac, bias=False):
        S = self.S
        banks = [[self.bank() for cb in range(ncb)] for s in range(nsub)]
        nch = len(chunk_names)
        if bias:
            for s in range(nsub):
                for cb in range(ncb):
                    S.op("pe", "matmul", reads=["ones_b", "bout_b"], writes=[("ps", banks[s][cb])], signal=False,
                         out=self.ps[banks[s][cb]][:, :], lhsT=self.ones_b[0:1, :], rhs=self.bout_b[0:1, cb * 512:(cb + 1) * 512],
                         start=True, stop=False)
        for c, name in enumerate(chunk_names):
            wkey, w = self.fetch(name)
            for s in range(nsub):
                for cb in range(ncb):
                    last = (s == nsub - 1 and cb == ncb - 1)
                    ak = actkey(c) if callable(actkey) else (actkey if isinstance(actkey, list) else [actkey])
                    S.op("pe", "matmul", reads=ak + [wkey], writes=[("ps", banks[s][cb])], signal=last,
                         out=self.ps[banks[s][cb]][:, :], lhsT=actT[:, c, s * 128:(s + 1) * 128],
                         rhs=w[:, cb * 512:(cb + 1) * 512], start=(c == 0 and not bias), stop=(c == nch - 1))
        self.phase("tm_mm")
        for s in range(nsub):
            for cb in range(ncb):
                evac(s, cb, banks[s][cb])
                self.phase("tm_evac1")

    def resid_evac(self, xsl, slot):
        def evac(s, cb, b):
            xkey = self.xk(slot, s)
            self.S.op("dve", "tensor_tensor", reads=[("ps", b), xkey], writes=[xkey], out=xsl[:, s, cb * 512:(cb + 1) * 512],
                      in0=self.ps[b][:, :], in1=xsl[:, s, cb * 512:(cb + 1) * 512], op=ALU.add)
        return evac

    def tmp_fence(self):
        self.S.op("dve", "memset", writes=list(self.tmp_keys), ap=self.st[:, 120:121], constant=0.0)

    def qk_front(self, pviews, pkeys, sq, ksq, qg, kqg, gbc, gkey):
        S = self.S
        off = 0
        for pv, pk in zip(pviews, pkeys):
            w = pv.shape[1]
            S.op("act", "activation", reads=[pk], writes=[ksq], out=sq[:, off:off + w], in_=pv, func=AF.Square)
            S.op("dve", "tensor_tensor", reads=[pk, gkey, ksq], writes=[kqg], out=qg[:, off:off + w].rearrange("p (h d) -> p h d", d=64),
                 in0=pv.rearrange("p (h d) -> p h d", d=64), in1=gbc[:, :].unsqueeze(1).broadcast_to([128, w // 64, 64]), op=ALU.mult)
            off += w

    def qk_chain(self, G, H, sq, ksq, qg, kqg, kh, tbufs, tkeys, rt, rtkey, out, outkeys):
        S = self.S
        GH = G * H
        ssq = self.st[:, 16:16 + GH]
        rs = self.st[:, 32:32 + GH]
        S.op("dve", "tensor_reduce", reads=[ksq], writes=["ssq"], out=ssq, in_=sq.rearrange("p (h d) -> p h d", d=64),
             axis=AX.X, op=ALU.add)
        S.op("pool", "tensor_scalar", reads=["ssq"], writes=["rs"], out=rs, in0=ssq, scalar1=1.0 / HD, scalar2=RMS_EPS,
             op0=ALU.mult, op1=ALU.add)
        S.op("pool", "tensor_tensor", reads=["rs", "nhalf"], writes=["rs"], out=rs, in0=rs, in1=self.nhalf[:, 0:GH], op=ALU.pow)
        v6 = qg.rearrange("p (g h r f d) -> p g h r f d", g=G, h=H, r=2, f=2, d=16)
        shp = [128, G, H, 16]
        for half, eng in ((0, "dve"), (1, ROPE_ENG1)):
            Aa = v6[:, :, :, half, 0, :]
            Bb = v6[:, :, :, half, 1, :]
            if half == 0:
                C = rt[:, :, 0, :].unsqueeze(2).broadcast_to(shp)
                Sn = rt[:, :, 1, :].unsqueeze(2).broadcast_to(shp)
                tk = [rtkey]
            else:
                C = self.ct[:, 0, :].unsqueeze(1).unsqueeze(1).broadcast_to(shp)
                Sn = self.ct[:, 1, :].unsqueeze(1).unsqueeze(1).broadcast_to(shp)
                tk = ["ct"]
            t = [b.rearrange("p (g h d) -> p g h d", g=G, h=H) for b in tbufs[half]]
            k = tkeys[half]
            qk = (kh, half)
            S.op(eng, "tensor_tensor", reads=[kqg, qk] + tk, writes=[k[0]], out=t[0], in0=Aa, in1=C, op=ALU.mult)
            S.op(eng, "tensor_tensor", reads=[kqg, qk] + tk, writes=[k[1]], out=t[1], in0=Aa, in1=Sn, op=ALU.mult)
            S.op(eng, "tensor_tensor", reads=[kqg, qk] + tk, writes=[k[2]], out=t[2], in0=Bb, in1=Sn, op=ALU.mult)
            S.op(eng, "tensor_tensor", reads=[kqg, k[0], k[2]], writes=[qk], out=Aa, in0=t[0], in1=t[2], op=ALU.subtract)
            S.op(eng, "tensor_tensor", reads=[kqg, qk] + tk, writes=[k[0]], out=t[0], in0=Bb, in1=C, op=ALU.mult)
            S.op(eng, "tensor_tensor", reads=[kqg, k[0], k[1]], writes=[qk], out=Bb, in0=t[0], in1=t[1], op=ALU.add)
        S.op("dve", "tensor_tensor", reads=[kqg, (kh, 0), (kh, 1), "rs"], writes=list(outkeys) + [kqg],
             out=out.rearrange("p g (h d) -> p g h d", d=64), in0=qg.rearrange("p (g h d) -> p g h d", g=G, h=H),
             in1=rs.rearrange("p (g h) -> p g h", g=G).unsqueeze(3).broadcast_to([128, G, H, 64]), op=ALU.mult)

    def qk_post(self, pviews, pkeys, G, H, gbc, gkey, rt, rtkey, out, outkeys):
        n = G * H * 64
        GH = G * H
        sq, qg = self.sq[:, 0:n], self.qg[:, 0:n]
        self.qk_front(pviews, pkeys, sq, "k_sq", qg, "k_qg", gbc, gkey)
        tb = [[self.rtmp[:, (hf * 3 + i) * 256:(hf * 3 + i) * 256 + GH * 16] for i in range(3)] for hf in range(2)]
        tkk = [[("k_rt", hf * 3 + i) for i in range(3)] for hf in range(2)]
        self.qk_chain(G, H, sq, "k_sq", qg, "k_qg", "k_qgh", tb, tkk, rt, rtkey, out, outkeys)

    def q_bufs(self, s_, cb):
        hi = self.hi
        tb = [hi[:, 3072 + i * 128:3072 + (i + 1) * 128] for i in range(3)]
        tk = [("qq_rt", i) for i in range(3)]
        if cb == 1:
            return (hi[:, s_ * 512:(s_ + 1) * 512], ("qq_sq1", s_), self.hTf[:, s_ * 512:(s_ + 1) * 512], ("qq_qg1", s_),
                    ("qq_qgh1", s_), [tb, tb], [tk, tk])
        return (hi[:, 2048:2560], "qq_sq0", hi[:, 2560:3072], "qq_qg0", "qq_qgh0", [tb, tb], [tk, tk])

    def k_transposes(self, g, nsub):
        S = self.S
        T = nsub * 128
        kt = self.ktm2[g % 2]
        kk = ("ktm", g % 2)
        for m in range(2):
            b = self.bank()
            pk = ("ps", b)
            pv = self.ps[b][:, :].bitcast(BF16)
            for s in range(nsub):
                S.op("pe", "transpose", reads=[kk, "ident"], writes=[pk],
                     out=pv[:, s * 128:(s + 1) * 128], in_=kt[:, s, m * 128:(m + 1) * 128], identity=self.ident[:])
            S.op("dve", "tensor_copy", reads=[pk], writes=["KT"], out=self.KT[m][:, g * 512:g * 512 + T], in_=pv[:, 0:T])

    def stage1(self, xsrc, S_len):
        S = self.S
        ngrp = (S_len + 511) // 512
        pending = None
        self.tmp_fence()

        def load_x(g):
            nsub_ = min(4, (S_len - g * 512) // 128)
            for s in range(nsub_):
                S.dma("sp", out=self.xsl[g % 2][:, s, :], in_=xsrc[g * 512 + s * 128:g * 512 + (s + 1) * 128, :],
                      writes=[self.xk(g % 2, s)], sem=("xld", g % 2, s))

        load_x(0)
        self.norm_scale(self.xsl[0], 0, min(4, S_len // 128), True)
        for g in range(ngrp):
            nsub = min(4, (S_len - g * 512) // 128)
            slot = g % 2
            xsl = self.xsl[slot]
            rtv, rtk = self.load_rt(self.rt_all_d, g * 4, nsub)
            if g + 1 < ngrp:
                load_x(g + 1)
            self.norm_transposes(nsub, "act")
            kbanks = []
            self.tm_proj(self.hT, "hT", [("kv", kc) for kc in range(8)], nsub, 1, lambda s, cb, b, kb=kbanks: kb.append(b))
            if pending is not None:
                self.k_transposes(*pending)
            if g + 1 < ngrp:
                self.norm_scale(self.xsl[(g + 1) % 2], (g + 1) % 2, min(4, (S_len - (g + 1) * 512) // 128), True)
            for s, b in enumerate(kbanks):
                pv = self.ps[b]
                pk = ("ps", b)
                j = g * 4 + s
                for (dst0, src0) in ((0, 256), (128, 320), (192, 448), (320, 384)):
                    S.op("act", "activation", reads=[pk], writes=["V"], out=self.V[:, j, dst0:dst0 + 64], in_=pv[:, src0:src0 + 64],
                         func=AF.Copy)
            self.qk_post([self.ps[bb][:, 0:256] for bb in kbanks], [("ps", bb) for bb in kbanks], nsub, 4, self.gk, "gk",
                         rtv[:, 0:nsub], rtk, self.ktm2[g % 2][:, 0:nsub, :], [("ktm", g % 2)])
            pending = (g, nsub)
            self.phase("s1")
        self.k_transposes(*pending)

    def ffn(self, l, xsl, slot, nsub):
        S = self.S
        T = nsub * 128
        self.norm_to_hT(xsl, slot, nsub)
        for fc in range(NFC):
            kg, wg = self.fetch(("g", l, fc))
            ku, wu = self.fetch(("u", l, fc))
            bg, bu = self.bank(), self.bank()
            for (kk, w, b) in ((kg, wg, bg), (ku, wu, bu)):
                for kc in range(8):
                    S.op("pe", "matmul", reads=["hT", kk], writes=[("ps", b)], signal=(kc == 7), out=self.ps[b][:, 0:T],
                         lhsT=w[:, kc * 128:(kc + 1) * 128], rhs=self.hT[:, kc, 0:T], start=(kc == 0), stop=(kc == 7))
            sgi = fc % 2
            S.op("act", "activation", reads=[("ps", bg)], writes=[("PT", 2 * sgi), ("PT", 2 * sgi + 1)], out=self.sg[sgi][:, 0:T], in_=self.ps[bg][:, 0:T],
                 func=AF.Silu)
            S.op("dve", "tensor_tensor", reads=[("ps", bu), ("PT", 2 * sgi), ("PT", 2 * sgi + 1)], writes=[self.hkey(fc)], out=self.hidT[:, fc, 0:T],
                 in0=self.ps[bu][:, 0:T], in1=self.sg[sgi][:, 0:T], op=ALU.mult)
        self.tm_proj(self.hidT, ["Ra", "Rb", "Rc", "Rd"], [("d", l, fc) for fc in range(NFC)], nsub, 2, self.resid_evac(xsl, slot))

    def attention(self, nsub, nkc, hook=None):
        S = self.S
        T = nsub * 128
        sbanks = [(0, 1), (2, 3), (4, 5)]
        oa, ob = 6, 7
        if nsub == 1:
            N = 512
            units = []
            for m in range(2):
                qa = self.Rf[0:64, m * 512:(m + 1) * 512]
                qb = self.Rf[64:128, m * 512:(m + 1) * 512]
                units.append((m, qa, qb, lambda lo, hi, m=m: self.Rf[lo:hi, 4096 + m * 512:4096 + (m + 1) * 512], "Rb"))
        else:
            N = T
            units = []
            for pr in range(8):
                units.append((pr // 4, self.qT[0:64, pr, 0:T], self.qT[64:128, pr, 0:T],
                              lambda lo, hi, pr=pr: self.oT[lo:hi, pr, 0:T], ("oT", pr)))
        items = [(u, c) for u in range(len(units)) for c in range(nkc)]

        def vviews(m, c):
            if m == 0:
                return self.V[:, c, 0:128], self.V[:, c, 64:192]
            return self.V[:, c, 256:384], self.V[:, c, 192:320]

        def s_mm(k):
            u, c = items[k]
            m, qa, qb, _, _ = units[u]
            ba, bb = sbanks[k % 3]
            qk = "Ra" if nsub == 1 else ("qT", u)
            if hook is not None and c == 0:
                hook(u, (ba, bb))
            S.op("pe", "matmul", reads=["KT", qk], writes=[("ps", ba)], out=self.ps[ba][:, 0:N],
                 lhsT=self.KT[m][0:64, c * 128:(c + 1) * 128], rhs=qa, start=True, stop=True, tile_position=(0, 0))
            S.op("pe", "matmul", reads=["KT", qk], writes=[("ps", bb)], out=self.ps[bb][:, 0:N],
                 lhsT=self.KT[m][64:128, c * 128:(c + 1) * 128], rhs=qb, start=True, stop=True, tile_position=(64, 0))

        def finish(u):
            m, _, _, oview, ok = units[u]
            a_o, b_o = (0, 64) if m == 0 else (64, 0)
            S.op("dve", "tensor_copy", reads=[("ps", oa)], writes=[("oraw", 0)], out=self.oraw[:, 0, 0:N], in_=self.ps[oa][:, 0:N])
            S.op("dve", "tensor_copy", reads=[("ps", ob)], writes=[("oraw", 1)], out=self.oraw[:, 1, 0:N], in_=self.ps[ob][:, 0:N])
            for h, o_off in enumerate((a_o, b_o)):
                s_off = 64 - o_off
                rk = ("rec", o_off)
                S.op("dve", "reciprocal", reads=[("oraw", h)], writes=[rk], out=self.rec[o_off:o_off + 64, 0:N],
                     in_=self.oraw[s_off:s_off + 64, h, 0:N])
                S.op("dve", "tensor_tensor", reads=[("oraw", h), rk], writes=[ok], out=oview(o_off, o_off + 64),
                     in0=self.oraw[o_off:o_off + 64, h, 0:N], in1=self.rec[o_off:o_off + 64, 0:N], op=ALU.mult)

        def pv_mm(k):
            u, c = items[k]
            m = units[u][0]
            ba, bb = sbanks[k % 3]
            pa = (k % 3) * 2
            va, vb = vviews(m, c)
            if N == 512 and bb == ba + 1:
                S.op("act", "activation", reads=[("ps", ba), ("ps", bb)], writes=[("PT", pa), ("PT", pa + 1)],
                     out=self.PT[:, pa:pa + 2, :].rearrange("p a b -> p (a b)"), in_=self.psall[:, ba * 512:(ba + 2) * 512],
                     func=AF.Exp)
            else:
                S.op("act", "activation", reads=[("ps", ba)], writes=[("PT", pa)], out=self.PT[:, pa, 0:N], in_=self.ps[ba][:, 0:N],
                     func=AF.Exp)
                S.op("act", "activation", reads=[("ps", bb)], writes=[("PT", pa + 1)], out=self.PT[:, pa + 1, 0:N],
                     in_=self.ps[bb][:, 0:N], func=AF.Exp)
            S.op("pe", "matmul", reads=["V", ("PT", pa)], writes=[("ps", oa)], out=self.ps[oa][:, 0:N],
                 lhsT=va, rhs=self.PT[:, pa, 0:N], start=(c == 0), stop=(c == nkc - 1))
            S.op("pe", "matmul", reads=["V", ("PT", pa + 1)], writes=[("ps", ob)], out=self.ps[ob][:, 0:N],
                 lhsT=vb, rhs=self.PT[:, pa + 1, 0:N], start=(c == 0), stop=(c == nkc - 1))
            if c == nkc - 1:
                finish(u)

        n = len(items)
        for k in range(min(2, n)):
            s_mm(k)
        for k in range(n):
            if k + 2 < n:
                s_mm(k + 2)
            pv_mm(k)

    def bank_t(self):
        b = self._t_rr % 6
        self._t_rr += 1
        return b

    def okey(self, pr):
        return "Rb" if pr < 4 else "Rc"

    def hkey(self, fc):
        return "Ra" if fc < 8 else ("Rb" if fc < 12 else ("Rc" if fc < 16 else "Rd"))

    def bank_o(self, pr):
        return (6, 7)

    def S_a(self, xsrc_ap, slot, nsub, nkc, rt_tab, rt_key, rt_j0, u_dst, u_key, u_col0):
        S = self.S
        T = nsub * 128
        xsl = self.xsl[slot]
        for s in range(nsub):
            S.dma("sp", out=xsl[:, s, :], in_=xsrc_ap[s * 128:(s + 1) * 128, :], writes=[self.xk(slot, s)], sem=("xld", slot, s))
        rtv, rtk = self.load_rt(rt_tab, rt_j0, nsub)
        self.norm_to_hT(xsl, slot, nsub)

        qb = {}
        self.tm_proj(self.hT, "hT", [("q", kc) for kc in range(8)], nsub, 2, lambda s_, cb, b: qb.__setitem__((s_, cb), b))

        def q_front(s_, cb):
            b = qb[(s_, cb)]
            sq, ksq, qg, kqg, kh, tb, tk = self.q_bufs(s_, cb)
            self.qk_front([self.ps[b][:, :]], [("ps", b)], sq, ksq, qg, kqg, self.gq, "gq")

        def q_chain(s_, cb):
            sq, ksq, qg, kqg, kh, tb, tk = self.q_bufs(s_, cb)
            self.qk_chain(1, 8, sq, ksq, qg, kqg, kh, tb, tk, rtv[:, s_:s_ + 1], rtk,
                          self.h_tm[:, s_:s_ + 1, cb * 512:(cb + 1) * 512], [("h_tmh", s_, cb)])

        def q_transposes(cb, free_banks=None):
            qdst = self.qTh if nsub == 1 else self.qT
            for c in range(4 * cb, 4 * cb + 4):
                if free_banks is None:
                    b = qb[((c - 4 * cb) % nsub, cb)]
                else:
                    b = free_banks[c % len(free_banks)]
                pk = ("ps", b)
                pv = self.ps[b][:, :].bitcast(BF16)
                for s_ in range(nsub):
                    S.op("pe", "transpose", reads=[("h_tmh", s_, cb), "ident"], writes=[pk], out=pv[:, s_ * 128:(s_ + 1) * 128],
                         in_=self.h_tm[:, s_, c * 128:(c + 1) * 128], identity=self.ident[:])
                S.op("dve", "tensor_copy", reads=[pk], writes=[("qT", c)], out=qdst[:, c, 0:T], in_=pv[:, 0:T])

        self._t_rr = 0
        self.tmp_fence()
        for s_ in range(nsub):
            q_front(s_, 1)
        for s_ in range(nsub):
            q_front(s_, 0)
            q_chain(s_, 0)
        q_transposes(0)
        for s_ in range(nsub):
            q_chain(s_, 1)
        self.phase("q")
        self._ps_rr = 0
        if nsub == 1:
            q_transposes(1)
            self.attention(nsub, nkc)
        else:
            self.attention(nsub, nkc, hook=lambda u, fb: q_transposes(1, fb) if u == 4 else None)
        self.phase("att")
        self.tm_proj(self.oTh if nsub == 1 else self.oT, ["Rb", "Rc"] if nsub == 1 else (lambda c: [("oT", c)]),
                     [("o", pr) for pr in range(8)], nsub, 2, self.resid_evac(xsl, slot))
        self.phase("wo")
        self.ffn(0, xsl, slot, nsub)
        self.phase("ffn0")
        self.norm_to_hT(xsl, slot, nsub)
        for cc in range(8):
            ka, wa = self.fetch(("ci", cc))
            kg, wg = self.fetch(("ci", 8 + cc))
            ba, bg = self.bank(), self.bank()
            for (kk, w, b) in ((ka, wa, ba), (kg, wg, bg)):
                for kc in range(8):
                    S.op("pe", "matmul", reads=["hT", kk], writes=[("ps", b)], signal=(kc == 7), out=self.ps[b][:, 0:T],
                         lhsT=w[:, kc * 128:(kc + 1) * 128], rhs=self.hT[:, kc, 0:T], start=(kc == 0), stop=(kc == 7))
            sgi = cc % 2
            S.op("act", "activation", reads=[("ps", bg), "fvec"], writes=[("PT", 2 * sgi), ("PT", 2 * sgi + 1)], out=self.sg[sgi][:, 0:T],
                 in_=self.ps[bg][:, 0:T], func=AF.Sigmoid, bias=self.fvec[:, 8 + cc:9 + cc])
            S.op("dve", "scalar_tensor_tensor", reads=[("ps", ba), ("PT", 2 * sgi), ("PT", 2 * sgi + 1), "fvec"], writes=[u_key],
                 out=u_dst[:, cc, u_col0:u_col0 + T], in0=self.ps[ba][:, 0:T], scalar=self.fvec[:, cc:cc + 1],
                 in1=self.sg[sgi][:, 0:T], op0=ALU.add, op1=ALU.mult)

    def S_b(self, slot, uslot, nsub, ydst_ap):
        S = self.S
        T = nsub * 128
        xsl = self.xsl[slot]
        U = self.U[uslot]
        ukey = ("U", uslot)
        for cc in range(8):
            dg = self.diag[cc % 2]
            dk = self.diag_keys[cc % 2]
            S.op("dve", "tensor_tensor", reads=["ident", "dww"], writes=dk, out=dg,
                 in0=self.ident[:, :].unsqueeze(1).broadcast_to([128, CW, 128]),
                 in1=self.dww[:, cc, :].unsqueeze(2).broadcast_to([128, CW, 128]), op=ALU.mult)
            b = self.bank()
            for j in range(CW):
                S.op("pe", "matmul", reads=dk + [ukey], writes=[("ps", b)], out=self.ps[b][:, 0:T],
                     lhsT=dg[:, j, :], rhs=U[:, cc, j:j + T], start=(j == 0), stop=(j == CW - 1))
            S.op("act", "activation", reads=[("ps", b), "fvec"], writes=["hT"], out=self.hT[:, cc, 0:T], in_=self.ps[b][:, 0:T],
                 func=AF.Identity, bias=self.fvec[:, 16 + cc:17 + cc])
        self.phase("conv")
        lnb = []
        for s in range(nsub):
            b = self.bank()
            pk = ("ps", b)
            pv = self.ps[b][:, :].bitcast(BF16)
            for cc in range(8):
                S.op("pe", "transpose", reads=["hT", "ident"], writes=[pk], out=pv[:, cc * 128:(cc + 1) * 128],
                     in_=self.hT[:, cc, s * 128:(s + 1) * 128], identity=self.ident[:])
            S.op("act", "activation", reads=[pk], writes=[("h_tm", s), ("lns", s)], out=self.h_tm[:, s, :], in_=pv, func=AF.Identity,
                 accum_out=self.st[:, 48 + s:49 + s])
            S.op("act", "activation", reads=[pk], writes=[("h_tm", s), ("lns", s)], out=self.h_tm[:, s, :], in_=pv, func=AF.Square,
                 accum_out=self.st[:, 52 + s:53 + s])
            lnb.append((pk, pv))
        lk = [("lns", s) for s in range(nsub)]
        mean = self.st[:, 56:56 + nsub]
        var = self.st[:, 60:60 + nsub]
        msq = self.st[:, 64:64 + nsub]
        S.op("dve", "tensor_scalar", reads=lk, writes=["lnm"], out=mean, in0=self.st[:, 48:48 + nsub], scalar1=1.0 / D,
             scalar2=None, op0=ALU.mult)
        S.op("dve", "tensor_tensor", reads=["lnm"], writes=["lnq"], out=msq, in0=mean, in1=mean, op=ALU.mult)
        S.op("dve", "scalar_tensor_tensor", reads=lk + ["lnq"], writes=["lnv"], out=var, in0=self.st[:, 52:52 + nsub], scalar=1.0 / D,
             in1=msq, op0=ALU.mult, op1=ALU.subtract)
        S.op("pool", "tensor_scalar", reads=["lnv"], writes=["lnv"], out=var, in0=var, scalar1=LN_EPS, scalar2=None, op0=ALU.add)
        S.op("pool", "tensor_tensor", reads=["lnv", "nhalf"], writes=["lnv"], out=var, in0=var, in1=self.nhalf[:, 0:nsub], op=ALU.pow)
        for s in range(nsub):
            pk, pv = lnb[s]
            S.op("dve", "tensor_scalar", reads=[pk, "lnm", "lnv"], writes=[("h_tm", s)], out=self.h_tm[:, s, :], in0=pv,
                 scalar1=self.st[:, 56 + s:57 + s], scalar2=self.st[:, 60 + s:61 + s], op0=ALU.subtract, op1=ALU.mult)
        self.transposes([self.h_tm[:, s, :] for s in range(nsub)], [("h_tm", s) for s in range(nsub)], 8, self.hT, "hT",
                        act_kw=lambda c: dict(func=AF.Silu, scale=self.fvec[:, 24 + c:25 + c], bias=self.fvec[:, 32 + c:33 + c],
                                              reads=["fvec"]))
        self.phase("ln")
        self.tm_proj(self.hT, "hT", [("co", cc) for cc in range(8)], nsub, 2, self.resid_evac(xsl, slot), bias=True)
        self.phase("co")
        self.ffn(1, xsl, slot, nsub)
        self.phase("ffn1")
        for s in range(nsub):
            S.op("act", "activation", reads=[self.xk(slot, s)], writes=[("h_tm", s), ("fs", s)], out=self.h_tm[:, s, :], in_=xsl[:, s, :],
                 func=AF.Square, accum_out=self.st[:, 68 + s:69 + s])
        fr = self.st[:, 72:72 + nsub]
        S.op("pool", "tensor_scalar", reads=[("fs", s) for s in range(nsub)], writes=["fr"], out=fr, in0=self.st[:, 68:68 + nsub],
             scalar1=1.0 / D, scalar2=RMS_EPS, op0=ALU.mult, op1=ALU.add)
        S.op("pool", "tensor_tensor", reads=["fr", "nhalf"], writes=["fr"], out=fr, in0=fr, in1=self.nhalf[:, 0:nsub], op=ALU.pow)
        for s in range(nsub):
            xkey = self.xk(slot, s)
            S.op("dve", "scalar_tensor_tensor", reads=[xkey, "fr", "gf"], writes=[xkey], out=xsl[:, s, :], in0=xsl[:, s, :],
                 scalar=self.st[:, 72 + s:73 + s], in1=self.gf[:], op0=ALU.mult, op1=ALU.mult)
            S.dma("sp", out=ydst_ap[s * 128:(s + 1) * 128, :], in_=xsl[:, s, :], reads=[xkey], sem=("yst", slot, s))
        self.phase("fin")

    def run_pass(self, xkv, S_len, xown, n_own, rt_tab, rt_key, ydst, has_halo):
        S = self.S
        nkc = S_len // 128
        self.stage1(xkv, S_len)
        ntile = n_own // 512
        if has_halo:
            self.S_a(xown[n_own:n_own + 128, :], 0, 1, nkc, rt_tab, rt_key, n_own // 128, self.UH, "UH", 0)
            S.op("pool", "tensor_scalar", reads=["UH", "flags"], writes=[("U", 0)], out=self.U[0][:, :, 0:CP],
                 in0=self.UH[:, :, 64 - CP:64], scalar1=self.flags[:, 0:1], scalar2=None, op0=ALU.mult)
        else:
            S.op("pool", "memset", writes=[("U", 0)], ap=self.U[0][:, :, 0:CP], constant=0.0)
        for i in range(ntile + 1):
            if i < ntile:
                us = i % 2
                self.extra_stage = (i == 0 and not has_halo)
                self.S_a(xown[i * 512:(i + 1) * 512, :], i % 2, 4, nkc, rt_tab, rt_key, i * 4, self.U[us], ("U", us), CP)
                self.extra_stage = False
                if i > 0:
                    S.op("pool", "tensor_copy", reads=[("U", 1 - us)], writes=[("U", us)], out=self.U[us][:, :, 0:CP],
                         in_=self.U[1 - us][:, :, 512:512 + CP])
                    S.op("pool", "tensor_copy", reads=[("U", us)], writes=[("U", 1 - us)], out=self.U[1 - us][:, :, 512 + CP:512 + 2 * CP],
                         in_=self.U[us][:, :, CP:2 * CP])
            if i == ntile:
                us = (ntile - 1) % 2
                if has_halo:
                    S.op("pool", "tensor_scalar", reads=["UH", "flags"], writes=[("U", us)], out=self.U[us][:, :, 512 + CP:512 + 2 * CP],
                         in0=self.UH[:, :, 64:64 + CP], scalar1=self.flags[:, 1:2], scalar2=None, op0=ALU.mult)
                else:
                    S.op("pool", "memset", writes=[("U", us)], ap=self.U[us][:, :, 512 + CP:512 + 2 * CP], constant=0.0)
            if i >= 1:
                j = i - 1
                self.S_b(j % 2, j % 2, 4, ydst[j * 512:(j + 1) * 512, :])

    def build(self):
        self.declare()
        self.alloc()
        self.make_chunks()
        self.setup()
        try:
            self.phase("setup")
            self.v_stage_on = True
            self.run_pass(self.xp, self.Sp, self.xp, self.Sp, self.rt_all_d, "rt", self.yp, False)
            self.v_stage_on = False
            if self.vstage:
                vk = ["V"] + [k for k, _ in self.vstage]
                c0 = self.Sp // 128
                self.S.op("pool", "memset", writes=vk, ap=self.V[:, c0:, 64:128], constant=1.0)
                self.S.op("pool", "memset", writes=vk, ap=self.V[:, c0:, 256:320], constant=1.0)
            self.phase("passP")
            self.run_pass(self.xs, self.Ss, self.xq, self.Q, self.rt_q_d, "rt", self.yq, True)
        except StopBuild:
            print("STOPPED at", self.stop_at)
        self.flush_stores(0)
        self.S.final_wait("pool")
        self.S.final_wait("sp")
        self.S.emit()
        print("instructions", self.S.nins, dict(self.S.nops), "signals", dict(self.S.cnt))
        return self.nc


def _rope_tabs(pos_rows):
    inv = (10000.0 ** (-np.arange(0, 32, 2, dtype=np.float64) / 32.0))
    ang = pos_rows[..., None].astype(np.float64) * inv
    return np.cos(ang).astype(np.float32), np.sin(ang).astype(np.float32)


def host_layout(inp, Sp, Ss):
    Q = Ss // 4
    f = lambda a: np.ascontiguousarray(np.asarray(a, dtype=np.float32))
    wqkv = f(inp["w_qkv"])[0]
    heads = [8 * m + 4 * hh + i for m in range(2) for i in range(4) for hh in range(2)]
    heads_o = []
    for m in range(2):
        for i in range(4):
            a, b = 8 * m + i, 8 * m + 4 + i
            heads_o += [a, b] if m == 0 else [b, a]
    qcols = np.concatenate([np.arange(h * 64, (h + 1) * 64) for h in heads])
    orows = np.concatenate([np.arange(h * 64, (h + 1) * 64) for h in heads_o])
    shared = {}
    shared["w_q"] = f(wqkv[:, :1024][:, qcols].reshape(8, 128, 1024))
    shared["w_kv"] = f(wqkv[:, 1024:].reshape(8, 128, 512))
    shared["w_o"] = f(f(inp["w_o"])[0][orows].reshape(8, 128, 1024))

    def up(w):
        Fo = w.shape[1]
        return f(w.reshape(8, 128, Fo // 128, 128).transpose(2, 1, 0, 3).reshape(Fo // 128, 128, 1024))
    shared["w_g"] = f(np.stack([up(f(inp["w_gate"])[l]) for l in range(2)]))
    shared["w_u"] = f(np.stack([up(f(inp["w_up"])[l]) for l in range(2)]))
    shared["w_d"] = f(f(inp["w_down"]).reshape(2, NFC, 128, 1024))
    shared["w_ci"] = up(f(inp["conv_w_in"])[0])
    shared["w_co"] = f(f(inp["conv_w_out"])[0].reshape(8, 128, 1024))
    col = lambda v: f(v).reshape(-1, 128).T
    gcols = np.stack([col(f(inp["attn_norm_g"])[0]), col(f(inp["ffn_norm_g"])[0]), col(f(inp["conv_norm_g"])[0]),
                      col(f(inp["ffn_norm_g"])[1])], axis=1)
    shared["gcols"] = f(gcols)
    shared["fvec"] = f(np.concatenate([col(f(inp["conv_b_in"])[0]), col(f(inp["dw_b"])[0]), col(f(inp["conv_ln_g"])[0]),
                                       col(f(inp["conv_ln_b"])[0])], axis=1))
    shared["dww"] = f(f(inp["dw_w"])[0].T.reshape(8, 128, CW).transpose(1, 0, 2))
    shared["gq"] = f(inp["q_norm_g"]).reshape(1, 64)
    shared["gk"] = f(inp["k_norm_g"]).reshape(1, 64)
    shared["gf"] = f(inp["final_norm_g"]).reshape(1, D)
    shared["bout"] = f(inp["conv_b_out"]).reshape(1, D)
    p = np.arange(128)
    j = np.arange(Ss // 128)
    rows = 2 * j[None, :] + (p[:, None] >= 64)
    c, s = _rope_tabs(rows)
    shared["rt_all"] = f(np.stack([c, s], axis=2))
    c, s = _rope_tabs(p % 64)
    shared["ct"] = f(np.stack([c, s], axis=1))
    xp = f(inp["x_prompt"])
    xs = f(inp["x_sample"])
    maps = []
    for core in range(NCORES):
        sb, qi = core // 4, core % 4
        qs = qi * Q
        m = dict(shared)
        m["xp"] = xp[core]
        m["xs"] = xs[sb]
        halo = np.zeros((128, D), np.float32)
        if qs > 0:
            halo[0:64] = xs[sb, qs - 64:qs]
        if qs + Q < Ss:
            halo[64:128] = xs[sb, qs + Q:qs + Q + 64]
        m["xq"] = f(np.concatenate([xs[sb, qs:qs + Q], halo], axis=0))
        jq = np.arange(Q // 128)
        rows = np.concatenate([2 * (jq[None, :] + qs // 128) + (p[:, None] >= 64),
                               np.where(p[:, None] >= 64, (qs + Q) // 64, qs // 64 - 1)], axis=1)
        c, s = _rope_tabs(rows)
        m["rt_q"] = f(np.stack([c, s], axis=2))
        fl = np.zeros((128, 2), np.float32)
        fl[:, 0] = 1.0 if qs > 0 else 0.0
        fl[:, 1] = 1.0 if qs + Q < Ss else 0.0
        m["flags"] = fl
        maps.append(m)
    return maps


_CACHE = {}


def kernel(**inputs):
    xp = np.asarray(inputs["x_prompt"])
    xs = np.asarray(inputs["x_sample"])
    Sp, Ss = xp.shape[1], xs.shape[1]
    Q = Ss // 4
    key = (Sp, Ss)
    if key not in _CACHE:
        _CACHE[key] = Builder(Sp, Ss).build()
    nc = _CACHE[key]
    maps = host_layout(inputs, Sp, Ss)
    res = run_bass_kernel_spmd(nc, maps, core_ids=list(range(NCORES)))
    yp = np.stack([np.asarray(res.results[c]["yp"], dtype=np.float32) for c in range(NCORES)], axis=0)
    ys = np.zeros((2, Ss, D), np.float32)
    for c in range(NCORES):
        sb, qi = c // 4, c % 4
        ys[sb, qi * Q:(qi + 1) * Q] = np.asarray(res.results[c]["yq"], dtype=np.float32)
    return yp, ys
```
